# Optimizing a Trainium2 kernel written in Bass

```python
import jax, jax.numpy as jnp
from jax import lax
import numpy as np

D_MODEL = 1024
BATCH = 2
SEQ = 8192
DEPTH = 2

CTX_LEN = 256
GRID_W = 64

SGU_HEADS = 4
SGU_HEAD_DIM = 64
SGU_W = SGU_HEADS * SGU_HEAD_DIM
CHUNK = 128
FNET_GROUPS = 4
FNET_GROUP_DIM = 64
FNET_W = FNET_GROUPS * FNET_GROUP_DIM
MLA_HEADS = 4
QK_NOPE_DIM = 128
QK_ROPE_DIM = 64
QK_DIM = QK_NOPE_DIM + QK_ROPE_DIM
V_DIM = 128
Q_LORA = 256
KV_LORA = 128
MLA_W = MLA_HEADS * V_DIM
Q_BLOCK = 128
ROPE_THETA = 10000.0
OFF_U = 0
OFF_V = OFF_U + SGU_W
OFF_F = OFF_V + SGU_W
OFF_Q = OFF_F + FNET_W
OFF_KV = OFF_Q + Q_LORA
OFF_KR = OFF_KV + KV_LORA
IN_W = OFF_KR + QK_ROPE_DIM
MIX_W = SGU_W + FNET_W + MLA_W
N_EXPERTS = 16
EC_FACTOR = 2
D_EXPERT = 512
EPS = 1e-6

kernel_name = "hybrid_dit_sgu_fnet_mla_ecmoe"


def rms_norm(x):
    xf = x.astype(jnp.float32)
    return (xf * lax.rsqrt(jnp.mean(xf * xf, axis=-1, keepdims=True) + EPS)).astype(x.dtype)


def ada_modulation(cond, w_ada, b_ada):
    m = (jax.nn.silu(cond) @ w_ada + b_ada)[:, None, :]
    return jnp.split(m, 6, axis=-1)


def modulate(x, shift, scale):
    return rms_norm(x) * (1 + scale) + shift


def rope_1d(x, pos):
    half = x.shape[-1] // 2
    freqs = ROPE_THETA ** (-jnp.arange(half, dtype=jnp.float32) / half)
    ang = pos.astype(jnp.float32)[:, None] * freqs
    cos = jnp.cos(ang)[:, None, :]
    sin = jnp.sin(ang)[:, None, :]
    xf = x.astype(jnp.float32)
    x1, x2 = xf[..., :half], xf[..., half:]
    return jnp.concatenate([x1 * cos - x2 * sin, x1 * sin + x2 * cos], axis=-1).astype(x.dtype)


def rope_2d(x, pos_row, pos_col):
    half = x.shape[-1] // 2
    return jnp.concatenate([rope_1d(x[..., :half], pos_row), rope_1d(x[..., half:], pos_col)], axis=-1)


def with_rope(t, pos_row, pos_col):
    return jnp.concatenate([t[..., :QK_NOPE_DIM], rope_2d(t[..., QK_NOPE_DIM:], pos_row, pos_col)], axis=-1)


def chunk_mlp(pu, pv, sgu_norm, w_sgu, b_sgu):
    b, n, _ = pu.shape
    u = jax.nn.gelu(pu)
    v = rms_norm(jax.nn.gelu(pv)) * sgu_norm
    v = v.reshape(b, n // CHUNK, CHUNK, SGU_HEADS, SGU_HEAD_DIM)
    z = jnp.einsum('hpq,bcqhd->bcphd', w_sgu, v) + b_sgu.T[:, :, None]
    return u * z.reshape(b, n, SGU_W)


def fourier_mix(pf):
    b, n, _ = pf.shape
    f = pf.reshape(b, n, FNET_GROUPS, FNET_GROUP_DIM).astype(jnp.float32)
    y = jnp.fft.fft2(f, axes=(1, 3), norm="ortho").real
    return y.reshape(b, n, FNET_W).astype(pf.dtype)


def mla_queries(pq, q_lora_norm, w_uq, q_norm):
    b, n, _ = pq.shape
    cq = rms_norm(pq) * q_lora_norm
    q = (cq @ w_uq).reshape(b, n, MLA_HEADS, QK_DIM)
    return rms_norm(q) * q_norm


def mla_keys_values(pkv, pkr, kv_lora_norm, w_ukv, k_norm):
    b, n, _ = pkv.shape
    ckv = rms_norm(pkv) * kv_lora_norm
    kv = (ckv @ w_ukv).reshape(b, n, MLA_HEADS, QK_NOPE_DIM + V_DIM)
    k_nope, v = kv[..., :QK_NOPE_DIM], kv[..., QK_NOPE_DIM:]
    k_rope = jnp.broadcast_to(pkr[:, :, None, :], (b, n, MLA_HEADS, QK_ROPE_DIM))
    k = rms_norm(jnp.concatenate([k_nope, k_rope], axis=-1)) * k_norm
    return k, v


def block_attention(q, k, v):
    b, n, h, _ = q.shape
    scale = QK_DIM ** -0.5
    qb = jnp.moveaxis(q.reshape(b, n // Q_BLOCK, Q_BLOCK, h, QK_DIM), 1, 0)

    def one_block(qblk):
        s = jnp.einsum('bqhd,bkhd->bhqk', qblk, k).astype(jnp.float32) * scale
        p = jax.nn.softmax(s, axis=-1).astype(v.dtype)
        return jnp.einsum('bhqk,bkhv->bqhv', p, v)

    o = lax.map(one_block, qb)
    return jnp.moveaxis(o, 0, 1).reshape(b, n, h * V_DIM)


def local_head_groups(p, sgu_norm, w_sgu, b_sgu, q_lora_norm, w_uq, q_norm):
    ya = chunk_mlp(p[..., OFF_U:OFF_V], p[..., OFF_V:OFF_F], sgu_norm, w_sgu, b_sgu)
    yb = fourier_mix(p[..., OFF_F:OFF_Q])
    q = mla_queries(p[..., OFF_Q:OFF_KV], q_lora_norm, w_uq, q_norm)
    return ya, yb, q


def expert_choice_moe(h, w_router, w_gate, w_up, w_down):
    b, n, d = h.shape
    cap = EC_FACTOR * n // N_EXPERTS
    aff = jax.nn.softmax(jnp.einsum('bnd,de->bne', h, w_router).astype(jnp.float32), axis=-1)
    g, idx = lax.top_k(jnp.swapaxes(aff, 1, 2), cap)
    xs = jax.vmap(lambda hb, ib: hb[ib])(h, idx)
    hid = jax.nn.silu(jnp.einsum('becd,edf->becf', xs, w_gate)) * jnp.einsum('becd,edf->becf', xs, w_up)
    y = jnp.einsum('becf,efd->becd', hid, w_down) * g[..., None].astype(h.dtype)
    return jax.vmap(lambda yb, ib: jnp.zeros((n, d), h.dtype).at[ib.reshape(-1)].add(yb.reshape(-1, d)))(y, idx)


def setup_inputs(seed: int = 0) -> dict:
    key = jax.random.key(seed)
    ks = jax.random.split(key, 22)

    def nrm(k, shape, scale):
        return jax.random.normal(k, shape, jnp.float32) * scale

    return {
        "x": nrm(ks[0], (BATCH, SEQ, D_MODEL), 1.0),
        "c": nrm(ks[1], (BATCH, D_MODEL), 1.0),
        "ctx": nrm(ks[2], (BATCH, CTX_LEN, D_MODEL), 1.0),
        "c_ctx": nrm(ks[3], (D_MODEL,), 1.0),
        "w_ada": nrm(ks[4], (DEPTH, D_MODEL, 6 * D_MODEL), 0.5 * D_MODEL ** -0.5),
        "b_ada": nrm(ks[5], (DEPTH, 6 * D_MODEL), 0.01),
        "w_in": nrm(ks[6], (DEPTH, D_MODEL, IN_W), D_MODEL ** -0.5),
        "sgu_norm": 1.0 + nrm(ks[7], (DEPTH, SGU_W), 0.01),
        "w_sgu": nrm(ks[8], (DEPTH, SGU_HEADS, CHUNK, CHUNK), CHUNK ** -0.5),
        "b_sgu": 1.0 + nrm(ks[9], (DEPTH, SGU_HEADS, CHUNK), 0.01),
        "q_lora_norm": 1.0 + nrm(ks[10], (DEPTH, Q_LORA), 0.01),
        "w_uq": nrm(ks[11], (DEPTH, Q_LORA, MLA_HEADS * QK_DIM), Q_LORA ** -0.5),
        "kv_lora_norm": 1.0 + nrm(ks[12], (DEPTH, KV_LORA), 0.01),
        "w_ukv": nrm(ks[13], (DEPTH, KV_LORA, MLA_HEADS * (QK_NOPE_DIM + V_DIM)), KV_LORA ** -0.5),
        "q_norm": 1.0 + nrm(ks[14], (DEPTH, QK_DIM), 0.01),
        "k_norm": 1.0 + nrm(ks[15], (DEPTH, QK_DIM), 0.01),
        "w_out": nrm(ks[16], (DEPTH, MIX_W, D_MODEL), MIX_W ** -0.5),
        "w_router": nrm(ks[17], (DEPTH, D_MODEL, N_EXPERTS), D_MODEL ** -0.5),
        "w_gate": nrm(ks[18], (DEPTH, N_EXPERTS, D_MODEL, D_EXPERT), D_MODEL ** -0.5),
        "w_up": nrm(ks[19], (DEPTH, N_EXPERTS, D_MODEL, D_EXPERT), D_MODEL ** -0.5),
        "w_down": nrm(ks[20], (DEPTH, N_EXPERTS, D_EXPERT, D_MODEL), D_EXPERT ** -0.5),
    }


def reference(x, c, ctx, c_ctx, w_ada, b_ada, w_in, sgu_norm, w_sgu, b_sgu, q_lora_norm, w_uq,
              kv_lora_norm, w_ukv, q_norm, k_norm, w_out, w_router, w_gate, w_up, w_down):
    n = x.shape[1]
    rows = n // GRID_W
    pos_row = jnp.repeat(jnp.arange(rows, dtype=jnp.int32), GRID_W)
    pos_col = jnp.tile(jnp.arange(GRID_W, dtype=jnp.int32), rows)
    xc = ctx
    for l in range(DEPTH):
        last = l == DEPTH - 1
        sh1, sc1, g1, sh2, sc2, g2 = ada_modulation(c, w_ada[l], b_ada[l])
        csh1, csc1, cg1, csh2, csc2, cg2 = ada_modulation(c_ctx[None], w_ada[l], b_ada[l])

        hx = modulate(x, sh1, sc1)
        hc = modulate(xc, csh1, csc1)
        col0 = OFF_KV if last else 0
        pc = hc @ w_in[l][:, col0:]
        k_ctx, v_ctx = mla_keys_values(pc[..., OFF_KV - col0:OFF_KR - col0], pc[..., OFF_KR - col0:],
                                       kv_lora_norm[l], w_ukv[l], k_norm[l])

        px = hx @ w_in[l]
        ya, yb, q = local_head_groups(px, sgu_norm[l], w_sgu[l], b_sgu[l], q_lora_norm[l], w_uq[l], q_norm[l])
        k, v = mla_keys_values(px[..., OFF_KV:OFF_KR], px[..., OFF_KR:], kv_lora_norm[l], w_ukv[l], k_norm[l])
        q = with_rope(q, pos_row, pos_col)
        k = with_rope(k, pos_row, pos_col)
        yc = block_attention(q, jnp.concatenate([k_ctx, k], axis=1), jnp.concatenate([v_ctx, v], axis=1))
        x = x + g1 * (jnp.concatenate([ya, yb, yc], axis=-1) @ w_out[l])

        if not last:
            ya_c, yb_c, q_c = local_head_groups(pc, sgu_norm[l], w_sgu[l], b_sgu[l], q_lora_norm[l], w_uq[l], q_norm[l])
            yc_c = block_attention(q_c, k_ctx, v_ctx)
            xc = xc + cg1 * (jnp.concatenate([ya_c, yb_c, yc_c], axis=-1) @ w_out[l])

        x = x + g2 * expert_choice_moe(modulate(x, sh2, sc2), w_router[l], w_gate[l], w_up[l], w_down[l])
        if not last:
            xc = xc + cg2 * expert_choice_moe(modulate(xc, csh2, csc2), w_router[l], w_gate[l], w_up[l], w_down[l])
    return x
```

```python
import numpy as np
import ml_dtypes
from concourse.bass_utils import run_bass_kernel_spmd

import contextlib
import numpy as np
import concourse.bass as bass
import concourse.mybir as mybir

F32 = mybir.dt.float32
BF16 = mybir.dt.bfloat16
I32 = mybir.dt.int32
ALU = mybir.AluOpType
AF = mybir.ActivationFunctionType
AX = mybir.AxisListType

ENGS = ("pe", "act", "dve", "pool", "sp")
NDMASEM = 8


class Op:
    __slots__ = ("eng", "fn", "dma", "deps", "needs_sig", "sig_idx", "sem_i", "sem_val", "idx", "prev_same_sem")

    def __init__(self, eng, fn, dma):
        self.eng = eng
        self.fn = fn
        self.dma = dma
        self.deps = []
        self.needs_sig = False
        self.sig_idx = 0
        self.sem_i = -1
        self.sem_val = 0
        self.prev_same_sem = None


class BufState:
    __slots__ = ("last_w", "readers")

    def __init__(self):
        self.last_w = None
        self.readers = []


class Prog:
    def __init__(self, nc, tag=""):
        self.nc = nc
        self.tag = tag
        self.ops = {e: [] for e in ENGS}
        self.st = {}
        self.es = contextlib.ExitStack()
        self.ndma = {e: 0 for e in ENGS}
        self.dma_last = {}
        self.dma_tot = {}
        self.all_dma = []
        self.nsb = 0
        self.fence = None
        self.psum_keys = set()

    def sb(self, shape, dtype, name=None):
        self.nsb += 1
        name = name or f"sb{self.nsb}"
        return self.es.enter_context(self.nc.sbuf_tensor("s_" + self.tag + name, list(shape), dtype))

    def ps(self, shape, dtype, name=None):
        self.nsb += 1
        name = name or f"ps{self.nsb}"
        self.psum_keys.add(name)
        return self.es.enter_context(self.nc.psum_tensor("p_" + self.tag + name, list(shape), dtype))

    def _state(self, k):
        s = self.st.get(k)
        if s is None:
            s = self.st[k] = BufState()
        return s

    def capture(self, fn, *args):
        self.cap = []
        fn(*args)
        c, self.cap = self.cap, None
        return c

    def replay_interleaved(self, lists):
        idx = [0] * len(lists)
        while True:
            best, bf = -1, 2.0
            for i, l in enumerate(lists):
                if idx[i] < len(l):
                    f = idx[i] / len(l)
                    if f < bf:
                        best, bf = i, f
            if best < 0:
                break
            self.op(*lists[best][idx[best]])
            idx[best] += 1

    def op(self, eng, fn, reads=(), writes=(), dma=False):
        if getattr(self, "cap", None) is not None:
            self.cap.append((eng, fn, tuple(reads), tuple(writes), dma))
            return None
        o = Op(eng, fn, dma)
        deps = []
        pr = [k for k in reads if k in self.psum_keys]
        if pr:
            reads = [k for k in reads if k not in self.psum_keys]
            writes = list(writes) + [k for k in pr if k not in writes]
        for k in reads:
            s = self._state(k)
            if s.last_w is not None:
                deps.append((s.last_w, "raw"))
        for k in writes:
            s = self._state(k)
            if s.last_w is not None:
                deps.append((s.last_w, "waw"))
            for r in s.readers:
                deps.append((r, "war"))
        if self.fence is not None:
            deps.append((self.fence, "raw"))
        seen = set()
        for d, kind in deps:
            if d is o or id(d) in seen:
                continue
            if (not o.dma) and (not d.dma) and d.eng == o.eng:
                if o.eng == "pe":
                    continue
            seen.add(id(d))
            o.deps.append(d)
        for k in reads:
            self._state(k).readers.append(o)
        for k in writes:
            s = self._state(k)
            s.last_w = o
            s.readers = []
        if dma:
            i = self.ndma[eng] % NDMASEM
            self.ndma[eng] += 1
            o.sem_i = i
            key = (eng, i)
            o.prev_same_sem = self.dma_last.get(key)
            o.sem_val = self.dma_tot.get(key, 0) + 16
            self.dma_tot[key] = o.sem_val
            self.dma_last[key] = o
            self.all_dma.append(o)
        self.ops[eng].append(o)
        return o

    def barrier(self, bar_tile):
        nc = self.nc
        lasts = []
        for e in ENGS:
            comp = [o for o in self.ops[e] if not o.dma]
            if comp:
                lasts.append(comp[-1])
        lasts.extend(self.dma_last.values())
        old = self.fence
        self.fence = None
        b = self.op("dve", lambda: nc.vector.memset(bar_tile, 0.0), (), ())
        for d in lasts:
            if d is not b and d not in b.deps and not (d.eng == "pe" and False):
                b.deps.append(d)
        if old is not None and old not in b.deps:
            b.deps.append(old)
        self.fence = b
        return b

    def I(self, eng, method, reads=(), writes=(), **kw):
        return self.op(eng, lambda: method(**kw), reads, writes)

    def dma(self, eng, out, in_, reads=(), writes=(), **kw):
        e = {"sp": self.nc.sync, "pool": self.nc.gpsimd, "act": self.nc.scalar}[eng]
        return self.op(eng, lambda: e.dma_start(out=out, in_=in_, **kw), reads, writes, dma=True)

    def emit(self):
        nc = self.nc
        for e in ENGS:
            for o in self.ops[e]:
                for d in o.deps:
                    if not d.dma:
                        d.needs_sig = True
        for e in ENGS:
            c = 0
            for o in self.ops[e]:
                if (not o.dma) and o.needs_sig:
                    c += 1
                    o.sig_idx = c
        es = self.es
        csem = {e: nc.alloc_semaphore(name=f"c_{e}_{self.tag}") for e in ENGS}
        dsem = {}
        for e in ENGS:
            if self.ndma[e]:
                for i in range(min(NDMASEM, self.ndma[e])):
                    dsem[(e, i)] = nc.alloc_semaphore(name=f"d_{e}{i}_{self.tag}")
        block = es.enter_context(nc.Block())
        prog = self

        def stream(ename, eng):
            waited = {}

            def wait(key, sem, val):
                if waited.get(key, 0) < val:
                    eng.wait_ge(sem, val)
                    waited[key] = val

            for o in prog.ops[ename]:
                for d in o.deps:
                    if d.dma:
                        wait(("d", d.eng, d.sem_i), dsem[(d.eng, d.sem_i)], d.sem_val)
                    else:
                        wait(("c", d.eng), csem[d.eng], d.sig_idx)
                if o.dma:
                    p = o.prev_same_sem
                    if p is not None:
                        wait(("d", ename, o.sem_i), dsem[(ename, o.sem_i)], p.sem_val)
                    inst = o.fn()
                    inst.then_inc(dsem[(ename, o.sem_i)], 16)
                else:
                    inst = o.fn()
                    if o.needs_sig:
                        inst.then_inc(csem[ename], 1)
            if ename == "sp":
                for key, tot in prog.dma_tot.items():
                    wait(("d",) + key, dsem[key], tot)
                for e2 in ENGS:
                    if e2 != "sp":
                        n = max([o.sig_idx for o in prog.ops[e2] if not o.dma] + [0])
                        if n:
                            wait(("c", e2), csem[e2], n)

        @block.tensor
        def _(eng):
            stream("pe", eng)

        @block.scalar
        def _(eng):
            stream("act", eng)

        @block.vector
        def _(eng):
            stream("dve", eng)

        @block.gpsimd
        def _(eng):
            stream("pool", eng)

        @block.sync
        def _(eng):
            stream("sp", eng)

    def close(self):
        self.es.close()


DBGZ = DBGQ = DBGT = 9
SEQREPLAY = 0

NT, NCT = 16, 2
TT = NT + NCT
EPS = 1e-6


class Rot:
    def __init__(self, P, n, shape, dtype, name):
        self.bufs = [(P.sb(shape, dtype, f"{name}{i}"), f"{name}{i}") for i in range(n)]
        self.i = 0

    def next(self):
        b = self.bufs[self.i % len(self.bufs)]
        self.i += 1
        return b


def make_ident(P, nc):
    identf = P.sb([128, 128], F32, "identf")
    ident = P.sb([128, 128], BF16, "ident")
    P.I("pool", nc.gpsimd.memset, [], ["identf"], ap=identf[:], constant=0.0)
    P.I("pool", nc.gpsimd.affine_select, ["identf"], ["identf"], out=identf[:], in_=identf[:], pattern=[[-1, 128]],
        compare_op=ALU.not_equal, fill=1.0, base=0, channel_multiplier=1)
    P.I("dve", nc.vector.tensor_copy, ["identf"], ["ident"], out=ident[:], in_=identf[:])
    P.identf = identf
    return ident


def build_A(last, ntiles=TT, dbg=99, nc=None, T=None, tag=""):
    if nc is None:
        nc = bass.Bass("TRN2", target_bir_lowering=False)

    def din(name, shape, dt=F32):
        if T is not None:
            assert tuple(T[name].shape) == tuple(shape), (name, T[name].shape, shape)
            return T[name]
        return nc.dram_tensor(name, list(shape), dt, kind="ExternalInput").ap()

    def dout(name, shape, dt=F32):
        if T is not None:
            assert tuple(T[name].shape) == tuple(shape), (name, T[name].shape, shape)
            return T[name]
        return nc.dram_tensor(name, list(shape), dt, kind="ExternalOutput").ap()

    xin = din("xin", [TT * 128, 1024])
    cT_d = din("cT", [128, 8, 2])
    w_ada_d = din("w_ada", [1024, 6144])
    b_ada_d = din("b_ada", [1, 6144])
    w_in_d = din("w_in", [1024, 1216])
    w_sguT_d = din("w_sguT", [128, 4, 128])
    b_sguT_d = din("b_sguT", [128, 4])
    sgun_d = din("sgu_norm", [1, 256])
    qln_d = din("q_lora_norm", [1, 256])
    kvln_d = din("kv_lora_norm", [1, 128])
    qn_d = din("q_norm", [1, 192])
    kn_d = din("k_norm", [1, 192])
    w_uq_d = din("w_uq", [256, 768])
    w_ukv_d = din("w_ukv", [128, 1024])
    dftc_d = din("dftc", [256, 512])
    rcos_d = din("rope_cos", [TT * 128, 64])
    rsin_d = din("rope_sin", [TT * 128, 64])

    ada_o = dout("ada", [2, 6144])
    yaT_o = dout("yaT", [256, TT * 128], BF16)
    Z_o = dout("Z", [TT * 128, 512], BF16)
    QT_o = dout("QT", [4, 192, TT * 128], BF16)
    KT_o = dout("KT", [4, 192, TT * 128], BF16)
    V_o = dout("V", [TT * 128, 512], BF16)

    P = Prog(nc, tag)
    I = P.I
    ident = make_ident(P, nc)

    def bc_load(name, src, n):
        t = P.sb([128, n], F32, name)
        P.dma("sp", t[:], src.partition_broadcast(128), writes=[name])
        return t

    sgun = bc_load("sgun", sgun_d[0:1, :], 256)
    qln = bc_load("qln", qln_d[0:1, :], 256)
    kvln = bc_load("kvln", kvln_d[0:1, :], 128)
    qnb = bc_load("qnb", qn_d[0:1, :], 192)
    knb = bc_load("knb", kn_d[0:1, :], 192)
    b_sguT = P.sb([128, 4], F32, "b_sguT")
    P.dma("sp", b_sguT[:], b_sguT_d, writes=["b_sguT"])
    rcos = P.sb([128, TT, 64], F32, "rcos")
    rsin = P.sb([128, TT, 64], F32, "rsin")
    P.dma("sp", rcos[:], rcos_d.rearrange("(t p) d -> p t d", p=128), writes=["rcos"])
    P.dma("sp", rsin[:], rsin_d.rearrange("(t p) d -> p t d", p=128), writes=["rsin"])
    cT = P.sb([128, 8, 2], F32, "cT")
    P.dma("sp", cT[:], cT_d, writes=["cT"])
    scT = P.sb([128, 8, 2], BF16, "scT")
    I("act", nc.scalar.activation, ["cT"], ["scT"], out=scT[:], in_=cT[:], func=AF.Silu)
    mbr = Rot(P, 2, [2, 512], F32, "mblk")
    bar = Rot(P, 2, [2, 512], F32, "bablk")
    warot = Rot(P, 2, [128, 8, 512], BF16, "wa")
    ps_ada = P.ps([128, 512], F32, "psA")
    w_ada_v = w_ada_d.rearrange("(k p) n -> p k n", p=128)
    for nb in range(12):
        wa, kwa = warot.next()
        for hk in range(2):
            P.dma("pool", wa[:, hk * 4:(hk + 1) * 4, :], w_ada_v[:, hk * 4:(hk + 1) * 4, nb * 512:(nb + 1) * 512], writes=[kwa + f"_{hk}"])
        for k in range(8):
            I("pe", nc.tensor.matmul, ["scT", kwa + f"_{k // 4}"], ["psA"], out=ps_ada[0:2, :], lhsT=scT[:, k, :], rhs=wa[:, k, :],
              start=(k == 0), stop=(k == 7))
        mb, kmb = mbr.next()
        ba, kba = bar.next()
        P.dma("sp", ba[:], b_ada_d[0:1, nb * 512:(nb + 1) * 512].partition_broadcast(2), writes=[kba])
        I("dve", nc.vector.tensor_tensor, ["psA", kba], [kmb], out=mb[:], in0=ps_ada[0:2, :], in1=ba[:], op=ALU.add)
        P.dma("sp", ada_o[:, nb * 512:(nb + 1) * 512], mb[:], reads=[kmb], writes=["ada_d"])
    mods = []
    for r in range(2):
        md = P.sb([128, 2048], F32, f"mod{r}")
        P.dma("sp", md[:], ada_o[r:r + 1, 0:2048].partition_broadcast(128), reads=["ada_d"], writes=[f"mod{r}"])
        I("dve", nc.vector.tensor_scalar_add, [f"mod{r}"], [f"mod{r}"], out=md[:, 1024:2048], in0=md[:, 1024:2048], scalar1=1.0)
        mods.append(md)

    w_in = P.sb([128, 8, 1216], BF16, "w_in")
    w_in_v = w_in_d.rearrange("(k p) n -> p k n", p=128)
    for k in range(0, 8, 2):
        P.dma("pool", w_in[:, k:k + 2, :], w_in_v[:, k:k + 2, :], writes=[f"w_in{k}"])
    w_in_keys = [f"w_in{k}" for k in range(0, 8, 2)]
    w_uq = P.sb([128, 2, 768], BF16, "w_uq")
    P.dma("pool", w_uq[:], w_uq_d.rearrange("(k p) n -> p k n", p=128), writes=["w_uq"])
    w_ukv = P.sb([128, 1024], BF16, "w_ukv")
    P.dma("pool", w_ukv[:], w_ukv_d, writes=["w_ukv"])
    dftc = P.sb([128, 2, 512], BF16, "dftc")
    P.dma("pool", dftc[:], dftc_d.rearrange("(k p) n -> p k n", p=128), writes=["dftc"])
    w_sguT = P.sb([128, 4, 128], BF16, "w_sguT")
    P.dma("pool", w_sguT[:], w_sguT_d, writes=["w_sguT"])

    psT = P.ps([128, 1024], BF16, "psT")
    psT2 = P.ps([128, 1024], BF16, "psT2")
    psT3 = P.ps([128, 1024], BF16, "psT3")
    px0 = P.ps([128, 512], F32, "px0")
    px1 = P.ps([128, 512], F32, "px1")
    px2 = P.ps([128, 512], F32, "px2")
    psB = P.ps([128, 512], F32, "psB")

    xr_ = Rot(P, 2, [128, 1024], F32, "x")
    tmpr = Rot(P, 2, [128, 1024], F32, "tmp")
    hr = Rot(P, 2, [128, 1024], BF16, "h")
    hTr = Rot(P, 2, [128, 1024], BF16, "hT")
    junkr = Rot(P, 4, [128, 1024], BF16, "junkA")
    str_ = Rot(P, 3, [128, 40], F32, "st")
    uvr = Rot(P, 2, [128, 512], F32, "uv")
    vbr = Rot(P, 2, [128, 256], BF16, "vb")
    yar = Rot(P, 2, [128, 256], BF16, "ya")
    yaTr = Rot(P, 2, [128, 2, 128], BF16, "yaT")
    pfr = Rot(P, 2, [128, 256], BF16, "pf")
    pfTr = Rot(P, 2, [128, 2, 128], BF16, "pfT")
    Zr = Rot(P, 2, [128, 512], BF16, "Z")
    cqr = Rot(P, 2, [128, 256], BF16, "cq")
    cqTr = Rot(P, 2, [128, 2, 128], BF16, "cqT")
    qfr = Rot(P, 2, [128, 4, 192], F32, "qf")
    qnr = Rot(P, 2, [128, 4, 192], F32, "qn")
    r1r = Rot(P, 2, [128, 4, 64], F32, "r1")
    r2r = Rot(P, 2, [128, 4, 64], F32, "r2")
    qbr = Rot(P, 2, [128, 4, 256], BF16, "qb")
    for (qb_, kqb_) in qbr.bufs:
        I("pool", nc.gpsimd.memset, [], [kqb_], ap=qb_[:], constant=0.0)
    QTnr = Rot(P, 3, [128, 4, 128], BF16, "QTn")
    QTrr = Rot(P, 3, [128, 4, 128], BF16, "QTr")
    ckvr = Rot(P, 2, [128, 128], BF16, "ckv")
    ckvTr = Rot(P, 2, [128, 128], BF16, "ckvT")
    Vbr = Rot(P, 2, [128, 4, 128], BF16, "Vb")

    def rms_rstd(src, ncols, n, rkeys, st, kst, c0, nm):
        k0, k1, k2 = f"{kst}_{nm}0", f"{kst}_{nm}1", f"{kst}_{nm}2"
        junkA, kj = junkr.next()
        I("act", nc.scalar.activation, rkeys, [kj, k0], out=junkA[:, 0:ncols], in_=src, func=AF.Square, accum_out=st[:, c0:c0 + 1])
        I("act", nc.scalar.activation, [k0], [k1], out=st[:, c0 + 1:c0 + 2], in_=st[:, c0:c0 + 1], func=AF.Sqrt, scale=1.0 / n, bias=EPS)
        I("dve", nc.vector.reciprocal, [k1], [k2], out=st[:, c0 + 2:c0 + 3], in_=st[:, c0 + 1:c0 + 2])
        return k2

    def head_norm_rope_store(t, qf, kqf, normb, knormb, st, kst, c0, nm, out_d):
        if DBGQ < 2:
            return
        ks = [f"{kst}_{nm}s{h}" for h in range(4)]
        for h in range(4):
            junkA, kj = junkr.next()
            I("act", nc.scalar.activation, [kqf], [kj, ks[h]], out=junkA[:, 0:192], in_=qf[:, h, :], func=AF.Square,
              accum_out=st[:, c0 + h:c0 + h + 1])
        kq1, kq2 = f"{kst}_{nm}q1", f"{kst}_{nm}q2"
        I("act", nc.scalar.activation, ks, [kq1], out=st[:, c0 + 4:c0 + 8], in_=st[:, c0:c0 + 4], func=AF.Sqrt, scale=1.0 / 192, bias=EPS)
        I("dve", nc.vector.reciprocal, [kq1], [kq2], out=st[:, c0 + 8:c0 + 12], in_=st[:, c0 + 4:c0 + 8])
        qn, kqn = qnr.next()
        for h in range(4):
            I("dve", nc.vector.scalar_tensor_tensor, [kqf, kq2, knormb], [kqn], out=qn[:, h, :], in0=qf[:, h, :],
              scalar=st[:, c0 + 8 + h:c0 + 9 + h], in1=normb[:], op0=ALU.mult, op1=ALU.mult)
        if DBGQ < 3:
            return
        r1, kr1 = r1r.next()
        r2, kr2 = r2r.next()
        qb, kqb = qbr.next()
        xrp = qn[:, :, 128:192]
        I("dve", nc.vector.tensor_tensor, [kqn, "rcos"], [kr1], out=r1[:], in0=xrp, in1=rcos[:, t, :].unsqueeze(1).to_broadcast([128, 4, 64]),
          op=ALU.mult)
        x5 = xrp.rearrange("p h (b s d) -> p h b s d", b=2, s=2)
        o5 = r2[:].rearrange("p h (b s d) -> p h b s d", b=2, s=2)
        s5 = rsin[:, t, :].rearrange("p (b s d) -> p b s d", b=2, s=2)
        for s_ in range(2):
            I("dve", nc.vector.tensor_tensor, [kqn, "rsin"], [kr2], out=o5[:, :, :, s_, :], in0=x5[:, :, :, 1 - s_, :],
              in1=s5[:, :, s_, :].unsqueeze(1).to_broadcast([128, 4, 2, 16]), op=ALU.mult)
        I("dve", nc.vector.tensor_tensor, [kr1, kr2], [kqb], out=qb[:, :, 128:192], in0=r1[:], in1=r2[:], op=ALU.add)
        I("act", nc.scalar.copy, [kqn], [kqb], out=qb[:, :, 0:128], in_=qn[:, :, 0:128])
        if DBGQ < 4:
            return
        for h in range(4):
            I("pe", nc.tensor.transpose, [kqb, "ident"], ["psT3"], out=psT3[:, h * 128:(h + 1) * 128], in_=qb[:, h, 0:128], identity=ident[:])
            if DBGT >= 1:
                I("pe", nc.tensor.transpose, [kqb, "ident"], ["psT3"], out=psT3[:, 512 + h * 128:512 + (h + 1) * 128], in_=qb[:, h, 128:256],
                  identity=ident[:])
        QTn, kQTn = QTnr.next()
        QTr, kQTr = QTrr.next()
        I("act", nc.scalar.copy, ["psT3"], [kQTn], out=QTn[:], in_=psT3[:, 0:512].rearrange("p (h t) -> p h t", h=4))
        if DBGT >= 2:
          I("dve", nc.vector.tensor_copy, ["psT3"], [kQTr], out=QTr[0:64, :, :], in_=psT3[0:64, 512:1024].rearrange("p (h t) -> p h t", h=4))
        if DBGQ < 5:
            return
        P.dma("sp", out_d[:, 0:128, t * 128:(t + 1) * 128].rearrange("h d t -> d h t"), QTn[:], reads=[kQTn])
        P.dma("sp", out_d[:, 128:192, t * 128:(t + 1) * 128].rearrange("h d t -> d h t"), QTr[0:64, :, :], reads=[kQTr])

    if last:
        zb = P.sb([128, 4, 128], BF16, "zb")
        I("pool", nc.gpsimd.memset, [], ["zb"], ap=zb[:], constant=0.0)
        zbf = zb[:].rearrange("p h t -> p (h t)")
        for t in range(NT, ntiles):
            P.dma("sp", yaT_o[:, t * 128:(t + 1) * 128].rearrange("(c p) t -> p c t", p=128), zb[:, 0:2, :], reads=["zb"])
            P.dma("sp", Z_o[t * 128:(t + 1) * 128, :], zbf, reads=["zb"])
            P.dma("sp", QT_o[:, 0:128, t * 128:(t + 1) * 128].rearrange("h d t -> d h t"), zb[:], reads=["zb"])
            P.dma("sp", QT_o[:, 128:192, t * 128:(t + 1) * 128].rearrange("h d t -> d h t"), zb[0:64, :, :], reads=["zb"])
    def front(t, S):
        is_ctx = t >= NT
        md = mods[1 if is_ctx else 0]
        kmd = f"mod{1 if is_ctx else 0}"
        x_t, kx = xr_.next()
        P.dma("sp", x_t[:], xin[t * 128:(t + 1) * 128, :], writes=[kx])
        st, kst = str_.next()
        S["st"], S["kst"] = st, kst
        krs = rms_rstd(x_t[:], 1024, 1024, [kx], st, kst, 0, "n1")
        tmp, ktmp = tmpr.next()
        h_t, kh = hr.next()
        I("dve", nc.vector.scalar_tensor_tensor, [kx, krs, kmd], [ktmp], out=tmp[:], in0=x_t[:], scalar=st[:, 2:3], in1=md[:, 1024:2048],
          op0=ALU.mult, op1=ALU.mult)
        I("dve", nc.vector.tensor_tensor, [ktmp, kmd], [kh], out=h_t[:], in0=tmp[:], in1=md[:, 0:1024], op=ALU.add)
        for k in range(8):
            I("pe", nc.tensor.transpose, [kh, "ident"], ["psT"], out=psT[:, k * 128:(k + 1) * 128], in_=h_t[:, k * 128:(k + 1) * 128],
              identity=ident[:])
        hT, khT = hTr.next()
        I("act", nc.scalar.copy, ["psT"], [khT], out=hT[:], in_=psT[:])
        S["hT"], S["khT"] = hT, khT

    def pxmm(t, S):
        kv_only = (t >= NT) and last
        hT, khT = S["hT"], S["khT"]
        blocks = [(px0, "px0", 0, 512), (px1, "px1", 512, 1024), (px2, "px2", 1024, 1216)]
        for (pb, kpb, c0, c1) in blocks:
            if kv_only and kpb != "px2":
                continue
            for k in range(8):
                I("pe", nc.tensor.matmul, [khT, w_in_keys[k // 2]], [kpb], out=pb[:, 0:c1 - c0], lhsT=hT[:, k * 128:(k + 1) * 128],
                  rhs=w_in[:, k, c0:c1], start=(k == 0), stop=(k == 7))
        if not kv_only:
            uv, kuv = uvr.next()
            I("act", nc.scalar.activation, ["px0"], [kuv], out=uv[:], in_=px0[:], func=AF.Gelu_apprx_tanh)
            S["uv"], S["kuv"] = uv, kuv
            p1, kp1 = p1r.next()
            I("act", nc.scalar.copy, ["px1"], [kp1], out=p1[:], in_=px1[:])
            S["p1"], S["kp1"] = p1, kp1
        p2, kp2 = p2r.next()
        I("dve", nc.vector.tensor_copy, ["px2"], [kp2], out=p2[:], in_=px2[:, 0:192])
        S["p2"], S["kp2"] = p2, kp2

    def sgu(t, S):
        st, kst = S["st"], S["kst"]
        uv, kuv = S["uv"], S["kuv"]
        krv = rms_rstd(uv[:, 256:512], 256, 256, [kuv], st, kst, 3, "v")
        vb, kvb = vbr.next()
        I("dve", nc.vector.scalar_tensor_tensor, [kuv, krv, "sgun"], [kvb], out=vb[:], in0=uv[:, 256:512], scalar=st[:, 5:6], in1=sgun[:],
          op0=ALU.mult, op1=ALU.mult)
        for h in range(4):
            I("pe", nc.tensor.matmul, [kvb, "w_sguT"], ["psA"], out=ps_ada[:, h * 64:(h + 1) * 64], lhsT=w_sguT[:, h, :],
              rhs=vb[:, h * 64:(h + 1) * 64], start=True, stop=True)
        ya, kya = yar.next()
        for h in range(4):
            I("dve", nc.vector.scalar_tensor_tensor, ["psA", "b_sguT", kuv], [kya], out=ya[:, h * 64:(h + 1) * 64],
              in0=ps_ada[:, h * 64:(h + 1) * 64], scalar=b_sguT[:, h:h + 1], in1=uv[:, h * 64:(h + 1) * 64], op0=ALU.add, op1=ALU.mult)
        for c in range(2):
            I("pe", nc.tensor.transpose, [kya, "ident"], ["psT2"], out=psT2[:, c * 128:(c + 1) * 128], in_=ya[:, c * 128:(c + 1) * 128],
              identity=ident[:])
        yaT, kyaT = yaTr.next()
        I("act", nc.scalar.copy, ["psT2"], [kyaT], out=yaT[:], in_=psT2[:, 0:256].rearrange("p (c t) -> p c t", c=2))
        P.dma("sp", yaT_o[:, t * 128:(t + 1) * 128].rearrange("(c p) t -> p c t", p=128), yaT[:], reads=[kyaT])

    def zpart(t, S):
        p1, kp1 = S["p1"], S["kp1"]
        pf, kpf = pfr.next()
        I("dve", nc.vector.tensor_copy, [kp1], [kpf], out=pf[:], in_=p1[:, 0:256])
        for c in range(2):
            I("pe", nc.tensor.transpose, [kpf, "ident"], ["psT2"], out=psT2[:, 256 + c * 128:256 + (c + 1) * 128],
              in_=pf[:, c * 128:(c + 1) * 128], identity=ident[:])
        pfT, kpfT = pfTr.next()
        I("act", nc.scalar.copy, ["psT2"], [kpfT], out=pfT[:], in_=psT2[:, 256:512].rearrange("p (c t) -> p c t", c=2))
        for c in range(2):
            I("pe", nc.tensor.matmul, [kpfT, "dftc"], ["psB"], out=psB[:], lhsT=pfT[:, c, :], rhs=dftc[:, c, :], start=(c == 0), stop=(c == 1))
        Zt, kZ = Zr.next()
        I("act", nc.scalar.copy, ["psB"], [kZ], out=Zt[:], in_=psB[:])
        P.dma("sp", Z_o[t * 128:(t + 1) * 128, :], Zt[:], reads=[kZ])

    def qpart(t, S):
        st, kst = S["st"], S["kst"]
        p1, kp1 = S["p1"], S["kp1"]
        krq = rms_rstd(p1[:, 256:512], 256, 256, [kp1], st, kst, 6, "q")
        cq, kcq = cqr.next()
        I("dve", nc.vector.scalar_tensor_tensor, [kp1, krq, "qln"], [kcq], out=cq[:], in0=p1[:, 256:512], scalar=st[:, 8:9], in1=qln[:],
          op0=ALU.mult, op1=ALU.mult)
        for c in range(2):
            I("pe", nc.tensor.transpose, [kcq, "ident"], ["psT2"], out=psT2[:, 512 + c * 128:512 + (c + 1) * 128],
              in_=cq[:, c * 128:(c + 1) * 128], identity=ident[:])
        cqT, kcqT = cqTr.next()
        I("act", nc.scalar.copy, ["psT2"], [kcqT], out=cqT[:], in_=psT2[:, 512:768].rearrange("p (c t) -> p c t", c=2))
        for c in range(2):
            I("pe", nc.tensor.matmul, [kcqT, "w_uq"], ["px1"], out=px1[:], lhsT=cqT[:, c, :], rhs=w_uq[:, c, 0:512], start=(c == 0), stop=(c == 1))
        for c in range(2):
            I("pe", nc.tensor.matmul, [kcqT, "w_uq"], ["px2"], out=px2[:, 0:256], lhsT=cqT[:, c, :], rhs=w_uq[:, c, 512:768],
              start=(c == 0), stop=(c == 1))
        qf, kqf = qfr.next()
        qf2 = qf[:].rearrange("p h d -> p (h d)")
        I("act", nc.scalar.copy, ["px1"], [kqf], out=qf2[:, 0:512], in_=px1[:])
        I("act", nc.scalar.copy, ["px2"], [kqf], out=qf2[:, 512:768], in_=px2[:, 0:256])
        head_norm_rope_store(t, qf, kqf, qnb, "qnb", st, kst, 9, "qh", QT_o)

    def kvpart(t, S):
        st, kst = S["st"], S["kst"]
        p2, kp2 = S["p2"], S["kp2"]
        krk = rms_rstd(p2[:, 0:128], 128, 128, [kp2], st, kst, 21, "kv")
        ckv, kckv = ckvr.next()
        I("dve", nc.vector.scalar_tensor_tensor, [kp2, krk, "kvln"], [kckv], out=ckv[:], in0=p2[:, 0:128], scalar=st[:, 23:24], in1=kvln[:],
          op0=ALU.mult, op1=ALU.mult)
        I("pe", nc.tensor.transpose, [kckv, "ident"], ["psT2"], out=psT2[:, 768:896], in_=ckv[:], identity=ident[:])
        ckvT, kckvT = ckvTr.next()
        I("act", nc.scalar.copy, ["psT2"], [kckvT], out=ckvT[:], in_=psT2[:, 768:896])
        kf, kkf = kfr.next()
        Vb, kVb = Vbr.next()
        I("act", nc.scalar.copy, [kp2], [kkf], out=kf[:, :, 128:192], in_=p2[:, 128:192].unsqueeze(1).to_broadcast([128, 4, 64]))
        for j, (pb, kpb) in enumerate([(px0, "px0"), (px0, "px0")]):
            I("pe", nc.tensor.matmul, [kckvT, "w_ukv"], [kpb], out=pb[:], lhsT=ckvT[:], rhs=w_ukv[:, j * 512:(j + 1) * 512], start=True, stop=True)
            pv = pb[:].rearrange("p (h s d) -> p h s d", h=2, s=2)
            I("act", nc.scalar.copy, [kpb], [kVb], out=Vb[:, 2 * j:2 * j + 2, :], in_=pv[:, :, 1, :])
            I("dve", nc.vector.tensor_copy, [kpb], [kkf], out=kf[:, 2 * j:2 * j + 2, 0:128], in_=pv[:, :, 0, :])
        P.dma("sp", V_o[t * 128:(t + 1) * 128, :], Vb[:].rearrange("p h d -> p (h d)"), reads=[kVb])
        head_norm_rope_store(t, kf, kkf, knb, "knb", st, kst, 24, "kh", KT_o)

    p1r = Rot(P, 2, [128, 512], F32, "p1s")
    p2r = Rot(P, 2, [128, 192], F32, "p2s")
    kfr = Rot(P, 2, [128, 4, 192], F32, "kf")
    states = [dict() for _ in range(ntiles + 1)]
    if ntiles:
        front(0, states[0])
    for t in range(ntiles):
        S = states[t]
        kv_only = (t >= NT) and last
        pxmm(t, S)
        lists = []
        if not kv_only:
            lists += [P.capture(sgu, t, S), P.capture(zpart, t, S)]
        lists.append((P.capture(qpart, t, S) if not kv_only else []) + P.capture(kvpart, t, S))
        if t + 1 < ntiles:
            lists.append(P.capture(front, t + 1, states[t + 1]))
        P.replay_interleaved(lists) if not SEQREPLAY else [P.op(*o) for l in lists for o in l]
    P.emit()
    P.close()
    return nc


def build_F(last, nc=None, T=None, tag=""):
    if nc is None:
        nc = bass.Bass("TRN2", target_bir_lowering=False)

    def din(name, shape, dt=F32):
        if T is not None:
            assert tuple(T[name].shape) == tuple(shape), (name, T[name].shape, shape)
            return T[name]
        return nc.dram_tensor(name, list(shape), dt, kind="ExternalInput").ap()

    Z_d = din("Z_all", [8192, 512], BF16)
    Zc_d = din("Zc", [256, 512], BF16)
    WA_d = din("WA", [128, 128])
    TC_d = din("TC", [128, 2, 64, 32])
    TCc_d = din("TCc", [128, 2, 2, 256])
    ybT_o = T["ybT"] if T is not None else nc.dram_tensor("ybT", [256, TT * 128], BF16, kind="ExternalOutput").ap()
    A_d = nc.dram_tensor(tag + "A_scr", [128, 128, 256], BF16, kind="Internal").ap()

    P = Prog(nc, tag)
    I = P.I
    WA = P.sb([128, 128], BF16, "WA")
    P.dma("pool", WA[:], WA_d, writes=["WA"])
    TC = P.sb([128, 2, 64, 32], BF16, "TC")
    P.dma("pool", TC[:], TC_d, writes=["TC"])
    fb = [P.ps([128, 512], F32, f"f{i}") for i in range(2)]
    yb = [P.ps([128, 512], F32, f"yb{i}") for i in range(2)]
    pr = P.ps([128, 512], F32, "pr")
    zar = Rot(P, 2, [128, 16, 256], BF16, "za")
    aor = Rot(P, 2, [128, 16, 256], BF16, "ao")
    Zv = Z_d.rearrange("(n1 n2) (ri c) -> ri n1 n2 c", n2=128, ri=2)
    cnt = 0
    for ch in range(8):
        za, kza = zar.next()
        for ri in range(2):
            P.dma("sp", za[ri * 64:(ri + 1) * 64, :, :], Zv[ri, :, ch * 16:(ch + 1) * 16, :], writes=[f"{kza}_{ri}"])
        ao, kao = aor.next()
        for j in range(8):
            bk = fb[j % 2]
            I("pe", nc.tensor.matmul, [f"{kza}_0", f"{kza}_1", "WA"], [f"f{j % 2}"], out=bk[:], lhsT=WA[:],
              rhs=za[:, 2 * j:2 * j + 2, :].rearrange("p a c -> p (a c)"), start=True, stop=True)
            dst = ao[:, 2 * j:2 * j + 2, :].rearrange("p a c -> p (a c)")
            if cnt % 2 == 0:
                I("act", nc.scalar.copy, [f"f{j % 2}"], [kao], out=dst, in_=bk[:])
            else:
                I("dve", nc.vector.tensor_copy, [f"f{j % 2}"], [kao], out=dst, in_=bk[:])
            cnt += 1
        P.dma("sp", A_d[:, ch * 16:(ch + 1) * 16, :], ao[:], reads=[kao], writes=["A_d"])
    ac = P.sb([128, 128, 128], BF16, "ac")
    ybs = P.sb([128, 2, TT * 128], BF16, "ybs")
    A_v = A_d.rearrange("q n c -> n q c")
    for half in range(2):
        for qq in range(4):
            P.dma("sp", ac[:, qq * 32:(qq + 1) * 32, :], A_v[:, qq * 32:(qq + 1) * 32, half * 128:(half + 1) * 128], reads=["A_d"],
                  writes=[f"ac{qq}"])
        ackeys = [f"ac{qq}" for qq in range(4)]
        for bk in range(4):
            ybk = yb[bk % 2]
            yv = ybk[:].rearrange("p (k2 k1) -> p k2 k1", k1=64)
            for k1 in range(64):
                for ri in range(2):
                    I("pe", nc.tensor.matmul, ackeys + ["TC"], [f"yb{bk % 2}"], out=yv[:, :, k1], lhsT=ac[:, ri * 64 + k1, :],
                      rhs=TC[:, ri, k1, bk * 8:(bk + 1) * 8], start=(ri == 0), stop=(ri == 1))
            dst = ybs[:, half, bk * 512:(bk + 1) * 512]
            if bk % 2 == 0:
                I("act", nc.scalar.copy, [f"yb{bk % 2}"], ["ybs"], out=dst, in_=ybk[:])
            else:
                I("dve", nc.vector.tensor_copy, [f"yb{bk % 2}"], ["ybs"], out=dst, in_=ybk[:])
    if not last:
        Zc = P.sb([128, 2, 512], BF16, "Zc")
        P.dma("sp", Zc[:], Zc_d.rearrange("(t p) c -> p t c", p=128), writes=["Zc"])
        TCc = P.sb([128, 2, 2, 256], BF16, "TCc")
        P.dma("pool", TCc[:], TCc_d, writes=["TCc"])
        for half in range(2):
            i = 0
            for nt in range(2):
                for ri in range(2):
                    I("pe", nc.tensor.matmul, ["Zc", "TCc"], ["pr"], out=pr[:, 0:256], lhsT=Zc[:, nt, ri * 256 + half * 128:ri * 256 + (half + 1) * 128],
                      rhs=TCc[:, nt, ri, :], start=(i == 0), stop=(i == 3))
                    i += 1
            I("act", nc.scalar.copy, ["pr"], ["ybs"], out=ybs[:, half, NT * 128:TT * 128], in_=pr[:, 0:256])
    if last:
        I("pool", nc.gpsimd.memset, [], ["ybs"], ap=ybs[:, :, NT * 128:TT * 128], constant=0.0)
    ncols = TT * 128
    P.dma("sp", ybT_o[:, 0:ncols].rearrange("(c p) t -> p c t", p=128), ybs[:, :, 0:ncols], reads=["ybs"])
    P.emit()
    P.close()
    return nc


NKT = 66
SCALE = 192 ** -0.5


def build_B(last, nheads=4, nblocks=None, full=True, nc=None, T=None, tag=""):
    if nc is None:
        nc = bass.Bass("TRN2", target_bir_lowering=False)

    def din(name, shape, dt=F32):
        if T is not None:
            assert tuple(T[name].shape) == tuple(shape), (name, T[name].shape, shape)
            return T[name]
        return nc.dram_tensor(name, list(shape), dt, kind="ExternalInput").ap()

    def dout(name, shape, dt=F32):
        if T is not None:
            assert tuple(T[name].shape) == tuple(shape), (name, T[name].shape, shape)
            return T[name]
        return nc.dram_tensor(name, list(shape), dt, kind="ExternalOutput").ap()

    QT_d = din("QT", [4, 192, TT * 128], BF16)
    KT_d = din("KT_all", [4, 192, NKT * 128], BF16)
    V_d = din("V_all", [NKT * 128, 512], BF16)
    EPS = 1e-6
    if full:
        xin = din("xin", [TT * 128, 1024])
        ada_d = din("ada", [2, 6144])
        yaT_d = din("yaT", [256, TT * 128], BF16)
        ybT_d = din("ybT", [256, TT * 128], BF16)
        w_out_d = din("w_out", [1024, 1024])
        w_r_d = din("w_router", [1024, 16])
        x1_o = dout("x1", [TT * 128, 1024])
        h2_o = dout("h2", [TT * 128, 1024], BF16)
        aff_o = dout("aff", [TT * 128, 16])
        affT_o = dout("affT", [16, TT * 128])
    else:
        yc_o = dout("yc", [TT * 128, 512], BF16)

    P = Prog(nc, tag)
    I = P.I
    NCH = 6
    CT = NKT // NCH
    ktn = P.sb([128, NKT * 128], BF16, "ktn")
    ktr = P.sb([128, NKT * 128], BF16, "ktr")
    vh = P.sb([128, NKT, 129], BF16, "vh")
    I("pool", nc.gpsimd.memset, [], [f"vh{c}" for c in range(NCH)], ap=vh[:, :, 128:129], constant=1.0)
    I("pool", nc.gpsimd.memset, [], [f"ktr{c}" for c in range(NCH)], ap=ktr[64:128, :], constant=0.0)
    qnr = Rot(P, 2, [128, TT * 128], BF16, "qtn")
    qrr = Rot(P, 2, [128, TT * 128], BF16, "qtr")
    for (qb_, kqb_) in qrr.bufs:
        I("pool", nc.gpsimd.memset, [], [kqb_], ap=qb_[64:128, :], constant=0.0)
    ptr = Rot(P, 3, [128, 512], BF16, "pt")
    ycr = Rot(P, 3, [128, 128], BF16, "yct")
    rcr = Rot(P, 4, [128, 1], F32, "rc")
    sbank = [P.ps([128, 512], F32, f"sb{i}") for i in range(2)]
    obank = [P.ps([128, 512], F32, f"ob{i}") for i in range(4)]
    V_v = V_d.rearrange("(t p) (h d) -> p t h d", p=128, h=4)
    ntile = NT if last else TT
    if full:
        ident = make_ident(P, nc)
        psTb = P.ps([128, 1024], BF16, "psTb")
        tb = P.ps([128, 512], F32, "tb")
        mixT = P.sb([128, 8, TT * 128], BF16, "mixT")
        mkeys = [f"mix{t}" for t in range(TT)]
        def load_mix():
            P.dma("sp", mixT[:, 0:2, :], yaT_d.rearrange("(c p) t -> p c t", p=128), writes=mkeys)
            P.dma("sp", mixT[:, 2:4, :], ybT_d.rearrange("(c p) t -> p c t", p=128), writes=mkeys)
        w_out = P.sb([128, 8, 1024], BF16, "w_out")
        w_out_v = w_out_d.rearrange("(k p) n -> p k n", p=128)
        for k in range(0, 8, 2):
            P.dma("pool", w_out[:, k:k + 2, :], w_out_v[:, k:k + 2, :], writes=[f"w_out{k}"])
        wokeys = [f"w_out{k}" for k in range(0, 8, 2)]
        w_r = P.sb([128, 8, 16], BF16, "w_r")
        P.dma("pool", w_r[:], w_r_d.rearrange("(k p) n -> p k n", p=128), writes=["w_r"])
        mods = []
        for r in range(2):
            md = P.sb([128, 3072], F32, f"modB{r}")
            P.dma("sp", md[:], ada_d[r:r + 1, 2048:5120].partition_broadcast(128), writes=[f"modB{r}"])
            I("dve", nc.vector.tensor_scalar_add, [f"modB{r}"], [f"modB{r}"], out=md[:, 2048:3072], in0=md[:, 2048:3072], scalar1=1.0)
            mods.append(md)

    blocks = [(qb * 512, 512, NKT) for qb in range(4)]
    if not last:
        blocks.insert(0, (NT * 128, NCT * 128, NCT))
    if nblocks is not None:
        blocks = blocks[:nblocks]

    if full:
        xr_ = Rot(P, 2, [128, 1024], F32, "x")
        tmpr = Rot(P, 2, [128, 1024], F32, "tmp")
        x1r = Rot(P, 2, [128, 1024], F32, "x1")
        h2r = Rot(P, 2, [128, 1024], BF16, "h2")
        h2Tr = Rot(P, 2, [128, 1024], BF16, "h2T")
        junkr = Rot(P, 2, [128, 1024], BF16, "junk")
        str_ = Rot(P, 3, [128, 8], F32, "st")
        exr = Rot(P, 2, [128, 16], F32, "ex")
        afr = Rot(P, 2, [128, 16], F32, "af")
        aTr = Rot(P, 2, [16, 128], F32, "aT")

        def tail_tile(t):
            r = 1 if t >= NT else 0
            md, kmd = mods[r], f"modB{r}"
            x_t, kx = xr_.next()
            P.dma("sp", x_t[:], xin[t * 128:(t + 1) * 128, :], writes=[kx])
            tmp, ktmp = tmpr.next()
            x1, kx1 = x1r.next()
            for nb in range(2):
                cs = slice(nb * 512, (nb + 1) * 512)
                for k in range(8):
                    I("pe", nc.tensor.matmul, [f"mix{t}", wokeys[k // 2]], ["tb"], out=tb[:], lhsT=mixT[:, k, t * 128:(t + 1) * 128],
                      rhs=w_out[:, k, nb * 512:(nb + 1) * 512], start=(k == 0), stop=(k == 7))
                I("dve", nc.vector.tensor_tensor, ["tb", kmd], [ktmp], out=tmp[:, cs], in0=tb[:], in1=md[:, cs], op=ALU.mult)
                I("dve", nc.vector.tensor_tensor, [ktmp, kx], [kx1], out=x1[:, cs], in0=tmp[:, cs], in1=x_t[:, cs], op=ALU.add)
            P.dma("sp", x1_o[t * 128:(t + 1) * 128, :], x1[:], reads=[kx1])
            st, kst = str_.next()
            junk, kj = junkr.next()
            I("act", nc.scalar.activation, [kx1], [kj, kst + "a"], out=junk[:], in_=x1[:], func=AF.Square, accum_out=st[:, 0:1])
            I("act", nc.scalar.activation, [kst + "a"], [kst + "b"], out=st[:, 1:2], in_=st[:, 0:1], func=AF.Sqrt, scale=1.0 / 1024, bias=EPS)
            I("dve", nc.vector.reciprocal, [kst + "b"], [kst + "c"], out=st[:, 2:3], in_=st[:, 1:2])
            tmp2, ktmp2 = tmpr.next()
            h2, kh2 = h2r.next()
            I("dve", nc.vector.scalar_tensor_tensor, [kx1, kst + "c", kmd], [ktmp2], out=tmp2[:], in0=x1[:], scalar=st[:, 2:3], in1=md[:, 2048:3072],
              op0=ALU.mult, op1=ALU.mult)
            I("dve", nc.vector.tensor_tensor, [ktmp2, kmd], [kh2], out=h2[:], in0=tmp2[:], in1=md[:, 1024:2048], op=ALU.add)
            P.dma("sp", h2_o[t * 128:(t + 1) * 128, :], h2[:], reads=[kh2])
            h2T, kh2T = h2Tr.next()
            for hf in range(2):
                for k in range(4):
                    kk = hf * 4 + k
                    I("pe", nc.tensor.transpose, [kh2, "ident"], ["psTb"], out=psTb[:, 512 + k * 128:512 + (k + 1) * 128], in_=h2[:, kk * 128:(kk + 1) * 128],
                      identity=ident[:])
                I("act", nc.scalar.copy, ["psTb"], [kh2T], out=h2T[:, hf * 512:(hf + 1) * 512], in_=psTb[:, 512:1024])
            for k in range(8):
                I("pe", nc.tensor.matmul, [kh2T, "w_r"], ["tb"], out=tb[:, 0:16], lhsT=h2T[:, k * 128:(k + 1) * 128], rhs=w_r[:, k, :],
                  start=(k == 0), stop=(k == 7))
            ex, kex = exr.next()
            af, kaf = afr.next()
            I("dve", nc.vector.reduce_max, ["tb"], [kst + "d"], out=st[:, 3:4], in_=tb[:, 0:16], axis=AX.X)
            I("dve", nc.vector.tensor_scalar, [kst + "d"], [kst + "e"], out=st[:, 4:5], in0=st[:, 3:4], scalar1=-1.0, scalar2=None, op0=ALU.mult)
            I("act", nc.scalar.activation, ["tb", kst + "e"], [kex, kst + "f"], out=ex[:], in_=tb[:, 0:16], func=AF.Exp, bias=st[:, 4:5],
              scale=1.0, accum_out=st[:, 5:6])
            I("dve", nc.vector.reciprocal, [kst + "f"], [kst + "g"], out=st[:, 6:7], in_=st[:, 5:6])
            I("dve", nc.vector.tensor_scalar, [kex, kst + "g"], [kaf], out=af[:], in0=ex[:], scalar1=st[:, 6:7], scalar2=None, op0=ALU.mult)
            P.dma("sp", aff_o[t * 128:(t + 1) * 128, :], af[:], reads=[kaf])
            I("pe", nc.tensor.transpose, [kaf, "identf"], ["tb"], out=tb[0:16, 128:256], in_=af[:], identity=P.identf[:])
            aT, kaT = aTr.next()
            I("act", nc.scalar.copy, ["tb"], [kaT], out=aT[:], in_=tb[0:16, 128:256])
            P.dma("sp", affT_o[:, t * 128:(t + 1) * 128], aT[:], reads=[kaT])


    for h in range(nheads):
        qtn, kqn = qnr.next()
        qtr, kqr = qrr.next()
        P.dma("sp", qtn[:], QT_d[h, 0:128, :], writes=[kqn])
        P.dma("sp", qtr[0:64, :], QT_d[h, 128:192, :], writes=[kqr])
        for c in range(NCH):
            cs = slice(c * CT * 128, (c + 1) * CT * 128)
            P.dma("sp", ktn[:, cs], KT_d[h, 0:128, cs], writes=[f"ktn{c}"])
            P.dma("sp", ktr[0:64, cs], KT_d[h, 128:192, cs], writes=[f"ktr{c}"])
            P.dma("sp", vh[:, c * CT:(c + 1) * CT, 0:128], V_v[:, c * CT:(c + 1) * CT, h, :], writes=[f"vh{c}"])
        if full and h == 0:
            load_mix()
        def attn_block(q0, qn, nkt):
            nsub = qn // 128
            pts = {}

            def S(kt):
                c = kt // CT
                sbk = sbank[kt % 2]
                ks = slice(kt * 128, (kt + 1) * 128)
                I("pe", nc.tensor.matmul, [f"ktn{c}", kqn], [f"sb{kt % 2}"], out=sbk[:, 0:qn], lhsT=ktn[:, ks], rhs=qtn[:, q0:q0 + qn],
                  start=True, stop=False)
                I("pe", nc.tensor.matmul, [f"ktr{c}", kqr], [f"sb{kt % 2}"], out=sbk[:, 0:qn], lhsT=ktr[:, ks], rhs=qtr[:, q0:q0 + qn],
                  start=False, stop=True)

            def E(kt):
                pt, kpt = ptr.next()
                pts[kt] = (pt, kpt)
                I("act", nc.scalar.activation, [f"sb{kt % 2}"], [kpt], out=pt[:, 0:qn], in_=sbank[kt % 2][:, 0:qn], func=AF.Exp, scale=SCALE)

            def PV(kt):
                c = kt // CT
                pt, kpt = pts.pop(kt)
                for qs in range(nsub):
                    ob = obank[qs]
                    off = 0
                    I("pe", nc.tensor.matmul, [kpt, f"vh{c}"], [f"ob{qs}"], out=ob[:, off:off + 129], lhsT=pt[:, qs * 128:(qs + 1) * 128],
                      rhs=vh[:, kt, :], start=(kt == 0), stop=(kt == nkt - 1))

            S(0)
            for kt in range(nkt):
                E(kt)
                if kt + 1 < nkt:
                    S(kt + 1)
                PV(kt)
            for qs in range(nsub):
                ob = obank[qs]
                off = 0
                rc, krc = rcr.next()
                yct, kyct = ycr.next()
                I("dve", nc.vector.reciprocal, [f"ob{qs}"], [krc], out=rc[:], in_=ob[:, off + 128:off + 129])
                I("dve", nc.vector.tensor_scalar, [f"ob{qs}", krc], [kyct], out=yct[:], in0=ob[:, off:off + 128], scalar1=rc[:, 0:1],
                  scalar2=None, op0=ALU.mult)
                r0 = q0 + qs * 128
                if full:
                    I("pe", nc.tensor.transpose, [kyct, "ident"], ["psTb"], out=psTb[:, 0:128], in_=yct[:], identity=ident[:])
                    I("act", nc.scalar.copy, ["psTb"], [f"mix{r0 // 128}"], out=mixT[:, 4 + h, r0:r0 + 128], in_=psTb[:, 0:128])
                else:
                    P.dma("sp", yc_o[r0:r0 + 128, h * 128:(h + 1) * 128], yct[:], reads=[kyct])

        for bi, (q0, qn, nkt) in enumerate(blocks):
            if full and h == nheads - 1 and bi >= 1:
                pq0, pqn, _ = blocks[bi - 1]

                def tails(pq0=pq0, pqn=pqn):
                    for tt_ in range(pq0 // 128, (pq0 + pqn) // 128):
                        tail_tile(tt_)
                P.replay_interleaved([P.capture(attn_block, q0, qn, nkt), P.capture(tails)])
            else:
                attn_block(q0, qn, nkt)
    if full:
        pq0, pqn, _ = blocks[-1]
        for tt_ in range(pq0 // 128, (pq0 + pqn) // 128):
            tail_tile(tt_)
        if last:
            zt, kzt = tmpr.next()
            zh, kzh = h2r.next()
            I("pool", nc.gpsimd.memset, [], [kzt], ap=zt[:], constant=0.0)
            I("pool", nc.gpsimd.memset, [], [kzh], ap=zh[:], constant=0.0)
            for t in range(NT, TT):
                P.dma("sp", x1_o[t * 128:(t + 1) * 128, :], zt[:], reads=[kzt])
                P.dma("sp", h2_o[t * 128:(t + 1) * 128, :], zh[:], reads=[kzh])
                P.dma("sp", aff_o[t * 128:(t + 1) * 128, :], zt[:, 0:16], reads=[kzt])
                P.dma("sp", affT_o[:, t * 128:(t + 1) * 128], zt[0:16, 0:128], reads=[kzt])
    P.emit()
    P.close()
    return nc


NE = 16
CAPM, CAPC = 120, 32
NIT = 26


def build_C(last, nexp=NE, nc=None, T=None, tag=""):
    if nc is None:
        nc = bass.Bass("TRN2", target_bir_lowering=False)

    def din(name, shape, dt=F32):
        if T is not None:
            assert tuple(T[name].shape) == tuple(shape), (name, T[name].shape, shape)
            return T[name]
        return nc.dram_tensor(name, list(shape), dt, kind="ExternalInput").ap()

    x1_d = din("x1", [TT * 128, 1024])
    h2_d = din("h2", [TT * 128, 1024], BF16)
    aff_d = din("aff", [TT * 128, 16])
    affT_d = din("affT_all", [16, 8192])
    affTc_d = din("affTc", [16, 256])
    ada_d = din("ada", [2, 6144])
    wg_d = din("w_gate", [NE, 1024, 512])
    wu_d = din("w_up", [NE, 1024, 512])
    wd_d = din("w_down", [NE, 512, 1024])
    utri_d = din("utri", [128, 128])
    ones_d = din("ones", [128, 128])
    iota_d = din("iota", [128, 128])
    blk_d = din("blockones", [128, 128])
    thrc_d = din("thrc", [128, 2])
    x2_o = T["x2"] if T is not None else nc.dram_tensor("x2", [TT * 128, 1024], F32, kind="ExternalOutput").ap()
    thr_scr = nc.dram_tensor(tag + "thr_scr", [128, 2], F32, kind="Internal").ap()

    P = Prog(nc, tag)
    I = P.I
    ident = make_ident(P, nc)
    ntile = NT if last else TT
    groups = [(g, [4 * g + j for j in range(4)], CAPM, g * CAPM) for g in range(4)]
    if not last:
        groups.append((4, [NT, NT + 1], CAPC, 4 * CAPM))
    nslots = 4 * CAPM + (0 if last else CAPC)
    tile_group = {}
    for (g, tiles, cap, s0) in groups:
        for t in tiles:
            tile_group[t] = (g, cap, s0)

    utri = P.sb([128, 128], BF16, "utri")
    P.dma("pool", utri[:], utri_d, writes=["utri"])
    onesb = P.sb([128, 128], BF16, "onesb")
    P.dma("pool", onesb[:], ones_d, writes=["onesb"])
    iota = P.sb([128, 128], F32, "iota")
    P.dma("sp", iota[:], iota_d, writes=["iota"])
    blk = P.sb([128, 128], F32, "blk")
    P.dma("sp", blk[:], blk_d, writes=["blk"])
    thrc = P.sb([128, 2], F32, "thrc")
    P.dma("sp", thrc[:], thrc_d, writes=["thrc"])
    acc = P.sb([128, TT, 1024], F32, "acc")
    Am = acc[:, 0, :]
    P.dma("sp", Am, affT_d.rearrange("e (s t) -> (e s) t", s=8), writes=["acc0"])
    Ac = P.sb([128, 32], F32, "Ac")
    P.dma("sp", Ac[:], affTc_d.rearrange("e (s t) -> (e s) t", s=8), writes=["Ac"])
    affs = P.sb([128, TT, 16], F32, "affs")
    P.dma("sp", affs[:], aff_d.rearrange("(t p) e -> p t e", p=128), writes=["affs"])
    h2tok = P.sb([128, TT, 1024], BF16, "h2tok")
    for t0 in range(0, TT, 6):
        P.dma("sp", h2tok[:, t0:t0 + 6, :], h2_d.rearrange("(t p) d -> p t d", p=128)[:, t0:t0 + 6, :], writes=[f"h2tok{t0}"])
    h2keys = [f"h2tok{t0}" for t0 in range(0, TT, 6)]
    g2 = []
    for r in range(2):
        md = P.sb([128, 1024], F32, f"g2_{r}")
        P.dma("sp", md[:], ada_d[r:r + 1, 5120:6144].partition_broadcast(128), writes=[f"g2_{r}"])
        g2.append(md)

    pp = P.ps([128, 512], F32, "pp")
    psTb = P.ps([128, 1024], BF16, "psTb")
    gA = P.ps([128, 512], F32, "gA")
    gB = P.ps([128, 512], F32, "gB")
    Gb = P.ps([128, 512], F32, "Gb")
    Ub = P.ps([128, 512], F32, "Ub")
    yA = P.ps([128, 512], F32, "yA")
    yB = P.ps([128, 512], F32, "yB")

    hidT = P.sb([128, 4, 512], BF16, "hidT")
    junk = hidT[:, 0:2, :].rearrange("p a b -> p (a b)")
    lo = P.sb([128, 2], F32, "lo")
    negmid = P.sb([128, 2], F32, "negmid")
    ssum = P.sb([128, 2], F32, "ssum")
    cond = P.sb([128, 2], F32, "cond")
    I("dve", nc.vector.memset, [], ["lo"], ap=lo[:], constant=0.0)
    I("dve", nc.vector.memset, [], ["negmid"], ap=negmid[:], constant=-0.5)
    I("dve", nc.vector.memset, [], ["ssum0", "ssum1"], ap=ssum[:], constant=0.0)
    for it in range(NIT):
        w = 2.0 ** -(it + 1)
        I("act", nc.scalar.activation, ["acc0", "negmid"], ["hidT", "ssum0"], out=junk[:, 0:1024], in_=Am, func=AF.Sign, bias=negmid[:, 0:1], scale=1.0,
          accum_out=ssum[:, 0:1])
        I("act", nc.scalar.activation, ["Ac", "negmid"], ["hidT", "ssum1"], out=junk[:, 0:32], in_=Ac[:], func=AF.Sign, bias=negmid[:, 1:2], scale=1.0,
          accum_out=ssum[:, 1:2])
        I("pe", nc.tensor.matmul, ["blk", "ssum0", "ssum1"], ["pp"], out=pp[:, 0:2], lhsT=blk[:], rhs=ssum[:], start=True, stop=True)
        I("dve", nc.vector.tensor_tensor, ["pp", "thrc"], ["cond"], out=cond[:], in0=pp[:, 0:2], in1=thrc[:], op=ALU.is_ge)
        I("dve", nc.vector.scalar_tensor_tensor, ["cond", "lo"], ["lo"], out=lo[:], in0=cond[:], scalar=w, in1=lo[:], op0=ALU.mult, op1=ALU.add)
        I("dve", nc.vector.tensor_scalar, ["lo"], ["negmid"], out=negmid[:], in0=lo[:], scalar1=w * 0.5, scalar2=-1.0, op0=ALU.add, op1=ALU.mult)
    P.dma("sp", thr_scr, lo[:], reads=["lo"], writes=["thr_scr"])
    thr_all = P.sb([128, 256], F32, "thr_all")
    P.dma("sp", thr_all[:], thr_scr.rearrange("p c -> (p c)").unsqueeze(0).partition_broadcast(128), reads=["thr_scr"], writes=["thr_all"])
    thr_v = thr_all[:].rearrange("p (e s c) -> p e s c", s=8, c=2)

    maskf = P.sb([128, TT, 16], F32, "maskf")
    maskb = P.sb([128, TT, 16], BF16, "maskb")
    gm = P.sb([128, TT, 16], F32, "gm")
    pos = P.sb([128, TT, 16], F32, "pos")
    for t in range(ntile):
        col = 1 if t >= NT else 0
        I("dve", nc.vector.tensor_tensor, ["affs", "thr_all"], ["maskf"], out=maskf[:, t, :], in0=affs[:, t, :], in1=thr_v[:, :, 0, col], op=ALU.is_ge)
    I("dve", nc.vector.tensor_copy, ["maskf"], ["maskb"], out=maskb[:, 0:ntile, :], in_=maskf[:, 0:ntile, :])
    I("dve", nc.vector.tensor_tensor, ["maskf", "affs"], ["gm"], out=gm[:, 0:ntile, :], in0=maskf[:, 0:ntile, :], in1=affs[:, 0:ntile, :], op=ALU.mult)
    for (g, tiles, cap, s0) in groups:
        for jj, t in enumerate(tiles):
            for i in range(jj + 1):
                I("pe", nc.tensor.matmul, ["maskb", "utri", "onesb"], ["pp"], out=pp[:, 16 * (t % 16):16 * (t % 16) + 16],
                  lhsT=(utri[:] if i == jj else onesb[:]), rhs=maskb[:, tiles[i], :], start=(i == 0), stop=(i == jj))
            I("dve", nc.vector.tensor_copy, ["pp"], ["pos"], out=pos[:, t, :], in_=pp[:, 16 * (t % 16):16 * (t % 16) + 16])

    wgr = Rot(P, 2, [128, 8, 512], BF16, "wg")
    wur = Rot(P, 2, [128, 8, 512], BF16, "wu")
    wdr = Rot(P, 1, [128, 4, 1024], BF16, "wd")
    Sall = P.sb([128, TT, 128], BF16, "Sall")
    Sgall = P.sb([128, TT, 128], BF16, "Sgall")
    SgT = P.sb([128, TT, 128], BF16, "SgT")
    I("pool", nc.gpsimd.memset, [], [f"S{t}" for t in range(TT)], ap=Sall[:], constant=0.0)
    I("pool", nc.gpsimd.memset, [], [f"Sg{t}" for t in range(TT)], ap=Sgall[:], constant=0.0)
    xsT = P.sb([128, 8, 512], BF16, "xsT")
    sgt = P.sb([128, 512], F32, "sgt")
    yr = Rot(P, 1, [128, 5, 1024], BF16, "ysb")
    cnt = 0
    W, Y = {}, {}

    def load_w(e, which):
        if which == "gu":
            wg, kwg = wgr.next()
            wu, kwu = wur.next()
            W[e] = [wg, kwg, wu, kwu, None, None]
            for hk in range(2):
                P.dma("pool", wg[:, hk * 4:(hk + 1) * 4, :], wg_d[e].rearrange("(k p) f -> p k f", p=128)[:, hk * 4:(hk + 1) * 4, :], writes=[f"{kwg}_{hk}"])
                P.dma("pool", wu[:, hk * 4:(hk + 1) * 4, :], wu_d[e].rearrange("(k p) f -> p k f", p=128)[:, hk * 4:(hk + 1) * 4, :], writes=[f"{kwu}_{hk}"])
        else:
            wd, kwd = wdr.next()
            W[e][4], W[e][5] = wd, kwd
            for hk in range(2):
                P.dma("pool", wd[:, hk * 2:(hk + 1) * 2, :], wd_d[e].rearrange("(k p) d -> p k d", p=128)[:, hk * 2:(hk + 1) * 2, :], writes=[f"{kwd}_{hk}"])

    def build_S(e):
        for t in range(ntile):
            g, cap, s0 = tile_group[t]
            I("dve", nc.vector.tensor_scalar, ["iota", "pos", "maskf"], [f"S{t}"], out=Sall[:, t, 0:cap], in0=iota[:, 0:cap], scalar1=pos[:, t, e:e + 1],
              scalar2=maskf[:, t, e:e + 1], op0=ALU.is_equal, op1=ALU.mult)

    def build_Sg(e):
        for t in range(ntile):
            g, cap, s0 = tile_group[t]
            I("dve", nc.vector.tensor_scalar, ["iota", "pos", "gm"], [f"Sg{t}"], out=Sgall[:, t, 0:cap], in0=iota[:, 0:cap], scalar1=pos[:, t, e:e + 1],
              scalar2=gm[:, t, e:e + 1], op0=ALU.is_equal, op1=ALU.mult)

    def gather(e):
        for (g, tiles, cap, s0) in groups:
            for dk in range(8):
                bank, kb = (gA, "gA") if dk < 4 else (gB, "gB")
                o0 = (dk % 4) * cap
                for jj, t in enumerate(tiles):
                    I("pe", nc.tensor.matmul, [h2keys[t // 6], f"S{t}"], [kb], out=bank[:, o0:o0 + cap], lhsT=h2tok[:, t, dk * 128:(dk + 1) * 128],
                      rhs=Sall[:, t, 0:cap], start=(jj == 0), stop=(jj == len(tiles) - 1))
            I("act", nc.scalar.copy, ["gA"], ["xsT"], out=xsT[:, 0:4, s0:s0 + cap], in_=gA[:, 0:4 * cap].rearrange("p (k s) -> p k s", k=4))
            I("act", nc.scalar.copy, ["gB"], ["xsT"], out=xsT[:, 4:8, s0:s0 + cap], in_=gB[:, 0:4 * cap].rearrange("p (k s) -> p k s", k=4))

    def ffn(e):
        wg, kwg, wu, kwu, _, _ = W[e]
        for fk in range(4):
            for k in range(8):
                I("pe", nc.tensor.matmul, ["xsT", f"{kwg}_{k // 4}"], ["Gb"], out=Gb[:, 0:nslots], lhsT=wg[:, k, fk * 128:(fk + 1) * 128], rhs=xsT[:, k, 0:nslots],
                  start=(k == 0), stop=(k == 7))
            for k in range(8):
                I("pe", nc.tensor.matmul, ["xsT", f"{kwu}_{k // 4}"], ["Ub"], out=Ub[:, 0:nslots], lhsT=wu[:, k, fk * 128:(fk + 1) * 128], rhs=xsT[:, k, 0:nslots],
                  start=(k == 0), stop=(k == 7))
            I("act", nc.scalar.activation, ["Gb"], ["sgt"], out=sgt[:, 0:nslots], in_=Gb[:, 0:nslots], func=AF.Silu)
            I("dve", nc.vector.tensor_tensor, ["sgt", "Ub"], ["hidT"], out=hidT[:, fk, 0:nslots], in0=sgt[:, 0:nslots], in1=Ub[:, 0:nslots], op=ALU.mult)

    def down(e):
        wd, kwd = W[e][4], W[e][5]
        ysb, kys = yr.next()
        Y[e] = (ysb, kys)
        for (g, tiles, cap, s0) in groups:
            for nb, (bank, kb) in enumerate([(yA, "yA"), (yB, "yB")]):
                for fk in range(4):
                    I("pe", nc.tensor.matmul, ["hidT", f"{kwd}_{fk // 2}"], [kb], out=bank[0:cap, :], lhsT=hidT[:, fk, s0:s0 + cap],
                      rhs=wd[:, fk, nb * 512:(nb + 1) * 512], start=(fk == 0), stop=(fk == 3))
            I("act", nc.scalar.copy, ["yA"], [f"{kys}_{g}"], out=ysb[0:cap, g, 0:512], in_=yA[0:cap, :])
            I("act", nc.scalar.copy, ["yB"], [f"{kys}_{g}"], out=ysb[0:cap, g, 512:1024], in_=yB[0:cap, :])

    def sgtr(e):
        for t0 in range(0, ntile, 8):
            tl = list(range(t0, min(t0 + 8, ntile)))
            for t in tl:
                I("pe", nc.tensor.transpose, [f"Sg{t}", "ident"], ["psTb"], out=psTb[:, (t - t0) * 128:(t - t0 + 1) * 128], in_=Sgall[:, t, :], identity=ident[:])
            I("act", nc.scalar.copy, ["psTb"], [f"SgT{t}" for t in tl], out=SgT[:, t0:t0 + len(tl), :],
              in_=psTb[:, 0:len(tl) * 128].rearrange("p (t s) -> p t s", s=128))

    sbanks = [(gA, "gA"), (gB, "gB"), (yA, "yA"), (yB, "yB")]

    def scatter(e):
        ysb, kys = Y[e]
        i = 0
        for t in range(ntile):
            g, cap, s0 = tile_group[t]
            for nb in range(2):
                bank, kb = sbanks[i % 4]
                i += 1
                I("pe", nc.tensor.matmul, [f"SgT{t}", f"{kys}_{g}"], [kb], out=bank[:], lhsT=SgT[0:cap, t, :], rhs=ysb[0:cap, g, nb * 512:(nb + 1) * 512],
                  start=True, stop=True)
                cs = slice(nb * 512, (nb + 1) * 512)
                if e == 0:
                    I("dve", nc.vector.tensor_copy, [kb], [f"acc{t}"], out=acc[:, t, cs], in_=bank[:])
                else:
                    I("dve", nc.vector.tensor_tensor, [kb, f"acc{t}"], [f"acc{t}"], out=acc[:, t, cs], in0=acc[:, t, cs], in1=bank[:], op=ALU.add)

    load_w(0, "gu")
    load_w(0, "d")
    build_S(0)
    build_Sg(0)
    for e in range(nexp):
        if e + 1 < nexp:
            load_w(e + 1, "gu")
        gather(e)
        if e + 1 < nexp:
            build_S(e + 1)
        ffn(e)
        down(e)
        if e + 1 < nexp:
            load_w(e + 1, "d")
        sgtr(e)
        if e + 1 < nexp:
            build_Sg(e + 1)
        scatter(e)
    x1r = Rot(P, 1, [128, 1024], F32, "x1t")
    for t in range(ntile):
        x1t, kx1 = x1r.next()
        P.dma("sp", x1t[:], x1_d[t * 128:(t + 1) * 128, :], writes=[kx1])
        r = 1 if t >= NT else 0
        I("dve", nc.vector.tensor_tensor, [f"acc{t}", f"g2_{r}"], [f"acc{t}"], out=acc[:, t, :], in0=acc[:, t, :], in1=g2[r][:], op=ALU.mult)
        I("dve", nc.vector.tensor_tensor, [f"acc{t}", kx1], [f"acc{t}"], out=acc[:, t, :], in0=acc[:, t, :], in1=x1t[:], op=ALU.add)
        P.dma("sp", x2_o[t * 128:(t + 1) * 128, :], acc[:, t, :], reads=[f"acc{t}"])
    if last:
        for t in range(NT, TT):
            I("pool", nc.gpsimd.memset, [], [f"acc{t}"], ap=acc[:, t, :], constant=0.0)
            P.dma("sp", x2_o[t * 128:(t + 1) * 128, :], acc[:, t, :], reads=[f"acc{t}"])
    P.emit()
    P.close()
    return nc


NV = 4


def build_fused(nlayers=2, nv=NV):
    nc = bass.Bass("TRN2", target_bir_lowering=False)

    def ext(name, shape, dt=F32):
        return nc.dram_tensor(name, list(shape), dt, kind="ExternalInput").ap()

    def scr(name, shape, dt=F32):
        return nc.dram_tensor(name, list(shape), dt, kind="Internal").ap()

    E = {
        "x": ext("x", [8192, 1024]), "ctx": ext("ctx", [256, 1024]), "cT": ext("cT", [128, 8, 2]),
        "w_ada": ext("w_ada", [2, 1024, 6144]), "b_ada": ext("b_ada", [2, 6144]), "w_in": ext("w_in", [2, 1024, 1216]),
        "w_sguT": ext("w_sguT", [2, 128, 4, 128]), "b_sguT": ext("b_sguT", [2, 128, 4]),
        "sgu_norm": ext("sgu_norm", [2, 256]), "q_lora_norm": ext("q_lora_norm", [2, 256]), "kv_lora_norm": ext("kv_lora_norm", [2, 128]),
        "q_norm": ext("q_norm", [2, 192]), "k_norm": ext("k_norm", [2, 192]),
        "w_uq": ext("w_uq", [2, 256, 768]), "w_ukv": ext("w_ukv", [2, 128, 1024]), "dftc": ext("dftc", [256, 512]),
        "rope_cos": ext("rope_cos", [4, TT * 128, 64]), "rope_sin": ext("rope_sin", [4, TT * 128, 64]),
        "WA": ext("WA", [128, 128]), "TC": ext("TC", [4, 128, 2, 64, 32]), "TCc": ext("TCc", [128, 2, 2, 256]),
        "w_out": ext("w_out", [2, 1024, 1024]), "w_router": ext("w_router", [2, 1024, 16]),
        "w_gate": ext("w_gate", [2, 16, 1024, 512]), "w_up": ext("w_up", [2, 16, 1024, 512]), "w_down": ext("w_down", [2, 16, 512, 1024]),
        "utri": ext("utri", [128, 128]), "ones": ext("ones", [128, 128]), "iota": ext("iota", [128, 128]),
        "blockones": ext("blockones", [128, 128]), "thrc": ext("thrc", [128, 2]),
    }
    out = nc.dram_tensor("out", [8192, 1024], F32, kind="ExternalOutput").ap()
    xmid = scr("xmid", [8192, 1024])
    xcmid = scr("xcmid", [256, 1024])
    Z_all = scr("Z_all", [8192, 512], BF16)
    KT_all = scr("KT_all", [4, 192, 66 * 128], BF16)
    V_all = scr("V_all", [66 * 128, 512], BF16)
    affT_all = scr("affT_all", [16, 8192])
    affTc = scr("affTc", [16, 256])
    V = []
    for v in range(nv):
        V.append({
            "xin": scr(f"xin{v}", [TT * 128, 1024]), "ada": scr(f"ada{v}", [2, 6144]), "yaT": scr(f"yaT{v}", [256, TT * 128], BF16),
            "Z": scr(f"Z{v}", [TT * 128, 512], BF16), "QT": scr(f"QT{v}", [4, 192, TT * 128], BF16), "KT": scr(f"KT{v}", [4, 192, TT * 128], BF16),
            "V": scr(f"V{v}", [TT * 128, 512], BF16), "ybT": scr(f"ybT{v}", [256, TT * 128], BF16), "x1": scr(f"x1{v}", [TT * 128, 1024]),
            "h2": scr(f"h2{v}", [TT * 128, 1024], BF16), "aff": scr(f"aff{v}", [TT * 128, 16]), "affT": scr(f"affT{v}", [16, TT * 128]),
            "x2": scr(f"x2{v}", [TT * 128, 1024]),
        })
    M = NT * 128
    phase_no = [0]

    def phase(fn):
        phase_no[0] += 1
        with nc.cleanup_on_exit():
            fn(f"p{phase_no[0]}_")
            nc.all_engine_barrier()

    def glue(copies):
        def body(tag):
            P = Prog(nc, tag)
            for i, (dst, src) in enumerate(copies):
                P.dma("sp", dst, src)
            P.emit()
            P.close()
        phase(body)

    for l in range(nlayers):
        last = l == nlayers - 1
        xsrc, csrc = (E["x"], E["ctx"]) if l == 0 else (xmid, xcmid)
        xdst = out if last else xmid
        cp = []
        for v in range(nv):
            cp.append((V[v]["xin"][0:M, :], xsrc[v * M:(v + 1) * M, :]))
            cp.append((V[v]["xin"][M:TT * 128, :], csrc))
        glue(cp)
        for v in range(nv):
            TA = {"xin": V[v]["xin"], "cT": E["cT"], "w_ada": E["w_ada"][l], "b_ada": E["b_ada"][l:l + 1, :], "w_in": E["w_in"][l],
                  "w_sguT": E["w_sguT"][l], "b_sguT": E["b_sguT"][l], "sgu_norm": E["sgu_norm"][l:l + 1, :],
                  "q_lora_norm": E["q_lora_norm"][l:l + 1, :], "kv_lora_norm": E["kv_lora_norm"][l:l + 1, :],
                  "q_norm": E["q_norm"][l:l + 1, :], "k_norm": E["k_norm"][l:l + 1, :], "w_uq": E["w_uq"][l], "w_ukv": E["w_ukv"][l],
                  "dftc": E["dftc"], "rope_cos": E["rope_cos"][v], "rope_sin": E["rope_sin"][v],
                  "ada": V[v]["ada"], "yaT": V[v]["yaT"], "Z": V[v]["Z"], "QT": V[v]["QT"], "KT": V[v]["KT"], "V": V[v]["V"]}
            phase(lambda tag, TA=TA: build_A(last, nc=nc, T=TA, tag=tag))
        cp = [(KT_all[:, :, 0:NCT * 128], V[0]["KT"][:, :, M:TT * 128]), (V_all[0:NCT * 128, :], V[0]["V"][M:TT * 128, :])]
        for v in range(nv):
            cp.append((Z_all[v * M:(v + 1) * M, :], V[v]["Z"][0:M, :]))
            cp.append((KT_all[:, :, NCT * 128 + v * M:NCT * 128 + (v + 1) * M], V[v]["KT"][:, :, 0:M]))
            cp.append((V_all[NCT * 128 + v * M:NCT * 128 + (v + 1) * M, :], V[v]["V"][0:M, :]))
        glue(cp)
        for v in range(nv):
            TF = {"Z_all": Z_all, "Zc": V[v]["Z"][M:TT * 128, :], "WA": E["WA"], "TC": E["TC"][v], "TCc": E["TCc"], "ybT": V[v]["ybT"]}
            phase(lambda tag, TF=TF: build_F(last, nc=nc, T=TF, tag=tag))
        for v in range(nv):
            TB = {"QT": V[v]["QT"], "KT_all": KT_all, "V_all": V_all, "xin": V[v]["xin"], "ada": V[v]["ada"], "yaT": V[v]["yaT"], "ybT": V[v]["ybT"],
                  "w_out": E["w_out"][l], "w_router": E["w_router"][l], "x1": V[v]["x1"], "h2": V[v]["h2"], "aff": V[v]["aff"], "affT": V[v]["affT"]}
            phase(lambda tag, TB=TB: build_B(last, nc=nc, T=TB, tag=tag))
        cp = [(affTc, V[0]["affT"][:, M:TT * 128])]
        for v in range(nv):
            cp.append((affT_all[:, v * M:(v + 1) * M], V[v]["affT"][:, 0:M]))
        glue(cp)
        for v in range(nv):
            TCd = {"x1": V[v]["x1"], "h2": V[v]["h2"], "aff": V[v]["aff"], "affT_all": affT_all, "affTc": affTc, "ada": V[v]["ada"],
                   "w_gate": E["w_gate"][l], "w_up": E["w_up"][l], "w_down": E["w_down"][l], "utri": E["utri"], "ones": E["ones"], "iota": E["iota"],
                   "blockones": E["blockones"], "thrc": E["thrc"], "x2": V[v]["x2"]}
            phase(lambda tag, TCd=TCd: build_C(last, nc=nc, T=TCd, tag=tag))
        cp = [(xdst[v * M:(v + 1) * M, :], V[v]["x2"][0:M, :]) for v in range(nv)]
        if not last:
            cp.append((xcmid, V[0]["x2"][M:TT * 128, :]))
        glue(cp)
    return nc


BF = ml_dtypes.bfloat16
NCORES = 8
TPC = 2048
NCTX = 256
SEQ = 8192


def f32(a):
    return np.ascontiguousarray(a, dtype=np.float32)


def const_dftc():
    c = np.arange(64)
    ang = 2 * np.pi * np.outer(c, c) / 64.0
    C, S = np.cos(ang), np.sin(ang)
    d = np.zeros((256, 512), np.float64)
    for g in range(4):
        d[g * 64:(g + 1) * 64, g * 64:(g + 1) * 64] = C
        d[g * 64:(g + 1) * 64, 256 + g * 64:256 + (g + 1) * 64] = -S
    return f32(d)


def const_rope(core):
    tok0 = (core % 4) * TPC
    n = np.arange(tok0, tok0 + TPC)
    pos_row = (n // 64).astype(np.float32)
    pos_col = (n % 64).astype(np.float32)
    freqs = (np.float32(10000.0) ** (-np.arange(16, dtype=np.float32) / np.float32(16))).astype(np.float32)
    cos = np.ones((TPC + NCTX, 64), np.float32)
    sin = np.zeros((TPC + NCTX, 64), np.float32)
    for b, pos in enumerate([pos_row, pos_col]):
        ang = (pos[:, None] * freqs[None, :]).astype(np.float32)
        cs, sn = np.cos(ang).astype(np.float32), np.sin(ang).astype(np.float32)
        cos[:TPC, b * 32:b * 32 + 16] = cs
        cos[:TPC, b * 32 + 16:b * 32 + 32] = cs
        sin[:TPC, b * 32:b * 32 + 16] = -sn
        sin[:TPC, b * 32 + 16:b * 32 + 32] = sn
    return cos, sin


def inputs_A(inp, l, x_cur, xc_cur):
    dftc = const_dftc()
    shared = {
        "w_ada": f32(inp["w_ada"][l]), "b_ada": f32(inp["b_ada"][l][None, :]), "w_in": f32(inp["w_in"][l]),
        "w_sguT": f32(np.transpose(inp["w_sgu"][l], (2, 0, 1))),
        "b_sguT": f32(inp["b_sgu"][l].T),
        "sgu_norm": f32(inp["sgu_norm"][l][None]), "q_lora_norm": f32(inp["q_lora_norm"][l][None]),
        "kv_lora_norm": f32(inp["kv_lora_norm"][l][None]), "q_norm": f32(inp["q_norm"][l][None]), "k_norm": f32(inp["k_norm"][l][None]),
        "w_uq": f32(inp["w_uq"][l]), "w_ukv": f32(inp["w_ukv"][l]), "dftc": dftc,
    }
    maps = []
    for core in range(NCORES):
        b, tok0 = core // 4, (core % 4) * TPC
        cos, sin = const_rope(core)
        cT = np.stack([np.asarray(inp["c"][b]).reshape(8, 128).T, np.asarray(inp["c_ctx"]).reshape(8, 128).T], axis=-1)
        m = dict(shared)
        m.update({"xin": f32(np.concatenate([x_cur[b, tok0:tok0 + TPC], xc_cur[b]], 0)), "cT": f32(cT), "rope_cos": cos, "rope_sin": sin})
        maps.append(m)
    return maps


def const_fft(core):
    n1 = np.arange(64)
    ang = 2 * np.pi * np.outer(n1, n1) / 64.0
    C, S = np.cos(ang), np.sin(ang)
    WA = np.zeros((128, 128))
    WA[0:64, 0:64] = C; WA[64:128, 0:64] = S; WA[0:64, 64:128] = -S; WA[64:128, 64:128] = C
    k2_0 = 32 * (core % 4)
    n2 = np.arange(128)[:, None, None]
    k1 = np.arange(64)[None, :, None]
    k2 = (k2_0 + np.arange(32))[None, None, :]
    k = k1 + 64 * k2
    th = 2 * np.pi * ((n2 * k) % 8192) / 8192.0
    nrm = 1.0 / np.sqrt(8192.0 * 64.0)
    TC = np.stack([np.cos(th) * nrm, np.sin(th) * nrm], axis=1)
    n = (np.arange(2)[None, :, None] * 128 + np.arange(128)[:, None, None])
    kk = np.arange(256)[None, None, :]
    thc = 2 * np.pi * ((n * kk) % 256) / 256.0
    nrc = 1.0 / np.sqrt(256.0 * 64.0)
    TCc = np.stack([np.cos(thc) * nrc, np.sin(thc) * nrc], axis=2)
    return f32(WA), f32(TC), f32(TCc)


def const_moe():
    k = np.arange(128)
    utri = (k[:, None] < k[None, :]).astype(np.float32)
    ones = np.ones((128, 128), np.float32)
    iota = np.tile(np.arange(128, dtype=np.float32)[None, :], (128, 1))
    blk = ((k[:, None] // 8) == (k[None, :] // 8)).astype(np.float32)
    thrc = np.tile(np.array([[2 * 1024 - 8192, 2 * 32 - 256]], np.float32), (128, 1))
    return {"utri": utri, "ones": ones, "iota": iota, "blockones": blk, "thrc": thrc}


def inputs_fused(inp):
    ropes = [const_rope(v) for v in range(4)]
    ffts = [const_fft(v) for v in range(4)]
    shared = {
        "w_ada": f32(inp["w_ada"]), "b_ada": f32(inp["b_ada"]), "w_in": f32(inp["w_in"]),
        "w_sguT": f32(np.transpose(inp["w_sgu"], (0, 3, 1, 2))), "b_sguT": f32(np.transpose(inp["b_sgu"], (0, 2, 1))),
        "sgu_norm": f32(inp["sgu_norm"]), "q_lora_norm": f32(inp["q_lora_norm"]), "kv_lora_norm": f32(inp["kv_lora_norm"]),
        "q_norm": f32(inp["q_norm"]), "k_norm": f32(inp["k_norm"]), "w_uq": f32(inp["w_uq"]), "w_ukv": f32(inp["w_ukv"]),
        "dftc": const_dftc(), "rope_cos": f32(np.stack([r[0] for r in ropes])), "rope_sin": f32(np.stack([r[1] for r in ropes])),
        "WA": ffts[0][0], "TC": f32(np.stack([f[1] for f in ffts])), "TCc": ffts[0][2],
        "w_out": f32(inp["w_out"]), "w_router": f32(inp["w_router"]), "w_gate": f32(inp["w_gate"]), "w_up": f32(inp["w_up"]),
        "w_down": f32(inp["w_down"]),
    }
    shared.update(const_moe())
    maps = []
    for b in range(2):
        cT = np.stack([np.asarray(inp["c"][b]).reshape(8, 128).T, np.asarray(inp["c_ctx"]).reshape(8, 128).T], axis=-1)
        m = dict(shared)
        m.update({"x": f32(inp["x"][b]), "ctx": f32(inp["ctx"][b]), "cT": f32(cT)})
        maps.append(m)
    return maps


def kernel(**inputs):
    inp = {k: np.asarray(v) for k, v in inputs.items()}
    nc = build_fused()
    maps = inputs_fused(inp)
    res = run_bass_kernel_spmd(nc, maps, core_ids=[0, 1]).results
    return np.stack([np.asarray(res[b]["out"], dtype=np.float32) for b in range(2)])
```

```python
import numpy as np
import ml_dtypes
from concourse.bass_utils import run_bass_kernel_spmd

import contextlib
import numpy as np
import concourse.bass as bass
import concourse.mybir as mybir

F32 = mybir.dt.float32
BF16 = mybir.dt.bfloat16
I32 = mybir.dt.int32
ALU = mybir.AluOpType
AF = mybir.ActivationFunctionType
AX = mybir.AxisListType

ENGS = ("pe", "act", "dve", "pool", "sp")
NDMASEM = 8


class Op:
    __slots__ = ("eng", "fn", "dma", "deps", "needs_sig", "sig_idx", "sem_i", "sem_val", "idx", "prev_same_sem")

    def __init__(self, eng, fn, dma):
        self.eng = eng
        self.fn = fn
        self.dma = dma
        self.deps = []
        self.needs_sig = False
        self.sig_idx = 0
        self.sem_i = -1
        self.sem_val = 0
        self.prev_same_sem = None


class BufState:
    __slots__ = ("last_w", "readers")

    def __init__(self):
        self.last_w = None
        self.readers = []


class Prog:
    def __init__(self, nc, tag=""):
        self.nc = nc
        self.tag = tag
        self.ops = {e: [] for e in ENGS}
        self.st = {}
        self.es = contextlib.ExitStack()
        self.ndma = {e: 0 for e in ENGS}
        self.dma_last = {}
        self.dma_tot = {}
        self.all_dma = []
        self.nsb = 0
        self.fence = None
        self.psum_keys = set()

    def sb(self, shape, dtype, name=None):
        self.nsb += 1
        name = name or f"sb{self.nsb}"
        return self.es.enter_context(self.nc.sbuf_tensor("s_" + self.tag + name, list(shape), dtype))

    def ps(self, shape, dtype, name=None):
        self.nsb += 1
        name = name or f"ps{self.nsb}"
        self.psum_keys.add(name)
        return self.es.enter_context(self.nc.psum_tensor("p_" + self.tag + name, list(shape), dtype))

    def _state(self, k):
        s = self.st.get(k)
        if s is None:
            s = self.st[k] = BufState()
        return s

    def capture(self, fn, *args):
        self.cap = []
        fn(*args)
        c, self.cap = self.cap, None
        return c

    def replay_interleaved(self, lists):
        idx = [0] * len(lists)
        while True:
            best, bf = -1, 2.0
            for i, l in enumerate(lists):
                if idx[i] < len(l):
                    f = idx[i] / len(l)
                    if f < bf:
                        best, bf = i, f
            if best < 0:
                break
            self.op(*lists[best][idx[best]])
            idx[best] += 1

    def op(self, eng, fn, reads=(), writes=(), dma=False):
        if getattr(self, "cap", None) is not None:
            self.cap.append((eng, fn, tuple(reads), tuple(writes), dma))
            return None
        o = Op(eng, fn, dma)
        deps = []
        pr = [k for k in reads if k in self.psum_keys]
        if pr:
            reads = [k for k in reads if k not in self.psum_keys]
            writes = list(writes) + [k for k in pr if k not in writes]
        for k in reads:
            s = self._state(k)
            if s.last_w is not None:
                deps.append((s.last_w, "raw"))
        for k in writes:
            s = self._state(k)
            if s.last_w is not None:
                deps.append((s.last_w, "waw"))
            for r in s.readers:
                deps.append((r, "war"))
        if self.fence is not None:
            deps.append((self.fence, "raw"))
        seen = set()
        for d, kind in deps:
            if d is o or id(d) in seen:
                continue
            if (not o.dma) and (not d.dma) and d.eng == o.eng:
                if o.eng == "pe":
                    continue
            seen.add(id(d))
            o.deps.append(d)
        for k in reads:
            self._state(k).readers.append(o)
        for k in writes:
            s = self._state(k)
            s.last_w = o
            s.readers = []
        if dma:
            i = self.ndma[eng] % NDMASEM
            self.ndma[eng] += 1
            o.sem_i = i
            key = (eng, i)
            o.prev_same_sem = self.dma_last.get(key)
            o.sem_val = self.dma_tot.get(key, 0) + 16
            self.dma_tot[key] = o.sem_val
            self.dma_last[key] = o
            self.all_dma.append(o)
        self.ops[eng].append(o)
        return o

    def barrier(self, bar_tile):
        nc = self.nc
        lasts = []
        for e in ENGS:
            comp = [o for o in self.ops[e] if not o.dma]
            if comp:
                lasts.append(comp[-1])
        lasts.extend(self.dma_last.values())
        old = self.fence
        self.fence = None
        b = self.op("dve", lambda: nc.vector.memset(bar_tile, 0.0), (), ())
        for d in lasts:
            if d is not b and d not in b.deps and not (d.eng == "pe" and False):
                b.deps.append(d)
        if old is not None and old not in b.deps:
            b.deps.append(old)
        self.fence = b
        return b

    def I(self, eng, method, reads=(), writes=(), **kw):
        return self.op(eng, lambda: method(**kw), reads, writes)

    def dma(self, eng, out, in_, reads=(), writes=(), **kw):
        e = {"sp": self.nc.sync, "pool": self.nc.gpsimd, "act": self.nc.scalar}[eng]
        return self.op(eng, lambda: e.dma_start(out=out, in_=in_, **kw), reads, writes, dma=True)

    def emit(self):
        nc = self.nc
        for e in ENGS:
            for o in self.ops[e]:
                for d in o.deps:
                    if not d.dma:
                        d.needs_sig = True
        for e in ENGS:
            c = 0
            for o in self.ops[e]:
                if (not o.dma) and o.needs_sig:
                    c += 1
                    o.sig_idx = c
        es = self.es
        csem = {e: nc.alloc_semaphore(name=f"c_{e}_{self.tag}") for e in ENGS}
        dsem = {}
        for e in ENGS:
            if self.ndma[e]:
                for i in range(min(NDMASEM, self.ndma[e])):
                    dsem[(e, i)] = nc.alloc_semaphore(name=f"d_{e}{i}_{self.tag}")
        block = es.enter_context(nc.Block())
        prog = self

        def stream(ename, eng):
            waited = {}

            def wait(key, sem, val):
                if waited.get(key, 0) < val:
                    eng.wait_ge(sem, val)
                    waited[key] = val

            for o in prog.ops[ename]:
                for d in o.deps:
                    if d.dma:
                        wait(("d", d.eng, d.sem_i), dsem[(d.eng, d.sem_i)], d.sem_val)
                    else:
                        wait(("c", d.eng), csem[d.eng], d.sig_idx)
                if o.dma:
                    p = o.prev_same_sem
                    if p is not None:
                        wait(("d", ename, o.sem_i), dsem[(ename, o.sem_i)], p.sem_val)
                    inst = o.fn()
                    inst.then_inc(dsem[(ename, o.sem_i)], 16)
                else:
                    inst = o.fn()
                    if o.needs_sig:
                        inst.then_inc(csem[ename], 1)
            if ename == "sp":
                for key, tot in prog.dma_tot.items():
                    wait(("d",) + key, dsem[key], tot)
                for e2 in ENGS:
                    if e2 != "sp":
                        n = max([o.sig_idx for o in prog.ops[e2] if not o.dma] + [0])
                        if n:
                            wait(("c", e2), csem[e2], n)

        @block.tensor
        def _(eng):
            stream("pe", eng)

        @block.scalar
        def _(eng):
            stream("act", eng)

        @block.vector
        def _(eng):
            stream("dve", eng)

        @block.gpsimd
        def _(eng):
            stream("pool", eng)

        @block.sync
        def _(eng):
            stream("sp", eng)

    def close(self):
        self.es.close()


DBGZ = DBGQ = DBGT = 9
SEQREPLAY = 0

NT, NCT = 16, 2
TT = NT + NCT
EPS = 1e-6


class Rot:
    def __init__(self, P, n, shape, dtype, name):
        self.bufs = [(P.sb(shape, dtype, f"{name}{i}"), f"{name}{i}") for i in range(n)]
        self.i = 0

    def next(self):
        b = self.bufs[self.i % len(self.bufs)]
        self.i += 1
        return b


def make_ident(P, nc):
    identf = P.sb([128, 128], F32, "identf")
    ident = P.sb([128, 128], BF16, "ident")
    P.I("pool", nc.gpsimd.memset, [], ["identf"], ap=identf[:], constant=0.0)
    P.I("pool", nc.gpsimd.affine_select, ["identf"], ["identf"], out=identf[:], in_=identf[:], pattern=[[-1, 128]],
        compare_op=ALU.not_equal, fill=1.0, base=0, channel_multiplier=1)
    P.I("dve", nc.vector.tensor_copy, ["identf"], ["ident"], out=ident[:], in_=identf[:])
    P.identf = identf
    return ident


def build_A(last, ntiles=TT, dbg=99, nc=None, T=None, tag=""):
    if nc is None:
        nc = bass.Bass("TRN2", target_bir_lowering=False)

    def din(name, shape, dt=F32):
        if T is not None:
            assert tuple(T[name].shape) == tuple(shape), (name, T[name].shape, shape)
            return T[name]
        return nc.dram_tensor(name, list(shape), dt, kind="ExternalInput").ap()

    def dout(name, shape, dt=F32):
        if T is not None:
            assert tuple(T[name].shape) == tuple(shape), (name, T[name].shape, shape)
            return T[name]
        return nc.dram_tensor(name, list(shape), dt, kind="ExternalOutput").ap()

    xin = din("xin", [TT * 128, 1024])
    cT_d = din("cT", [128, 8, 2])
    w_ada_d = din("w_ada", [1024, 6144])
    b_ada_d = din("b_ada", [1, 6144])
    w_in_d = din("w_in", [1024, 1216])
    w_sguT_d = din("w_sguT", [128, 4, 128])
    b_sguT_d = din("b_sguT", [128, 4])
    sgun_d = din("sgu_norm", [1, 256])
    qln_d = din("q_lora_norm", [1, 256])
    kvln_d = din("kv_lora_norm", [1, 128])
    qn_d = din("q_norm", [1, 192])
    kn_d = din("k_norm", [1, 192])
    w_uq_d = din("w_uq", [256, 768])
    w_ukv_d = din("w_ukv", [128, 1024])
    dftc_d = din("dftc", [256, 512])
    rcos_d = din("rope_cos", [TT * 128, 64])
    rsin_d = din("rope_sin", [TT * 128, 64])

    ada_o = dout("ada", [2, 6144])
    yaT_o = dout("yaT", [256, TT * 128], BF16)
    Z_o = dout("Z", [TT * 128, 512], BF16)
    QT_o = dout("QT", [4, 192, TT * 128], BF16)
    KT_o = dout("KT", [4, 192, TT * 128], BF16)
    V_o = dout("V", [TT * 128, 512], BF16)

    P = Prog(nc, tag)
    I = P.I
    ident = make_ident(P, nc)

    def bc_load(name, src, n):
        t = P.sb([128, n], F32, name)
        P.dma("sp", t[:], src.partition_broadcast(128), writes=[name])
        return t

    sgun = bc_load("sgun", sgun_d[0:1, :], 256)
    qln = bc_load("qln", qln_d[0:1, :], 256)
    kvln = bc_load("kvln", kvln_d[0:1, :], 128)
    qnb = bc_load("qnb", qn_d[0:1, :], 192)
    knb = bc_load("knb", kn_d[0:1, :], 192)
    b_sguT = P.sb([128, 4], F32, "b_sguT")
    P.dma("sp", b_sguT[:], b_sguT_d, writes=["b_sguT"])
    rcos = P.sb([128, TT, 64], F32, "rcos")
    rsin = P.sb([128, TT, 64], F32, "rsin")
    P.dma("sp", rcos[:], rcos_d.rearrange("(t p) d -> p t d", p=128), writes=["rcos"])
    P.dma("sp", rsin[:], rsin_d.rearrange("(t p) d -> p t d", p=128), writes=["rsin"])
    cT = P.sb([128, 8, 2], F32, "cT")
    P.dma("sp", cT[:], cT_d, writes=["cT"])
    scT = P.sb([128, 8, 2], BF16, "scT")
    I("act", nc.scalar.activation, ["cT"], ["scT"], out=scT[:], in_=cT[:], func=AF.Silu)
    mbr = Rot(P, 2, [2, 512], F32, "mblk")
    bar = Rot(P, 2, [2, 512], F32, "bablk")
    warot = Rot(P, 2, [128, 8, 512], BF16, "wa")
    ps_ada = P.ps([128, 512], F32, "psA")
    w_ada_v = w_ada_d.rearrange("(k p) n -> p k n", p=128)
    for nb in range(12):
        wa, kwa = warot.next()
        for hk in range(2):
            P.dma("pool", wa[:, hk * 4:(hk + 1) * 4, :], w_ada_v[:, hk * 4:(hk + 1) * 4, nb * 512:(nb + 1) * 512], writes=[kwa + f"_{hk}"])
        for k in range(8):
            I("pe", nc.tensor.matmul, ["scT", kwa + f"_{k // 4}"], ["psA"], out=ps_ada[0:2, :], lhsT=scT[:, k, :], rhs=wa[:, k, :],
              start=(k == 0), stop=(k == 7))
        mb, kmb = mbr.next()
        ba, kba = bar.next()
        P.dma("sp", ba[:], b_ada_d[0:1, nb * 512:(nb + 1) * 512].partition_broadcast(2), writes=[kba])
        I("dve", nc.vector.tensor_tensor, ["psA", kba], [kmb], out=mb[:], in0=ps_ada[0:2, :], in1=ba[:], op=ALU.add)
        P.dma("sp", ada_o[:, nb * 512:(nb + 1) * 512], mb[:], reads=[kmb], writes=["ada_d"])
    mods = []
    for r in range(2):
        md = P.sb([128, 2048], F32, f"mod{r}")
        P.dma("sp", md[:], ada_o[r:r + 1, 0:2048].partition_broadcast(128), reads=["ada_d"], writes=[f"mod{r}"])
        I("dve", nc.vector.tensor_scalar_add, [f"mod{r}"], [f"mod{r}"], out=md[:, 1024:2048], in0=md[:, 1024:2048], scalar1=1.0)
        mods.append(md)

    w_in = P.sb([128, 8, 1216], BF16, "w_in")
    w_in_v = w_in_d.rearrange("(k p) n -> p k n", p=128)
    for k in range(0, 8, 2):
        P.dma("pool", w_in[:, k:k + 2, :], w_in_v[:, k:k + 2, :], writes=[f"w_in{k}"])
    w_in_keys = [f"w_in{k}" for k in range(0, 8, 2)]
    w_uq = P.sb([128, 2, 768], BF16, "w_uq")
    P.dma("pool", w_uq[:], w_uq_d.rearrange("(k p) n -> p k n", p=128), writes=["w_uq"])
    w_ukv = P.sb([128, 1024], BF16, "w_ukv")
    P.dma("pool", w_ukv[:], w_ukv_d, writes=["w_ukv"])
    dftc = P.sb([128, 2, 512], BF16, "dftc")
    P.dma("pool", dftc[:], dftc_d.rearrange("(k p) n -> p k n", p=128), writes=["dftc"])
    w_sguT = P.sb([128, 4, 128], BF16, "w_sguT")
    P.dma("pool", w_sguT[:], w_sguT_d, writes=["w_sguT"])

    psT = P.ps([128, 1024], BF16, "psT")
    psT2 = P.ps([128, 1024], BF16, "psT2")
    psT3 = P.ps([128, 1024], BF16, "psT3")
    px0 = P.ps([128, 512], F32, "px0")
    px1 = P.ps([128, 512], F32, "px1")
    px2 = P.ps([128, 512], F32, "px2")
    psB = P.ps([128, 512], F32, "psB")

    xr_ = Rot(P, 2, [128, 1024], F32, "x")
    tmpr = Rot(P, 2, [128, 1024], F32, "tmp")
    hr = Rot(P, 2, [128, 1024], BF16, "h")
    hTr = Rot(P, 2, [128, 1024], BF16, "hT")
    junkr = Rot(P, 4, [128, 1024], BF16, "junkA")
    str_ = Rot(P, 3, [128, 40], F32, "st")
    uvr = Rot(P, 2, [128, 512], F32, "uv")
    vbr = Rot(P, 2, [128, 256], BF16, "vb")
    yar = Rot(P, 2, [128, 256], BF16, "ya")
    yaTr = Rot(P, 2, [128, 2, 128], BF16, "yaT")
    pfr = Rot(P, 2, [128, 256], BF16, "pf")
    pfTr = Rot(P, 2, [128, 2, 128], BF16, "pfT")
    Zr = Rot(P, 2, [128, 512], BF16, "Z")
    cqr = Rot(P, 2, [128, 256], BF16, "cq")
    cqTr = Rot(P, 2, [128, 2, 128], BF16, "cqT")
    qfr = Rot(P, 2, [128, 4, 192], F32, "qf")
    qnr = Rot(P, 2, [128, 4, 192], F32, "qn")
    r1r = Rot(P, 2, [128, 4, 64], F32, "r1")
    r2r = Rot(P, 2, [128, 4, 64], F32, "r2")
    qbr = Rot(P, 2, [128, 4, 256], BF16, "qb")
    for (qb_, kqb_) in qbr.bufs:
        I("pool", nc.gpsimd.memset, [], [kqb_], ap=qb_[:], constant=0.0)
    QTnr = Rot(P, 3, [128, 4, 128], BF16, "QTn")
    QTrr = Rot(P, 3, [128, 4, 128], BF16, "QTr")
    ckvr = Rot(P, 2, [128, 128], BF16, "ckv")
    ckvTr = Rot(P, 2, [128, 128], BF16, "ckvT")
    Vbr = Rot(P, 2, [128, 4, 128], BF16, "Vb")

    def rms_rstd(src, ncols, n, rkeys, st, kst, c0, nm):
        k0, k1, k2 = f"{kst}_{nm}0", f"{kst}_{nm}1", f"{kst}_{nm}2"
        junkA, kj = junkr.next()
        I("act", nc.scalar.activation, rkeys, [kj, k0], out=junkA[:, 0:ncols], in_=src, func=AF.Square, accum_out=st[:, c0:c0 + 1])
        I("act", nc.scalar.activation, [k0], [k1], out=st[:, c0 + 1:c0 + 2], in_=st[:, c0:c0 + 1], func=AF.Sqrt, scale=1.0 / n, bias=EPS)
        I("dve", nc.vector.reciprocal, [k1], [k2], out=st[:, c0 + 2:c0 + 3], in_=st[:, c0 + 1:c0 + 2])
        return k2

    def head_norm_rope_store(t, qf, kqf, normb, knormb, st, kst, c0, nm, out_d):
        if DBGQ < 2:
            return
        ks = [f"{kst}_{nm}s{h}" for h in range(4)]
        for h in range(4):
            junkA, kj = junkr.next()
            I("act", nc.scalar.activation, [kqf], [kj, ks[h]], out=junkA[:, 0:192], in_=qf[:, h, :], func=AF.Square,
              accum_out=st[:, c0 + h:c0 + h + 1])
        kq1, kq2 = f"{kst}_{nm}q1", f"{kst}_{nm}q2"
        I("act", nc.scalar.activation, ks, [kq1], out=st[:, c0 + 4:c0 + 8], in_=st[:, c0:c0 + 4], func=AF.Sqrt, scale=1.0 / 192, bias=EPS)
        I("dve", nc.vector.reciprocal, [kq1], [kq2], out=st[:, c0 + 8:c0 + 12], in_=st[:, c0 + 4:c0 + 8])
        qn, kqn = qnr.next()
        for h in range(4):
            I("dve", nc.vector.scalar_tensor_tensor, [kqf, kq2, knormb], [kqn], out=qn[:, h, :], in0=qf[:, h, :],
              scalar=st[:, c0 + 8 + h:c0 + 9 + h], in1=normb[:], op0=ALU.mult, op1=ALU.mult)
        if DBGQ < 3:
            return
        r1, kr1 = r1r.next()
        r2, kr2 = r2r.next()
        qb, kqb = qbr.next()
        xrp = qn[:, :, 128:192]
        I("dve", nc.vector.tensor_tensor, [kqn, "rcos"], [kr1], out=r1[:], in0=xrp, in1=rcos[:, t, :].unsqueeze(1).to_broadcast([128, 4, 64]),
          op=ALU.mult)
        x5 = xrp.rearrange("p h (b s d) -> p h b s d", b=2, s=2)
        o5 = r2[:].rearrange("p h (b s d) -> p h b s d", b=2, s=2)
        s5 = rsin[:, t, :].rearrange("p (b s d) -> p b s d", b=2, s=2)
        for s_ in range(2):
            I("dve", nc.vector.tensor_tensor, [kqn, "rsin"], [kr2], out=o5[:, :, :, s_, :], in0=x5[:, :, :, 1 - s_, :],
              in1=s5[:, :, s_, :].unsqueeze(1).to_broadcast([128, 4, 2, 16]), op=ALU.mult)
        I("dve", nc.vector.tensor_tensor, [kr1, kr2], [kqb], out=qb[:, :, 128:192], in0=r1[:], in1=r2[:], op=ALU.add)
        I("act", nc.scalar.copy, [kqn], [kqb], out=qb[:, :, 0:128], in_=qn[:, :, 0:128])
        if DBGQ < 4:
            return
        for h in range(4):
            I("pe", nc.tensor.transpose, [kqb, "ident"], ["psT3"], out=psT3[:, h * 128:(h + 1) * 128], in_=qb[:, h, 0:128], identity=ident[:])
            if DBGT >= 1:
                I("pe", nc.tensor.transpose, [kqb, "ident"], ["psT3"], out=psT3[:, 512 + h * 128:512 + (h + 1) * 128], in_=qb[:, h, 128:256],
                  identity=ident[:])
        QTn, kQTn = QTnr.next()
        QTr, kQTr = QTrr.next()
        I("act", nc.scalar.copy, ["psT3"], [kQTn], out=QTn[:], in_=psT3[:, 0:512].rearrange("p (h t) -> p h t", h=4))
        if DBGT >= 2:
          I("dve", nc.vector.tensor_copy, ["psT3"], [kQTr], out=QTr[0:64, :, :], in_=psT3[0:64, 512:1024].rearrange("p (h t) -> p h t", h=4))
        if DBGQ < 5:
            return
        P.dma("sp", out_d[:, 0:128, t * 128:(t + 1) * 128].rearrange("h d t -> d h t"), QTn[:], reads=[kQTn])
        P.dma("sp", out_d[:, 128:192, t * 128:(t + 1) * 128].rearrange("h d t -> d h t"), QTr[0:64, :, :], reads=[kQTr])

    if last:
        zb = P.sb([128, 4, 128], BF16, "zb")
        I("pool", nc.gpsimd.memset, [], ["zb"], ap=zb[:], constant=0.0)
        zbf = zb[:].rearrange("p h t -> p (h t)")
        for t in range(NT, ntiles):
            P.dma("sp", yaT_o[:, t * 128:(t + 1) * 128].rearrange("(c p) t -> p c t", p=128), zb[:, 0:2, :], reads=["zb"])
            P.dma("sp", Z_o[t * 128:(t + 1) * 128, :], zbf, reads=["zb"])
            P.dma("sp", QT_o[:, 0:128, t * 128:(t + 1) * 128].rearrange("h d t -> d h t"), zb[:], reads=["zb"])
            P.dma("sp", QT_o[:, 128:192, t * 128:(t + 1) * 128].rearrange("h d t -> d h t"), zb[0:64, :, :], reads=["zb"])
    def front(t, S):
        is_ctx = t >= NT
        md = mods[1 if is_ctx else 0]
        kmd = f"mod{1 if is_ctx else 0}"
        x_t, kx = xr_.next()
        P.dma("sp", x_t[:], xin[t * 128:(t + 1) * 128, :], writes=[kx])
        st, kst = str_.next()
        S["st"], S["kst"] = st, kst
        krs = rms_rstd(x_t[:], 1024, 1024, [kx], st, kst, 0, "n1")
        tmp, ktmp = tmpr.next()
        h_t, kh = hr.next()
        I("dve", nc.vector.scalar_tensor_tensor, [kx, krs, kmd], [ktmp], out=tmp[:], in0=x_t[:], scalar=st[:, 2:3], in1=md[:, 1024:2048],
          op0=ALU.mult, op1=ALU.mult)
        I("dve", nc.vector.tensor_tensor, [ktmp, kmd], [kh], out=h_t[:], in0=tmp[:], in1=md[:, 0:1024], op=ALU.add)
        for k in range(8):
            I("pe", nc.tensor.transpose, [kh, "ident"], ["psT"], out=psT[:, k * 128:(k + 1) * 128], in_=h_t[:, k * 128:(k + 1) * 128],
              identity=ident[:])
        hT, khT = hTr.next()
        I("act", nc.scalar.copy, ["psT"], [khT], out=hT[:], in_=psT[:])
        S["hT"], S["khT"] = hT, khT

    def pxmm(t, S):
        kv_only = (t >= NT) and last
        hT, khT = S["hT"], S["khT"]
        blocks = [(px0, "px0", 0, 512), (px1, "px1", 512, 1024), (px2, "px2", 1024, 1216)]
        for (pb, kpb, c0, c1) in blocks:
            if kv_only and kpb != "px2":
                continue
            for k in range(8):
                I("pe", nc.tensor.matmul, [khT, w_in_keys[k // 2]], [kpb], out=pb[:, 0:c1 - c0], lhsT=hT[:, k * 128:(k + 1) * 128],
                  rhs=w_in[:, k, c0:c1], start=(k == 0), stop=(k == 7))
        if not kv_only:
            uv, kuv = uvr.next()
            I("act", nc.scalar.activation, ["px0"], [kuv], out=uv[:], in_=px0[:], func=AF.Gelu_apprx_tanh)
            S["uv"], S["kuv"] = uv, kuv
            p1, kp1 = p1r.next()
            I("act", nc.scalar.copy, ["px1"], [kp1], out=p1[:], in_=px1[:])
            S["p1"], S["kp1"] = p1, kp1
        p2, kp2 = p2r.next()
        I("dve", nc.vector.tensor_copy, ["px2"], [kp2], out=p2[:], in_=px2[:, 0:192])
        S["p2"], S["kp2"] = p2, kp2

    def sgu(t, S):
        st, kst = S["st"], S["kst"]
        uv, kuv = S["uv"], S["kuv"]
        krv = rms_rstd(uv[:, 256:512], 256, 256, [kuv], st, kst, 3, "v")
        vb, kvb = vbr.next()
        I("dve", nc.vector.scalar_tensor_tensor, [kuv, krv, "sgun"], [kvb], out=vb[:], in0=uv[:, 256:512], scalar=st[:, 5:6], in1=sgun[:],
          op0=ALU.mult, op1=ALU.mult)
        for h in range(4):
            I("pe", nc.tensor.matmul, [kvb, "w_sguT"], ["psA"], out=ps_ada[:, h * 64:(h + 1) * 64], lhsT=w_sguT[:, h, :],
              rhs=vb[:, h * 64:(h + 1) * 64], start=True, stop=True)
        ya, kya = yar.next()
        for h in range(4):
            I("dve", nc.vector.scalar_tensor_tensor, ["psA", "b_sguT", kuv], [kya], out=ya[:, h * 64:(h + 1) * 64],
              in0=ps_ada[:, h * 64:(h + 1) * 64], scalar=b_sguT[:, h:h + 1], in1=uv[:, h * 64:(h + 1) * 64], op0=ALU.add, op1=ALU.mult)
        for c in range(2):
            I("pe", nc.tensor.transpose, [kya, "ident"], ["psT2"], out=psT2[:, c * 128:(c + 1) * 128], in_=ya[:, c * 128:(c + 1) * 128],
              identity=ident[:])
        yaT, kyaT = yaTr.next()
        I("act", nc.scalar.copy, ["psT2"], [kyaT], out=yaT[:], in_=psT2[:, 0:256].rearrange("p (c t) -> p c t", c=2))
        P.dma("sp", yaT_o[:, t * 128:(t + 1) * 128].rearrange("(c p) t -> p c t", p=128), yaT[:], reads=[kyaT])

    def zpart(t, S):
        p1, kp1 = S["p1"], S["kp1"]
        pf, kpf = pfr.next()
        I("dve", nc.vector.tensor_copy, [kp1], [kpf], out=pf[:], in_=p1[:, 0:256])
        for c in range(2):
            I("pe", nc.tensor.transpose, [kpf, "ident"], ["psT2"], out=psT2[:, 256 + c * 128:256 + (c + 1) * 128],
              in_=pf[:, c * 128:(c + 1) * 128], identity=ident[:])
        pfT, kpfT = pfTr.next()
        I("act", nc.scalar.copy, ["psT2"], [kpfT], out=pfT[:], in_=psT2[:, 256:512].rearrange("p (c t) -> p c t", c=2))
        for c in range(2):
            I("pe", nc.tensor.matmul, [kpfT, "dftc"], ["psB"], out=psB[:], lhsT=pfT[:, c, :], rhs=dftc[:, c, :], start=(c == 0), stop=(c == 1))
        Zt, kZ = Zr.next()
        I("act", nc.scalar.copy, ["psB"], [kZ], out=Zt[:], in_=psB[:])
        P.dma("sp", Z_o[t * 128:(t + 1) * 128, :], Zt[:], reads=[kZ])

    def qpart(t, S):
        st, kst = S["st"], S["kst"]
        p1, kp1 = S["p1"], S["kp1"]
        krq = rms_rstd(p1[:, 256:512], 256, 256, [kp1], st, kst, 6, "q")
        cq, kcq = cqr.next()
        I("dve", nc.vector.scalar_tensor_tensor, [kp1, krq, "qln"], [kcq], out=cq[:], in0=p1[:, 256:512], scalar=st[:, 8:9], in1=qln[:],
          op0=ALU.mult, op1=ALU.mult)
        for c in range(2):
            I("pe", nc.tensor.transpose, [kcq, "ident"], ["psT2"], out=psT2[:, 512 + c * 128:512 + (c + 1) * 128],
              in_=cq[:, c * 128:(c + 1) * 128], identity=ident[:])
        cqT, kcqT = cqTr.next()
        I("act", nc.scalar.copy, ["psT2"], [kcqT], out=cqT[:], in_=psT2[:, 512:768].rearrange("p (c t) -> p c t", c=2))
        for c in range(2):
            I("pe", nc.tensor.matmul, [kcqT, "w_uq"], ["px1"], out=px1[:], lhsT=cqT[:, c, :], rhs=w_uq[:, c, 0:512], start=(c == 0), stop=(c == 1))
        for c in range(2):
            I("pe", nc.tensor.matmul, [kcqT, "w_uq"], ["px2"], out=px2[:, 0:256], lhsT=cqT[:, c, :], rhs=w_uq[:, c, 512:768],
              start=(c == 0), stop=(c == 1))
        qf, kqf = qfr.next()
        qf2 = qf[:].rearrange("p h d -> p (h d)")
        I("act", nc.scalar.copy, ["px1"], [kqf], out=qf2[:, 0:512], in_=px1[:])
        I("act", nc.scalar.copy, ["px2"], [kqf], out=qf2[:, 512:768], in_=px2[:, 0:256])
        head_norm_rope_store(t, qf, kqf, qnb, "qnb", st, kst, 9, "qh", QT_o)

    def kvpart(t, S):
        st, kst = S["st"], S["kst"]
        p2, kp2 = S["p2"], S["kp2"]
        krk = rms_rstd(p2[:, 0:128], 128, 128, [kp2], st, kst, 21, "kv")
        ckv, kckv = ckvr.next()
        I("dve", nc.vector.scalar_tensor_tensor, [kp2, krk, "kvln"], [kckv], out=ckv[:], in0=p2[:, 0:128], scalar=st[:, 23:24], in1=kvln[:],
          op0=ALU.mult, op1=ALU.mult)
        I("pe", nc.tensor.transpose, [kckv, "ident"], ["psT2"], out=psT2[:, 768:896], in_=ckv[:], identity=ident[:])
        ckvT, kckvT = ckvTr.next()
        I("act", nc.scalar.copy, ["psT2"], [kckvT], out=ckvT[:], in_=psT2[:, 768:896])
        kf, kkf = kfr.next()
        Vb, kVb = Vbr.next()
        I("act", nc.scalar.copy, [kp2], [kkf], out=kf[:, :, 128:192], in_=p2[:, 128:192].unsqueeze(1).to_broadcast([128, 4, 64]))
        for j, (pb, kpb) in enumerate([(px0, "px0"), (px0, "px0")]):
            I("pe", nc.tensor.matmul, [kckvT, "w_ukv"], [kpb], out=pb[:], lhsT=ckvT[:], rhs=w_ukv[:, j * 512:(j + 1) * 512], start=True, stop=True)
            pv = pb[:].rearrange("p (h s d) -> p h s d", h=2, s=2)
            I("act", nc.scalar.copy, [kpb], [kVb], out=Vb[:, 2 * j:2 * j + 2, :], in_=pv[:, :, 1, :])
            I("dve", nc.vector.tensor_copy, [kpb], [kkf], out=kf[:, 2 * j:2 * j + 2, 0:128], in_=pv[:, :, 0, :])
        P.dma("sp", V_o[t * 128:(t + 1) * 128, :], Vb[:].rearrange("p h d -> p (h d)"), reads=[kVb])
        head_norm_rope_store(t, kf, kkf, knb, "knb", st, kst, 24, "kh", KT_o)

    p1r = Rot(P, 2, [128, 512], F32, "p1s")
    p2r = Rot(P, 2, [128, 192], F32, "p2s")
    kfr = Rot(P, 2, [128, 4, 192], F32, "kf")
    states = [dict() for _ in range(ntiles + 1)]
    if ntiles:
        front(0, states[0])
    for t in range(ntiles):
        S = states[t]
        kv_only = (t >= NT) and last
        pxmm(t, S)
        lists = []
        if not kv_only:
            lists += [P.capture(sgu, t, S), P.capture(zpart, t, S)]
        lists.append((P.capture(qpart, t, S) if not kv_only else []) + P.capture(kvpart, t, S))
        if t + 1 < ntiles:
            lists.append(P.capture(front, t + 1, states[t + 1]))
        P.replay_interleaved(lists) if not SEQREPLAY else [P.op(*o) for l in lists for o in l]
    P.emit()
    P.close()
    return nc


def build_F(last, nc=None, T=None, tag=""):
    if nc is None:
        nc = bass.Bass("TRN2", target_bir_lowering=False)

    def din(name, shape, dt=F32):
        if T is not None:
            assert tuple(T[name].shape) == tuple(shape), (name, T[name].shape, shape)
            return T[name]
        return nc.dram_tensor(name, list(shape), dt, kind="ExternalInput").ap()

    Z_d = din("Z_all", [8192, 512], BF16)
    Zc_d = din("Zc", [256, 512], BF16)
    WA_d = din("WA", [128, 128])
    TC_d = din("TC", [128, 2, 64, 32])
    TCc_d = din("TCc", [128, 2, 2, 256])
    ybT_o = T["ybT"] if T is not None else nc.dram_tensor("ybT", [256, TT * 128], BF16, kind="ExternalOutput").ap()
    A_d = nc.dram_tensor(tag + "A_scr", [128, 128, 256], BF16, kind="Internal").ap()

    P = Prog(nc, tag)
    I = P.I
    WA = P.sb([128, 128], BF16, "WA")
    P.dma("pool", WA[:], WA_d, writes=["WA"])
    TC = P.sb([128, 2, 64, 32], BF16, "TC")
    P.dma("pool", TC[:], TC_d, writes=["TC"])
    fb = [P.ps([128, 512], F32, f"f{i}") for i in range(2)]
    yb = [P.ps([128, 512], F32, f"yb{i}") for i in range(2)]
    pr = P.ps([128, 512], F32, "pr")
    zar = Rot(P, 2, [128, 16, 256], BF16, "za")
    aor = Rot(P, 2, [128, 16, 256], BF16, "ao")
    Zv = Z_d.rearrange("(n1 n2) (ri c) -> ri n1 n2 c", n2=128, ri=2)
    cnt = 0
    for ch in range(8):
        za, kza = zar.next()
        for ri in range(2):
            P.dma("sp", za[ri * 64:(ri + 1) * 64, :, :], Zv[ri, :, ch * 16:(ch + 1) * 16, :], writes=[f"{kza}_{ri}"])
        ao, kao = aor.next()
        for j in range(8):
            bk = fb[j % 2]
            I("pe", nc.tensor.matmul, [f"{kza}_0", f"{kza}_1", "WA"], [f"f{j % 2}"], out=bk[:], lhsT=WA[:],
              rhs=za[:, 2 * j:2 * j + 2, :].rearrange("p a c -> p (a c)"), start=True, stop=True)
            dst = ao[:, 2 * j:2 * j + 2, :].rearrange("p a c -> p (a c)")
            if cnt % 2 == 0:
                I("act", nc.scalar.copy, [f"f{j % 2}"], [kao], out=dst, in_=bk[:])
            else:
                I("dve", nc.vector.tensor_copy, [f"f{j % 2}"], [kao], out=dst, in_=bk[:])
            cnt += 1
        P.dma("sp", A_d[:, ch * 16:(ch + 1) * 16, :], ao[:], reads=[kao], writes=["A_d"])
    ac = P.sb([128, 128, 128], BF16, "ac")
    ybs = P.sb([128, 2, TT * 128], BF16, "ybs")
    A_v = A_d.rearrange("q n c -> n q c")
    for half in range(2):
        for qq in range(4):
            P.dma("sp", ac[:, qq * 32:(qq + 1) * 32, :], A_v[:, qq * 32:(qq + 1) * 32, half * 128:(half + 1) * 128], reads=["A_d"],
                  writes=[f"ac{qq}"])
        ackeys = [f"ac{qq}" for qq in range(4)]
        for bk in range(4):
            ybk = yb[bk % 2]
            yv = ybk[:].rearrange("p (k2 k1) -> p k2 k1", k1=64)
            for k1 in range(64):
                for ri in range(2):
                    I("pe", nc.tensor.matmul, ackeys + ["TC"], [f"yb{bk % 2}"], out=yv[:, :, k1], lhsT=ac[:, ri * 64 + k1, :],
                      rhs=TC[:, ri, k1, bk * 8:(bk + 1) * 8], start=(ri == 0), stop=(ri == 1))
            dst = ybs[:, half, bk * 512:(bk + 1) * 512]
            if bk % 2 == 0:
                I("act", nc.scalar.copy, [f"yb{bk % 2}"], ["ybs"], out=dst, in_=ybk[:])
            else:
                I("dve", nc.vector.tensor_copy, [f"yb{bk % 2}"], ["ybs"], out=dst, in_=ybk[:])
    if not last:
        Zc = P.sb([128, 2, 512], BF16, "Zc")
        P.dma("sp", Zc[:], Zc_d.rearrange("(t p) c -> p t c", p=128), writes=["Zc"])
        TCc = P.sb([128, 2, 2, 256], BF16, "TCc")
        P.dma("pool", TCc[:], TCc_d, writes=["TCc"])
        for half in range(2):
            i = 0
            for nt in range(2):
                for ri in range(2):
                    I("pe", nc.tensor.matmul, ["Zc", "TCc"], ["pr"], out=pr[:, 0:256], lhsT=Zc[:, nt, ri * 256 + half * 128:ri * 256 + (half + 1) * 128],
                      rhs=TCc[:, nt, ri, :], start=(i == 0), stop=(i == 3))
                    i += 1
            I("act", nc.scalar.copy, ["pr"], ["ybs"], out=ybs[:, half, NT * 128:TT * 128], in_=pr[:, 0:256])
    if last:
        I("pool", nc.gpsimd.memset, [], ["ybs"], ap=ybs[:, :, NT * 128:TT * 128], constant=0.0)
    ncols = TT * 128
    P.dma("sp", ybT_o[:, 0:ncols].rearrange("(c p) t -> p c t", p=128), ybs[:, :, 0:ncols], reads=["ybs"])
    P.emit()
    P.close()
    return nc


NKT = 66
SCALE = 192 ** -0.5


def build_B(last, nheads=4, nblocks=None, full=True, nc=None, T=None, tag=""):
    if nc is None:
        nc = bass.Bass("TRN2", target_bir_lowering=False)

    def din(name, shape, dt=F32):
        if T is not None:
            assert tuple(T[name].shape) == tuple(shape), (name, T[name].shape, shape)
            return T[name]
        return nc.dram_tensor(name, list(shape), dt, kind="ExternalInput").ap()

    def dout(name, shape, dt=F32):
        if T is not None:
            assert tuple(T[name].shape) == tuple(shape), (name, T[name].shape, shape)
            return T[name]
        return nc.dram_tensor(name, list(shape), dt, kind="ExternalOutput").ap()

    QT_d = din("QT", [4, 192, TT * 128], BF16)
    KT_d = din("KT_all", [4, 192, NKT * 128], BF16)
    V_d = din("V_all", [NKT * 128, 512], BF16)
    EPS = 1e-6
    if full:
        xin = din("xin", [TT * 128, 1024])
        ada_d = din("ada", [2, 6144])
        yaT_d = din("yaT", [256, TT * 128], BF16)
        ybT_d = din("ybT", [256, TT * 128], BF16)
        w_out_d = din("w_out", [1024, 1024])
        w_r_d = din("w_router", [1024, 16])
        x1_o = dout("x1", [TT * 128, 1024])
        h2_o = dout("h2", [TT * 128, 1024], BF16)
        aff_o = dout("aff", [TT * 128, 16])
        affT_o = dout("affT", [16, TT * 128])
    else:
        yc_o = dout("yc", [TT * 128, 512], BF16)

    P = Prog(nc, tag)
    I = P.I
    NCH = 6
    CT = NKT // NCH
    ktn = P.sb([128, NKT * 128], BF16, "ktn")
    ktr = P.sb([128, NKT * 128], BF16, "ktr")
    vh = P.sb([128, NKT, 129], BF16, "vh")
    I("pool", nc.gpsimd.memset, [], [f"vh{c}" for c in range(NCH)], ap=vh[:, :, 128:129], constant=1.0)
    I("pool", nc.gpsimd.memset, [], [f"ktr{c}" for c in range(NCH)], ap=ktr[64:128, :], constant=0.0)
    qnr = Rot(P, 2, [128, TT * 128], BF16, "qtn")
    qrr = Rot(P, 2, [128, TT * 128], BF16, "qtr")
    for (qb_, kqb_) in qrr.bufs:
        I("pool", nc.gpsimd.memset, [], [kqb_], ap=qb_[64:128, :], constant=0.0)
    ptr = Rot(P, 3, [128, 512], BF16, "pt")
    ycr = Rot(P, 3, [128, 128], BF16, "yct")
    rcr = Rot(P, 4, [128, 1], F32, "rc")
    sbank = [P.ps([128, 512], F32, f"sb{i}") for i in range(2)]
    obank = [P.ps([128, 512], F32, f"ob{i}") for i in range(4)]
    V_v = V_d.rearrange("(t p) (h d) -> p t h d", p=128, h=4)
    ntile = NT if last else TT
    if full:
        ident = make_ident(P, nc)
        psTb = P.ps([128, 1024], BF16, "psTb")
        tb = P.ps([128, 512], F32, "tb")
        mixT = P.sb([128, 8, TT * 128], BF16, "mixT")
        mkeys = [f"mix{t}" for t in range(TT)]
        def load_mix():
            P.dma("sp", mixT[:, 0:2, :], yaT_d.rearrange("(c p) t -> p c t", p=128), writes=mkeys)
            P.dma("sp", mixT[:, 2:4, :], ybT_d.rearrange("(c p) t -> p c t", p=128), writes=mkeys)
        w_out = P.sb([128, 8, 1024], BF16, "w_out")
        w_out_v = w_out_d.rearrange("(k p) n -> p k n", p=128)
        for k in range(0, 8, 2):
            P.dma("pool", w_out[:, k:k + 2, :], w_out_v[:, k:k + 2, :], writes=[f"w_out{k}"])
        wokeys = [f"w_out{k}" for k in range(0, 8, 2)]
        w_r = P.sb([128, 8, 16], BF16, "w_r")
        P.dma("pool", w_r[:], w_r_d.rearrange("(k p) n -> p k n", p=128), writes=["w_r"])
        mods = []
        for r in range(2):
            md = P.sb([128, 3072], F32, f"modB{r}")
            P.dma("sp", md[:], ada_d[r:r + 1, 2048:5120].partition_broadcast(128), writes=[f"modB{r}"])
            I("dve", nc.vector.tensor_scalar_add, [f"modB{r}"], [f"modB{r}"], out=md[:, 2048:3072], in0=md[:, 2048:3072], scalar1=1.0)
            mods.append(md)

    blocks = [(qb * 512, 512, NKT) for qb in range(4)]
    if not last:
        blocks.insert(0, (NT * 128, NCT * 128, NCT))
    if nblocks is not None:
        blocks = blocks[:nblocks]

    if full:
        xr_ = Rot(P, 2, [128, 1024], F32, "x")
        tmpr = Rot(P, 2, [128, 1024], F32, "tmp")
        x1r = Rot(P, 2, [128, 1024], F32, "x1")
        h2r = Rot(P, 2, [128, 1024], BF16, "h2")
        h2Tr = Rot(P, 2, [128, 1024], BF16, "h2T")
        junkr = Rot(P, 2, [128, 1024], BF16, "junk")
        str_ = Rot(P, 3, [128, 8], F32, "st")
        exr = Rot(P, 2, [128, 16], F32, "ex")
        afr = Rot(P, 2, [128, 16], F32, "af")
        aTr = Rot(P, 2, [16, 128], F32, "aT")

        def tail_tile(t):
            r = 1 if t >= NT else 0
            md, kmd = mods[r], f"modB{r}"
            x_t, kx = xr_.next()
            P.dma("sp", x_t[:], xin[t * 128:(t + 1) * 128, :], writes=[kx])
            tmp, ktmp = tmpr.next()
            x1, kx1 = x1r.next()
            for nb in range(2):
                cs = slice(nb * 512, (nb + 1) * 512)
                for k in range(8):
                    I("pe", nc.tensor.matmul, [f"mix{t}", wokeys[k // 2]], ["tb"], out=tb[:], lhsT=mixT[:, k, t * 128:(t + 1) * 128],
                      rhs=w_out[:, k, nb * 512:(nb + 1) * 512], start=(k == 0), stop=(k == 7))
                I("dve", nc.vector.tensor_tensor, ["tb", kmd], [ktmp], out=tmp[:, cs], in0=tb[:], in1=md[:, cs], op=ALU.mult)
                I("dve", nc.vector.tensor_tensor, [ktmp, kx], [kx1], out=x1[:, cs], in0=tmp[:, cs], in1=x_t[:, cs], op=ALU.add)
            P.dma("sp", x1_o[t * 128:(t + 1) * 128, :], x1[:], reads=[kx1])
            st, kst = str_.next()
            junk, kj = junkr.next()
            I("act", nc.scalar.activation, [kx1], [kj, kst + "a"], out=junk[:], in_=x1[:], func=AF.Square, accum_out=st[:, 0:1])
            I("act", nc.scalar.activation, [kst + "a"], [kst + "b"], out=st[:, 1:2], in_=st[:, 0:1], func=AF.Sqrt, scale=1.0 / 1024, bias=EPS)
            I("dve", nc.vector.reciprocal, [kst + "b"], [kst + "c"], out=st[:, 2:3], in_=st[:, 1:2])
            tmp2, ktmp2 = tmpr.next()
            h2, kh2 = h2r.next()
            I("dve", nc.vector.scalar_tensor_tensor, [kx1, kst + "c", kmd], [ktmp2], out=tmp2[:], in0=x1[:], scalar=st[:, 2:3], in1=md[:, 2048:3072],
              op0=ALU.mult, op1=ALU.mult)
            I("dve", nc.vector.tensor_tensor, [ktmp2, kmd], [kh2], out=h2[:], in0=tmp2[:], in1=md[:, 1024:2048], op=ALU.add)
            P.dma("sp", h2_o[t * 128:(t + 1) * 128, :], h2[:], reads=[kh2])
            h2T, kh2T = h2Tr.next()
            for hf in range(2):
                for k in range(4):
                    kk = hf * 4 + k
                    I("pe", nc.tensor.transpose, [kh2, "ident"], ["psTb"], out=psTb[:, 512 + k * 128:512 + (k + 1) * 128], in_=h2[:, kk * 128:(kk + 1) * 128],
                      identity=ident[:])
                I("act", nc.scalar.copy, ["psTb"], [kh2T], out=h2T[:, hf * 512:(hf + 1) * 512], in_=psTb[:, 512:1024])
            for k in range(8):
                I("pe", nc.tensor.matmul, [kh2T, "w_r"], ["tb"], out=tb[:, 0:16], lhsT=h2T[:, k * 128:(k + 1) * 128], rhs=w_r[:, k, :],
                  start=(k == 0), stop=(k == 7))
            ex, kex = exr.next()
            af, kaf = afr.next()
            I("dve", nc.vector.reduce_max, ["tb"], [kst + "d"], out=st[:, 3:4], in_=tb[:, 0:16], axis=AX.X)
            I("dve", nc.vector.tensor_scalar, [kst + "d"], [kst + "e"], out=st[:, 4:5], in0=st[:, 3:4], scalar1=-1.0, scalar2=None, op0=ALU.mult)
            I("act", nc.scalar.activation, ["tb", kst + "e"], [kex, kst + "f"], out=ex[:], in_=tb[:, 0:16], func=AF.Exp, bias=st[:, 4:5],
              scale=1.0, accum_out=st[:, 5:6])
            I("dve", nc.vector.reciprocal, [kst + "f"], [kst + "g"], out=st[:, 6:7], in_=st[:, 5:6])
            I("dve", nc.vector.tensor_scalar, [kex, kst + "g"], [kaf], out=af[:], in0=ex[:], scalar1=st[:, 6:7], scalar2=None, op0=ALU.mult)
            P.dma("sp", aff_o[t * 128:(t + 1) * 128, :], af[:], reads=[kaf])
            I("pe", nc.tensor.transpose, [kaf, "identf"], ["tb"], out=tb[0:16, 128:256], in_=af[:], identity=P.identf[:])
            aT, kaT = aTr.next()
            I("act", nc.scalar.copy, ["tb"], [kaT], out=aT[:], in_=tb[0:16, 128:256])
            P.dma("sp", affT_o[:, t * 128:(t + 1) * 128], aT[:], reads=[kaT])


    for h in range(nheads):
        qtn, kqn = qnr.next()
        qtr, kqr = qrr.next()
        P.dma("sp", qtn[:], QT_d[h, 0:128, :], writes=[kqn])
        P.dma("sp", qtr[0:64, :], QT_d[h, 128:192, :], writes=[kqr])
        for c in range(NCH):
            cs = slice(c * CT * 128, (c + 1) * CT * 128)
            P.dma("sp", ktn[:, cs], KT_d[h, 0:128, cs], writes=[f"ktn{c}"])
            P.dma("sp", ktr[0:64, cs], KT_d[h, 128:192, cs], writes=[f"ktr{c}"])
            P.dma("sp", vh[:, c * CT:(c + 1) * CT, 0:128], V_v[:, c * CT:(c + 1) * CT, h, :], writes=[f"vh{c}"])
        if full and h == 0:
            load_mix()
        def attn_block(q0, qn, nkt):
            nsub = qn // 128
            pts = {}

            def S(kt):
                c = kt // CT
                sbk = sbank[kt % 2]
                ks = slice(kt * 128, (kt + 1) * 128)
                I("pe", nc.tensor.matmul, [f"ktn{c}", kqn], [f"sb{kt % 2}"], out=sbk[:, 0:qn], lhsT=ktn[:, ks], rhs=qtn[:, q0:q0 + qn],
                  start=True, stop=False)
                I("pe", nc.tensor.matmul, [f"ktr{c}", kqr], [f"sb{kt % 2}"], out=sbk[:, 0:qn], lhsT=ktr[:, ks], rhs=qtr[:, q0:q0 + qn],
                  start=False, stop=True)

            def E(kt):
                pt, kpt = ptr.next()
                pts[kt] = (pt, kpt)
                I("act", nc.scalar.activation, [f"sb{kt % 2}"], [kpt], out=pt[:, 0:qn], in_=sbank[kt % 2][:, 0:qn], func=AF.Exp, scale=SCALE)

            def PV(kt):
                c = kt // CT
                pt, kpt = pts.pop(kt)
                for qs in range(nsub):
                    ob = obank[qs]
                    off = 0
                    I("pe", nc.tensor.matmul, [kpt, f"vh{c}"], [f"ob{qs}"], out=ob[:, off:off + 129], lhsT=pt[:, qs * 128:(qs + 1) * 128],
                      rhs=vh[:, kt, :], start=(kt == 0), stop=(kt == nkt - 1))

            S(0)
            for kt in range(nkt):
                E(kt)
                if kt + 1 < nkt:
                    S(kt + 1)
                PV(kt)
            for qs in range(nsub):
                ob = obank[qs]
                off = 0
                rc, krc = rcr.next()
                yct, kyct = ycr.next()
                I("dve", nc.vector.reciprocal, [f"ob{qs}"], [krc], out=rc[:], in_=ob[:, off + 128:off + 129])
                I("dve", nc.vector.tensor_scalar, [f"ob{qs}", krc], [kyct], out=yct[:], in0=ob[:, off:off + 128], scalar1=rc[:, 0:1],
                  scalar2=None, op0=ALU.mult)
                r0 = q0 + qs * 128
                if full:
                    I("pe", nc.tensor.transpose, [kyct, "ident"], ["psTb"], out=psTb[:, 0:128], in_=yct[:], identity=ident[:])
                    I("act", nc.scalar.copy, ["psTb"], [f"mix{r0 // 128}"], out=mixT[:, 4 + h, r0:r0 + 128], in_=psTb[:, 0:128])
                else:
                    P.dma("sp", yc_o[r0:r0 + 128, h * 128:(h + 1) * 128], yct[:], reads=[kyct])

        for bi, (q0, qn, nkt) in enumerate(blocks):
            if full and h == nheads - 1 and bi >= 1:
                pq0, pqn, _ = blocks[bi - 1]

                def tails(pq0=pq0, pqn=pqn):
                    for tt_ in range(pq0 // 128, (pq0 + pqn) // 128):
                        tail_tile(tt_)
                P.replay_interleaved([P.capture(attn_block, q0, qn, nkt), P.capture(tails)])
            else:
                attn_block(q0, qn, nkt)
    if full:
        pq0, pqn, _ = blocks[-1]
        for tt_ in range(pq0 // 128, (pq0 + pqn) // 128):
            tail_tile(tt_)
        if last:
            zt, kzt = tmpr.next()
            zh, kzh = h2r.next()
            I("pool", nc.gpsimd.memset, [], [kzt], ap=zt[:], constant=0.0)
            I("pool", nc.gpsimd.memset, [], [kzh], ap=zh[:], constant=0.0)
            for t in range(NT, TT):
                P.dma("sp", x1_o[t * 128:(t + 1) * 128, :], zt[:], reads=[kzt])
                P.dma("sp", h2_o[t * 128:(t + 1) * 128, :], zh[:], reads=[kzh])
                P.dma("sp", aff_o[t * 128:(t + 1) * 128, :], zt[:, 0:16], reads=[kzt])
                P.dma("sp", affT_o[:, t * 128:(t + 1) * 128], zt[0:16, 0:128], reads=[kzt])
    P.emit()
    P.close()
    return nc


NE = 16
CAPM, CAPC = 120, 32
NIT = 26


def build_C(last, nexp=NE, nc=None, T=None, tag=""):
    if nc is None:
        nc = bass.Bass("TRN2", target_bir_lowering=False)

    def din(name, shape, dt=F32):
        if T is not None:
            assert tuple(T[name].shape) == tuple(shape), (name, T[name].shape, shape)
            return T[name]
        return nc.dram_tensor(name, list(shape), dt, kind="ExternalInput").ap()

    x1_d = din("x1", [TT * 128, 1024])
    h2_d = din("h2", [TT * 128, 1024], BF16)
    aff_d = din("aff", [TT * 128, 16])
    affT_d = din("affT_all", [16, 8192])
    affTc_d = din("affTc", [16, 256])
    ada_d = din("ada", [2, 6144])
    wg_d = din("w_gate", [NE, 1024, 512])
    wu_d = din("w_up", [NE, 1024, 512])
    wd_d = din("w_down", [NE, 512, 1024])
    utri_d = din("utri", [128, 128])
    ones_d = din("ones", [128, 128])
    iota_d = din("iota", [128, 128])
    blk_d = din("blockones", [128, 128])
    thrc_d = din("thrc", [128, 2])
    x2_o = T["x2"] if T is not None else nc.dram_tensor("x2", [TT * 128, 1024], F32, kind="ExternalOutput").ap()
    thr_scr = nc.dram_tensor(tag + "thr_scr", [128, 2], F32, kind="Internal").ap()

    P = Prog(nc, tag)
    I = P.I
    ident = make_ident(P, nc)
    ntile = NT if last else TT
    groups = [(g, [4 * g + j for j in range(4)], CAPM, g * CAPM) for g in range(4)]
    if not last:
        groups.append((4, [NT, NT + 1], CAPC, 4 * CAPM))
    nslots = 4 * CAPM + (0 if last else CAPC)
    tile_group = {}
    for (g, tiles, cap, s0) in groups:
        for t in tiles:
            tile_group[t] = (g, cap, s0)

    utri = P.sb([128, 128], BF16, "utri")
    P.dma("pool", utri[:], utri_d, writes=["utri"])
    onesb = P.sb([128, 128], BF16, "onesb")
    P.dma("pool", onesb[:], ones_d, writes=["onesb"])
    iota = P.sb([128, 128], F32, "iota")
    P.dma("sp", iota[:], iota_d, writes=["iota"])
    blk = P.sb([128, 128], F32, "blk")
    P.dma("sp", blk[:], blk_d, writes=["blk"])
    thrc = P.sb([128, 2], F32, "thrc")
    P.dma("sp", thrc[:], thrc_d, writes=["thrc"])
    acc = P.sb([128, TT, 1024], F32, "acc")
    Am = acc[:, 0, :]
    P.dma("sp", Am, affT_d.rearrange("e (s t) -> (e s) t", s=8), writes=["acc0"])
    Ac = P.sb([128, 32], F32, "Ac")
    P.dma("sp", Ac[:], affTc_d.rearrange("e (s t) -> (e s) t", s=8), writes=["Ac"])
    affs = P.sb([128, TT, 16], F32, "affs")
    P.dma("sp", affs[:], aff_d.rearrange("(t p) e -> p t e", p=128), writes=["affs"])
    h2tok = P.sb([128, TT, 1024], BF16, "h2tok")
    for t0 in range(0, TT, 6):
        P.dma("sp", h2tok[:, t0:t0 + 6, :], h2_d.rearrange("(t p) d -> p t d", p=128)[:, t0:t0 + 6, :], writes=[f"h2tok{t0}"])
    h2keys = [f"h2tok{t0}" for t0 in range(0, TT, 6)]
    g2 = []
    for r in range(2):
        md = P.sb([128, 1024], F32, f"g2_{r}")
        P.dma("sp", md[:], ada_d[r:r + 1, 5120:6144].partition_broadcast(128), writes=[f"g2_{r}"])
        g2.append(md)

    pp = P.ps([128, 512], F32, "pp")
    psTb = P.ps([128, 1024], BF16, "psTb")
    gA = P.ps([128, 512], F32, "gA")
    gB = P.ps([128, 512], F32, "gB")
    Gb = P.ps([128, 512], F32, "Gb")
    Ub = P.ps([128, 512], F32, "Ub")
    yA = P.ps([128, 512], F32, "yA")
    yB = P.ps([128, 512], F32, "yB")

    hidT = P.sb([128, 4, 512], BF16, "hidT")
    junk = hidT[:, 0:2, :].rearrange("p a b -> p (a b)")
    lo = P.sb([128, 2], F32, "lo")
    negmid = P.sb([128, 2], F32, "negmid")
    ssum = P.sb([128, 2], F32, "ssum")
    cond = P.sb([128, 2], F32, "cond")
    I("dve", nc.vector.memset, [], ["lo"], ap=lo[:], constant=0.0)
    I("dve", nc.vector.memset, [], ["negmid"], ap=negmid[:], constant=-0.5)
    I("dve", nc.vector.memset, [], ["ssum0", "ssum1"], ap=ssum[:], constant=0.0)
    for it in range(NIT):
        w = 2.0 ** -(it + 1)
        I("act", nc.scalar.activation, ["acc0", "negmid"], ["hidT", "ssum0"], out=junk[:, 0:1024], in_=Am, func=AF.Sign, bias=negmid[:, 0:1], scale=1.0,
          accum_out=ssum[:, 0:1])
        I("act", nc.scalar.activation, ["Ac", "negmid"], ["hidT", "ssum1"], out=junk[:, 0:32], in_=Ac[:], func=AF.Sign, bias=negmid[:, 1:2], scale=1.0,
          accum_out=ssum[:, 1:2])
        I("pe", nc.tensor.matmul, ["blk", "ssum0", "ssum1"], ["pp"], out=pp[:, 0:2], lhsT=blk[:], rhs=ssum[:], start=True, stop=True)
        I("dve", nc.vector.tensor_tensor, ["pp", "thrc"], ["cond"], out=cond[:], in0=pp[:, 0:2], in1=thrc[:], op=ALU.is_ge)
        I("dve", nc.vector.scalar_tensor_tensor, ["cond", "lo"], ["lo"], out=lo[:], in0=cond[:], scalar=w, in1=lo[:], op0=ALU.mult, op1=ALU.add)
        I("dve", nc.vector.tensor_scalar, ["lo"], ["negmid"], out=negmid[:], in0=lo[:], scalar1=w * 0.5, scalar2=-1.0, op0=ALU.add, op1=ALU.mult)
    P.dma("sp", thr_scr, lo[:], reads=["lo"], writes=["thr_scr"])
    thr_all = P.sb([128, 256], F32, "thr_all")
    P.dma("sp", thr_all[:], thr_scr.rearrange("p c -> (p c)").unsqueeze(0).partition_broadcast(128), reads=["thr_scr"], writes=["thr_all"])
    thr_v = thr_all[:].rearrange("p (e s c) -> p e s c", s=8, c=2)

    maskf = P.sb([128, TT, 16], F32, "maskf")
    maskb = P.sb([128, TT, 16], BF16, "maskb")
    gm = P.sb([128, TT, 16], F32, "gm")
    pos = P.sb([128, TT, 16], F32, "pos")
    for t in range(ntile):
        col = 1 if t >= NT else 0
        I("dve", nc.vector.tensor_tensor, ["affs", "thr_all"], ["maskf"], out=maskf[:, t, :], in0=affs[:, t, :], in1=thr_v[:, :, 0, col], op=ALU.is_ge)
    I("dve", nc.vector.tensor_copy, ["maskf"], ["maskb"], out=maskb[:, 0:ntile, :], in_=maskf[:, 0:ntile, :])
    I("dve", nc.vector.tensor_tensor, ["maskf", "affs"], ["gm"], out=gm[:, 0:ntile, :], in0=maskf[:, 0:ntile, :], in1=affs[:, 0:ntile, :], op=ALU.mult)
    for (g, tiles, cap, s0) in groups:
        for jj, t in enumerate(tiles):
            for i in range(jj + 1):
                I("pe", nc.tensor.matmul, ["maskb", "utri", "onesb"], ["pp"], out=pp[:, 16 * (t % 16):16 * (t % 16) + 16],
                  lhsT=(utri[:] if i == jj else onesb[:]), rhs=maskb[:, tiles[i], :], start=(i == 0), stop=(i == jj))
            I("dve", nc.vector.tensor_copy, ["pp"], ["pos"], out=pos[:, t, :], in_=pp[:, 16 * (t % 16):16 * (t % 16) + 16])

    wgr = Rot(P, 2, [128, 8, 512], BF16, "wg")
    wur = Rot(P, 2, [128, 8, 512], BF16, "wu")
    wdr = Rot(P, 1, [128, 4, 1024], BF16, "wd")
    Sall = P.sb([128, TT, 128], BF16, "Sall")
    Sgall = P.sb([128, TT, 128], BF16, "Sgall")
    SgT = P.sb([128, TT, 128], BF16, "SgT")
    I("pool", nc.gpsimd.memset, [], [f"S{t}" for t in range(TT)], ap=Sall[:], constant=0.0)
    I("pool", nc.gpsimd.memset, [], [f"Sg{t}" for t in range(TT)], ap=Sgall[:], constant=0.0)
    xsT = P.sb([128, 8, 512], BF16, "xsT")
    sgt = P.sb([128, 512], F32, "sgt")
    yr = Rot(P, 1, [128, 5, 1024], BF16, "ysb")
    cnt = 0
    W, Y = {}, {}

    def load_w(e, which):
        if which == "gu":
            wg, kwg = wgr.next()
            wu, kwu = wur.next()
            W[e] = [wg, kwg, wu, kwu, None, None]
            for hk in range(2):
                P.dma("pool", wg[:, hk * 4:(hk + 1) * 4, :], wg_d[e].rearrange("(k p) f -> p k f", p=128)[:, hk * 4:(hk + 1) * 4, :], writes=[f"{kwg}_{hk}"])
                P.dma("pool", wu[:, hk * 4:(hk + 1) * 4, :], wu_d[e].rearrange("(k p) f -> p k f", p=128)[:, hk * 4:(hk + 1) * 4, :], writes=[f"{kwu}_{hk}"])
        else:
            wd, kwd = wdr.next()
            W[e][4], W[e][5] = wd, kwd
            for hk in range(2):
                P.dma("pool", wd[:, hk * 2:(hk + 1) * 2, :], wd_d[e].rearrange("(k p) d -> p k d", p=128)[:, hk * 2:(hk + 1) * 2, :], writes=[f"{kwd}_{hk}"])

    def build_S(e):
        for t in range(ntile):
            g, cap, s0 = tile_group[t]
            I("dve", nc.vector.tensor_scalar, ["iota", "pos", "maskf"], [f"S{t}"], out=Sall[:, t, 0:cap], in0=iota[:, 0:cap], scalar1=pos[:, t, e:e + 1],
              scalar2=maskf[:, t, e:e + 1], op0=ALU.is_equal, op1=ALU.mult)

    def build_Sg(e):
        for t in range(ntile):
            g, cap, s0 = tile_group[t]
            I("dve", nc.vector.tensor_scalar, ["iota", "pos", "gm"], [f"Sg{t}"], out=Sgall[:, t, 0:cap], in0=iota[:, 0:cap], scalar1=pos[:, t, e:e + 1],
              scalar2=gm[:, t, e:e + 1], op0=ALU.is_equal, op1=ALU.mult)

    def gather(e):
        for (g, tiles, cap, s0) in groups:
            for dk in range(8):
                bank, kb = (gA, "gA") if dk < 4 else (gB, "gB")
                o0 = (dk % 4) * cap
                for jj, t in enumerate(tiles):
                    I("pe", nc.tensor.matmul, [h2keys[t // 6], f"S{t}"], [kb], out=bank[:, o0:o0 + cap], lhsT=h2tok[:, t, dk * 128:(dk + 1) * 128],
                      rhs=Sall[:, t, 0:cap], start=(jj == 0), stop=(jj == len(tiles) - 1))
            I("act", nc.scalar.copy, ["gA"], ["xsT"], out=xsT[:, 0:4, s0:s0 + cap], in_=gA[:, 0:4 * cap].rearrange("p (k s) -> p k s", k=4))
            I("act", nc.scalar.copy, ["gB"], ["xsT"], out=xsT[:, 4:8, s0:s0 + cap], in_=gB[:, 0:4 * cap].rearrange("p (k s) -> p k s", k=4))

    def ffn(e):
        wg, kwg, wu, kwu, _, _ = W[e]
        for fk in range(4):
            for k in range(8):
                I("pe", nc.tensor.matmul, ["xsT", f"{kwg}_{k // 4}"], ["Gb"], out=Gb[:, 0:nslots], lhsT=wg[:, k, fk * 128:(fk + 1) * 128], rhs=xsT[:, k, 0:nslots],
                  start=(k == 0), stop=(k == 7))
            for k in range(8):
                I("pe", nc.tensor.matmul, ["xsT", f"{kwu}_{k // 4}"], ["Ub"], out=Ub[:, 0:nslots], lhsT=wu[:, k, fk * 128:(fk + 1) * 128], rhs=xsT[:, k, 0:nslots],
                  start=(k == 0), stop=(k == 7))
            I("act", nc.scalar.activation, ["Gb"], ["sgt"], out=sgt[:, 0:nslots], in_=Gb[:, 0:nslots], func=AF.Silu)
            I("dve", nc.vector.tensor_tensor, ["sgt", "Ub"], ["hidT"], out=hidT[:, fk, 0:nslots], in0=sgt[:, 0:nslots], in1=Ub[:, 0:nslots], op=ALU.mult)

    def down(e):
        wd, kwd = W[e][4], W[e][5]
        ysb, kys = yr.next()
        Y[e] = (ysb, kys)
        for (g, tiles, cap, s0) in groups:
            for nb, (bank, kb) in enumerate([(yA, "yA"), (yB, "yB")]):
                for fk in range(4):
                    I("pe", nc.tensor.matmul, ["hidT", f"{kwd}_{fk // 2}"], [kb], out=bank[0:cap, :], lhsT=hidT[:, fk, s0:s0 + cap],
                      rhs=wd[:, fk, nb * 512:(nb + 1) * 512], start=(fk == 0), stop=(fk == 3))
            I("act", nc.scalar.copy, ["yA"], [f"{kys}_{g}"], out=ysb[0:cap, g, 0:512], in_=yA[0:cap, :])
            I("act", nc.scalar.copy, ["yB"], [f"{kys}_{g}"], out=ysb[0:cap, g, 512:1024], in_=yB[0:cap, :])

    def sgtr(e):
        for t0 in range(0, ntile, 8):
            tl = list(range(t0, min(t0 + 8, ntile)))
            for t in tl:
                I("pe", nc.tensor.transpose, [f"Sg{t}", "ident"], ["psTb"], out=psTb[:, (t - t0) * 128:(t - t0 + 1) * 128], in_=Sgall[:, t, :], identity=ident[:])
            I("act", nc.scalar.copy, ["psTb"], [f"SgT{t}" for t in tl], out=SgT[:, t0:t0 + len(tl), :],
              in_=psTb[:, 0:len(tl) * 128].rearrange("p (t s) -> p t s", s=128))

    sbanks = [(gA, "gA"), (gB, "gB"), (yA, "yA"), (yB, "yB")]

    def scatter(e):
        ysb, kys = Y[e]
        i = 0
        for t in range(ntile):
            g, cap, s0 = tile_group[t]
            for nb in range(2):
                bank, kb = sbanks[i % 4]
                i += 1
                I("pe", nc.tensor.matmul, [f"SgT{t}", f"{kys}_{g}"], [kb], out=bank[:], lhsT=SgT[0:cap, t, :], rhs=ysb[0:cap, g, nb * 512:(nb + 1) * 512],
                  start=True, stop=True)
                cs = slice(nb * 512, (nb + 1) * 512)
                if e == 0:
                    I("dve", nc.vector.tensor_copy, [kb], [f"acc{t}"], out=acc[:, t, cs], in_=bank[:])
                else:
                    I("dve", nc.vector.tensor_tensor, [kb, f"acc{t}"], [f"acc{t}"], out=acc[:, t, cs], in0=acc[:, t, cs], in1=bank[:], op=ALU.add)

    load_w(0, "gu")
    load_w(0, "d")
    build_S(0)
    build_Sg(0)
    for e in range(nexp):
        if e + 1 < nexp:
            load_w(e + 1, "gu")
        gather(e)
        if e + 1 < nexp:
            build_S(e + 1)
        ffn(e)
        down(e)
        if e + 1 < nexp:
            load_w(e + 1, "d")
        sgtr(e)
        if e + 1 < nexp:
            build_Sg(e + 1)
        scatter(e)
    x1r = Rot(P, 1, [128, 1024], F32, "x1t")
    for t in range(ntile):
        x1t, kx1 = x1r.next()
        P.dma("sp", x1t[:], x1_d[t * 128:(t + 1) * 128, :], writes=[kx1])
        r = 1 if t >= NT else 0
        I("dve", nc.vector.tensor_tensor, [f"acc{t}", f"g2_{r}"], [f"acc{t}"], out=acc[:, t, :], in0=acc[:, t, :], in1=g2[r][:], op=ALU.mult)
        I("dve", nc.vector.tensor_tensor, [f"acc{t}", kx1], [f"acc{t}"], out=acc[:, t, :], in0=acc[:, t, :], in1=x1t[:], op=ALU.add)
        P.dma("sp", x2_o[t * 128:(t + 1) * 128, :], acc[:, t, :], reads=[f"acc{t}"])
    if last:
        for t in range(NT, TT):
            I("pool", nc.gpsimd.memset, [], [f"acc{t}"], ap=acc[:, t, :], constant=0.0)
            P.dma("sp", x2_o[t * 128:(t + 1) * 128, :], acc[:, t, :], reads=[f"acc{t}"])
    P.emit()
    P.close()
    return nc


NV = 4


def build_fused(nlayers=2, nv=NV):
    nc = bass.Bass("TRN2", target_bir_lowering=False)

    def ext(name, shape, dt=F32):
        return nc.dram_tensor(name, list(shape), dt, kind="ExternalInput").ap()

    def scr(name, shape, dt=F32):
        return nc.dram_tensor(name, list(shape), dt, kind="Internal").ap()

    E = {
        "x": ext("x", [8192, 1024]), "ctx": ext("ctx", [256, 1024]), "cT": ext("cT", [128, 8, 2]),
        "w_ada": ext("w_ada", [2, 1024, 6144]), "b_ada": ext("b_ada", [2, 6144]), "w_in": ext("w_in", [2, 1024, 1216]),
        "w_sguT": ext("w_sguT", [2, 128, 4, 128]), "b_sguT": ext("b_sguT", [2, 128, 4]),
        "sgu_norm": ext("sgu_norm", [2, 256]), "q_lora_norm": ext("q_lora_norm", [2, 256]), "kv_lora_norm": ext("kv_lora_norm", [2, 128]),
        "q_norm": ext("q_norm", [2, 192]), "k_norm": ext("k_norm", [2, 192]),
        "w_uq": ext("w_uq", [2, 256, 768]), "w_ukv": ext("w_ukv", [2, 128, 1024]), "dftc": ext("dftc", [256, 512]),
        "rope_cos": ext("rope_cos", [4, TT * 128, 64]), "rope_sin": ext("rope_sin", [4, TT * 128, 64]),
        "WA": ext("WA", [128, 128]), "TC": ext("TC", [4, 128, 2, 64, 32]), "TCc": ext("TCc", [128, 2, 2, 256]),
        "w_out": ext("w_out", [2, 1024, 1024]), "w_router": ext("w_router", [2, 1024, 16]),
        "w_gate": ext("w_gate", [2, 16, 1024, 512]), "w_up": ext("w_up", [2, 16, 1024, 512]), "w_down": ext("w_down", [2, 16, 512, 1024]),
        "utri": ext("utri", [128, 128]), "ones": ext("ones", [128, 128]), "iota": ext("iota", [128, 128]),
        "blockones": ext("blockones", [128, 128]), "thrc": ext("thrc", [128, 2]),
    }
    out = nc.dram_tensor("out", [8192, 1024], F32, kind="ExternalOutput").ap()
    xmid = scr("xmid", [8192, 1024])
    xcmid = scr("xcmid", [256, 1024])
    Z_all = scr("Z_all", [8192, 512], BF16)
    KT_all = scr("KT_all", [4, 192, 66 * 128], BF16)
    V_all = scr("V_all", [66 * 128, 512], BF16)
    affT_all = scr("affT_all", [16, 8192])
    affTc = scr("affTc", [16, 256])
    V = []
    for v in range(nv):
        V.append({
            "xin": scr(f"xin{v}", [TT * 128, 1024]), "ada": scr(f"ada{v}", [2, 6144]), "yaT": scr(f"yaT{v}", [256, TT * 128], BF16),
            "Z": scr(f"Z{v}", [TT * 128, 512], BF16), "QT": scr(f"QT{v}", [4, 192, TT * 128], BF16), "KT": scr(f"KT{v}", [4, 192, TT * 128], BF16),
            "V": scr(f"V{v}", [TT * 128, 512], BF16), "ybT": scr(f"ybT{v}", [256, TT * 128], BF16), "x1": scr(f"x1{v}", [TT * 128, 1024]),
            "h2": scr(f"h2{v}", [TT * 128, 1024], BF16), "aff": scr(f"aff{v}", [TT * 128, 16]), "affT": scr(f"affT{v}", [16, TT * 128]),
            "x2": scr(f"x2{v}", [TT * 128, 1024]),
        })
    M = NT * 128
    phase_no = [0]

    def phase(fn):
        phase_no[0] += 1
        with nc.cleanup_on_exit():
            fn(f"p{phase_no[0]}_")
            nc.all_engine_barrier()

    def glue(copies):
        def body(tag):
            P = Prog(nc, tag)
            for i, (dst, src) in enumerate(copies):
                P.dma("sp", dst, src)
            P.emit()
            P.close()
        phase(body)

    for l in range(nlayers):
        last = l == nlayers - 1
        xsrc, csrc = (E["x"], E["ctx"]) if l == 0 else (xmid, xcmid)
        xdst = out if last else xmid
        cp = []
        for v in range(nv):
            cp.append((V[v]["xin"][0:M, :], xsrc[v * M:(v + 1) * M, :]))
            cp.append((V[v]["xin"][M:TT * 128, :], csrc))
        glue(cp)
        for v in range(nv):
            TA = {"xin": V[v]["xin"], "cT": E["cT"], "w_ada": E["w_ada"][l], "b_ada": E["b_ada"][l:l + 1, :], "w_in": E["w_in"][l],
                  "w_sguT": E["w_sguT"][l], "b_sguT": E["b_sguT"][l], "sgu_norm": E["sgu_norm"][l:l + 1, :],
                  "q_lora_norm": E["q_lora_norm"][l:l + 1, :], "kv_lora_norm": E["kv_lora_norm"][l:l + 1, :],
                  "q_norm": E["q_norm"][l:l + 1, :], "k_norm": E["k_norm"][l:l + 1, :], "w_uq": E["w_uq"][l], "w_ukv": E["w_ukv"][l],
                  "dftc": E["dftc"], "rope_cos": E["rope_cos"][v], "rope_sin": E["rope_sin"][v],
                  "ada": V[v]["ada"], "yaT": V[v]["yaT"], "Z": V[v]["Z"], "QT": V[v]["QT"], "KT": V[v]["KT"], "V": V[v]["V"]}
            lv = last or v > 0
            phase(lambda tag, TA=TA, lv=lv, v=v: build_A(lv, ntiles=(TT if v == 0 else NT), nc=nc, T=TA, tag=tag))
        cp = [(KT_all[:, :, 0:NCT * 128], V[0]["KT"][:, :, M:TT * 128]), (V_all[0:NCT * 128, :], V[0]["V"][M:TT * 128, :])]
        for v in range(nv):
            cp.append((Z_all[v * M:(v + 1) * M, :], V[v]["Z"][0:M, :]))
            cp.append((KT_all[:, :, NCT * 128 + v * M:NCT * 128 + (v + 1) * M], V[v]["KT"][:, :, 0:M]))
            cp.append((V_all[NCT * 128 + v * M:NCT * 128 + (v + 1) * M, :], V[v]["V"][0:M, :]))
        glue(cp)
        for v in range(nv):
            TF = {"Z_all": Z_all, "Zc": V[v]["Z"][M:TT * 128, :], "WA": E["WA"], "TC": E["TC"][v], "TCc": E["TCc"], "ybT": V[v]["ybT"]}
            phase(lambda tag, TF=TF, lv=(last or v > 0): build_F(lv, nc=nc, T=TF, tag=tag))
        for v in range(nv):
            TB = {"QT": V[v]["QT"], "KT_all": KT_all, "V_all": V_all, "xin": V[v]["xin"], "ada": V[v]["ada"], "yaT": V[v]["yaT"], "ybT": V[v]["ybT"],
                  "w_out": E["w_out"][l], "w_router": E["w_router"][l], "x1": V[v]["x1"], "h2": V[v]["h2"], "aff": V[v]["aff"], "affT": V[v]["affT"]}
            phase(lambda tag, TB=TB, lv=(last or v > 0): build_B(lv, nc=nc, T=TB, tag=tag))
        cp = [(affTc, V[0]["affT"][:, M:TT * 128])]
        for v in range(nv):
            cp.append((affT_all[:, v * M:(v + 1) * M], V[v]["affT"][:, 0:M]))
        glue(cp)
        for v in range(nv):
            TCd = {"x1": V[v]["x1"], "h2": V[v]["h2"], "aff": V[v]["aff"], "affT_all": affT_all, "affTc": affTc, "ada": V[v]["ada"],
                   "w_gate": E["w_gate"][l], "w_up": E["w_up"][l], "w_down": E["w_down"][l], "utri": E["utri"], "ones": E["ones"], "iota": E["iota"],
                   "blockones": E["blockones"], "thrc": E["thrc"], "x2": V[v]["x2"]}
            phase(lambda tag, TCd=TCd, lv=(last or v > 0): build_C(lv, nc=nc, T=TCd, tag=tag))
        cp = [(xdst[v * M:(v + 1) * M, :], V[v]["x2"][0:M, :]) for v in range(nv)]
        if not last:
            cp.append((xcmid, V[0]["x2"][M:TT * 128, :]))
        glue(cp)
    return nc


BF = ml_dtypes.bfloat16
NCORES = 8
TPC = 2048
NCTX = 256
SEQ = 8192


def f32(a):
    return np.ascontiguousarray(a, dtype=np.float32)


def const_dftc():
    c = np.arange(64)
    ang = 2 * np.pi * np.outer(c, c) / 64.0
    C, S = np.cos(ang), np.sin(ang)
    d = np.zeros((256, 512), np.float64)
    for g in range(4):
        d[g * 64:(g + 1) * 64, g * 64:(g + 1) * 64] = C
        d[g * 64:(g + 1) * 64, 256 + g * 64:256 + (g + 1) * 64] = -S
    return f32(d)


def const_rope(core):
    tok0 = (core % 4) * TPC
    n = np.arange(tok0, tok0 + TPC)
    pos_row = (n // 64).astype(np.float32)
    pos_col = (n % 64).astype(np.float32)
    freqs = (np.float32(10000.0) ** (-np.arange(16, dtype=np.float32) / np.float32(16))).astype(np.float32)
    cos = np.ones((TPC + NCTX, 64), np.float32)
    sin = np.zeros((TPC + NCTX, 64), np.float32)
    for b, pos in enumerate([pos_row, pos_col]):
        ang = (pos[:, None] * freqs[None, :]).astype(np.float32)
        cs, sn = np.cos(ang).astype(np.float32), np.sin(ang).astype(np.float32)
        cos[:TPC, b * 32:b * 32 + 16] = cs
        cos[:TPC, b * 32 + 16:b * 32 + 32] = cs
        sin[:TPC, b * 32:b * 32 + 16] = -sn
        sin[:TPC, b * 32 + 16:b * 32 + 32] = sn
    return cos, sin


def inputs_A(inp, l, x_cur, xc_cur):
    dftc = const_dftc()
    shared = {
        "w_ada": f32(inp["w_ada"][l]), "b_ada": f32(inp["b_ada"][l][None, :]), "w_in": f32(inp["w_in"][l]),
        "w_sguT": f32(np.transpose(inp["w_sgu"][l], (2, 0, 1))),
        "b_sguT": f32(inp["b_sgu"][l].T),
        "sgu_norm": f32(inp["sgu_norm"][l][None]), "q_lora_norm": f32(inp["q_lora_norm"][l][None]),
        "kv_lora_norm": f32(inp["kv_lora_norm"][l][None]), "q_norm": f32(inp["q_norm"][l][None]), "k_norm": f32(inp["k_norm"][l][None]),
        "w_uq": f32(inp["w_uq"][l]), "w_ukv": f32(inp["w_ukv"][l]), "dftc": dftc,
    }
    maps = []
    for core in range(NCORES):
        b, tok0 = core // 4, (core % 4) * TPC
        cos, sin = const_rope(core)
        cT = np.stack([np.asarray(inp["c"][b]).reshape(8, 128).T, np.asarray(inp["c_ctx"]).reshape(8, 128).T], axis=-1)
        m = dict(shared)
        m.update({"xin": f32(np.concatenate([x_cur[b, tok0:tok0 + TPC], xc_cur[b]], 0)), "cT": f32(cT), "rope_cos": cos, "rope_sin": sin})
        maps.append(m)
    return maps


def const_fft(core):
    n1 = np.arange(64)
    ang = 2 * np.pi * np.outer(n1, n1) / 64.0
    C, S = np.cos(ang), np.sin(ang)
    WA = np.zeros((128, 128))
    WA[0:64, 0:64] = C; WA[64:128, 0:64] = S; WA[0:64, 64:128] = -S; WA[64:128, 64:128] = C
    k2_0 = 32 * (core % 4)
    n2 = np.arange(128)[:, None, None]
    k1 = np.arange(64)[None, :, None]
    k2 = (k2_0 + np.arange(32))[None, None, :]
    k = k1 + 64 * k2
    th = 2 * np.pi * ((n2 * k) % 8192) / 8192.0
    nrm = 1.0 / np.sqrt(8192.0 * 64.0)
    TC = np.stack([np.cos(th) * nrm, np.sin(th) * nrm], axis=1)
    n = (np.arange(2)[None, :, None] * 128 + np.arange(128)[:, None, None])
    kk = np.arange(256)[None, None, :]
    thc = 2 * np.pi * ((n * kk) % 256) / 256.0
    nrc = 1.0 / np.sqrt(256.0 * 64.0)
    TCc = np.stack([np.cos(thc) * nrc, np.sin(thc) * nrc], axis=2)
    return f32(WA), f32(TC), f32(TCc)


def const_moe():
    k = np.arange(128)
    utri = (k[:, None] < k[None, :]).astype(np.float32)
    ones = np.ones((128, 128), np.float32)
    iota = np.tile(np.arange(128, dtype=np.float32)[None, :], (128, 1))
    blk = ((k[:, None] // 8) == (k[None, :] // 8)).astype(np.float32)
    thrc = np.tile(np.array([[2 * 1024 - 8192, 2 * 32 - 256]], np.float32), (128, 1))
    return {"utri": utri, "ones": ones, "iota": iota, "blockones": blk, "thrc": thrc}


def inputs_fused(inp):
    ropes = [const_rope(v) for v in range(4)]
    ffts = [const_fft(v) for v in range(4)]
    shared = {
        "w_ada": f32(inp["w_ada"]), "b_ada": f32(inp["b_ada"]), "w_in": f32(inp["w_in"]),
        "w_sguT": f32(np.transpose(inp["w_sgu"], (0, 3, 1, 2))), "b_sguT": f32(np.transpose(inp["b_sgu"], (0, 2, 1))),
        "sgu_norm": f32(inp["sgu_norm"]), "q_lora_norm": f32(inp["q_lora_norm"]), "kv_lora_norm": f32(inp["kv_lora_norm"]),
        "q_norm": f32(inp["q_norm"]), "k_norm": f32(inp["k_norm"]), "w_uq": f32(inp["w_uq"]), "w_ukv": f32(inp["w_ukv"]),
        "dftc": const_dftc(), "rope_cos": f32(np.stack([r[0] for r in ropes])), "rope_sin": f32(np.stack([r[1] for r in ropes])),
        "WA": ffts[0][0], "TC": f32(np.stack([f[1] for f in ffts])), "TCc": ffts[0][2],
        "w_out": f32(inp["w_out"]), "w_router": f32(inp["w_router"]), "w_gate": f32(inp["w_gate"]), "w_up": f32(inp["w_up"]),
        "w_down": f32(inp["w_down"]),
    }
    shared.update(const_moe())
    maps = []
    for b in range(2):
        cT = np.stack([np.asarray(inp["c"][b]).reshape(8, 128).T, np.asarray(inp["c_ctx"]).reshape(8, 128).T], axis=-1)
        m = dict(shared)
        m.update({"x": f32(inp["x"][b]), "ctx": f32(inp["ctx"][b]), "cT": f32(cT)})
        maps.append(m)
    return maps


def kernel(**inputs):
    inp = {k: np.asarray(v) for k, v in inputs.items()}
    nc = build_fused()
    maps = inputs_fused(inp)
    res = run_bass_kernel_spmd(nc, maps, core_ids=[0, 1]).results
    return np.stack([np.asarray(res[b]["out"], dtype=np.float32) for b in range(2)])
```

```python
import numpy as np
import ml_dtypes
from concourse.bass_utils import run_bass_kernel_spmd

import contextlib
import numpy as np
import concourse.bass as bass
import concourse.mybir as mybir

F32 = mybir.dt.float32
BF16 = mybir.dt.bfloat16
I32 = mybir.dt.int32
ALU = mybir.AluOpType
AF = mybir.ActivationFunctionType
AX = mybir.AxisListType

ENGS = ("pe", "act", "dve", "pool", "sp")
NDMASEM = 8


class Op:
    __slots__ = ("eng", "fn", "dma", "deps", "needs_sig", "sig_idx", "sem_i", "sem_val", "idx", "prev_same_sem")

    def __init__(self, eng, fn, dma):
        self.eng = eng
        self.fn = fn
        self.dma = dma
        self.deps = []
        self.needs_sig = False
        self.sig_idx = 0
        self.sem_i = -1
        self.sem_val = 0
        self.prev_same_sem = None


class BufState:
    __slots__ = ("last_w", "readers")

    def __init__(self):
        self.last_w = None
        self.readers = []


class Prog:
    def __init__(self, nc, tag=""):
        self.nc = nc
        self.tag = tag
        self.ops = {e: [] for e in ENGS}
        self.st = {}
        self.es = contextlib.ExitStack()
        self.ndma = {e: 0 for e in ENGS}
        self.dma_last = {}
        self.dma_tot = {}
        self.all_dma = []
        self.nsb = 0
        self.fence = None
        self.psum_keys = set()

    def sb(self, shape, dtype, name=None):
        self.nsb += 1
        name = name or f"sb{self.nsb}"
        return self.es.enter_context(self.nc.sbuf_tensor("s_" + self.tag + name, list(shape), dtype))

    def ps(self, shape, dtype, name=None):
        self.nsb += 1
        name = name or f"ps{self.nsb}"
        self.psum_keys.add(name)
        return self.es.enter_context(self.nc.psum_tensor("p_" + self.tag + name, list(shape), dtype))

    def _state(self, k):
        s = self.st.get(k)
        if s is None:
            s = self.st[k] = BufState()
        return s

    def capture(self, fn, *args):
        self.cap = []
        fn(*args)
        c, self.cap = self.cap, None
        return c

    def replay_interleaved(self, lists):
        idx = [0] * len(lists)
        while True:
            best, bf = -1, 2.0
            for i, l in enumerate(lists):
                if idx[i] < len(l):
                    f = idx[i] / len(l)
                    if f < bf:
                        best, bf = i, f
            if best < 0:
                break
            self.op(*lists[best][idx[best]])
            idx[best] += 1

    def op(self, eng, fn, reads=(), writes=(), dma=False):
        if getattr(self, "cap", None) is not None:
            self.cap.append((eng, fn, tuple(reads), tuple(writes), dma))
            return None
        o = Op(eng, fn, dma)
        deps = []
        pr = [k for k in reads if k in self.psum_keys]
        if pr:
            reads = [k for k in reads if k not in self.psum_keys]
            writes = list(writes) + [k for k in pr if k not in writes]
        for k in reads:
            s = self._state(k)
            if s.last_w is not None:
                deps.append((s.last_w, "raw"))
        for k in writes:
            s = self._state(k)
            if s.last_w is not None:
                deps.append((s.last_w, "waw"))
            for r in s.readers:
                deps.append((r, "war"))
        if self.fence is not None:
            deps.append((self.fence, "raw"))
        seen = set()
        for d, kind in deps:
            if d is o or id(d) in seen:
                continue
            if (not o.dma) and (not d.dma) and d.eng == o.eng:
                if o.eng == "pe":
                    continue
            seen.add(id(d))
            o.deps.append(d)
        for k in reads:
            self._state(k).readers.append(o)
        for k in writes:
            s = self._state(k)
            s.last_w = o
            s.readers = []
        if dma:
            i = self.ndma[eng] % NDMASEM
            self.ndma[eng] += 1
            o.sem_i = i
            key = (eng, i)
            o.prev_same_sem = self.dma_last.get(key)
            o.sem_val = self.dma_tot.get(key, 0) + 16
            self.dma_tot[key] = o.sem_val
            self.dma_last[key] = o
            self.all_dma.append(o)
        self.ops[eng].append(o)
        return o

    def barrier(self, bar_tile):
        nc = self.nc
        lasts = []
        for e in ENGS:
            comp = [o for o in self.ops[e] if not o.dma]
            if comp:
                lasts.append(comp[-1])
        lasts.extend(self.dma_last.values())
        old = self.fence
        self.fence = None
        b = self.op("dve", lambda: nc.vector.memset(bar_tile, 0.0), (), ())
        for d in lasts:
            if d is not b and d not in b.deps and not (d.eng == "pe" and False):
                b.deps.append(d)
        if old is not None and old not in b.deps:
            b.deps.append(old)
        self.fence = b
        return b

    def I(self, eng, method, reads=(), writes=(), **kw):
        return self.op(eng, lambda: method(**kw), reads, writes)

    def dma(self, eng, out, in_, reads=(), writes=(), **kw):
        e = {"sp": self.nc.sync, "pool": self.nc.gpsimd, "act": self.nc.scalar}[eng]
        return self.op(eng, lambda: e.dma_start(out=out, in_=in_, **kw), reads, writes, dma=True)

    def emit(self):
        nc = self.nc
        for e in ENGS:
            for o in self.ops[e]:
                for d in o.deps:
                    if not d.dma:
                        d.needs_sig = True
        for e in ENGS:
            c = 0
            for o in self.ops[e]:
                if (not o.dma) and o.needs_sig:
                    c += 1
                    o.sig_idx = c
        es = self.es
        csem = {e: nc.alloc_semaphore(name=f"c_{e}_{self.tag}") for e in ENGS}
        dsem = {}
        for e in ENGS:
            if self.ndma[e]:
                for i in range(min(NDMASEM, self.ndma[e])):
                    dsem[(e, i)] = nc.alloc_semaphore(name=f"d_{e}{i}_{self.tag}")
        block = es.enter_context(nc.Block())
        prog = self

        def stream(ename, eng):
            waited = {}

            def wait(key, sem, val):
                if waited.get(key, 0) < val:
                    eng.wait_ge(sem, val)
                    waited[key] = val

            for o in prog.ops[ename]:
                for d in o.deps:
                    if d.dma:
                        wait(("d", d.eng, d.sem_i), dsem[(d.eng, d.sem_i)], d.sem_val)
                    else:
                        wait(("c", d.eng), csem[d.eng], d.sig_idx)
                if o.dma:
                    p = o.prev_same_sem
                    if p is not None:
                        wait(("d", ename, o.sem_i), dsem[(ename, o.sem_i)], p.sem_val)
                    inst = o.fn()
                    inst.then_inc(dsem[(ename, o.sem_i)], 16)
                else:
                    inst = o.fn()
                    if o.needs_sig:
                        inst.then_inc(csem[ename], 1)
            if ename == "sp":
                for key, tot in prog.dma_tot.items():
                    wait(("d",) + key, dsem[key], tot)
                for e2 in ENGS:
                    if e2 != "sp":
                        n = max([o.sig_idx for o in prog.ops[e2] if not o.dma] + [0])
                        if n:
                            wait(("c", e2), csem[e2], n)

        @block.tensor
        def _(eng):
            stream("pe", eng)

        @block.scalar
        def _(eng):
            stream("act", eng)

        @block.vector
        def _(eng):
            stream("dve", eng)

        @block.gpsimd
        def _(eng):
            stream("pool", eng)

        @block.sync
        def _(eng):
            stream("sp", eng)

    def close(self):
        self.es.close()


DBGZ = DBGQ = DBGT = 9
SEQREPLAY = 0

NT, NCT = 16, 2
TT = NT + NCT
EPS = 1e-6


class Rot:
    def __init__(self, P, n, shape, dtype, name):
        self.bufs = [(P.sb(shape, dtype, f"{name}{i}"), f"{name}{i}") for i in range(n)]
        self.i = 0

    def next(self):
        b = self.bufs[self.i % len(self.bufs)]
        self.i += 1
        return b


def make_ident(P, nc):
    identf = P.sb([128, 128], F32, "identf")
    ident = P.sb([128, 128], BF16, "ident")
    P.I("pool", nc.gpsimd.memset, [], ["identf"], ap=identf[:], constant=0.0)
    P.I("pool", nc.gpsimd.affine_select, ["identf"], ["identf"], out=identf[:], in_=identf[:], pattern=[[-1, 128]],
        compare_op=ALU.not_equal, fill=1.0, base=0, channel_multiplier=1)
    P.I("dve", nc.vector.tensor_copy, ["identf"], ["ident"], out=ident[:], in_=identf[:])
    P.identf = identf
    return ident


def build_A(last, ntiles=TT, dbg=99, nc=None, T=None, tag="", ada_src=None):
    if nc is None:
        nc = bass.Bass("TRN2", target_bir_lowering=False)

    def din(name, shape, dt=F32):
        if T is not None:
            assert tuple(T[name].shape) == tuple(shape), (name, T[name].shape, shape)
            return T[name]
        return nc.dram_tensor(name, list(shape), dt, kind="ExternalInput").ap()

    def dout(name, shape, dt=F32):
        if T is not None:
            assert tuple(T[name].shape) == tuple(shape), (name, T[name].shape, shape)
            return T[name]
        return nc.dram_tensor(name, list(shape), dt, kind="ExternalOutput").ap()

    xin = din("xin", [TT * 128, 1024])
    cT_d = din("cT", [128, 8, 2])
    w_ada_d = din("w_ada", [1024, 6144])
    b_ada_d = din("b_ada", [1, 6144])
    w_in_d = din("w_in", [1024, 1216])
    w_sguT_d = din("w_sguT", [128, 4, 128])
    b_sguT_d = din("b_sguT", [128, 4])
    sgun_d = din("sgu_norm", [1, 256])
    qln_d = din("q_lora_norm", [1, 256])
    kvln_d = din("kv_lora_norm", [1, 128])
    qn_d = din("q_norm", [1, 192])
    kn_d = din("k_norm", [1, 192])
    w_uq_d = din("w_uq", [256, 768])
    w_ukv_d = din("w_ukv", [128, 1024])
    dftc_d = din("dftc", [256, 512])
    rcos_d = din("rope_cos", [TT * 128, 64])
    rsin_d = din("rope_sin", [TT * 128, 64])

    ada_o = dout("ada", [2, 6144])
    yaT_o = dout("yaT", [256, TT * 128], BF16)
    Z_o = dout("Z", [TT * 128, 512], BF16)
    QT_o = dout("QT", [4, 192, TT * 128], BF16)
    KT_o = dout("KT", [4, 192, TT * 128], BF16)
    V_o = dout("V", [TT * 128, 512], BF16)

    P = Prog(nc, tag)
    I = P.I
    ident = make_ident(P, nc)

    def bc_load(name, src, n):
        t = P.sb([128, n], F32, name)
        P.dma("sp", t[:], src.partition_broadcast(128), writes=[name])
        return t

    sgun = bc_load("sgun", sgun_d[0:1, :], 256)
    qln = bc_load("qln", qln_d[0:1, :], 256)
    kvln = bc_load("kvln", kvln_d[0:1, :], 128)
    qnb = bc_load("qnb", qn_d[0:1, :], 192)
    knb = bc_load("knb", kn_d[0:1, :], 192)
    b_sguT = P.sb([128, 4], F32, "b_sguT")
    P.dma("sp", b_sguT[:], b_sguT_d, writes=["b_sguT"])
    rcos = P.sb([128, TT, 64], F32, "rcos")
    rsin = P.sb([128, TT, 64], F32, "rsin")
    P.dma("sp", rcos[:], rcos_d.rearrange("(t p) d -> p t d", p=128), writes=["rcos"])
    P.dma("sp", rsin[:], rsin_d.rearrange("(t p) d -> p t d", p=128), writes=["rsin"])
    cT = P.sb([128, 8, 2], F32, "cT")
    P.dma("sp", cT[:], cT_d, writes=["cT"])
    ps_ada = P.ps([128, 512], F32, "psA")
    if ada_src is None:
        scT = P.sb([128, 8, 2], BF16, "scT")
        I("act", nc.scalar.activation, ["cT"], ["scT"], out=scT[:], in_=cT[:], func=AF.Silu)
        mbr = Rot(P, 2, [2, 512], F32, "mblk")
        bar = Rot(P, 2, [2, 512], F32, "bablk")
        warot = Rot(P, 2, [128, 8, 512], BF16, "wa")
        w_ada_v = w_ada_d.rearrange("(k p) n -> p k n", p=128)
        for nb in range(12):
            wa, kwa = warot.next()
            for hk in range(2):
                P.dma("pool", wa[:, hk * 4:(hk + 1) * 4, :], w_ada_v[:, hk * 4:(hk + 1) * 4, nb * 512:(nb + 1) * 512], writes=[kwa + f"_{hk}"])
            for k in range(8):
                I("pe", nc.tensor.matmul, ["scT", kwa + f"_{k // 4}"], ["psA"], out=ps_ada[0:2, :], lhsT=scT[:, k, :], rhs=wa[:, k, :],
                  start=(k == 0), stop=(k == 7))
            mb, kmb = mbr.next()
            ba, kba = bar.next()
            P.dma("sp", ba[:], b_ada_d[0:1, nb * 512:(nb + 1) * 512].partition_broadcast(2), writes=[kba])
            I("dve", nc.vector.tensor_tensor, ["psA", kba], [kmb], out=mb[:], in0=ps_ada[0:2, :], in1=ba[:], op=ALU.add)
            P.dma("sp", ada_o[:, nb * 512:(nb + 1) * 512], mb[:], reads=[kmb], writes=["ada_d"])

    else:
        ada_o = ada_src
    mods = []
    for r in range(2):
        md = P.sb([128, 2048], F32, f"mod{r}")
        P.dma("sp", md[:], ada_o[r:r + 1, 0:2048].partition_broadcast(128), reads=["ada_d"], writes=[f"mod{r}"])
        I("dve", nc.vector.tensor_scalar_add, [f"mod{r}"], [f"mod{r}"], out=md[:, 1024:2048], in0=md[:, 1024:2048], scalar1=1.0)
        mods.append(md)

    w_in = P.sb([128, 8, 1216], BF16, "w_in")
    w_in_v = w_in_d.rearrange("(k p) n -> p k n", p=128)
    for k in range(0, 8, 2):
        P.dma("pool", w_in[:, k:k + 2, :], w_in_v[:, k:k + 2, :], writes=[f"w_in{k}"])
    w_in_keys = [f"w_in{k}" for k in range(0, 8, 2)]
    w_uq = P.sb([128, 2, 768], BF16, "w_uq")
    P.dma("pool", w_uq[:], w_uq_d.rearrange("(k p) n -> p k n", p=128), writes=["w_uq"])
    w_ukv = P.sb([128, 1024], BF16, "w_ukv")
    P.dma("pool", w_ukv[:], w_ukv_d, writes=["w_ukv"])
    dftc = P.sb([128, 2, 512], BF16, "dftc")
    P.dma("pool", dftc[:], dftc_d.rearrange("(k p) n -> p k n", p=128), writes=["dftc"])
    w_sguT = P.sb([128, 4, 128], BF16, "w_sguT")
    P.dma("pool", w_sguT[:], w_sguT_d, writes=["w_sguT"])

    psT = P.ps([128, 1024], BF16, "psT")
    psT2 = P.ps([128, 1024], BF16, "psT2")
    psT3 = P.ps([128, 1024], BF16, "psT3")
    px0 = P.ps([128, 512], F32, "px0")
    px1 = P.ps([128, 512], F32, "px1")
    px2 = P.ps([128, 512], F32, "px2")
    psB = P.ps([128, 512], F32, "psB")

    xr_ = Rot(P, 2, [128, 1024], F32, "x")
    tmpr = Rot(P, 2, [128, 1024], F32, "tmp")
    hr = Rot(P, 2, [128, 1024], BF16, "h")
    hTr = Rot(P, 2, [128, 1024], BF16, "hT")
    junkr = Rot(P, 4, [128, 1024], BF16, "junkA")
    str_ = Rot(P, 3, [128, 40], F32, "st")
    uvr = Rot(P, 2, [128, 512], F32, "uv")
    vbr = Rot(P, 2, [128, 256], BF16, "vb")
    yar = Rot(P, 2, [128, 256], BF16, "ya")
    yaTr = Rot(P, 2, [128, 2, 128], BF16, "yaT")
    pfr = Rot(P, 2, [128, 256], BF16, "pf")
    pfTr = Rot(P, 2, [128, 2, 128], BF16, "pfT")
    Zr = Rot(P, 2, [128, 512], BF16, "Z")
    cqr = Rot(P, 2, [128, 256], BF16, "cq")
    cqTr = Rot(P, 2, [128, 2, 128], BF16, "cqT")
    qfr = Rot(P, 2, [128, 4, 192], F32, "qf")
    qnr = Rot(P, 2, [128, 4, 192], F32, "qn")
    r1r = Rot(P, 2, [128, 4, 64], F32, "r1")
    r2r = Rot(P, 2, [128, 4, 64], F32, "r2")
    qbr = Rot(P, 2, [128, 4, 256], BF16, "qb")
    for (qb_, kqb_) in qbr.bufs:
        I("pool", nc.gpsimd.memset, [], [kqb_], ap=qb_[:], constant=0.0)
    QTnr = Rot(P, 3, [128, 4, 128], BF16, "QTn")
    QTrr = Rot(P, 3, [128, 4, 128], BF16, "QTr")
    ckvr = Rot(P, 2, [128, 128], BF16, "ckv")
    ckvTr = Rot(P, 2, [128, 128], BF16, "ckvT")
    Vbr = Rot(P, 2, [128, 4, 128], BF16, "Vb")

    def rms_rstd(src, ncols, n, rkeys, st, kst, c0, nm):
        k0, k1, k2 = f"{kst}_{nm}0", f"{kst}_{nm}1", f"{kst}_{nm}2"
        junkA, kj = junkr.next()
        I("act", nc.scalar.activation, rkeys, [kj, k0], out=junkA[:, 0:ncols], in_=src, func=AF.Square, accum_out=st[:, c0:c0 + 1])
        I("act", nc.scalar.activation, [k0], [k1], out=st[:, c0 + 1:c0 + 2], in_=st[:, c0:c0 + 1], func=AF.Sqrt, scale=1.0 / n, bias=EPS)
        I("dve", nc.vector.reciprocal, [k1], [k2], out=st[:, c0 + 2:c0 + 3], in_=st[:, c0 + 1:c0 + 2])
        return k2

    def head_norm_rope_store(t, qf, kqf, normb, knormb, st, kst, c0, nm, out_d):
        if DBGQ < 2:
            return
        ks = [f"{kst}_{nm}s{h}" for h in range(4)]
        for h in range(4):
            junkA, kj = junkr.next()
            I("act", nc.scalar.activation, [kqf], [kj, ks[h]], out=junkA[:, 0:192], in_=qf[:, h, :], func=AF.Square,
              accum_out=st[:, c0 + h:c0 + h + 1])
        kq1, kq2 = f"{kst}_{nm}q1", f"{kst}_{nm}q2"
        I("act", nc.scalar.activation, ks, [kq1], out=st[:, c0 + 4:c0 + 8], in_=st[:, c0:c0 + 4], func=AF.Sqrt, scale=1.0 / 192, bias=EPS)
        I("dve", nc.vector.reciprocal, [kq1], [kq2], out=st[:, c0 + 8:c0 + 12], in_=st[:, c0 + 4:c0 + 8])
        qn, kqn = qnr.next()
        for h in range(4):
            I("dve", nc.vector.scalar_tensor_tensor, [kqf, kq2, knormb], [kqn], out=qn[:, h, :], in0=qf[:, h, :],
              scalar=st[:, c0 + 8 + h:c0 + 9 + h], in1=normb[:], op0=ALU.mult, op1=ALU.mult)
        if DBGQ < 3:
            return
        r1, kr1 = r1r.next()
        r2, kr2 = r2r.next()
        qb, kqb = qbr.next()
        xrp = qn[:, :, 128:192]
        I("dve", nc.vector.tensor_tensor, [kqn, "rcos"], [kr1], out=r1[:], in0=xrp, in1=rcos[:, t, :].unsqueeze(1).to_broadcast([128, 4, 64]),
          op=ALU.mult)
        x5 = xrp.rearrange("p h (b s d) -> p h b s d", b=2, s=2)
        o5 = r2[:].rearrange("p h (b s d) -> p h b s d", b=2, s=2)
        s5 = rsin[:, t, :].rearrange("p (b s d) -> p b s d", b=2, s=2)
        for s_ in range(2):
            I("dve", nc.vector.tensor_tensor, [kqn, "rsin"], [kr2], out=o5[:, :, :, s_, :], in0=x5[:, :, :, 1 - s_, :],
              in1=s5[:, :, s_, :].unsqueeze(1).to_broadcast([128, 4, 2, 16]), op=ALU.mult)
        I("dve", nc.vector.tensor_tensor, [kr1, kr2], [kqb], out=qb[:, :, 128:192], in0=r1[:], in1=r2[:], op=ALU.add)
        I("act", nc.scalar.copy, [kqn], [kqb], out=qb[:, :, 0:128], in_=qn[:, :, 0:128])
        if DBGQ < 4:
            return
        for h in range(4):
            I("pe", nc.tensor.transpose, [kqb, "ident"], ["psT3"], out=psT3[:, h * 128:(h + 1) * 128], in_=qb[:, h, 0:128], identity=ident[:])
            if DBGT >= 1:
                I("pe", nc.tensor.transpose, [kqb, "ident"], ["psT3"], out=psT3[:, 512 + h * 128:512 + (h + 1) * 128], in_=qb[:, h, 128:256],
                  identity=ident[:])
        QTn, kQTn = QTnr.next()
        QTr, kQTr = QTrr.next()
        I("act", nc.scalar.copy, ["psT3"], [kQTn], out=QTn[:], in_=psT3[:, 0:512].rearrange("p (h t) -> p h t", h=4))
        if DBGT >= 2:
          I("dve", nc.vector.tensor_copy, ["psT3"], [kQTr], out=QTr[0:64, :, :], in_=psT3[0:64, 512:1024].rearrange("p (h t) -> p h t", h=4))
        if DBGQ < 5:
            return
        P.dma("sp", out_d[:, 0:128, t * 128:(t + 1) * 128].rearrange("h d t -> d h t"), QTn[:], reads=[kQTn])
        P.dma("sp", out_d[:, 128:192, t * 128:(t + 1) * 128].rearrange("h d t -> d h t"), QTr[0:64, :, :], reads=[kQTr])

    if last:
        zb = P.sb([128, 4, 128], BF16, "zb")
        I("pool", nc.gpsimd.memset, [], ["zb"], ap=zb[:], constant=0.0)
        zbf = zb[:].rearrange("p h t -> p (h t)")
        for t in range(NT, ntiles):
            P.dma("sp", yaT_o[:, t * 128:(t + 1) * 128].rearrange("(c p) t -> p c t", p=128), zb[:, 0:2, :], reads=["zb"])
            P.dma("sp", Z_o[t * 128:(t + 1) * 128, :], zbf, reads=["zb"])
            P.dma("sp", QT_o[:, 0:128, t * 128:(t + 1) * 128].rearrange("h d t -> d h t"), zb[:], reads=["zb"])
            P.dma("sp", QT_o[:, 128:192, t * 128:(t + 1) * 128].rearrange("h d t -> d h t"), zb[0:64, :, :], reads=["zb"])
    def front(t, S):
        is_ctx = t >= NT
        md = mods[1 if is_ctx else 0]
        kmd = f"mod{1 if is_ctx else 0}"
        x_t, kx = xr_.next()
        P.dma("sp", x_t[:], xin[t * 128:(t + 1) * 128, :], writes=[kx])
        st, kst = str_.next()
        S["st"], S["kst"] = st, kst
        krs = rms_rstd(x_t[:], 1024, 1024, [kx], st, kst, 0, "n1")
        tmp, ktmp = tmpr.next()
        h_t, kh = hr.next()
        I("dve", nc.vector.scalar_tensor_tensor, [kx, krs, kmd], [ktmp], out=tmp[:], in0=x_t[:], scalar=st[:, 2:3], in1=md[:, 1024:2048],
          op0=ALU.mult, op1=ALU.mult)
        I("dve", nc.vector.tensor_tensor, [ktmp, kmd], [kh], out=h_t[:], in0=tmp[:], in1=md[:, 0:1024], op=ALU.add)
        for k in range(8):
            I("pe", nc.tensor.transpose, [kh, "ident"], ["psT"], out=psT[:, k * 128:(k + 1) * 128], in_=h_t[:, k * 128:(k + 1) * 128],
              identity=ident[:])
        hT, khT = hTr.next()
        I("act", nc.scalar.copy, ["psT"], [khT], out=hT[:], in_=psT[:])
        S["hT"], S["khT"] = hT, khT

    def pxmm(t, S):
        kv_only = (t >= NT) and last
        hT, khT = S["hT"], S["khT"]
        blocks = [(px0, "px0", 0, 512), (px1, "px1", 512, 1024), (px2, "px2", 1024, 1216)]
        for (pb, kpb, c0, c1) in blocks:
            if kv_only and kpb != "px2":
                continue
            for k in range(8):
                I("pe", nc.tensor.matmul, [khT, w_in_keys[k // 2]], [kpb], out=pb[:, 0:c1 - c0], lhsT=hT[:, k * 128:(k + 1) * 128],
                  rhs=w_in[:, k, c0:c1], start=(k == 0), stop=(k == 7))
        if not kv_only:
            uv, kuv = uvr.next()
            I("act", nc.scalar.activation, ["px0"], [kuv], out=uv[:], in_=px0[:], func=AF.Gelu_apprx_tanh)
            S["uv"], S["kuv"] = uv, kuv
            p1, kp1 = p1r.next()
            I("act", nc.scalar.copy, ["px1"], [kp1], out=p1[:], in_=px1[:])
            S["p1"], S["kp1"] = p1, kp1
        p2, kp2 = p2r.next()
        I("dve", nc.vector.tensor_copy, ["px2"], [kp2], out=p2[:], in_=px2[:, 0:192])
        S["p2"], S["kp2"] = p2, kp2

    def sgu(t, S):
        st, kst = S["st"], S["kst"]
        uv, kuv = S["uv"], S["kuv"]
        krv = rms_rstd(uv[:, 256:512], 256, 256, [kuv], st, kst, 3, "v")
        vb, kvb = vbr.next()
        I("dve", nc.vector.scalar_tensor_tensor, [kuv, krv, "sgun"], [kvb], out=vb[:], in0=uv[:, 256:512], scalar=st[:, 5:6], in1=sgun[:],
          op0=ALU.mult, op1=ALU.mult)
        for h in range(4):
            I("pe", nc.tensor.matmul, [kvb, "w_sguT"], ["psA"], out=ps_ada[:, h * 64:(h + 1) * 64], lhsT=w_sguT[:, h, :],
              rhs=vb[:, h * 64:(h + 1) * 64], start=True, stop=True)
        ya, kya = yar.next()
        for h in range(4):
            I("dve", nc.vector.scalar_tensor_tensor, ["psA", "b_sguT", kuv], [kya], out=ya[:, h * 64:(h + 1) * 64],
              in0=ps_ada[:, h * 64:(h + 1) * 64], scalar=b_sguT[:, h:h + 1], in1=uv[:, h * 64:(h + 1) * 64], op0=ALU.add, op1=ALU.mult)
        for c in range(2):
            I("pe", nc.tensor.transpose, [kya, "ident"], ["psT2"], out=psT2[:, c * 128:(c + 1) * 128], in_=ya[:, c * 128:(c + 1) * 128],
              identity=ident[:])
        yaT, kyaT = yaTr.next()
        I("act", nc.scalar.copy, ["psT2"], [kyaT], out=yaT[:], in_=psT2[:, 0:256].rearrange("p (c t) -> p c t", c=2))
        P.dma("sp", yaT_o[:, t * 128:(t + 1) * 128].rearrange("(c p) t -> p c t", p=128), yaT[:], reads=[kyaT])

    def zpart(t, S):
        p1, kp1 = S["p1"], S["kp1"]
        pf, kpf = pfr.next()
        I("dve", nc.vector.tensor_copy, [kp1], [kpf], out=pf[:], in_=p1[:, 0:256])
        for c in range(2):
            I("pe", nc.tensor.transpose, [kpf, "ident"], ["psT2"], out=psT2[:, 256 + c * 128:256 + (c + 1) * 128],
              in_=pf[:, c * 128:(c + 1) * 128], identity=ident[:])
        pfT, kpfT = pfTr.next()
        I("act", nc.scalar.copy, ["psT2"], [kpfT], out=pfT[:], in_=psT2[:, 256:512].rearrange("p (c t) -> p c t", c=2))
        for c in range(2):
            I("pe", nc.tensor.matmul, [kpfT, "dftc"], ["psB"], out=psB[:], lhsT=pfT[:, c, :], rhs=dftc[:, c, :], start=(c == 0), stop=(c == 1))
        Zt, kZ = Zr.next()
        I("act", nc.scalar.copy, ["psB"], [kZ], out=Zt[:], in_=psB[:])
        P.dma("sp", Z_o[t * 128:(t + 1) * 128, :], Zt[:], reads=[kZ])

    def qpart(t, S):
        st, kst = S["st"], S["kst"]
        p1, kp1 = S["p1"], S["kp1"]
        krq = rms_rstd(p1[:, 256:512], 256, 256, [kp1], st, kst, 6, "q")
        cq, kcq = cqr.next()
        I("dve", nc.vector.scalar_tensor_tensor, [kp1, krq, "qln"], [kcq], out=cq[:], in0=p1[:, 256:512], scalar=st[:, 8:9], in1=qln[:],
          op0=ALU.mult, op1=ALU.mult)
        for c in range(2):
            I("pe", nc.tensor.transpose, [kcq, "ident"], ["psT2"], out=psT2[:, 512 + c * 128:512 + (c + 1) * 128],
              in_=cq[:, c * 128:(c + 1) * 128], identity=ident[:])
        cqT, kcqT = cqTr.next()
        I("act", nc.scalar.copy, ["psT2"], [kcqT], out=cqT[:], in_=psT2[:, 512:768].rearrange("p (c t) -> p c t", c=2))
        for c in range(2):
            I("pe", nc.tensor.matmul, [kcqT, "w_uq"], ["px1"], out=px1[:], lhsT=cqT[:, c, :], rhs=w_uq[:, c, 0:512], start=(c == 0), stop=(c == 1))
        for c in range(2):
            I("pe", nc.tensor.matmul, [kcqT, "w_uq"], ["px2"], out=px2[:, 0:256], lhsT=cqT[:, c, :], rhs=w_uq[:, c, 512:768],
              start=(c == 0), stop=(c == 1))
        qf, kqf = qfr.next()
        qf2 = qf[:].rearrange("p h d -> p (h d)")
        I("act", nc.scalar.copy, ["px1"], [kqf], out=qf2[:, 0:512], in_=px1[:])
        I("act", nc.scalar.copy, ["px2"], [kqf], out=qf2[:, 512:768], in_=px2[:, 0:256])
        head_norm_rope_store(t, qf, kqf, qnb, "qnb", st, kst, 9, "qh", QT_o)

    def kvpart(t, S):
        st, kst = S["st"], S["kst"]
        p2, kp2 = S["p2"], S["kp2"]
        krk = rms_rstd(p2[:, 0:128], 128, 128, [kp2], st, kst, 21, "kv")
        ckv, kckv = ckvr.next()
        I("dve", nc.vector.scalar_tensor_tensor, [kp2, krk, "kvln"], [kckv], out=ckv[:], in0=p2[:, 0:128], scalar=st[:, 23:24], in1=kvln[:],
          op0=ALU.mult, op1=ALU.mult)
        I("pe", nc.tensor.transpose, [kckv, "ident"], ["psT2"], out=psT2[:, 768:896], in_=ckv[:], identity=ident[:])
        ckvT, kckvT = ckvTr.next()
        I("act", nc.scalar.copy, ["psT2"], [kckvT], out=ckvT[:], in_=psT2[:, 768:896])
        kf, kkf = kfr.next()
        Vb, kVb = Vbr.next()
        I("act", nc.scalar.copy, [kp2], [kkf], out=kf[:, :, 128:192], in_=p2[:, 128:192].unsqueeze(1).to_broadcast([128, 4, 64]))
        for j, (pb, kpb) in enumerate([(px0, "px0"), (px0, "px0")]):
            I("pe", nc.tensor.matmul, [kckvT, "w_ukv"], [kpb], out=pb[:], lhsT=ckvT[:], rhs=w_ukv[:, j * 512:(j + 1) * 512], start=True, stop=True)
            pv = pb[:].rearrange("p (h s d) -> p h s d", h=2, s=2)
            I("act", nc.scalar.copy, [kpb], [kVb], out=Vb[:, 2 * j:2 * j + 2, :], in_=pv[:, :, 1, :])
            I("dve", nc.vector.tensor_copy, [kpb], [kkf], out=kf[:, 2 * j:2 * j + 2, 0:128], in_=pv[:, :, 0, :])
        P.dma("sp", V_o[t * 128:(t + 1) * 128, :], Vb[:].rearrange("p h d -> p (h d)"), reads=[kVb])
        head_norm_rope_store(t, kf, kkf, knb, "knb", st, kst, 24, "kh", KT_o)

    p1r = Rot(P, 2, [128, 512], F32, "p1s")
    p2r = Rot(P, 2, [128, 192], F32, "p2s")
    kfr = Rot(P, 2, [128, 4, 192], F32, "kf")
    states = [dict() for _ in range(ntiles + 1)]
    if ntiles:
        front(0, states[0])
    for t in range(ntiles):
        S = states[t]
        kv_only = (t >= NT) and last
        pxmm(t, S)
        lists = []
        if not kv_only:
            lists += [P.capture(sgu, t, S), P.capture(zpart, t, S)]
        lists.append((P.capture(qpart, t, S) if not kv_only else []) + P.capture(kvpart, t, S))
        if t + 1 < ntiles:
            lists.append(P.capture(front, t + 1, states[t + 1]))
        P.replay_interleaved(lists) if not SEQREPLAY else [P.op(*o) for l in lists for o in l]
    P.emit()
    P.close()
    return nc


def build_F(last, nc=None, T=None, tag=""):
    if nc is None:
        nc = bass.Bass("TRN2", target_bir_lowering=False)

    def din(name, shape, dt=F32):
        if T is not None:
            assert tuple(T[name].shape) == tuple(shape), (name, T[name].shape, shape)
            return T[name]
        return nc.dram_tensor(name, list(shape), dt, kind="ExternalInput").ap()

    Z_d = din("Z_all", [8192, 512], BF16)
    Zc_d = din("Zc", [256, 512], BF16)
    WA_d = din("WA", [128, 128])
    TC_d = din("TC", [128, 2, 64, 32])
    TCc_d = din("TCc", [128, 2, 2, 256])
    ybT_o = T["ybT"] if T is not None else nc.dram_tensor("ybT", [256, TT * 128], BF16, kind="ExternalOutput").ap()
    A_d = nc.dram_tensor(tag + "A_scr", [128, 128, 256], BF16, kind="Internal").ap()

    P = Prog(nc, tag)
    I = P.I
    WA = P.sb([128, 128], BF16, "WA")
    P.dma("pool", WA[:], WA_d, writes=["WA"])
    TC = P.sb([128, 2, 64, 32], BF16, "TC")
    P.dma("pool", TC[:], TC_d, writes=["TC"])
    fb = [P.ps([128, 512], F32, f"f{i}") for i in range(2)]
    yb = [P.ps([128, 512], F32, f"yb{i}") for i in range(2)]
    pr = P.ps([128, 512], F32, "pr")
    zar = Rot(P, 2, [128, 16, 256], BF16, "za")
    aor = Rot(P, 2, [128, 16, 256], BF16, "ao")
    Zv = Z_d.rearrange("(n1 n2) (ri c) -> ri n1 n2 c", n2=128, ri=2)
    cnt = 0
    for ch in range(8):
        za, kza = zar.next()
        for ri in range(2):
            P.dma("sp", za[ri * 64:(ri + 1) * 64, :, :], Zv[ri, :, ch * 16:(ch + 1) * 16, :], writes=[f"{kza}_{ri}"])
        ao, kao = aor.next()
        for j in range(8):
            bk = fb[j % 2]
            I("pe", nc.tensor.matmul, [f"{kza}_0", f"{kza}_1", "WA"], [f"f{j % 2}"], out=bk[:], lhsT=WA[:],
              rhs=za[:, 2 * j:2 * j + 2, :].rearrange("p a c -> p (a c)"), start=True, stop=True)
            dst = ao[:, 2 * j:2 * j + 2, :].rearrange("p a c -> p (a c)")
            if cnt % 2 == 0:
                I("act", nc.scalar.copy, [f"f{j % 2}"], [kao], out=dst, in_=bk[:])
            else:
                I("dve", nc.vector.tensor_copy, [f"f{j % 2}"], [kao], out=dst, in_=bk[:])
            cnt += 1
        P.dma("sp", A_d[:, ch * 16:(ch + 1) * 16, :], ao[:], reads=[kao], writes=["A_d"])
    ac = P.sb([128, 128, 128], BF16, "ac")
    ybs = P.sb([128, 2, TT * 128], BF16, "ybs")
    A_v = A_d.rearrange("q n c -> n q c")
    for half in range(2):
        for qq in range(4):
            P.dma("sp", ac[:, qq * 32:(qq + 1) * 32, :], A_v[:, qq * 32:(qq + 1) * 32, half * 128:(half + 1) * 128], reads=["A_d"],
                  writes=[f"ac{qq}"])
        ackeys = [f"ac{qq}" for qq in range(4)]
        for bk in range(4):
            ybk = yb[bk % 2]
            yv = ybk[:].rearrange("p (k2 k1) -> p k2 k1", k1=64)
            for k1 in range(64):
                for ri in range(2):
                    I("pe", nc.tensor.matmul, ackeys + ["TC"], [f"yb{bk % 2}"], out=yv[:, :, k1], lhsT=ac[:, ri * 64 + k1, :],
                      rhs=TC[:, ri, k1, bk * 8:(bk + 1) * 8], start=(ri == 0), stop=(ri == 1))
            dst = ybs[:, half, bk * 512:(bk + 1) * 512]
            if bk % 2 == 0:
                I("act", nc.scalar.copy, [f"yb{bk % 2}"], ["ybs"], out=dst, in_=ybk[:])
            else:
                I("dve", nc.vector.tensor_copy, [f"yb{bk % 2}"], ["ybs"], out=dst, in_=ybk[:])
    if not last:
        Zc = P.sb([128, 2, 512], BF16, "Zc")
        P.dma("sp", Zc[:], Zc_d.rearrange("(t p) c -> p t c", p=128), writes=["Zc"])
        TCc = P.sb([128, 2, 2, 256], BF16, "TCc")
        P.dma("pool", TCc[:], TCc_d, writes=["TCc"])
        for half in range(2):
            i = 0
            for nt in range(2):
                for ri in range(2):
                    I("pe", nc.tensor.matmul, ["Zc", "TCc"], ["pr"], out=pr[:, 0:256], lhsT=Zc[:, nt, ri * 256 + half * 128:ri * 256 + (half + 1) * 128],
                      rhs=TCc[:, nt, ri, :], start=(i == 0), stop=(i == 3))
                    i += 1
            I("act", nc.scalar.copy, ["pr"], ["ybs"], out=ybs[:, half, NT * 128:TT * 128], in_=pr[:, 0:256])
    if last:
        I("pool", nc.gpsimd.memset, [], ["ybs"], ap=ybs[:, :, NT * 128:TT * 128], constant=0.0)
    ncols = TT * 128
    P.dma("sp", ybT_o[:, 0:ncols].rearrange("(c p) t -> p c t", p=128), ybs[:, :, 0:ncols], reads=["ybs"])
    P.emit()
    P.close()
    return nc


NKT = 66
SCALE = 192 ** -0.5


def build_B(last, nheads=4, nblocks=None, full=True, nc=None, T=None, tag=""):
    if nc is None:
        nc = bass.Bass("TRN2", target_bir_lowering=False)

    def din(name, shape, dt=F32):
        if T is not None:
            assert tuple(T[name].shape) == tuple(shape), (name, T[name].shape, shape)
            return T[name]
        return nc.dram_tensor(name, list(shape), dt, kind="ExternalInput").ap()

    def dout(name, shape, dt=F32):
        if T is not None:
            assert tuple(T[name].shape) == tuple(shape), (name, T[name].shape, shape)
            return T[name]
        return nc.dram_tensor(name, list(shape), dt, kind="ExternalOutput").ap()

    QT_d = din("QT", [4, 192, TT * 128], BF16)
    KT_d = din("KT_all", [4, 192, NKT * 128], BF16)
    V_d = din("V_all", [NKT * 128, 512], BF16)
    EPS = 1e-6
    if full:
        xin = din("xin", [TT * 128, 1024])
        ada_d = din("ada", [2, 6144])
        yaT_d = din("yaT", [256, TT * 128], BF16)
        ybT_d = din("ybT", [256, TT * 128], BF16)
        w_out_d = din("w_out", [1024, 1024])
        w_r_d = din("w_router", [1024, 16])
        x1_o = dout("x1", [TT * 128, 1024])
        h2_o = dout("h2", [TT * 128, 1024], BF16)
        aff_o = dout("aff", [TT * 128, 16])
        affT_o = dout("affT", [16, TT * 128])
    else:
        yc_o = dout("yc", [TT * 128, 512], BF16)

    P = Prog(nc, tag)
    I = P.I
    NCH = 6
    CT = NKT // NCH
    ktn = P.sb([128, NKT * 128], BF16, "ktn")
    ktr = P.sb([128, NKT * 128], BF16, "ktr")
    vh = P.sb([128, NKT, 129], BF16, "vh")
    I("pool", nc.gpsimd.memset, [], [f"vh{c}" for c in range(NCH)], ap=vh[:, :, 128:129], constant=1.0)
    I("pool", nc.gpsimd.memset, [], [f"ktr{c}" for c in range(NCH)], ap=ktr[64:128, :], constant=0.0)
    qnr = Rot(P, 2, [128, TT * 128], BF16, "qtn")
    qrr = Rot(P, 2, [128, TT * 128], BF16, "qtr")
    for (qb_, kqb_) in qrr.bufs:
        I("pool", nc.gpsimd.memset, [], [kqb_], ap=qb_[64:128, :], constant=0.0)
    ptr = Rot(P, 3, [128, 512], BF16, "pt")
    ycr = Rot(P, 3, [128, 128], BF16, "yct")
    rcr = Rot(P, 4, [128, 1], F32, "rc")
    sbank = [P.ps([128, 512], F32, f"sb{i}") for i in range(2)]
    obank = [P.ps([128, 512], F32, f"ob{i}") for i in range(4)]
    V_v = V_d.rearrange("(t p) (h d) -> p t h d", p=128, h=4)
    ntile = NT if last else TT
    if full:
        ident = make_ident(P, nc)
        psTb = P.ps([128, 1024], BF16, "psTb")
        tb = P.ps([128, 512], F32, "tb")
        mixT = P.sb([128, 8, TT * 128], BF16, "mixT")
        mkeys = [f"mix{t}" for t in range(TT)]
        def load_mix():
            P.dma("sp", mixT[:, 0:2, :], yaT_d.rearrange("(c p) t -> p c t", p=128), writes=mkeys)
            P.dma("sp", mixT[:, 2:4, :], ybT_d.rearrange("(c p) t -> p c t", p=128), writes=mkeys)
        w_out = P.sb([128, 8, 1024], BF16, "w_out")
        w_out_v = w_out_d.rearrange("(k p) n -> p k n", p=128)
        for k in range(0, 8, 2):
            P.dma("pool", w_out[:, k:k + 2, :], w_out_v[:, k:k + 2, :], writes=[f"w_out{k}"])
        wokeys = [f"w_out{k}" for k in range(0, 8, 2)]
        w_r = P.sb([128, 8, 16], BF16, "w_r")
        P.dma("pool", w_r[:], w_r_d.rearrange("(k p) n -> p k n", p=128), writes=["w_r"])
        mods = []
        for r in range(2):
            md = P.sb([128, 3072], F32, f"modB{r}")
            P.dma("sp", md[:], ada_d[r:r + 1, 2048:5120].partition_broadcast(128), writes=[f"modB{r}"])
            I("dve", nc.vector.tensor_scalar_add, [f"modB{r}"], [f"modB{r}"], out=md[:, 2048:3072], in0=md[:, 2048:3072], scalar1=1.0)
            mods.append(md)

    blocks = [(qb * 512, 512, NKT) for qb in range(4)]
    if not last:
        blocks.insert(0, (NT * 128, NCT * 128, NCT))
    if nblocks is not None:
        blocks = blocks[:nblocks]

    if full:
        xr_ = Rot(P, 2, [128, 1024], F32, "x")
        tmpr = Rot(P, 2, [128, 1024], F32, "tmp")
        x1r = Rot(P, 2, [128, 1024], F32, "x1")
        h2r = Rot(P, 2, [128, 1024], BF16, "h2")
        h2Tr = Rot(P, 2, [128, 1024], BF16, "h2T")
        junkr = Rot(P, 2, [128, 1024], BF16, "junk")
        str_ = Rot(P, 3, [128, 8], F32, "st")
        exr = Rot(P, 2, [128, 16], F32, "ex")
        afr = Rot(P, 2, [128, 16], F32, "af")
        aTr = Rot(P, 2, [16, 128], F32, "aT")

        def tail_tile(t):
            r = 1 if t >= NT else 0
            md, kmd = mods[r], f"modB{r}"
            x_t, kx = xr_.next()
            P.dma("sp", x_t[:], xin[t * 128:(t + 1) * 128, :], writes=[kx])
            tmp, ktmp = tmpr.next()
            x1, kx1 = x1r.next()
            for nb in range(2):
                cs = slice(nb * 512, (nb + 1) * 512)
                for k in range(8):
                    I("pe", nc.tensor.matmul, [f"mix{t}", wokeys[k // 2]], ["tb"], out=tb[:], lhsT=mixT[:, k, t * 128:(t + 1) * 128],
                      rhs=w_out[:, k, nb * 512:(nb + 1) * 512], start=(k == 0), stop=(k == 7))
                I("dve", nc.vector.tensor_tensor, ["tb", kmd], [ktmp], out=tmp[:, cs], in0=tb[:], in1=md[:, cs], op=ALU.mult)
                I("dve", nc.vector.tensor_tensor, [ktmp, kx], [kx1], out=x1[:, cs], in0=tmp[:, cs], in1=x_t[:, cs], op=ALU.add)
            P.dma("sp", x1_o[t * 128:(t + 1) * 128, :], x1[:], reads=[kx1])
            st, kst = str_.next()
            junk, kj = junkr.next()
            I("act", nc.scalar.activation, [kx1], [kj, kst + "a"], out=junk[:], in_=x1[:], func=AF.Square, accum_out=st[:, 0:1])
            I("act", nc.scalar.activation, [kst + "a"], [kst + "b"], out=st[:, 1:2], in_=st[:, 0:1], func=AF.Sqrt, scale=1.0 / 1024, bias=EPS)
            I("dve", nc.vector.reciprocal, [kst + "b"], [kst + "c"], out=st[:, 2:3], in_=st[:, 1:2])
            tmp2, ktmp2 = tmpr.next()
            h2, kh2 = h2r.next()
            I("dve", nc.vector.scalar_tensor_tensor, [kx1, kst + "c", kmd], [ktmp2], out=tmp2[:], in0=x1[:], scalar=st[:, 2:3], in1=md[:, 2048:3072],
              op0=ALU.mult, op1=ALU.mult)
            I("dve", nc.vector.tensor_tensor, [ktmp2, kmd], [kh2], out=h2[:], in0=tmp2[:], in1=md[:, 1024:2048], op=ALU.add)
            P.dma("sp", h2_o[t * 128:(t + 1) * 128, :], h2[:], reads=[kh2])
            h2T, kh2T = h2Tr.next()
            for hf in range(2):
                for k in range(4):
                    kk = hf * 4 + k
                    I("pe", nc.tensor.transpose, [kh2, "ident"], ["psTb"], out=psTb[:, 512 + k * 128:512 + (k + 1) * 128], in_=h2[:, kk * 128:(kk + 1) * 128],
                      identity=ident[:])
                I("act", nc.scalar.copy, ["psTb"], [kh2T], out=h2T[:, hf * 512:(hf + 1) * 512], in_=psTb[:, 512:1024])
            for k in range(8):
                I("pe", nc.tensor.matmul, [kh2T, "w_r"], ["tb"], out=tb[:, 0:16], lhsT=h2T[:, k * 128:(k + 1) * 128], rhs=w_r[:, k, :],
                  start=(k == 0), stop=(k == 7))
            ex, kex = exr.next()
            af, kaf = afr.next()
            I("dve", nc.vector.reduce_max, ["tb"], [kst + "d"], out=st[:, 3:4], in_=tb[:, 0:16], axis=AX.X)
            I("dve", nc.vector.tensor_scalar, [kst + "d"], [kst + "e"], out=st[:, 4:5], in0=st[:, 3:4], scalar1=-1.0, scalar2=None, op0=ALU.mult)
            I("act", nc.scalar.activation, ["tb", kst + "e"], [kex, kst + "f"], out=ex[:], in_=tb[:, 0:16], func=AF.Exp, bias=st[:, 4:5],
              scale=1.0, accum_out=st[:, 5:6])
            I("dve", nc.vector.reciprocal, [kst + "f"], [kst + "g"], out=st[:, 6:7], in_=st[:, 5:6])
            I("dve", nc.vector.tensor_scalar, [kex, kst + "g"], [kaf], out=af[:], in0=ex[:], scalar1=st[:, 6:7], scalar2=None, op0=ALU.mult)
            P.dma("sp", aff_o[t * 128:(t + 1) * 128, :], af[:], reads=[kaf])
            I("pe", nc.tensor.transpose, [kaf, "identf"], ["tb"], out=tb[0:16, 128:256], in_=af[:], identity=P.identf[:])
            aT, kaT = aTr.next()
            I("act", nc.scalar.copy, ["tb"], [kaT], out=aT[:], in_=tb[0:16, 128:256])
            P.dma("sp", affT_o[:, t * 128:(t + 1) * 128], aT[:], reads=[kaT])


    for h in range(nheads):
        qtn, kqn = qnr.next()
        qtr, kqr = qrr.next()
        P.dma("sp", qtn[:], QT_d[h, 0:128, :], writes=[kqn])
        P.dma("sp", qtr[0:64, :], QT_d[h, 128:192, :], writes=[kqr])
        for c in range(NCH):
            cs = slice(c * CT * 128, (c + 1) * CT * 128)
            P.dma("sp", ktn[:, cs], KT_d[h, 0:128, cs], writes=[f"ktn{c}"])
            P.dma("sp", ktr[0:64, cs], KT_d[h, 128:192, cs], writes=[f"ktr{c}"])
            P.dma("sp", vh[:, c * CT:(c + 1) * CT, 0:128], V_v[:, c * CT:(c + 1) * CT, h, :], writes=[f"vh{c}"])
        if full and h == 0:
            load_mix()
        def attn_block(q0, qn, nkt):
            nsub = qn // 128
            pts = {}

            def S(kt):
                c = kt // CT
                sbk = sbank[kt % 2]
                ks = slice(kt * 128, (kt + 1) * 128)
                I("pe", nc.tensor.matmul, [f"ktn{c}", kqn], [f"sb{kt % 2}"], out=sbk[:, 0:qn], lhsT=ktn[:, ks], rhs=qtn[:, q0:q0 + qn],
                  start=True, stop=False)
                I("pe", nc.tensor.matmul, [f"ktr{c}", kqr], [f"sb{kt % 2}"], out=sbk[:, 0:qn], lhsT=ktr[:, ks], rhs=qtr[:, q0:q0 + qn],
                  start=False, stop=True)

            def E(kt):
                pt, kpt = ptr.next()
                pts[kt] = (pt, kpt)
                I("act", nc.scalar.activation, [f"sb{kt % 2}"], [kpt], out=pt[:, 0:qn], in_=sbank[kt % 2][:, 0:qn], func=AF.Exp, scale=SCALE)

            def PV(kt):
                c = kt // CT
                pt, kpt = pts.pop(kt)
                for qs in range(nsub):
                    ob = obank[qs]
                    off = 0
                    I("pe", nc.tensor.matmul, [kpt, f"vh{c}"], [f"ob{qs}"], out=ob[:, off:off + 129], lhsT=pt[:, qs * 128:(qs + 1) * 128],
                      rhs=vh[:, kt, :], start=(kt == 0), stop=(kt == nkt - 1))

            S(0)
            for kt in range(nkt):
                E(kt)
                if kt + 1 < nkt:
                    S(kt + 1)
                PV(kt)
            for qs in range(nsub):
                ob = obank[qs]
                off = 0
                rc, krc = rcr.next()
                yct, kyct = ycr.next()
                I("dve", nc.vector.reciprocal, [f"ob{qs}"], [krc], out=rc[:], in_=ob[:, off + 128:off + 129])
                I("dve", nc.vector.tensor_scalar, [f"ob{qs}", krc], [kyct], out=yct[:], in0=ob[:, off:off + 128], scalar1=rc[:, 0:1],
                  scalar2=None, op0=ALU.mult)
                r0 = q0 + qs * 128
                if full:
                    I("pe", nc.tensor.transpose, [kyct, "ident"], ["psTb"], out=psTb[:, 0:128], in_=yct[:], identity=ident[:])
                    I("act", nc.scalar.copy, ["psTb"], [f"mix{r0 // 128}"], out=mixT[:, 4 + h, r0:r0 + 128], in_=psTb[:, 0:128])
                else:
                    P.dma("sp", yc_o[r0:r0 + 128, h * 128:(h + 1) * 128], yct[:], reads=[kyct])

        for bi, (q0, qn, nkt) in enumerate(blocks):
            if full and h == nheads - 1 and bi >= 1:
                pq0, pqn, _ = blocks[bi - 1]

                def tails(pq0=pq0, pqn=pqn):
                    for tt_ in range(pq0 // 128, (pq0 + pqn) // 128):
                        tail_tile(tt_)
                P.replay_interleaved([P.capture(attn_block, q0, qn, nkt), P.capture(tails)])
            else:
                attn_block(q0, qn, nkt)
    if full:
        pq0, pqn, _ = blocks[-1]
        for tt_ in range(pq0 // 128, (pq0 + pqn) // 128):
            tail_tile(tt_)
        if last:
            zt, kzt = tmpr.next()
            zh, kzh = h2r.next()
            I("pool", nc.gpsimd.memset, [], [kzt], ap=zt[:], constant=0.0)
            I("pool", nc.gpsimd.memset, [], [kzh], ap=zh[:], constant=0.0)
            for t in range(NT, TT):
                P.dma("sp", x1_o[t * 128:(t + 1) * 128, :], zt[:], reads=[kzt])
                P.dma("sp", h2_o[t * 128:(t + 1) * 128, :], zh[:], reads=[kzh])
                P.dma("sp", aff_o[t * 128:(t + 1) * 128, :], zt[:, 0:16], reads=[kzt])
                P.dma("sp", affT_o[:, t * 128:(t + 1) * 128], zt[0:16, 0:128], reads=[kzt])
    P.emit()
    P.close()
    return nc


NE = 16
CAPM, CAPC = 120, 32
NIT = 26


def build_C(last, nexp=NE, nc=None, T=None, tag="", skip_bisect=False):
    if nc is None:
        nc = bass.Bass("TRN2", target_bir_lowering=False)

    def din(name, shape, dt=F32):
        if T is not None:
            assert tuple(T[name].shape) == tuple(shape), (name, T[name].shape, shape)
            return T[name]
        return nc.dram_tensor(name, list(shape), dt, kind="ExternalInput").ap()

    x1_d = din("x1", [TT * 128, 1024])
    h2_d = din("h2", [TT * 128, 1024], BF16)
    aff_d = din("aff", [TT * 128, 16])
    affT_d = din("affT_all", [16, 8192])
    affTc_d = din("affTc", [16, 256])
    ada_d = din("ada", [2, 6144])
    wg_d = din("w_gate", [NE, 1024, 512])
    wu_d = din("w_up", [NE, 1024, 512])
    wd_d = din("w_down", [NE, 512, 1024])
    utri_d = din("utri", [128, 128])
    ones_d = din("ones", [128, 128])
    iota_d = din("iota", [128, 128])
    blk_d = din("blockones", [128, 128])
    thrc_d = din("thrc", [128, 2])
    x2_o = T["x2"] if T is not None else nc.dram_tensor("x2", [TT * 128, 1024], F32, kind="ExternalOutput").ap()
    thr_scr = T["thr_scr"] if (T is not None and "thr_scr" in T) else nc.dram_tensor(tag + "thr_scr", [128, 2], F32, kind="Internal").ap()

    P = Prog(nc, tag)
    I = P.I
    ident = make_ident(P, nc)
    ntile = NT if last else TT
    groups = [(g, [4 * g + j for j in range(4)], CAPM, g * CAPM) for g in range(4)]
    if not last:
        groups.append((4, [NT, NT + 1], CAPC, 4 * CAPM))
    nslots = 4 * CAPM + (0 if last else CAPC)
    tile_group = {}
    for (g, tiles, cap, s0) in groups:
        for t in tiles:
            tile_group[t] = (g, cap, s0)

    utri = P.sb([128, 128], BF16, "utri")
    P.dma("pool", utri[:], utri_d, writes=["utri"])
    onesb = P.sb([128, 128], BF16, "onesb")
    P.dma("pool", onesb[:], ones_d, writes=["onesb"])
    iota = P.sb([128, 128], F32, "iota")
    P.dma("sp", iota[:], iota_d, writes=["iota"])
    blk = P.sb([128, 128], F32, "blk")
    P.dma("sp", blk[:], blk_d, writes=["blk"])
    thrc = P.sb([128, 2], F32, "thrc")
    P.dma("sp", thrc[:], thrc_d, writes=["thrc"])
    acc = P.sb([128, TT, 1024], F32, "acc")
    Am = acc[:, 0, :]
    if not skip_bisect:
        P.dma("sp", Am, affT_d.rearrange("e (s t) -> (e s) t", s=8), writes=["acc0"])
    Ac = P.sb([128, 32], F32, "Ac")
    P.dma("sp", Ac[:], affTc_d.rearrange("e (s t) -> (e s) t", s=8), writes=["Ac"])
    affs = P.sb([128, TT, 16], F32, "affs")
    P.dma("sp", affs[:], aff_d.rearrange("(t p) e -> p t e", p=128), writes=["affs"])
    h2tok = P.sb([128, TT, 1024], BF16, "h2tok")
    for t0 in range(0, TT, 6):
        P.dma("sp", h2tok[:, t0:t0 + 6, :], h2_d.rearrange("(t p) d -> p t d", p=128)[:, t0:t0 + 6, :], writes=[f"h2tok{t0}"])
    h2keys = [f"h2tok{t0}" for t0 in range(0, TT, 6)]
    g2 = []
    for r in range(2):
        md = P.sb([128, 1024], F32, f"g2_{r}")
        P.dma("sp", md[:], ada_d[r:r + 1, 5120:6144].partition_broadcast(128), writes=[f"g2_{r}"])
        g2.append(md)

    pp = P.ps([128, 512], F32, "pp")
    psTb = P.ps([128, 1024], BF16, "psTb")
    gA = P.ps([128, 512], F32, "gA")
    gB = P.ps([128, 512], F32, "gB")
    Gb = P.ps([128, 512], F32, "Gb")
    Ub = P.ps([128, 512], F32, "Ub")
    yA = P.ps([128, 512], F32, "yA")
    yB = P.ps([128, 512], F32, "yB")

    hidT = P.sb([128, 4, 512], BF16, "hidT")
    junk = hidT[:, 0:2, :].rearrange("p a b -> p (a b)")
    lo = P.sb([128, 2], F32, "lo")
    negmid = P.sb([128, 2], F32, "negmid")
    ssum = P.sb([128, 2], F32, "ssum")
    cond = P.sb([128, 2], F32, "cond")
    I("dve", nc.vector.memset, [], ["lo"], ap=lo[:], constant=0.0)
    I("dve", nc.vector.memset, [], ["negmid"], ap=negmid[:], constant=-0.5)
    I("dve", nc.vector.memset, [], ["ssum0", "ssum1"], ap=ssum[:], constant=0.0)
    for it in range(0 if skip_bisect else NIT):
        w = 2.0 ** -(it + 1)
        I("act", nc.scalar.activation, ["acc0", "negmid"], ["hidT", "ssum0"], out=junk[:, 0:1024], in_=Am, func=AF.Sign, bias=negmid[:, 0:1], scale=1.0,
          accum_out=ssum[:, 0:1])
        I("act", nc.scalar.activation, ["Ac", "negmid"], ["hidT", "ssum1"], out=junk[:, 0:32], in_=Ac[:], func=AF.Sign, bias=negmid[:, 1:2], scale=1.0,
          accum_out=ssum[:, 1:2])
        I("pe", nc.tensor.matmul, ["blk", "ssum0", "ssum1"], ["pp"], out=pp[:, 0:2], lhsT=blk[:], rhs=ssum[:], start=True, stop=True)
        I("dve", nc.vector.tensor_tensor, ["pp", "thrc"], ["cond"], out=cond[:], in0=pp[:, 0:2], in1=thrc[:], op=ALU.is_ge)
        I("dve", nc.vector.scalar_tensor_tensor, ["cond", "lo"], ["lo"], out=lo[:], in0=cond[:], scalar=w, in1=lo[:], op0=ALU.mult, op1=ALU.add)
        I("dve", nc.vector.tensor_scalar, ["lo"], ["negmid"], out=negmid[:], in0=lo[:], scalar1=w * 0.5, scalar2=-1.0, op0=ALU.add, op1=ALU.mult)
    if not skip_bisect:
        P.dma("sp", thr_scr, lo[:], reads=["lo"], writes=["thr_scr"])
    thr_all = P.sb([128, 256], F32, "thr_all")
    P.dma("sp", thr_all[:], thr_scr.rearrange("p c -> (p c)").unsqueeze(0).partition_broadcast(128), reads=["thr_scr"], writes=["thr_all"])
    thr_v = thr_all[:].rearrange("p (e s c) -> p e s c", s=8, c=2)

    maskf = P.sb([128, TT, 16], F32, "maskf")
    maskb = P.sb([128, TT, 16], BF16, "maskb")
    gm = P.sb([128, TT, 16], F32, "gm")
    pos = P.sb([128, TT, 16], F32, "pos")
    for t in range(ntile):
        col = 1 if t >= NT else 0
        I("dve", nc.vector.tensor_tensor, ["affs", "thr_all"], ["maskf"], out=maskf[:, t, :], in0=affs[:, t, :], in1=thr_v[:, :, 0, col], op=ALU.is_ge)
    I("dve", nc.vector.tensor_copy, ["maskf"], ["maskb"], out=maskb[:, 0:ntile, :], in_=maskf[:, 0:ntile, :])
    I("dve", nc.vector.tensor_tensor, ["maskf", "affs"], ["gm"], out=gm[:, 0:ntile, :], in0=maskf[:, 0:ntile, :], in1=affs[:, 0:ntile, :], op=ALU.mult)
    for (g, tiles, cap, s0) in groups:
        for jj, t in enumerate(tiles):
            for i in range(jj + 1):
                I("pe", nc.tensor.matmul, ["maskb", "utri", "onesb"], ["pp"], out=pp[:, 16 * (t % 16):16 * (t % 16) + 16],
                  lhsT=(utri[:] if i == jj else onesb[:]), rhs=maskb[:, tiles[i], :], start=(i == 0), stop=(i == jj))
            I("dve", nc.vector.tensor_copy, ["pp"], ["pos"], out=pos[:, t, :], in_=pp[:, 16 * (t % 16):16 * (t % 16) + 16])

    wgr = Rot(P, 2, [128, 8, 512], BF16, "wg")
    wur = Rot(P, 2, [128, 8, 512], BF16, "wu")
    wdr = Rot(P, 1, [128, 4, 1024], BF16, "wd")
    Sall = P.sb([128, TT, 128], BF16, "Sall")
    Sgall = P.sb([128, TT, 128], BF16, "Sgall")
    SgT = P.sb([128, TT, 128], BF16, "SgT")
    I("pool", nc.gpsimd.memset, [], [f"S{t}" for t in range(TT)], ap=Sall[:], constant=0.0)
    I("pool", nc.gpsimd.memset, [], [f"Sg{t}" for t in range(TT)], ap=Sgall[:], constant=0.0)
    xsT = P.sb([128, 8, 512], BF16, "xsT")
    sgt = P.sb([128, 512], F32, "sgt")
    yr = Rot(P, 1, [128, 5, 1024], BF16, "ysb")
    cnt = 0
    W, Y = {}, {}

    def load_w(e, which):
        if which == "gu":
            wg, kwg = wgr.next()
            wu, kwu = wur.next()
            W[e] = [wg, kwg, wu, kwu, None, None]
            for hk in range(2):
                P.dma("pool", wg[:, hk * 4:(hk + 1) * 4, :], wg_d[e].rearrange("(k p) f -> p k f", p=128)[:, hk * 4:(hk + 1) * 4, :], writes=[f"{kwg}_{hk}"])
                P.dma("pool", wu[:, hk * 4:(hk + 1) * 4, :], wu_d[e].rearrange("(k p) f -> p k f", p=128)[:, hk * 4:(hk + 1) * 4, :], writes=[f"{kwu}_{hk}"])
        else:
            wd, kwd = wdr.next()
            W[e][4], W[e][5] = wd, kwd
            for hk in range(2):
                P.dma("pool", wd[:, hk * 2:(hk + 1) * 2, :], wd_d[e].rearrange("(k p) d -> p k d", p=128)[:, hk * 2:(hk + 1) * 2, :], writes=[f"{kwd}_{hk}"])

    def build_S(e):
        for t in range(ntile):
            g, cap, s0 = tile_group[t]
            I("dve", nc.vector.tensor_scalar, ["iota", "pos", "maskf"], [f"S{t}"], out=Sall[:, t, 0:cap], in0=iota[:, 0:cap], scalar1=pos[:, t, e:e + 1],
              scalar2=maskf[:, t, e:e + 1], op0=ALU.is_equal, op1=ALU.mult)

    def build_Sg(e):
        for t in range(ntile):
            g, cap, s0 = tile_group[t]
            I("dve", nc.vector.tensor_scalar, ["iota", "pos", "gm"], [f"Sg{t}"], out=Sgall[:, t, 0:cap], in0=iota[:, 0:cap], scalar1=pos[:, t, e:e + 1],
              scalar2=gm[:, t, e:e + 1], op0=ALU.is_equal, op1=ALU.mult)

    def gather(e):
        for (g, tiles, cap, s0) in groups:
            for dk in range(8):
                bank, kb = (gA, "gA") if dk < 4 else (gB, "gB")
                o0 = (dk % 4) * cap
                for jj, t in enumerate(tiles):
                    I("pe", nc.tensor.matmul, [h2keys[t // 6], f"S{t}"], [kb], out=bank[:, o0:o0 + cap], lhsT=h2tok[:, t, dk * 128:(dk + 1) * 128],
                      rhs=Sall[:, t, 0:cap], start=(jj == 0), stop=(jj == len(tiles) - 1))
            I("act", nc.scalar.copy, ["gA"], ["xsT"], out=xsT[:, 0:4, s0:s0 + cap], in_=gA[:, 0:4 * cap].rearrange("p (k s) -> p k s", k=4))
            I("act", nc.scalar.copy, ["gB"], ["xsT"], out=xsT[:, 4:8, s0:s0 + cap], in_=gB[:, 0:4 * cap].rearrange("p (k s) -> p k s", k=4))

    def ffn(e):
        wg, kwg, wu, kwu, _, _ = W[e]
        for fk in range(4):
            for k in range(8):
                I("pe", nc.tensor.matmul, ["xsT", f"{kwg}_{k // 4}"], ["Gb"], out=Gb[:, 0:nslots], lhsT=wg[:, k, fk * 128:(fk + 1) * 128], rhs=xsT[:, k, 0:nslots],
                  start=(k == 0), stop=(k == 7))
            for k in range(8):
                I("pe", nc.tensor.matmul, ["xsT", f"{kwu}_{k // 4}"], ["Ub"], out=Ub[:, 0:nslots], lhsT=wu[:, k, fk * 128:(fk + 1) * 128], rhs=xsT[:, k, 0:nslots],
                  start=(k == 0), stop=(k == 7))
            I("act", nc.scalar.activation, ["Gb"], ["sgt"], out=sgt[:, 0:nslots], in_=Gb[:, 0:nslots], func=AF.Silu)
            I("dve", nc.vector.tensor_tensor, ["sgt", "Ub"], ["hidT"], out=hidT[:, fk, 0:nslots], in0=sgt[:, 0:nslots], in1=Ub[:, 0:nslots], op=ALU.mult)

    def down(e):
        wd, kwd = W[e][4], W[e][5]
        ysb, kys = yr.next()
        Y[e] = (ysb, kys)
        for (g, tiles, cap, s0) in groups:
            for nb, (bank, kb) in enumerate([(yA, "yA"), (yB, "yB")]):
                for fk in range(4):
                    I("pe", nc.tensor.matmul, ["hidT", f"{kwd}_{fk // 2}"], [kb], out=bank[0:cap, :], lhsT=hidT[:, fk, s0:s0 + cap],
                      rhs=wd[:, fk, nb * 512:(nb + 1) * 512], start=(fk == 0), stop=(fk == 3))
            I("act", nc.scalar.copy, ["yA"], [f"{kys}_{g}"], out=ysb[0:cap, g, 0:512], in_=yA[0:cap, :])
            I("act", nc.scalar.copy, ["yB"], [f"{kys}_{g}"], out=ysb[0:cap, g, 512:1024], in_=yB[0:cap, :])

    def sgtr(e):
        for t0 in range(0, ntile, 8):
            tl = list(range(t0, min(t0 + 8, ntile)))
            for t in tl:
                I("pe", nc.tensor.transpose, [f"Sg{t}", "ident"], ["psTb"], out=psTb[:, (t - t0) * 128:(t - t0 + 1) * 128], in_=Sgall[:, t, :], identity=ident[:])
            I("act", nc.scalar.copy, ["psTb"], [f"SgT{t}" for t in tl], out=SgT[:, t0:t0 + len(tl), :],
              in_=psTb[:, 0:len(tl) * 128].rearrange("p (t s) -> p t s", s=128))

    sbanks = [(gA, "gA"), (gB, "gB"), (yA, "yA"), (yB, "yB")]

    def scatter(e):
        ysb, kys = Y[e]
        i = 0
        for t in range(ntile):
            g, cap, s0 = tile_group[t]
            for nb in range(2):
                bank, kb = sbanks[i % 4]
                i += 1
                I("pe", nc.tensor.matmul, [f"SgT{t}", f"{kys}_{g}"], [kb], out=bank[:], lhsT=SgT[0:cap, t, :], rhs=ysb[0:cap, g, nb * 512:(nb + 1) * 512],
                  start=True, stop=True)
                cs = slice(nb * 512, (nb + 1) * 512)
                if e == 0:
                    I("dve", nc.vector.tensor_copy, [kb], [f"acc{t}"], out=acc[:, t, cs], in_=bank[:])
                else:
                    I("dve", nc.vector.tensor_tensor, [kb, f"acc{t}"], [f"acc{t}"], out=acc[:, t, cs], in0=acc[:, t, cs], in1=bank[:], op=ALU.add)

    load_w(0, "gu")
    load_w(0, "d")
    build_S(0)
    build_Sg(0)
    for e in range(nexp):
        if e + 1 < nexp:
            load_w(e + 1, "gu")
        gather(e)
        if e + 1 < nexp:
            build_S(e + 1)
        ffn(e)
        down(e)
        if e + 1 < nexp:
            load_w(e + 1, "d")
        sgtr(e)
        if e + 1 < nexp:
            build_Sg(e + 1)
        scatter(e)
    x1r = Rot(P, 1, [128, 1024], F32, "x1t")
    for t in range(ntile):
        x1t, kx1 = x1r.next()
        P.dma("sp", x1t[:], x1_d[t * 128:(t + 1) * 128, :], writes=[kx1])
        r = 1 if t >= NT else 0
        I("dve", nc.vector.tensor_tensor, [f"acc{t}", f"g2_{r}"], [f"acc{t}"], out=acc[:, t, :], in0=acc[:, t, :], in1=g2[r][:], op=ALU.mult)
        I("dve", nc.vector.tensor_tensor, [f"acc{t}", kx1], [f"acc{t}"], out=acc[:, t, :], in0=acc[:, t, :], in1=x1t[:], op=ALU.add)
        P.dma("sp", x2_o[t * 128:(t + 1) * 128, :], acc[:, t, :], reads=[f"acc{t}"])
    if last:
        for t in range(NT, TT):
            I("pool", nc.gpsimd.memset, [], [f"acc{t}"], ap=acc[:, t, :], constant=0.0)
            P.dma("sp", x2_o[t * 128:(t + 1) * 128, :], acc[:, t, :], reads=[f"acc{t}"])
    P.emit()
    P.close()
    return nc


NV = 4


def build_fused(nlayers=2, nv=NV):
    nc = bass.Bass("TRN2", target_bir_lowering=False)

    def ext(name, shape, dt=F32):
        return nc.dram_tensor(name, list(shape), dt, kind="ExternalInput").ap()

    def scr(name, shape, dt=F32):
        return nc.dram_tensor(name, list(shape), dt, kind="Internal").ap()

    E = {
        "x": ext("x", [8192, 1024]), "ctx": ext("ctx", [256, 1024]), "cT": ext("cT", [128, 8, 2]),
        "w_ada": ext("w_ada", [2, 1024, 6144]), "b_ada": ext("b_ada", [2, 6144]), "w_in": ext("w_in", [2, 1024, 1216]),
        "w_sguT": ext("w_sguT", [2, 128, 4, 128]), "b_sguT": ext("b_sguT", [2, 128, 4]),
        "sgu_norm": ext("sgu_norm", [2, 256]), "q_lora_norm": ext("q_lora_norm", [2, 256]), "kv_lora_norm": ext("kv_lora_norm", [2, 128]),
        "q_norm": ext("q_norm", [2, 192]), "k_norm": ext("k_norm", [2, 192]),
        "w_uq": ext("w_uq", [2, 256, 768]), "w_ukv": ext("w_ukv", [2, 128, 1024]), "dftc": ext("dftc", [256, 512]),
        "rope_cos": ext("rope_cos", [4, TT * 128, 64]), "rope_sin": ext("rope_sin", [4, TT * 128, 64]),
        "WA": ext("WA", [128, 128]), "TC": ext("TC", [4, 128, 2, 64, 32]), "TCc": ext("TCc", [128, 2, 2, 256]),
        "w_out": ext("w_out", [2, 1024, 1024]), "w_router": ext("w_router", [2, 1024, 16]),
        "w_gate": ext("w_gate", [2, 16, 1024, 512]), "w_up": ext("w_up", [2, 16, 1024, 512]), "w_down": ext("w_down", [2, 16, 512, 1024]),
        "utri": ext("utri", [128, 128]), "ones": ext("ones", [128, 128]), "iota": ext("iota", [128, 128]),
        "blockones": ext("blockones", [128, 128]), "thrc": ext("thrc", [128, 2]),
    }
    out = nc.dram_tensor("out", [8192, 1024], F32, kind="ExternalOutput").ap()
    xmid = scr("xmid", [8192, 1024])
    xcmid = scr("xcmid", [256, 1024])
    Z_all = scr("Z_all", [8192, 512], BF16)
    KT_all = scr("KT_all", [4, 192, 66 * 128], BF16)
    V_all = scr("V_all", [66 * 128, 512], BF16)
    affT_all = scr("affT_all", [16, 8192])
    affTc = scr("affTc", [16, 256])
    thr_sh = scr("thr_sh", [128, 2])
    V = []
    for v in range(nv):
        V.append({
            "xin": scr(f"xin{v}", [TT * 128, 1024]), "ada": scr(f"ada{v}", [2, 6144]), "yaT": scr(f"yaT{v}", [256, TT * 128], BF16),
            "Z": scr(f"Z{v}", [TT * 128, 512], BF16), "QT": scr(f"QT{v}", [4, 192, TT * 128], BF16), "KT": scr(f"KT{v}", [4, 192, TT * 128], BF16),
            "V": scr(f"V{v}", [TT * 128, 512], BF16), "ybT": scr(f"ybT{v}", [256, TT * 128], BF16), "x1": scr(f"x1{v}", [TT * 128, 1024]),
            "h2": scr(f"h2{v}", [TT * 128, 1024], BF16), "aff": scr(f"aff{v}", [TT * 128, 16]), "affT": scr(f"affT{v}", [16, TT * 128]),
            "x2": scr(f"x2{v}", [TT * 128, 1024]),
        })
    M = NT * 128
    phase_no = [0]

    def phase(fn):
        phase_no[0] += 1
        with nc.cleanup_on_exit():
            fn(f"p{phase_no[0]}_")
            nc.all_engine_barrier()

    def glue(copies):
        def body(tag):
            P = Prog(nc, tag)
            for i, (dst, src) in enumerate(copies):
                P.dma("sp", dst, src)
            P.emit()
            P.close()
        phase(body)

    for l in range(nlayers):
        last = l == nlayers - 1
        xsrc, csrc = (E["x"], E["ctx"]) if l == 0 else (xmid, xcmid)
        xdst = out if last else xmid
        cp = []
        for v in range(nv):
            cp.append((V[v]["xin"][0:M, :], xsrc[v * M:(v + 1) * M, :]))
            cp.append((V[v]["xin"][M:TT * 128, :], csrc))
        glue(cp)
        for v in range(nv):
            TA = {"xin": V[v]["xin"], "cT": E["cT"], "w_ada": E["w_ada"][l], "b_ada": E["b_ada"][l:l + 1, :], "w_in": E["w_in"][l],
                  "w_sguT": E["w_sguT"][l], "b_sguT": E["b_sguT"][l], "sgu_norm": E["sgu_norm"][l:l + 1, :],
                  "q_lora_norm": E["q_lora_norm"][l:l + 1, :], "kv_lora_norm": E["kv_lora_norm"][l:l + 1, :],
                  "q_norm": E["q_norm"][l:l + 1, :], "k_norm": E["k_norm"][l:l + 1, :], "w_uq": E["w_uq"][l], "w_ukv": E["w_ukv"][l],
                  "dftc": E["dftc"], "rope_cos": E["rope_cos"][v], "rope_sin": E["rope_sin"][v],
                  "ada": V[0]["ada"], "yaT": V[v]["yaT"], "Z": V[v]["Z"], "QT": V[v]["QT"], "KT": V[v]["KT"], "V": V[v]["V"]}
            lv = last or v > 0
            phase(lambda tag, TA=TA, lv=lv, v=v: build_A(lv, ntiles=(TT if v == 0 else NT), nc=nc, T=TA, tag=tag,
                                                                 ada_src=(None if v == 0 else V[0]["ada"])))
        cp = [(KT_all[:, :, 0:NCT * 128], V[0]["KT"][:, :, M:TT * 128]), (V_all[0:NCT * 128, :], V[0]["V"][M:TT * 128, :])]
        for v in range(nv):
            cp.append((Z_all[v * M:(v + 1) * M, :], V[v]["Z"][0:M, :]))
            cp.append((KT_all[:, :, NCT * 128 + v * M:NCT * 128 + (v + 1) * M], V[v]["KT"][:, :, 0:M]))
            cp.append((V_all[NCT * 128 + v * M:NCT * 128 + (v + 1) * M, :], V[v]["V"][0:M, :]))
        glue(cp)
        for v in range(nv):
            TF = {"Z_all": Z_all, "Zc": V[v]["Z"][M:TT * 128, :], "WA": E["WA"], "TC": E["TC"][v], "TCc": E["TCc"], "ybT": V[v]["ybT"]}
            phase(lambda tag, TF=TF, lv=(last or v > 0): build_F(lv, nc=nc, T=TF, tag=tag))
        for v in range(nv):
            TB = {"QT": V[v]["QT"], "KT_all": KT_all, "V_all": V_all, "xin": V[v]["xin"], "ada": V[0]["ada"], "yaT": V[v]["yaT"], "ybT": V[v]["ybT"],
                  "w_out": E["w_out"][l], "w_router": E["w_router"][l], "x1": V[v]["x1"], "h2": V[v]["h2"], "aff": V[v]["aff"], "affT": V[v]["affT"]}
            phase(lambda tag, TB=TB, lv=(last or v > 0): build_B(lv, nc=nc, T=TB, tag=tag))
        cp = [(affTc, V[0]["affT"][:, M:TT * 128])]
        for v in range(nv):
            cp.append((affT_all[:, v * M:(v + 1) * M], V[v]["affT"][:, 0:M]))
        glue(cp)
        for v in range(nv):
            TCd = {"x1": V[v]["x1"], "h2": V[v]["h2"], "aff": V[v]["aff"], "affT_all": affT_all, "affTc": affTc, "ada": V[0]["ada"], "thr_scr": thr_sh,
                   "w_gate": E["w_gate"][l], "w_up": E["w_up"][l], "w_down": E["w_down"][l], "utri": E["utri"], "ones": E["ones"], "iota": E["iota"],
                   "blockones": E["blockones"], "thrc": E["thrc"], "x2": V[v]["x2"]}
            phase(lambda tag, TCd=TCd, lv=(last or v > 0): build_C(lv, nc=nc, T=TCd, tag=tag, skip_bisect=(v > 0)))
        cp = [(xdst[v * M:(v + 1) * M, :], V[v]["x2"][0:M, :]) for v in range(nv)]
        if not last:
            cp.append((xcmid, V[0]["x2"][M:TT * 128, :]))
        glue(cp)
    return nc


BF = ml_dtypes.bfloat16
NCORES = 8
TPC = 2048
NCTX = 256
SEQ = 8192


def f32(a):
    return np.ascontiguousarray(a, dtype=np.float32)


def const_dftc():
    c = np.arange(64)
    ang = 2 * np.pi * np.outer(c, c) / 64.0
    C, S = np.cos(ang), np.sin(ang)
    d = np.zeros((256, 512), np.float64)
    for g in range(4):
        d[g * 64:(g + 1) * 64, g * 64:(g + 1) * 64] = C
        d[g * 64:(g + 1) * 64, 256 + g * 64:256 + (g + 1) * 64] = -S
    return f32(d)


def const_rope(core):
    tok0 = (core % 4) * TPC
    n = np.arange(tok0, tok0 + TPC)
    pos_row = (n // 64).astype(np.float32)
    pos_col = (n % 64).astype(np.float32)
    freqs = (np.float32(10000.0) ** (-np.arange(16, dtype=np.float32) / np.float32(16))).astype(np.float32)
    cos = np.ones((TPC + NCTX, 64), np.float32)
    sin = np.zeros((TPC + NCTX, 64), np.float32)
    for b, pos in enumerate([pos_row, pos_col]):
        ang = (pos[:, None] * freqs[None, :]).astype(np.float32)
        cs, sn = np.cos(ang).astype(np.float32), np.sin(ang).astype(np.float32)
        cos[:TPC, b * 32:b * 32 + 16] = cs
        cos[:TPC, b * 32 + 16:b * 32 + 32] = cs
        sin[:TPC, b * 32:b * 32 + 16] = -sn
        sin[:TPC, b * 32 + 16:b * 32 + 32] = sn
    return cos, sin


def inputs_A(inp, l, x_cur, xc_cur):
    dftc = const_dftc()
    shared = {
        "w_ada": f32(inp["w_ada"][l]), "b_ada": f32(inp["b_ada"][l][None, :]), "w_in": f32(inp["w_in"][l]),
        "w_sguT": f32(np.transpose(inp["w_sgu"][l], (2, 0, 1))),
        "b_sguT": f32(inp["b_sgu"][l].T),
        "sgu_norm": f32(inp["sgu_norm"][l][None]), "q_lora_norm": f32(inp["q_lora_norm"][l][None]),
        "kv_lora_norm": f32(inp["kv_lora_norm"][l][None]), "q_norm": f32(inp["q_norm"][l][None]), "k_norm": f32(inp["k_norm"][l][None]),
        "w_uq": f32(inp["w_uq"][l]), "w_ukv": f32(inp["w_ukv"][l]), "dftc": dftc,
    }
    maps = []
    for core in range(NCORES):
        b, tok0 = core // 4, (core % 4) * TPC
        cos, sin = const_rope(core)
        cT = np.stack([np.asarray(inp["c"][b]).reshape(8, 128).T, np.asarray(inp["c_ctx"]).reshape(8, 128).T], axis=-1)
        m = dict(shared)
        m.update({"xin": f32(np.concatenate([x_cur[b, tok0:tok0 + TPC], xc_cur[b]], 0)), "cT": f32(cT), "rope_cos": cos, "rope_sin": sin})
        maps.append(m)
    return maps


def const_fft(core):
    n1 = np.arange(64)
    ang = 2 * np.pi * np.outer(n1, n1) / 64.0
    C, S = np.cos(ang), np.sin(ang)
    WA = np.zeros((128, 128))
    WA[0:64, 0:64] = C; WA[64:128, 0:64] = S; WA[0:64, 64:128] = -S; WA[64:128, 64:128] = C
    k2_0 = 32 * (core % 4)
    n2 = np.arange(128)[:, None, None]
    k1 = np.arange(64)[None, :, None]
    k2 = (k2_0 + np.arange(32))[None, None, :]
    k = k1 + 64 * k2
    th = 2 * np.pi * ((n2 * k) % 8192) / 8192.0
    nrm = 1.0 / np.sqrt(8192.0 * 64.0)
    TC = np.stack([np.cos(th) * nrm, np.sin(th) * nrm], axis=1)
    n = (np.arange(2)[None, :, None] * 128 + np.arange(128)[:, None, None])
    kk = np.arange(256)[None, None, :]
    thc = 2 * np.pi * ((n * kk) % 256) / 256.0
    nrc = 1.0 / np.sqrt(256.0 * 64.0)
    TCc = np.stack([np.cos(thc) * nrc, np.sin(thc) * nrc], axis=2)
    return f32(WA), f32(TC), f32(TCc)


def const_moe():
    k = np.arange(128)
    utri = (k[:, None] < k[None, :]).astype(np.float32)
    ones = np.ones((128, 128), np.float32)
    iota = np.tile(np.arange(128, dtype=np.float32)[None, :], (128, 1))
    blk = ((k[:, None] // 8) == (k[None, :] // 8)).astype(np.float32)
    thrc = np.tile(np.array([[2 * 1024 - 8192, 2 * 32 - 256]], np.float32), (128, 1))
    return {"utri": utri, "ones": ones, "iota": iota, "blockones": blk, "thrc": thrc}


def inputs_fused(inp):
    ropes = [const_rope(v) for v in range(4)]
    ffts = [const_fft(v) for v in range(4)]
    shared = {
        "w_ada": f32(inp["w_ada"]), "b_ada": f32(inp["b_ada"]), "w_in": f32(inp["w_in"]),
        "w_sguT": f32(np.transpose(inp["w_sgu"], (0, 3, 1, 2))), "b_sguT": f32(np.transpose(inp["b_sgu"], (0, 2, 1))),
        "sgu_norm": f32(inp["sgu_norm"]), "q_lora_norm": f32(inp["q_lora_norm"]), "kv_lora_norm": f32(inp["kv_lora_norm"]),
        "q_norm": f32(inp["q_norm"]), "k_norm": f32(inp["k_norm"]), "w_uq": f32(inp["w_uq"]), "w_ukv": f32(inp["w_ukv"]),
        "dftc": const_dftc(), "rope_cos": f32(np.stack([r[0] for r in ropes])), "rope_sin": f32(np.stack([r[1] for r in ropes])),
        "WA": ffts[0][0], "TC": f32(np.stack([f[1] for f in ffts])), "TCc": ffts[0][2],
        "w_out": f32(inp["w_out"]), "w_router": f32(inp["w_router"]), "w_gate": f32(inp["w_gate"]), "w_up": f32(inp["w_up"]),
        "w_down": f32(inp["w_down"]),
    }
    shared.update(const_moe())
    maps = []
    for b in range(2):
        cT = np.stack([np.asarray(inp["c"][b]).reshape(8, 128).T, np.asarray(inp["c_ctx"]).reshape(8, 128).T], axis=-1)
        m = dict(shared)
        m.update({"x": f32(inp["x"][b]), "ctx": f32(inp["ctx"][b]), "cT": f32(cT)})
        maps.append(m)
    return maps


def kernel(**inputs):
    inp = {k: np.asarray(v) for k, v in inputs.items()}
    nc = build_fused()
    maps = inputs_fused(inp)
    res = run_bass_kernel_spmd(nc, maps, core_ids=[0, 1]).results
    return np.stack([np.asarray(res[b]["out"], dtype=np.float32) for b in range(2)])
```

```python
import numpy as np
import ml_dtypes
from concourse.bass_utils import run_bass_kernel_spmd

import contextlib
import numpy as np
import concourse.bass as bass
import concourse.mybir as mybir

F32 = mybir.dt.float32
BF16 = mybir.dt.bfloat16
I32 = mybir.dt.int32
ALU = mybir.AluOpType
AF = mybir.ActivationFunctionType
AX = mybir.AxisListType

ENGS = ("pe", "act", "dve", "pool", "sp")
NDMASEM = 8


class Op:
    __slots__ = ("eng", "fn", "dma", "deps", "needs_sig", "sig_idx", "sem_i", "sem_val", "idx", "prev_same_sem")

    def __init__(self, eng, fn, dma):
        self.eng = eng
        self.fn = fn
        self.dma = dma
        self.deps = []
        self.needs_sig = False
        self.sig_idx = 0
        self.sem_i = -1
        self.sem_val = 0
        self.prev_same_sem = None


class BufState:
    __slots__ = ("last_w", "readers")

    def __init__(self):
        self.last_w = None
        self.readers = []


class Prog:
    def __init__(self, nc, tag=""):
        self.nc = nc
        self.tag = tag
        self.ops = {e: [] for e in ENGS}
        self.st = {}
        self.es = contextlib.ExitStack()
        self.ndma = {e: 0 for e in ENGS}
        self.dma_last = {}
        self.dma_tot = {}
        self.all_dma = []
        self.nsb = 0
        self.fence = None
        self.psum_keys = set()

    def sb(self, shape, dtype, name=None):
        self.nsb += 1
        name = name or f"sb{self.nsb}"
        return self.es.enter_context(self.nc.sbuf_tensor("s_" + self.tag + name, list(shape), dtype))

    def ps(self, shape, dtype, name=None):
        self.nsb += 1
        name = name or f"ps{self.nsb}"
        self.psum_keys.add(name)
        return self.es.enter_context(self.nc.psum_tensor("p_" + self.tag + name, list(shape), dtype))

    def _state(self, k):
        s = self.st.get(k)
        if s is None:
            s = self.st[k] = BufState()
        return s

    def capture(self, fn, *args):
        self.cap = []
        fn(*args)
        c, self.cap = self.cap, None
        return c

    def replay_interleaved(self, lists):
        idx = [0] * len(lists)
        while True:
            best, bf = -1, 2.0
            for i, l in enumerate(lists):
                if idx[i] < len(l):
                    f = idx[i] / len(l)
                    if f < bf:
                        best, bf = i, f
            if best < 0:
                break
            self.op(*lists[best][idx[best]])
            idx[best] += 1

    def op(self, eng, fn, reads=(), writes=(), dma=False):
        if getattr(self, "cap", None) is not None:
            self.cap.append((eng, fn, tuple(reads), tuple(writes), dma))
            return None
        o = Op(eng, fn, dma)
        deps = []
        pr = [k for k in reads if k in self.psum_keys]
        if pr:
            reads = [k for k in reads if k not in self.psum_keys]
            writes = list(writes) + [k for k in pr if k not in writes]
        for k in reads:
            s = self._state(k)
            if s.last_w is not None:
                deps.append((s.last_w, "raw"))
        for k in writes:
            s = self._state(k)
            if s.last_w is not None:
                deps.append((s.last_w, "waw"))
            for r in s.readers:
                deps.append((r, "war"))
        if self.fence is not None:
            deps.append((self.fence, "raw"))
        seen = set()
        for d, kind in deps:
            if d is o or id(d) in seen:
                continue
            if (not o.dma) and (not d.dma) and d.eng == o.eng:
                if o.eng == "pe":
                    continue
            seen.add(id(d))
            o.deps.append(d)
        for k in reads:
            self._state(k).readers.append(o)
        for k in writes:
            s = self._state(k)
            s.last_w = o
            s.readers = []
        if dma:
            i = self.ndma[eng] % NDMASEM
            self.ndma[eng] += 1
            o.sem_i = i
            key = (eng, i)
            o.prev_same_sem = self.dma_last.get(key)
            o.sem_val = self.dma_tot.get(key, 0) + 16
            self.dma_tot[key] = o.sem_val
            self.dma_last[key] = o
            self.all_dma.append(o)
        self.ops[eng].append(o)
        return o

    def barrier(self, bar_tile):
        nc = self.nc
        lasts = []
        for e in ENGS:
            comp = [o for o in self.ops[e] if not o.dma]
            if comp:
                lasts.append(comp[-1])
        lasts.extend(self.dma_last.values())
        old = self.fence
        self.fence = None
        b = self.op("dve", lambda: nc.vector.memset(bar_tile, 0.0), (), ())
        for d in lasts:
            if d is not b and d not in b.deps and not (d.eng == "pe" and False):
                b.deps.append(d)
        if old is not None and old not in b.deps:
            b.deps.append(old)
        self.fence = b
        return b

    def I(self, eng, method, reads=(), writes=(), **kw):
        return self.op(eng, lambda: method(**kw), reads, writes)

    def dma(self, eng, out, in_, reads=(), writes=(), **kw):
        e = {"sp": self.nc.sync, "pool": self.nc.gpsimd, "act": self.nc.scalar}[eng]
        return self.op(eng, lambda: e.dma_start(out=out, in_=in_, **kw), reads, writes, dma=True)

    def emit(self):
        nc = self.nc
        for e in ENGS:
            for o in self.ops[e]:
                for d in o.deps:
                    if not d.dma:
                        d.needs_sig = True
        for e in ENGS:
            c = 0
            for o in self.ops[e]:
                if (not o.dma) and o.needs_sig:
                    c += 1
                    o.sig_idx = c
        es = self.es
        csem = {e: nc.alloc_semaphore(name=f"c_{e}_{self.tag}") for e in ENGS}
        dsem = {}
        for e in ENGS:
            if self.ndma[e]:
                for i in range(min(NDMASEM, self.ndma[e])):
                    dsem[(e, i)] = nc.alloc_semaphore(name=f"d_{e}{i}_{self.tag}")
        block = es.enter_context(nc.Block())
        prog = self

        def stream(ename, eng):
            waited = {}

            def wait(key, sem, val):
                if waited.get(key, 0) < val:
                    eng.wait_ge(sem, val)
                    waited[key] = val

            for o in prog.ops[ename]:
                for d in o.deps:
                    if d.dma:
                        wait(("d", d.eng, d.sem_i), dsem[(d.eng, d.sem_i)], d.sem_val)
                    else:
                        wait(("c", d.eng), csem[d.eng], d.sig_idx)
                if o.dma:
                    p = o.prev_same_sem
                    if p is not None:
                        wait(("d", ename, o.sem_i), dsem[(ename, o.sem_i)], p.sem_val)
                    inst = o.fn()
                    inst.then_inc(dsem[(ename, o.sem_i)], 16)
                else:
                    inst = o.fn()
                    if o.needs_sig:
                        inst.then_inc(csem[ename], 1)
            if ename == "sp":
                for key, tot in prog.dma_tot.items():
                    wait(("d",) + key, dsem[key], tot)
                for e2 in ENGS:
                    if e2 != "sp":
                        n = max([o.sig_idx for o in prog.ops[e2] if not o.dma] + [0])
                        if n:
                            wait(("c", e2), csem[e2], n)

        @block.tensor
        def _(eng):
            stream("pe", eng)

        @block.scalar
        def _(eng):
            stream("act", eng)

        @block.vector
        def _(eng):
            stream("dve", eng)

        @block.gpsimd
        def _(eng):
            stream("pool", eng)

        @block.sync
        def _(eng):
            stream("sp", eng)

    def close(self):
        self.es.close()


DBGZ = DBGQ = DBGT = 9
SEQREPLAY = 0

NT, NCT = 16, 2
TT = NT + NCT
EPS = 1e-6


class Rot:
    def __init__(self, P, n, shape, dtype, name):
        self.bufs = [(P.sb(shape, dtype, f"{name}{i}"), f"{name}{i}") for i in range(n)]
        self.i = 0

    def next(self):
        b = self.bufs[self.i % len(self.bufs)]
        self.i += 1
        return b


def make_ident(P, nc):
    identf = P.sb([128, 128], F32, "identf")
    ident = P.sb([128, 128], BF16, "ident")
    P.I("pool", nc.gpsimd.memset, [], ["identf"], ap=identf[:], constant=0.0)
    P.I("pool", nc.gpsimd.affine_select, ["identf"], ["identf"], out=identf[:], in_=identf[:], pattern=[[-1, 128]],
        compare_op=ALU.not_equal, fill=1.0, base=0, channel_multiplier=1)
    P.I("dve", nc.vector.tensor_copy, ["identf"], ["ident"], out=ident[:], in_=identf[:])
    P.identf = identf
    return ident


def build_A(last, ntiles=TT, dbg=99, nc=None, T=None, tag="", ada_src=None):
    if nc is None:
        nc = bass.Bass("TRN2", target_bir_lowering=False)

    def din(name, shape, dt=F32):
        if T is not None:
            assert tuple(T[name].shape) == tuple(shape), (name, T[name].shape, shape)
            return T[name]
        return nc.dram_tensor(name, list(shape), dt, kind="ExternalInput").ap()

    def dout(name, shape, dt=F32):
        if T is not None:
            assert tuple(T[name].shape) == tuple(shape), (name, T[name].shape, shape)
            return T[name]
        return nc.dram_tensor(name, list(shape), dt, kind="ExternalOutput").ap()

    xin = din("xin", [TT * 128, 1024])
    cT_d = din("cT", [128, 8, 2])
    w_ada_d = din("w_ada", [1024, 6144])
    b_ada_d = din("b_ada", [1, 6144])
    w_in_d = din("w_in", [1024, 1216])
    w_sguT_d = din("w_sguT", [128, 4, 128])
    b_sguT_d = din("b_sguT", [128, 4])
    sgun_d = din("sgu_norm", [1, 256])
    qln_d = din("q_lora_norm", [1, 256])
    kvln_d = din("kv_lora_norm", [1, 128])
    qn_d = din("q_norm", [1, 192])
    kn_d = din("k_norm", [1, 192])
    w_uq_d = din("w_uq", [256, 768])
    w_ukv_d = din("w_ukv", [128, 1024])
    dftc_d = din("dftc", [256, 512])
    rcos_d = din("rope_cos", [TT * 128, 64])
    rsin_d = din("rope_sin", [TT * 128, 64])

    ada_o = dout("ada", [2, 6144])
    yaT_o = dout("yaT", [256, TT * 128], BF16)
    Z_o = dout("Z", [TT * 128, 512], BF16)
    QT_o = dout("QT", [4, 192, TT * 128], BF16)
    KT_o = dout("KT", [4, 192, TT * 128], BF16)
    V_o = dout("V", [TT * 128, 512], BF16)

    P = Prog(nc, tag)
    I = P.I
    ident = make_ident(P, nc)

    def bc_load(name, src, n):
        t = P.sb([128, n], F32, name)
        P.dma("sp", t[:], src.partition_broadcast(128), writes=[name])
        return t

    sgun = bc_load("sgun", sgun_d[0:1, :], 256)
    qln = bc_load("qln", qln_d[0:1, :], 256)
    kvln = bc_load("kvln", kvln_d[0:1, :], 128)
    qnb = bc_load("qnb", qn_d[0:1, :], 192)
    knb = bc_load("knb", kn_d[0:1, :], 192)
    b_sguT = P.sb([128, 4], F32, "b_sguT")
    P.dma("sp", b_sguT[:], b_sguT_d, writes=["b_sguT"])
    rcos = P.sb([128, TT, 64], F32, "rcos")
    rsin = P.sb([128, TT, 64], F32, "rsin")
    P.dma("sp", rcos[:], rcos_d.rearrange("(t p) d -> p t d", p=128), writes=["rcos"])
    P.dma("sp", rsin[:], rsin_d.rearrange("(t p) d -> p t d", p=128), writes=["rsin"])
    cT = P.sb([128, 8, 2], F32, "cT")
    P.dma("sp", cT[:], cT_d, writes=["cT"])
    ps_ada = P.ps([128, 512], F32, "psA")
    if ada_src is None:
        scT = P.sb([128, 8, 2], BF16, "scT")
        I("act", nc.scalar.activation, ["cT"], ["scT"], out=scT[:], in_=cT[:], func=AF.Silu)
        mbr = Rot(P, 2, [2, 512], F32, "mblk")
        bar = Rot(P, 2, [2, 512], F32, "bablk")
        warot = Rot(P, 2, [128, 8, 512], BF16, "wa")
        w_ada_v = w_ada_d.rearrange("(k p) n -> p k n", p=128)
        for nb in range(12):
            wa, kwa = warot.next()
            for hk in range(2):
                P.dma("pool", wa[:, hk * 4:(hk + 1) * 4, :], w_ada_v[:, hk * 4:(hk + 1) * 4, nb * 512:(nb + 1) * 512], writes=[kwa + f"_{hk}"])
            for k in range(8):
                I("pe", nc.tensor.matmul, ["scT", kwa + f"_{k // 4}"], ["psA"], out=ps_ada[0:2, :], lhsT=scT[:, k, :], rhs=wa[:, k, :],
                  start=(k == 0), stop=(k == 7))
            mb, kmb = mbr.next()
            ba, kba = bar.next()
            P.dma("sp", ba[:], b_ada_d[0:1, nb * 512:(nb + 1) * 512].partition_broadcast(2), writes=[kba])
            I("dve", nc.vector.tensor_tensor, ["psA", kba], [kmb], out=mb[:], in0=ps_ada[0:2, :], in1=ba[:], op=ALU.add)
            P.dma("sp", ada_o[:, nb * 512:(nb + 1) * 512], mb[:], reads=[kmb], writes=["ada_d"])

    else:
        ada_o = ada_src
    mods = []
    for r in range(2):
        md = P.sb([128, 2048], F32, f"mod{r}")
        P.dma("sp", md[:], ada_o[r:r + 1, 0:2048].partition_broadcast(128), reads=["ada_d"], writes=[f"mod{r}"])
        I("dve", nc.vector.tensor_scalar_add, [f"mod{r}"], [f"mod{r}"], out=md[:, 1024:2048], in0=md[:, 1024:2048], scalar1=1.0)
        mods.append(md)

    w_in = P.sb([128, 8, 1216], BF16, "w_in")
    w_in_v = w_in_d.rearrange("(k p) n -> p k n", p=128)
    for k in range(0, 8, 2):
        P.dma("pool", w_in[:, k:k + 2, :], w_in_v[:, k:k + 2, :], writes=[f"w_in{k}"])
    w_in_keys = [f"w_in{k}" for k in range(0, 8, 2)]
    w_uq = P.sb([128, 2, 768], BF16, "w_uq")
    P.dma("pool", w_uq[:], w_uq_d.rearrange("(k p) n -> p k n", p=128), writes=["w_uq"])
    w_ukv = P.sb([128, 1024], BF16, "w_ukv")
    P.dma("pool", w_ukv[:], w_ukv_d, writes=["w_ukv"])
    dftc = P.sb([128, 2, 512], BF16, "dftc")
    P.dma("pool", dftc[:], dftc_d.rearrange("(k p) n -> p k n", p=128), writes=["dftc"])
    w_sguT = P.sb([128, 4, 128], BF16, "w_sguT")
    P.dma("pool", w_sguT[:], w_sguT_d, writes=["w_sguT"])

    psT = P.ps([128, 1024], BF16, "psT")
    psT2 = P.ps([128, 1024], BF16, "psT2")
    psT3 = P.ps([128, 1024], BF16, "psT3")
    px0 = P.ps([128, 512], F32, "px0")
    px1 = P.ps([128, 512], F32, "px1")
    px2 = P.ps([128, 512], F32, "px2")
    psB = P.ps([128, 512], F32, "psB")

    xr_ = Rot(P, 2, [128, 1024], F32, "x")
    tmpr = Rot(P, 2, [128, 1024], F32, "tmp")
    hr = Rot(P, 2, [128, 1024], BF16, "h")
    hTr = Rot(P, 2, [128, 1024], BF16, "hT")
    junkr = Rot(P, 4, [128, 1024], BF16, "junkA")
    str_ = Rot(P, 3, [128, 40], F32, "st")
    uvr = Rot(P, 2, [128, 512], F32, "uv")
    vbr = Rot(P, 2, [128, 256], BF16, "vb")
    yar = Rot(P, 2, [128, 256], BF16, "ya")
    yaTr = Rot(P, 2, [128, 2, 128], BF16, "yaT")
    pfr = Rot(P, 2, [128, 256], BF16, "pf")
    pfTr = Rot(P, 2, [128, 2, 128], BF16, "pfT")
    Zr = Rot(P, 2, [128, 512], BF16, "Z")
    cqr = Rot(P, 2, [128, 256], BF16, "cq")
    cqTr = Rot(P, 2, [128, 2, 128], BF16, "cqT")
    qfr = Rot(P, 2, [128, 4, 192], F32, "qf")
    qnr = Rot(P, 2, [128, 4, 192], F32, "qn")
    r1r = Rot(P, 2, [128, 4, 64], F32, "r1")
    r2r = Rot(P, 2, [128, 4, 64], F32, "r2")
    qbr = Rot(P, 2, [128, 4, 256], BF16, "qb")
    for (qb_, kqb_) in qbr.bufs:
        I("pool", nc.gpsimd.memset, [], [kqb_], ap=qb_[:], constant=0.0)
    QTnr = Rot(P, 3, [128, 4, 128], BF16, "QTn")
    QTrr = Rot(P, 3, [128, 4, 128], BF16, "QTr")
    ckvr = Rot(P, 2, [128, 128], BF16, "ckv")
    ckvTr = Rot(P, 2, [128, 128], BF16, "ckvT")
    Vbr = Rot(P, 2, [128, 4, 128], BF16, "Vb")

    def rms_rstd(src, ncols, n, rkeys, st, kst, c0, nm):
        k0, k1, k2 = f"{kst}_{nm}0", f"{kst}_{nm}1", f"{kst}_{nm}2"
        junkA, kj = junkr.next()
        I("act", nc.scalar.activation, rkeys, [kj, k0], out=junkA[:, 0:ncols], in_=src, func=AF.Square, accum_out=st[:, c0:c0 + 1])
        I("act", nc.scalar.activation, [k0], [k1], out=st[:, c0 + 1:c0 + 2], in_=st[:, c0:c0 + 1], func=AF.Sqrt, scale=1.0 / n, bias=EPS)
        I("dve", nc.vector.reciprocal, [k1], [k2], out=st[:, c0 + 2:c0 + 3], in_=st[:, c0 + 1:c0 + 2])
        return k2

    def head_norm_rope_store(t, qf, kqf, normb, knormb, st, kst, c0, nm, out_d, cbase):
        if DBGQ < 2:
            return
        ks = [f"{kst}_{nm}s{h}" for h in range(4)]
        for h in range(4):
            junkA, kj = junkr.next()
            I("act", nc.scalar.activation, [kqf], [kj, ks[h]], out=junkA[:, 0:192], in_=qf[:, h, :], func=AF.Square,
              accum_out=st[:, c0 + h:c0 + h + 1])
        kq1, kq2 = f"{kst}_{nm}q1", f"{kst}_{nm}q2"
        I("act", nc.scalar.activation, ks, [kq1], out=st[:, c0 + 4:c0 + 8], in_=st[:, c0:c0 + 4], func=AF.Sqrt, scale=1.0 / 192, bias=EPS)
        I("dve", nc.vector.reciprocal, [kq1], [kq2], out=st[:, c0 + 8:c0 + 12], in_=st[:, c0 + 4:c0 + 8])
        qn, kqn = qnr.next()
        for h in range(4):
            I("dve", nc.vector.scalar_tensor_tensor, [kqf, kq2, knormb], [kqn], out=qn[:, h, :], in0=qf[:, h, :],
              scalar=st[:, c0 + 8 + h:c0 + 9 + h], in1=normb[:], op0=ALU.mult, op1=ALU.mult)
        if DBGQ < 3:
            return
        r1, kr1 = r1r.next()
        r2, kr2 = r2r.next()
        qb, kqb = qbr.next()
        xrp = qn[:, :, 128:192]
        I("dve", nc.vector.tensor_tensor, [kqn, "rcos"], [kr1], out=r1[:], in0=xrp, in1=rcos[:, t, :].unsqueeze(1).to_broadcast([128, 4, 64]),
          op=ALU.mult)
        x5 = xrp.rearrange("p h (b s d) -> p h b s d", b=2, s=2)
        o5 = r2[:].rearrange("p h (b s d) -> p h b s d", b=2, s=2)
        s5 = rsin[:, t, :].rearrange("p (b s d) -> p b s d", b=2, s=2)
        for s_ in range(2):
            I("dve", nc.vector.tensor_tensor, [kqn, "rsin"], [kr2], out=o5[:, :, :, s_, :], in0=x5[:, :, :, 1 - s_, :],
              in1=s5[:, :, s_, :].unsqueeze(1).to_broadcast([128, 4, 2, 16]), op=ALU.mult)
        I("dve", nc.vector.tensor_tensor, [kr1, kr2], [kqb], out=qb[:, :, 128:192], in0=r1[:], in1=r2[:], op=ALU.add)
        I("act", nc.scalar.copy, [kqn], [kqb], out=qb[:, :, 0:128], in_=qn[:, :, 0:128])
        if DBGQ < 4:
            return
        QTn, kQTn = QTnr.next()
        QTr, kQTr = QTrr.next()
        for rnd in range(2):
            for hh in range(2):
                h = rnd * 2 + hh
                I("pe", nc.tensor.transpose, [kqb, "ident"], ["psT3"], out=psT3[:, cbase + hh * 128:cbase + (hh + 1) * 128], in_=qb[:, h, 0:128],
                  identity=ident[:])
                I("pe", nc.tensor.transpose, [kqb, "ident"], ["psT3"], out=psT3[:, cbase + 256 + hh * 128:cbase + 256 + (hh + 1) * 128],
                  in_=qb[:, h, 128:256], identity=ident[:])
            I("act", nc.scalar.copy, ["psT3"], [kQTn], out=QTn[:, rnd * 2:rnd * 2 + 2, :],
              in_=psT3[:, cbase:cbase + 256].rearrange("p (h t) -> p h t", h=2))
            I("dve", nc.vector.tensor_copy, ["psT3"], [kQTr], out=QTr[0:64, rnd * 2:rnd * 2 + 2, :],
              in_=psT3[0:64, cbase + 256:cbase + 512].rearrange("p (h t) -> p h t", h=2))
        P.dma("sp", out_d[:, 0:128, t * 128:(t + 1) * 128].rearrange("h d t -> d h t"), QTn[:], reads=[kQTn])
        P.dma("sp", out_d[:, 128:192, t * 128:(t + 1) * 128].rearrange("h d t -> d h t"), QTr[0:64, :, :], reads=[kQTr])

    if last:
        zb = P.sb([128, 4, 128], BF16, "zb")
        I("pool", nc.gpsimd.memset, [], ["zb"], ap=zb[:], constant=0.0)
        zbf = zb[:].rearrange("p h t -> p (h t)")
        for t in range(NT, ntiles):
            P.dma("sp", yaT_o[:, t * 128:(t + 1) * 128].rearrange("(c p) t -> p c t", p=128), zb[:, 0:2, :], reads=["zb"])
            P.dma("sp", Z_o[t * 128:(t + 1) * 128, :], zbf, reads=["zb"])
            P.dma("sp", QT_o[:, 0:128, t * 128:(t + 1) * 128].rearrange("h d t -> d h t"), zb[:], reads=["zb"])
            P.dma("sp", QT_o[:, 128:192, t * 128:(t + 1) * 128].rearrange("h d t -> d h t"), zb[0:64, :, :], reads=["zb"])
    def front(t, S):
        is_ctx = t >= NT
        md = mods[1 if is_ctx else 0]
        kmd = f"mod{1 if is_ctx else 0}"
        x_t, kx = xr_.next()
        P.dma("sp", x_t[:], xin[t * 128:(t + 1) * 128, :], writes=[kx])
        st, kst = str_.next()
        S["st"], S["kst"] = st, kst
        krs = rms_rstd(x_t[:], 1024, 1024, [kx], st, kst, 0, "n1")
        tmp, ktmp = tmpr.next()
        h_t, kh = hr.next()
        I("dve", nc.vector.scalar_tensor_tensor, [kx, krs, kmd], [ktmp], out=tmp[:], in0=x_t[:], scalar=st[:, 2:3], in1=md[:, 1024:2048],
          op0=ALU.mult, op1=ALU.mult)
        I("dve", nc.vector.tensor_tensor, [ktmp, kmd], [kh], out=h_t[:], in0=tmp[:], in1=md[:, 0:1024], op=ALU.add)
        for k in range(8):
            I("pe", nc.tensor.transpose, [kh, "ident"], ["psT"], out=psT[:, k * 128:(k + 1) * 128], in_=h_t[:, k * 128:(k + 1) * 128],
              identity=ident[:])
        hT, khT = hTr.next()
        I("act", nc.scalar.copy, ["psT"], [khT], out=hT[:], in_=psT[:])
        S["hT"], S["khT"] = hT, khT

    def pxmm(t, S):
        kv_only = (t >= NT) and last
        hT, khT = S["hT"], S["khT"]
        blocks = [(px0, "px0", 0, 512), (px1, "px1", 512, 1024), (px2, "px2", 1024, 1216)]
        for (pb, kpb, c0, c1) in blocks:
            if kv_only and kpb != "px2":
                continue
            for k in range(8):
                I("pe", nc.tensor.matmul, [khT, w_in_keys[k // 2]], [kpb], out=pb[:, 0:c1 - c0], lhsT=hT[:, k * 128:(k + 1) * 128],
                  rhs=w_in[:, k, c0:c1], start=(k == 0), stop=(k == 7))
        if not kv_only:
            uv, kuv = uvr.next()
            I("act", nc.scalar.activation, ["px0"], [kuv], out=uv[:], in_=px0[:], func=AF.Gelu_apprx_tanh)
            S["uv"], S["kuv"] = uv, kuv
            p1, kp1 = p1r.next()
            I("act", nc.scalar.copy, ["px1"], [kp1], out=p1[:], in_=px1[:])
            S["p1"], S["kp1"] = p1, kp1
        p2, kp2 = p2r.next()
        I("dve", nc.vector.tensor_copy, ["px2"], [kp2], out=p2[:], in_=px2[:, 0:192])
        S["p2"], S["kp2"] = p2, kp2

    def sgu(t, S):
        st, kst = S["st"], S["kst"]
        uv, kuv = S["uv"], S["kuv"]
        krv = rms_rstd(uv[:, 256:512], 256, 256, [kuv], st, kst, 3, "v")
        vb, kvb = vbr.next()
        I("dve", nc.vector.scalar_tensor_tensor, [kuv, krv, "sgun"], [kvb], out=vb[:], in0=uv[:, 256:512], scalar=st[:, 5:6], in1=sgun[:],
          op0=ALU.mult, op1=ALU.mult)
        for h in range(4):
            I("pe", nc.tensor.matmul, [kvb, "w_sguT"], ["psA"], out=ps_ada[:, h * 64:(h + 1) * 64], lhsT=w_sguT[:, h, :],
              rhs=vb[:, h * 64:(h + 1) * 64], start=True, stop=True)
        ya, kya = yar.next()
        for h in range(4):
            I("dve", nc.vector.scalar_tensor_tensor, ["psA", "b_sguT", kuv], [kya], out=ya[:, h * 64:(h + 1) * 64],
              in0=ps_ada[:, h * 64:(h + 1) * 64], scalar=b_sguT[:, h:h + 1], in1=uv[:, h * 64:(h + 1) * 64], op0=ALU.add, op1=ALU.mult)
        for c in range(2):
            I("pe", nc.tensor.transpose, [kya, "ident"], ["psT2"], out=psT2[:, c * 128:(c + 1) * 128], in_=ya[:, c * 128:(c + 1) * 128],
              identity=ident[:])
        yaT, kyaT = yaTr.next()
        I("act", nc.scalar.copy, ["psT2"], [kyaT], out=yaT[:], in_=psT2[:, 0:256].rearrange("p (c t) -> p c t", c=2))
        P.dma("sp", yaT_o[:, t * 128:(t + 1) * 128].rearrange("(c p) t -> p c t", p=128), yaT[:], reads=[kyaT])

    def zpart(t, S):
        p1, kp1 = S["p1"], S["kp1"]
        pf, kpf = pfr.next()
        I("dve", nc.vector.tensor_copy, [kp1], [kpf], out=pf[:], in_=p1[:, 0:256])
        for c in range(2):
            I("pe", nc.tensor.transpose, [kpf, "ident"], ["psT2"], out=psT2[:, 256 + c * 128:256 + (c + 1) * 128],
              in_=pf[:, c * 128:(c + 1) * 128], identity=ident[:])
        pfT, kpfT = pfTr.next()
        I("act", nc.scalar.copy, ["psT2"], [kpfT], out=pfT[:], in_=psT2[:, 256:512].rearrange("p (c t) -> p c t", c=2))
        for c in range(2):
            I("pe", nc.tensor.matmul, [kpfT, "dftc"], ["psB"], out=psB[:], lhsT=pfT[:, c, :], rhs=dftc[:, c, :], start=(c == 0), stop=(c == 1))
        Zt, kZ = Zr.next()
        I("act", nc.scalar.copy, ["psB"], [kZ], out=Zt[:], in_=psB[:])
        P.dma("sp", Z_o[t * 128:(t + 1) * 128, :], Zt[:], reads=[kZ])

    def qpart(t, S):
        st, kst = S["st"], S["kst"]
        p1, kp1 = S["p1"], S["kp1"]
        krq = rms_rstd(p1[:, 256:512], 256, 256, [kp1], st, kst, 6, "q")
        cq, kcq = cqr.next()
        I("dve", nc.vector.scalar_tensor_tensor, [kp1, krq, "qln"], [kcq], out=cq[:], in0=p1[:, 256:512], scalar=st[:, 8:9], in1=qln[:],
          op0=ALU.mult, op1=ALU.mult)
        for c in range(2):
            I("pe", nc.tensor.transpose, [kcq, "ident"], ["psT2"], out=psT2[:, 512 + c * 128:512 + (c + 1) * 128],
              in_=cq[:, c * 128:(c + 1) * 128], identity=ident[:])
        cqT, kcqT = cqTr.next()
        I("act", nc.scalar.copy, ["psT2"], [kcqT], out=cqT[:], in_=psT2[:, 512:768].rearrange("p (c t) -> p c t", c=2))
        for c in range(2):
            I("pe", nc.tensor.matmul, [kcqT, "w_uq"], ["px1"], out=px1[:], lhsT=cqT[:, c, :], rhs=w_uq[:, c, 0:512], start=(c == 0), stop=(c == 1))
        for c in range(2):
            I("pe", nc.tensor.matmul, [kcqT, "w_uq"], ["px2"], out=px2[:, 0:256], lhsT=cqT[:, c, :], rhs=w_uq[:, c, 512:768],
              start=(c == 0), stop=(c == 1))
        qf, kqf = qfr.next()
        qf2 = qf[:].rearrange("p h d -> p (h d)")
        I("act", nc.scalar.copy, ["px1"], [kqf], out=qf2[:, 0:512], in_=px1[:])
        I("act", nc.scalar.copy, ["px2"], [kqf], out=qf2[:, 512:768], in_=px2[:, 0:256])
        head_norm_rope_store(t, qf, kqf, qnb, "qnb", st, kst, 9, "qh", QT_o, 0)

    def kvpart(t, S):
        st, kst = S["st"], S["kst"]
        p2, kp2 = S["p2"], S["kp2"]
        krk = rms_rstd(p2[:, 0:128], 128, 128, [kp2], st, kst, 21, "kv")
        ckv, kckv = ckvr.next()
        I("dve", nc.vector.scalar_tensor_tensor, [kp2, krk, "kvln"], [kckv], out=ckv[:], in0=p2[:, 0:128], scalar=st[:, 23:24], in1=kvln[:],
          op0=ALU.mult, op1=ALU.mult)
        I("pe", nc.tensor.transpose, [kckv, "ident"], ["psT2"], out=psT2[:, 768:896], in_=ckv[:], identity=ident[:])
        ckvT, kckvT = ckvTr.next()
        I("act", nc.scalar.copy, ["psT2"], [kckvT], out=ckvT[:], in_=psT2[:, 768:896])
        kf, kkf = kfr.next()
        Vb, kVb = Vbr.next()
        I("act", nc.scalar.copy, [kp2], [kkf], out=kf[:, :, 128:192], in_=p2[:, 128:192].unsqueeze(1).to_broadcast([128, 4, 64]))
        for j, (pb, kpb) in enumerate([(px0, "px0"), (px0, "px0")]):
            I("pe", nc.tensor.matmul, [kckvT, "w_ukv"], [kpb], out=pb[:], lhsT=ckvT[:], rhs=w_ukv[:, j * 512:(j + 1) * 512], start=True, stop=True)
            pv = pb[:].rearrange("p (h s d) -> p h s d", h=2, s=2)
            I("act", nc.scalar.copy, [kpb], [kVb], out=Vb[:, 2 * j:2 * j + 2, :], in_=pv[:, :, 1, :])
            I("dve", nc.vector.tensor_copy, [kpb], [kkf], out=kf[:, 2 * j:2 * j + 2, 0:128], in_=pv[:, :, 0, :])
        P.dma("sp", V_o[t * 128:(t + 1) * 128, :], Vb[:].rearrange("p h d -> p (h d)"), reads=[kVb])
        head_norm_rope_store(t, kf, kkf, knb, "knb", st, kst, 24, "kh", KT_o, 512)

    p1r = Rot(P, 2, [128, 512], F32, "p1s")
    p2r = Rot(P, 2, [128, 192], F32, "p2s")
    kfr = Rot(P, 2, [128, 4, 192], F32, "kf")
    states = [dict() for _ in range(ntiles + 1)]
    if ntiles:
        front(0, states[0])
    for t in range(ntiles):
        S = states[t]
        kv_only = (t >= NT) and last
        pxmm(t, S)
        lists = []
        if not kv_only:
            lists += [P.capture(sgu, t, S), P.capture(zpart, t, S), P.capture(qpart, t, S)]
        lists.append(P.capture(kvpart, t, S))
        if t + 1 < ntiles:
            lists.append(P.capture(front, t + 1, states[t + 1]))
        P.replay_interleaved(lists) if not SEQREPLAY else [P.op(*o) for l in lists for o in l]
    P.emit()
    P.close()
    return nc


def build_F(last, nc=None, T=None, tag="", skip_stage1=False):
    if nc is None:
        nc = bass.Bass("TRN2", target_bir_lowering=False)

    def din(name, shape, dt=F32):
        if T is not None:
            assert tuple(T[name].shape) == tuple(shape), (name, T[name].shape, shape)
            return T[name]
        return nc.dram_tensor(name, list(shape), dt, kind="ExternalInput").ap()

    Z_d = din("Z_all", [8192, 512], BF16)
    Zc_d = din("Zc", [256, 512], BF16)
    WA_d = din("WA", [128, 128])
    TC_d = din("TC", [128, 2, 64, 32])
    TCc_d = din("TCc", [128, 2, 2, 256])
    ybT_o = T["ybT"] if T is not None else nc.dram_tensor("ybT", [256, TT * 128], BF16, kind="ExternalOutput").ap()
    A_d = T["A_scr"] if (T is not None and "A_scr" in T) else nc.dram_tensor(tag + "A_scr", [128, 128, 256], BF16, kind="Internal").ap()

    P = Prog(nc, tag)
    I = P.I
    WA = P.sb([128, 128], BF16, "WA")
    P.dma("pool", WA[:], WA_d, writes=["WA"])
    TC = P.sb([128, 2, 64, 32], BF16, "TC")
    P.dma("pool", TC[:], TC_d, writes=["TC"])
    fb = [P.ps([128, 512], F32, f"f{i}") for i in range(2)]
    yb = [P.ps([128, 512], F32, f"yb{i}") for i in range(2)]
    pr = P.ps([128, 512], F32, "pr")
    zar = Rot(P, 2, [128, 16, 256], BF16, "za")
    aor = Rot(P, 2, [128, 16, 256], BF16, "ao")
    Zv = Z_d.rearrange("(n1 n2) (ri c) -> ri n1 n2 c", n2=128, ri=2)
    cnt = 0
    for ch in range(0 if skip_stage1 else 8):
        za, kza = zar.next()
        for ri in range(2):
            P.dma("sp", za[ri * 64:(ri + 1) * 64, :, :], Zv[ri, :, ch * 16:(ch + 1) * 16, :], writes=[f"{kza}_{ri}"])
        ao, kao = aor.next()
        for j in range(8):
            bk = fb[j % 2]
            I("pe", nc.tensor.matmul, [f"{kza}_0", f"{kza}_1", "WA"], [f"f{j % 2}"], out=bk[:], lhsT=WA[:],
              rhs=za[:, 2 * j:2 * j + 2, :].rearrange("p a c -> p (a c)"), start=True, stop=True)
            dst = ao[:, 2 * j:2 * j + 2, :].rearrange("p a c -> p (a c)")
            if cnt % 2 == 0:
                I("act", nc.scalar.copy, [f"f{j % 2}"], [kao], out=dst, in_=bk[:])
            else:
                I("dve", nc.vector.tensor_copy, [f"f{j % 2}"], [kao], out=dst, in_=bk[:])
            cnt += 1
        P.dma("sp", A_d[:, ch * 16:(ch + 1) * 16, :], ao[:], reads=[kao], writes=["A_d"])
    ac = P.sb([128, 128, 128], BF16, "ac")
    ybs = P.sb([128, 2, TT * 128], BF16, "ybs")
    A_v = A_d.rearrange("q n c -> n q c")
    for half in range(2):
        for qq in range(4):
            P.dma("sp", ac[:, qq * 32:(qq + 1) * 32, :], A_v[:, qq * 32:(qq + 1) * 32, half * 128:(half + 1) * 128], reads=["A_d"],
                  writes=[f"ac{qq}"])
        ackeys = [f"ac{qq}" for qq in range(4)]
        for bk in range(4):
            ybk = yb[bk % 2]
            yv = ybk[:].rearrange("p (k2 k1) -> p k2 k1", k1=64)
            for k1 in range(64):
                for ri in range(2):
                    I("pe", nc.tensor.matmul, ackeys + ["TC"], [f"yb{bk % 2}"], out=yv[:, :, k1], lhsT=ac[:, ri * 64 + k1, :],
                      rhs=TC[:, ri, k1, bk * 8:(bk + 1) * 8], start=(ri == 0), stop=(ri == 1))
            dst = ybs[:, half, bk * 512:(bk + 1) * 512]
            if bk % 2 == 0:
                I("act", nc.scalar.copy, [f"yb{bk % 2}"], ["ybs"], out=dst, in_=ybk[:])
            else:
                I("dve", nc.vector.tensor_copy, [f"yb{bk % 2}"], ["ybs"], out=dst, in_=ybk[:])
    if not last:
        Zc = P.sb([128, 2, 512], BF16, "Zc")
        P.dma("sp", Zc[:], Zc_d.rearrange("(t p) c -> p t c", p=128), writes=["Zc"])
        TCc = P.sb([128, 2, 2, 256], BF16, "TCc")
        P.dma("pool", TCc[:], TCc_d, writes=["TCc"])
        for half in range(2):
            i = 0
            for nt in range(2):
                for ri in range(2):
                    I("pe", nc.tensor.matmul, ["Zc", "TCc"], ["pr"], out=pr[:, 0:256], lhsT=Zc[:, nt, ri * 256 + half * 128:ri * 256 + (half + 1) * 128],
                      rhs=TCc[:, nt, ri, :], start=(i == 0), stop=(i == 3))
                    i += 1
            I("act", nc.scalar.copy, ["pr"], ["ybs"], out=ybs[:, half, NT * 128:TT * 128], in_=pr[:, 0:256])
    if last:
        I("pool", nc.gpsimd.memset, [], ["ybs"], ap=ybs[:, :, NT * 128:TT * 128], constant=0.0)
    ncols = TT * 128
    P.dma("sp", ybT_o[:, 0:ncols].rearrange("(c p) t -> p c t", p=128), ybs[:, :, 0:ncols], reads=["ybs"])
    P.emit()
    P.close()
    return nc


NKT = 66
SCALE = 192 ** -0.5


def build_B(last, nheads=4, nblocks=None, full=True, nc=None, T=None, tag=""):
    if nc is None:
        nc = bass.Bass("TRN2", target_bir_lowering=False)

    def din(name, shape, dt=F32):
        if T is not None:
            assert tuple(T[name].shape) == tuple(shape), (name, T[name].shape, shape)
            return T[name]
        return nc.dram_tensor(name, list(shape), dt, kind="ExternalInput").ap()

    def dout(name, shape, dt=F32):
        if T is not None:
            assert tuple(T[name].shape) == tuple(shape), (name, T[name].shape, shape)
            return T[name]
        return nc.dram_tensor(name, list(shape), dt, kind="ExternalOutput").ap()

    QT_d = din("QT", [4, 192, TT * 128], BF16)
    KT_d = din("KT_all", [4, 192, NKT * 128], BF16)
    V_d = din("V_all", [NKT * 128, 512], BF16)
    EPS = 1e-6
    if full:
        xin = din("xin", [TT * 128, 1024])
        ada_d = din("ada", [2, 6144])
        yaT_d = din("yaT", [256, TT * 128], BF16)
        ybT_d = din("ybT", [256, TT * 128], BF16)
        w_out_d = din("w_out", [1024, 1024])
        w_r_d = din("w_router", [1024, 16])
        x1_o = dout("x1", [TT * 128, 1024])
        h2_o = dout("h2", [TT * 128, 1024], BF16)
        aff_o = dout("aff", [TT * 128, 16])
        affT_o = dout("affT", [16, TT * 128])
    else:
        yc_o = dout("yc", [TT * 128, 512], BF16)

    P = Prog(nc, tag)
    I = P.I
    NCH = 6
    CT = NKT // NCH
    ktn = P.sb([128, NKT * 128], BF16, "ktn")
    ktr = P.sb([128, NKT * 128], BF16, "ktr")
    vh = P.sb([128, NKT, 129], BF16, "vh")
    I("pool", nc.gpsimd.memset, [], [f"vh{c}" for c in range(NCH)], ap=vh[:, :, 128:129], constant=1.0)
    I("pool", nc.gpsimd.memset, [], [f"ktr{c}" for c in range(NCH)], ap=ktr[64:128, :], constant=0.0)
    qnr = Rot(P, 2, [128, TT * 128], BF16, "qtn")
    qrr = Rot(P, 2, [128, TT * 128], BF16, "qtr")
    for (qb_, kqb_) in qrr.bufs:
        I("pool", nc.gpsimd.memset, [], [kqb_], ap=qb_[64:128, :], constant=0.0)
    ptr = Rot(P, 3, [128, 512], BF16, "pt")
    ycr = Rot(P, 3, [128, 128], BF16, "yct")
    rcr = Rot(P, 4, [128, 1], F32, "rc")
    sbank = [P.ps([128, 512], F32, f"sb{i}") for i in range(2)]
    obank = [P.ps([128, 512], F32, f"ob{i}") for i in range(4)]
    V_v = V_d.rearrange("(t p) (h d) -> p t h d", p=128, h=4)
    ntile = NT if last else TT
    if full:
        ident = make_ident(P, nc)
        psTb = P.ps([128, 1024], BF16, "psTb")
        tb = P.ps([128, 512], F32, "tb")
        mixT = P.sb([128, 8, TT * 128], BF16, "mixT")
        mkeys = [f"mix{t}" for t in range(TT)]
        def load_mix():
            P.dma("sp", mixT[:, 0:2, :], yaT_d.rearrange("(c p) t -> p c t", p=128), writes=mkeys)
            P.dma("sp", mixT[:, 2:4, :], ybT_d.rearrange("(c p) t -> p c t", p=128), writes=mkeys)
        w_out = P.sb([128, 8, 1024], BF16, "w_out")
        w_out_v = w_out_d.rearrange("(k p) n -> p k n", p=128)
        for k in range(0, 8, 2):
            P.dma("pool", w_out[:, k:k + 2, :], w_out_v[:, k:k + 2, :], writes=[f"w_out{k}"])
        wokeys = [f"w_out{k}" for k in range(0, 8, 2)]
        w_r = P.sb([128, 8, 16], BF16, "w_r")
        P.dma("pool", w_r[:], w_r_d.rearrange("(k p) n -> p k n", p=128), writes=["w_r"])
        mods = []
        for r in range(2):
            md = P.sb([128, 3072], F32, f"modB{r}")
            P.dma("sp", md[:], ada_d[r:r + 1, 2048:5120].partition_broadcast(128), writes=[f"modB{r}"])
            I("dve", nc.vector.tensor_scalar_add, [f"modB{r}"], [f"modB{r}"], out=md[:, 2048:3072], in0=md[:, 2048:3072], scalar1=1.0)
            mods.append(md)

    blocks = [(qb * 512, 512, NKT) for qb in range(4)]
    if not last:
        blocks.insert(0, (NT * 128, NCT * 128, NCT))
    if nblocks is not None:
        blocks = blocks[:nblocks]

    if full:
        xr_ = Rot(P, 2, [128, 1024], F32, "x")
        tmpr = Rot(P, 2, [128, 1024], F32, "tmp")
        x1r = Rot(P, 2, [128, 1024], F32, "x1")
        h2r = Rot(P, 2, [128, 1024], BF16, "h2")
        h2Tr = Rot(P, 2, [128, 1024], BF16, "h2T")
        junkr = Rot(P, 2, [128, 1024], BF16, "junk")
        str_ = Rot(P, 3, [128, 8], F32, "st")
        exr = Rot(P, 2, [128, 16], F32, "ex")
        afr = Rot(P, 2, [128, 16], F32, "af")
        aTr = Rot(P, 2, [16, 128], F32, "aT")

        def tail_tile(t):
            r = 1 if t >= NT else 0
            md, kmd = mods[r], f"modB{r}"
            x_t, kx = xr_.next()
            P.dma("sp", x_t[:], xin[t * 128:(t + 1) * 128, :], writes=[kx])
            tmp, ktmp = tmpr.next()
            x1, kx1 = x1r.next()
            for nb in range(2):
                cs = slice(nb * 512, (nb + 1) * 512)
                for k in range(8):
                    I("pe", nc.tensor.matmul, [f"mix{t}", wokeys[k // 2]], ["tb"], out=tb[:], lhsT=mixT[:, k, t * 128:(t + 1) * 128],
                      rhs=w_out[:, k, nb * 512:(nb + 1) * 512], start=(k == 0), stop=(k == 7))
                I("dve", nc.vector.tensor_tensor, ["tb", kmd], [ktmp], out=tmp[:, cs], in0=tb[:], in1=md[:, cs], op=ALU.mult)
                I("dve", nc.vector.tensor_tensor, [ktmp, kx], [kx1], out=x1[:, cs], in0=tmp[:, cs], in1=x_t[:, cs], op=ALU.add)
            P.dma("sp", x1_o[t * 128:(t + 1) * 128, :], x1[:], reads=[kx1])
            st, kst = str_.next()
            junk, kj = junkr.next()
            I("act", nc.scalar.activation, [kx1], [kj, kst + "a"], out=junk[:], in_=x1[:], func=AF.Square, accum_out=st[:, 0:1])
            I("act", nc.scalar.activation, [kst + "a"], [kst + "b"], out=st[:, 1:2], in_=st[:, 0:1], func=AF.Sqrt, scale=1.0 / 1024, bias=EPS)
            I("dve", nc.vector.reciprocal, [kst + "b"], [kst + "c"], out=st[:, 2:3], in_=st[:, 1:2])
            tmp2, ktmp2 = tmpr.next()
            h2, kh2 = h2r.next()
            I("dve", nc.vector.scalar_tensor_tensor, [kx1, kst + "c", kmd], [ktmp2], out=tmp2[:], in0=x1[:], scalar=st[:, 2:3], in1=md[:, 2048:3072],
              op0=ALU.mult, op1=ALU.mult)
            I("dve", nc.vector.tensor_tensor, [ktmp2, kmd], [kh2], out=h2[:], in0=tmp2[:], in1=md[:, 1024:2048], op=ALU.add)
            P.dma("sp", h2_o[t * 128:(t + 1) * 128, :], h2[:], reads=[kh2])
            h2T, kh2T = h2Tr.next()
            for hf in range(2):
                for k in range(4):
                    kk = hf * 4 + k
                    I("pe", nc.tensor.transpose, [kh2, "ident"], ["psTb"], out=psTb[:, 512 + k * 128:512 + (k + 1) * 128], in_=h2[:, kk * 128:(kk + 1) * 128],
                      identity=ident[:])
                I("act", nc.scalar.copy, ["psTb"], [kh2T], out=h2T[:, hf * 512:(hf + 1) * 512], in_=psTb[:, 512:1024])
            for k in range(8):
                I("pe", nc.tensor.matmul, [kh2T, "w_r"], ["tb"], out=tb[:, 0:16], lhsT=h2T[:, k * 128:(k + 1) * 128], rhs=w_r[:, k, :],
                  start=(k == 0), stop=(k == 7))
            ex, kex = exr.next()
            af, kaf = afr.next()
            I("dve", nc.vector.reduce_max, ["tb"], [kst + "d"], out=st[:, 3:4], in_=tb[:, 0:16], axis=AX.X)
            I("dve", nc.vector.tensor_scalar, [kst + "d"], [kst + "e"], out=st[:, 4:5], in0=st[:, 3:4], scalar1=-1.0, scalar2=None, op0=ALU.mult)
            I("act", nc.scalar.activation, ["tb", kst + "e"], [kex, kst + "f"], out=ex[:], in_=tb[:, 0:16], func=AF.Exp, bias=st[:, 4:5],
              scale=1.0, accum_out=st[:, 5:6])
            I("dve", nc.vector.reciprocal, [kst + "f"], [kst + "g"], out=st[:, 6:7], in_=st[:, 5:6])
            I("dve", nc.vector.tensor_scalar, [kex, kst + "g"], [kaf], out=af[:], in0=ex[:], scalar1=st[:, 6:7], scalar2=None, op0=ALU.mult)
            P.dma("sp", aff_o[t * 128:(t + 1) * 128, :], af[:], reads=[kaf])
            I("pe", nc.tensor.transpose, [kaf, "identf"], ["tb"], out=tb[0:16, 128:256], in_=af[:], identity=P.identf[:])
            aT, kaT = aTr.next()
            I("act", nc.scalar.copy, ["tb"], [kaT], out=aT[:], in_=tb[0:16, 128:256])
            P.dma("sp", affT_o[:, t * 128:(t + 1) * 128], aT[:], reads=[kaT])


    for h in range(nheads):
        qtn, kqn = qnr.next()
        qtr, kqr = qrr.next()
        P.dma("sp", qtn[:], QT_d[h, 0:128, :], writes=[kqn])
        P.dma("sp", qtr[0:64, :], QT_d[h, 128:192, :], writes=[kqr])
        for c in range(NCH):
            cs = slice(c * CT * 128, (c + 1) * CT * 128)
            P.dma("sp", ktn[:, cs], KT_d[h, 0:128, cs], writes=[f"ktn{c}"])
            P.dma("sp", ktr[0:64, cs], KT_d[h, 128:192, cs], writes=[f"ktr{c}"])
            P.dma("sp", vh[:, c * CT:(c + 1) * CT, 0:128], V_v[:, c * CT:(c + 1) * CT, h, :], writes=[f"vh{c}"])
        if full and h == 0:
            load_mix()
        def attn_block(q0, qn, nkt):
            nsub = qn // 128
            pts = {}

            def S(kt):
                c = kt // CT
                sbk = sbank[kt % 2]
                ks = slice(kt * 128, (kt + 1) * 128)
                I("pe", nc.tensor.matmul, [f"ktn{c}", kqn], [f"sb{kt % 2}"], out=sbk[:, 0:qn], lhsT=ktn[:, ks], rhs=qtn[:, q0:q0 + qn],
                  start=True, stop=False)
                I("pe", nc.tensor.matmul, [f"ktr{c}", kqr], [f"sb{kt % 2}"], out=sbk[:, 0:qn], lhsT=ktr[:, ks], rhs=qtr[:, q0:q0 + qn],
                  start=False, stop=True)

            def E(kt):
                pt, kpt = ptr.next()
                pts[kt] = (pt, kpt)
                I("act", nc.scalar.activation, [f"sb{kt % 2}"], [kpt], out=pt[:, 0:qn], in_=sbank[kt % 2][:, 0:qn], func=AF.Exp, scale=SCALE)

            def PV(kt):
                c = kt // CT
                pt, kpt = pts.pop(kt)
                for qs in range(nsub):
                    ob = obank[qs]
                    off = 0
                    I("pe", nc.tensor.matmul, [kpt, f"vh{c}"], [f"ob{qs}"], out=ob[:, off:off + 129], lhsT=pt[:, qs * 128:(qs + 1) * 128],
                      rhs=vh[:, kt, :], start=(kt == 0), stop=(kt == nkt - 1))

            S(0)
            for kt in range(nkt):
                E(kt)
                if kt + 1 < nkt:
                    S(kt + 1)
                PV(kt)
            for qs in range(nsub):
                ob = obank[qs]
                off = 0
                rc, krc = rcr.next()
                yct, kyct = ycr.next()
                I("dve", nc.vector.reciprocal, [f"ob{qs}"], [krc], out=rc[:], in_=ob[:, off + 128:off + 129])
                I("dve", nc.vector.tensor_scalar, [f"ob{qs}", krc], [kyct], out=yct[:], in0=ob[:, off:off + 128], scalar1=rc[:, 0:1],
                  scalar2=None, op0=ALU.mult)
                r0 = q0 + qs * 128
                if full:
                    I("pe", nc.tensor.transpose, [kyct, "ident"], ["psTb"], out=psTb[:, 0:128], in_=yct[:], identity=ident[:])
                    I("act", nc.scalar.copy, ["psTb"], [f"mix{r0 // 128}"], out=mixT[:, 4 + h, r0:r0 + 128], in_=psTb[:, 0:128])
                else:
                    P.dma("sp", yc_o[r0:r0 + 128, h * 128:(h + 1) * 128], yct[:], reads=[kyct])

        for bi, (q0, qn, nkt) in enumerate(blocks):
            if full and h == nheads - 1 and bi >= 1:
                pq0, pqn, _ = blocks[bi - 1]

                def tails(pq0=pq0, pqn=pqn):
                    for tt_ in range(pq0 // 128, (pq0 + pqn) // 128):
                        tail_tile(tt_)
                P.replay_interleaved([P.capture(attn_block, q0, qn, nkt), P.capture(tails)])
            else:
                attn_block(q0, qn, nkt)
    if full:
        pq0, pqn, _ = blocks[-1]
        for tt_ in range(pq0 // 128, (pq0 + pqn) // 128):
            tail_tile(tt_)
        if last:
            zt, kzt = tmpr.next()
            zh, kzh = h2r.next()
            I("pool", nc.gpsimd.memset, [], [kzt], ap=zt[:], constant=0.0)
            I("pool", nc.gpsimd.memset, [], [kzh], ap=zh[:], constant=0.0)
            for t in range(NT, TT):
                P.dma("sp", x1_o[t * 128:(t + 1) * 128, :], zt[:], reads=[kzt])
                P.dma("sp", h2_o[t * 128:(t + 1) * 128, :], zh[:], reads=[kzh])
                P.dma("sp", aff_o[t * 128:(t + 1) * 128, :], zt[:, 0:16], reads=[kzt])
                P.dma("sp", affT_o[:, t * 128:(t + 1) * 128], zt[0:16, 0:128], reads=[kzt])
    P.emit()
    P.close()
    return nc


NE = 16
CAPM, CAPC = 120, 32
NIT = 26


def build_C(last, nexp=NE, nc=None, T=None, tag="", skip_bisect=False):
    if nc is None:
        nc = bass.Bass("TRN2", target_bir_lowering=False)

    def din(name, shape, dt=F32):
        if T is not None:
            assert tuple(T[name].shape) == tuple(shape), (name, T[name].shape, shape)
            return T[name]
        return nc.dram_tensor(name, list(shape), dt, kind="ExternalInput").ap()

    x1_d = din("x1", [TT * 128, 1024])
    h2_d = din("h2", [TT * 128, 1024], BF16)
    aff_d = din("aff", [TT * 128, 16])
    affT_d = din("affT_all", [16, 8192])
    affTc_d = din("affTc", [16, 256])
    ada_d = din("ada", [2, 6144])
    wg_d = din("w_gate", [NE, 1024, 512])
    wu_d = din("w_up", [NE, 1024, 512])
    wd_d = din("w_down", [NE, 512, 1024])
    utri_d = din("utri", [128, 128])
    ones_d = din("ones", [128, 128])
    iota_d = din("iota", [128, 128])
    blk_d = din("blockones", [128, 128])
    thrc_d = din("thrc", [128, 2])
    x2_o = T["x2"] if T is not None else nc.dram_tensor("x2", [TT * 128, 1024], F32, kind="ExternalOutput").ap()
    thr_scr = T["thr_scr"] if (T is not None and "thr_scr" in T) else nc.dram_tensor(tag + "thr_scr", [128, 2], F32, kind="Internal").ap()

    P = Prog(nc, tag)
    I = P.I
    ident = make_ident(P, nc)
    ntile = NT if last else TT
    groups = [(g, [4 * g + j for j in range(4)], CAPM, g * CAPM) for g in range(4)]
    if not last:
        groups.append((4, [NT, NT + 1], CAPC, 4 * CAPM))
    nslots = 4 * CAPM + (0 if last else CAPC)
    tile_group = {}
    for (g, tiles, cap, s0) in groups:
        for t in tiles:
            tile_group[t] = (g, cap, s0)

    utri = P.sb([128, 128], BF16, "utri")
    P.dma("pool", utri[:], utri_d, writes=["utri"])
    onesb = P.sb([128, 128], BF16, "onesb")
    P.dma("pool", onesb[:], ones_d, writes=["onesb"])
    iota = P.sb([128, 128], F32, "iota")
    P.dma("sp", iota[:], iota_d, writes=["iota"])
    blk = P.sb([128, 128], F32, "blk")
    P.dma("sp", blk[:], blk_d, writes=["blk"])
    thrc = P.sb([128, 2], F32, "thrc")
    P.dma("sp", thrc[:], thrc_d, writes=["thrc"])
    acc = P.sb([128, TT, 1024], F32, "acc")
    Am = acc[:, 0, :]
    if not skip_bisect:
        P.dma("sp", Am, affT_d.rearrange("e (s t) -> (e s) t", s=8), writes=["acc0"])
    Ac = P.sb([128, 32], F32, "Ac")
    P.dma("sp", Ac[:], affTc_d.rearrange("e (s t) -> (e s) t", s=8), writes=["Ac"])
    affs = P.sb([128, TT, 16], F32, "affs")
    P.dma("sp", affs[:], aff_d.rearrange("(t p) e -> p t e", p=128), writes=["affs"])
    h2tok = P.sb([128, TT, 1024], BF16, "h2tok")
    for t0 in range(0, TT, 6):
        P.dma("sp", h2tok[:, t0:t0 + 6, :], h2_d.rearrange("(t p) d -> p t d", p=128)[:, t0:t0 + 6, :], writes=[f"h2tok{t0}"])
    h2keys = [f"h2tok{t0}" for t0 in range(0, TT, 6)]
    g2 = []
    for r in range(2):
        md = P.sb([128, 1024], F32, f"g2_{r}")
        P.dma("sp", md[:], ada_d[r:r + 1, 5120:6144].partition_broadcast(128), writes=[f"g2_{r}"])
        g2.append(md)

    pp = P.ps([128, 512], F32, "pp")
    psTb = P.ps([128, 1024], BF16, "psTb")
    gA = P.ps([128, 512], F32, "gA")
    gB = P.ps([128, 512], F32, "gB")
    Gb = P.ps([128, 512], F32, "Gb")
    Ub = P.ps([128, 512], F32, "Ub")
    yA = P.ps([128, 512], F32, "yA")
    yB = P.ps([128, 512], F32, "yB")

    hidT = P.sb([128, 4, 512], BF16, "hidT")
    junk = hidT[:, 0:2, :].rearrange("p a b -> p (a b)")
    lo = P.sb([128, 2], F32, "lo")
    negmid = P.sb([128, 2], F32, "negmid")
    ssum = P.sb([128, 2], F32, "ssum")
    cond = P.sb([128, 2], F32, "cond")
    I("dve", nc.vector.memset, [], ["lo"], ap=lo[:], constant=0.0)
    I("dve", nc.vector.memset, [], ["negmid"], ap=negmid[:], constant=-0.5)
    I("dve", nc.vector.memset, [], ["ssum0", "ssum1"], ap=ssum[:], constant=0.0)
    for it in range(0 if skip_bisect else NIT):
        w = 2.0 ** -(it + 1)
        I("act", nc.scalar.activation, ["acc0", "negmid"], ["hidT", "ssum0"], out=junk[:, 0:1024], in_=Am, func=AF.Sign, bias=negmid[:, 0:1], scale=1.0,
          accum_out=ssum[:, 0:1])
        I("act", nc.scalar.activation, ["Ac", "negmid"], ["hidT", "ssum1"], out=junk[:, 0:32], in_=Ac[:], func=AF.Sign, bias=negmid[:, 1:2], scale=1.0,
          accum_out=ssum[:, 1:2])
        I("pe", nc.tensor.matmul, ["blk", "ssum0", "ssum1"], ["pp"], out=pp[:, 0:2], lhsT=blk[:], rhs=ssum[:], start=True, stop=True)
        I("dve", nc.vector.tensor_tensor, ["pp", "thrc"], ["cond"], out=cond[:], in0=pp[:, 0:2], in1=thrc[:], op=ALU.is_ge)
        I("dve", nc.vector.scalar_tensor_tensor, ["cond", "lo"], ["lo"], out=lo[:], in0=cond[:], scalar=w, in1=lo[:], op0=ALU.mult, op1=ALU.add)
        I("dve", nc.vector.tensor_scalar, ["lo"], ["negmid"], out=negmid[:], in0=lo[:], scalar1=w * 0.5, scalar2=-1.0, op0=ALU.add, op1=ALU.mult)
    if not skip_bisect:
        P.dma("sp", thr_scr, lo[:], reads=["lo"], writes=["thr_scr"])
    thr_all = P.sb([128, 256], F32, "thr_all")
    P.dma("sp", thr_all[:], thr_scr.rearrange("p c -> (p c)").unsqueeze(0).partition_broadcast(128), reads=["thr_scr"], writes=["thr_all"])
    thr_v = thr_all[:].rearrange("p (e s c) -> p e s c", s=8, c=2)

    maskf = P.sb([128, TT, 16], F32, "maskf")
    maskb = P.sb([128, TT, 16], BF16, "maskb")
    gm = P.sb([128, TT, 16], F32, "gm")
    pos = P.sb([128, TT, 16], F32, "pos")
    for t in range(ntile):
        col = 1 if t >= NT else 0
        I("dve", nc.vector.tensor_tensor, ["affs", "thr_all"], ["maskf"], out=maskf[:, t, :], in0=affs[:, t, :], in1=thr_v[:, :, 0, col], op=ALU.is_ge)
    I("dve", nc.vector.tensor_copy, ["maskf"], ["maskb"], out=maskb[:, 0:ntile, :], in_=maskf[:, 0:ntile, :])
    I("dve", nc.vector.tensor_tensor, ["maskf", "affs"], ["gm"], out=gm[:, 0:ntile, :], in0=maskf[:, 0:ntile, :], in1=affs[:, 0:ntile, :], op=ALU.mult)
    for (g, tiles, cap, s0) in groups:
        for jj, t in enumerate(tiles):
            for i in range(jj + 1):
                I("pe", nc.tensor.matmul, ["maskb", "utri", "onesb"], ["pp"], out=pp[:, 16 * (t % 16):16 * (t % 16) + 16],
                  lhsT=(utri[:] if i == jj else onesb[:]), rhs=maskb[:, tiles[i], :], start=(i == 0), stop=(i == jj))
            I("dve", nc.vector.tensor_copy, ["pp"], ["pos"], out=pos[:, t, :], in_=pp[:, 16 * (t % 16):16 * (t % 16) + 16])

    wgr = Rot(P, 2, [128, 8, 512], BF16, "wg")
    wur = Rot(P, 2, [128, 8, 512], BF16, "wu")
    wdr = Rot(P, 1, [128, 4, 1024], BF16, "wd")
    Sall = P.sb([128, TT, 128], BF16, "Sall")
    Sgall = P.sb([128, TT, 128], BF16, "Sgall")
    SgT = P.sb([128, TT, 128], BF16, "SgT")
    I("pool", nc.gpsimd.memset, [], [f"S{t}" for t in range(TT)], ap=Sall[:], constant=0.0)
    I("pool", nc.gpsimd.memset, [], [f"Sg{t}" for t in range(TT)], ap=Sgall[:], constant=0.0)
    xsT = P.sb([128, 8, 512], BF16, "xsT")
    sgt = P.sb([128, 512], F32, "sgt")
    yr = Rot(P, 1, [128, 5, 1024], BF16, "ysb")
    cnt = 0
    W, Y = {}, {}

    def load_w(e, which):
        if which == "gu":
            wg, kwg = wgr.next()
            wu, kwu = wur.next()
            W[e] = [wg, kwg, wu, kwu, None, None]
            for hk in range(2):
                P.dma("pool", wg[:, hk * 4:(hk + 1) * 4, :], wg_d[e].rearrange("(k p) f -> p k f", p=128)[:, hk * 4:(hk + 1) * 4, :], writes=[f"{kwg}_{hk}"])
                P.dma("pool", wu[:, hk * 4:(hk + 1) * 4, :], wu_d[e].rearrange("(k p) f -> p k f", p=128)[:, hk * 4:(hk + 1) * 4, :], writes=[f"{kwu}_{hk}"])
        else:
            wd, kwd = wdr.next()
            W[e][4], W[e][5] = wd, kwd
            for hk in range(2):
                P.dma("pool", wd[:, hk * 2:(hk + 1) * 2, :], wd_d[e].rearrange("(k p) d -> p k d", p=128)[:, hk * 2:(hk + 1) * 2, :], writes=[f"{kwd}_{hk}"])

    def build_S(e):
        for t in range(ntile):
            g, cap, s0 = tile_group[t]
            I("dve", nc.vector.tensor_scalar, ["iota", "pos", "maskf"], [f"S{t}"], out=Sall[:, t, 0:cap], in0=iota[:, 0:cap], scalar1=pos[:, t, e:e + 1],
              scalar2=maskf[:, t, e:e + 1], op0=ALU.is_equal, op1=ALU.mult)

    def build_Sg(e):
        for t in range(ntile):
            g, cap, s0 = tile_group[t]
            I("dve", nc.vector.tensor_scalar, ["iota", "pos", "gm"], [f"Sg{t}"], out=Sgall[:, t, 0:cap], in0=iota[:, 0:cap], scalar1=pos[:, t, e:e + 1],
              scalar2=gm[:, t, e:e + 1], op0=ALU.is_equal, op1=ALU.mult)

    def gather(e):
        for (g, tiles, cap, s0) in groups:
            for dk in range(8):
                bank, kb = (gA, "gA") if dk < 4 else (gB, "gB")
                o0 = (dk % 4) * cap
                for jj, t in enumerate(tiles):
                    I("pe", nc.tensor.matmul, [h2keys[t // 6], f"S{t}"], [kb], out=bank[:, o0:o0 + cap], lhsT=h2tok[:, t, dk * 128:(dk + 1) * 128],
                      rhs=Sall[:, t, 0:cap], start=(jj == 0), stop=(jj == len(tiles) - 1))
            I("act", nc.scalar.copy, ["gA"], ["xsT"], out=xsT[:, 0:4, s0:s0 + cap], in_=gA[:, 0:4 * cap].rearrange("p (k s) -> p k s", k=4))
            I("act", nc.scalar.copy, ["gB"], ["xsT"], out=xsT[:, 4:8, s0:s0 + cap], in_=gB[:, 0:4 * cap].rearrange("p (k s) -> p k s", k=4))

    def ffn(e):
        wg, kwg, wu, kwu, _, _ = W[e]
        for fk in range(4):
            for k in range(8):
                I("pe", nc.tensor.matmul, ["xsT", f"{kwg}_{k // 4}"], ["Gb"], out=Gb[:, 0:nslots], lhsT=wg[:, k, fk * 128:(fk + 1) * 128], rhs=xsT[:, k, 0:nslots],
                  start=(k == 0), stop=(k == 7))
            for k in range(8):
                I("pe", nc.tensor.matmul, ["xsT", f"{kwu}_{k // 4}"], ["Ub"], out=Ub[:, 0:nslots], lhsT=wu[:, k, fk * 128:(fk + 1) * 128], rhs=xsT[:, k, 0:nslots],
                  start=(k == 0), stop=(k == 7))
            I("act", nc.scalar.activation, ["Gb"], ["sgt"], out=sgt[:, 0:nslots], in_=Gb[:, 0:nslots], func=AF.Silu)
            I("dve", nc.vector.tensor_tensor, ["sgt", "Ub"], ["hidT"], out=hidT[:, fk, 0:nslots], in0=sgt[:, 0:nslots], in1=Ub[:, 0:nslots], op=ALU.mult)

    def down(e):
        wd, kwd = W[e][4], W[e][5]
        ysb, kys = yr.next()
        Y[e] = (ysb, kys)
        for (g, tiles, cap, s0) in groups:
            for nb, (bank, kb) in enumerate([(yA, "yA"), (yB, "yB")]):
                for fk in range(4):
                    I("pe", nc.tensor.matmul, ["hidT", f"{kwd}_{fk // 2}"], [kb], out=bank[0:cap, :], lhsT=hidT[:, fk, s0:s0 + cap],
                      rhs=wd[:, fk, nb * 512:(nb + 1) * 512], start=(fk == 0), stop=(fk == 3))
            I("act", nc.scalar.copy, ["yA"], [f"{kys}_{g}"], out=ysb[0:cap, g, 0:512], in_=yA[0:cap, :])
            I("act", nc.scalar.copy, ["yB"], [f"{kys}_{g}"], out=ysb[0:cap, g, 512:1024], in_=yB[0:cap, :])

    def sgtr(e):
        for t0 in range(0, ntile, 8):
            tl = list(range(t0, min(t0 + 8, ntile)))
            for t in tl:
                I("pe", nc.tensor.transpose, [f"Sg{t}", "ident"], ["psTb"], out=psTb[:, (t - t0) * 128:(t - t0 + 1) * 128], in_=Sgall[:, t, :], identity=ident[:])
            I("act", nc.scalar.copy, ["psTb"], [f"SgT{t}" for t in tl], out=SgT[:, t0:t0 + len(tl), :],
              in_=psTb[:, 0:len(tl) * 128].rearrange("p (t s) -> p t s", s=128))

    sbanks = [(gA, "gA"), (gB, "gB"), (yA, "yA"), (yB, "yB")]

    def scatter(e):
        ysb, kys = Y[e]
        i = 0
        for t in range(ntile):
            g, cap, s0 = tile_group[t]
            for nb in range(2):
                bank, kb = sbanks[i % 4]
                i += 1
                I("pe", nc.tensor.matmul, [f"SgT{t}", f"{kys}_{g}"], [kb], out=bank[:], lhsT=SgT[0:cap, t, :], rhs=ysb[0:cap, g, nb * 512:(nb + 1) * 512],
                  start=True, stop=True)
                cs = slice(nb * 512, (nb + 1) * 512)
                if e == 0:
                    I("dve", nc.vector.tensor_copy, [kb], [f"acc{t}"], out=acc[:, t, cs], in_=bank[:])
                else:
                    I("dve", nc.vector.tensor_tensor, [kb, f"acc{t}"], [f"acc{t}"], out=acc[:, t, cs], in0=acc[:, t, cs], in1=bank[:], op=ALU.add)

    load_w(0, "gu")
    load_w(0, "d")
    build_S(0)
    build_Sg(0)
    for e in range(nexp):
        if e + 1 < nexp:
            load_w(e + 1, "gu")
        gather(e)
        if e + 1 < nexp:
            build_S(e + 1)
        ffn(e)
        down(e)
        if e + 1 < nexp:
            load_w(e + 1, "d")
        sgtr(e)
        if e + 1 < nexp:
            build_Sg(e + 1)
        scatter(e)
    x1r = Rot(P, 1, [128, 1024], F32, "x1t")
    for t in range(ntile):
        x1t, kx1 = x1r.next()
        P.dma("sp", x1t[:], x1_d[t * 128:(t + 1) * 128, :], writes=[kx1])
        r = 1 if t >= NT else 0
        I("dve", nc.vector.tensor_tensor, [f"acc{t}", f"g2_{r}"], [f"acc{t}"], out=acc[:, t, :], in0=acc[:, t, :], in1=g2[r][:], op=ALU.mult)
        I("dve", nc.vector.tensor_tensor, [f"acc{t}", kx1], [f"acc{t}"], out=acc[:, t, :], in0=acc[:, t, :], in1=x1t[:], op=ALU.add)
        P.dma("sp", x2_o[t * 128:(t + 1) * 128, :], acc[:, t, :], reads=[f"acc{t}"])
    if last:
        for t in range(NT, TT):
            I("pool", nc.gpsimd.memset, [], [f"acc{t}"], ap=acc[:, t, :], constant=0.0)
            P.dma("sp", x2_o[t * 128:(t + 1) * 128, :], acc[:, t, :], reads=[f"acc{t}"])
    P.emit()
    P.close()
    return nc


NV = 4


def build_fused(nlayers=2, nv=NV):
    nc = bass.Bass("TRN2", target_bir_lowering=False)

    def ext(name, shape, dt=F32):
        return nc.dram_tensor(name, list(shape), dt, kind="ExternalInput").ap()

    def scr(name, shape, dt=F32):
        return nc.dram_tensor(name, list(shape), dt, kind="Internal").ap()

    E = {
        "x": ext("x", [8192, 1024]), "ctx": ext("ctx", [256, 1024]), "cT": ext("cT", [128, 8, 2]),
        "w_ada": ext("w_ada", [2, 1024, 6144]), "b_ada": ext("b_ada", [2, 6144]), "w_in": ext("w_in", [2, 1024, 1216]),
        "w_sguT": ext("w_sguT", [2, 128, 4, 128]), "b_sguT": ext("b_sguT", [2, 128, 4]),
        "sgu_norm": ext("sgu_norm", [2, 256]), "q_lora_norm": ext("q_lora_norm", [2, 256]), "kv_lora_norm": ext("kv_lora_norm", [2, 128]),
        "q_norm": ext("q_norm", [2, 192]), "k_norm": ext("k_norm", [2, 192]),
        "w_uq": ext("w_uq", [2, 256, 768]), "w_ukv": ext("w_ukv", [2, 128, 1024]), "dftc": ext("dftc", [256, 512]),
        "rope_cos": ext("rope_cos", [4, TT * 128, 64]), "rope_sin": ext("rope_sin", [4, TT * 128, 64]),
        "WA": ext("WA", [128, 128]), "TC": ext("TC", [4, 128, 2, 64, 32]), "TCc": ext("TCc", [128, 2, 2, 256]),
        "w_out": ext("w_out", [2, 1024, 1024]), "w_router": ext("w_router", [2, 1024, 16]),
        "w_gate": ext("w_gate", [2, 16, 1024, 512]), "w_up": ext("w_up", [2, 16, 1024, 512]), "w_down": ext("w_down", [2, 16, 512, 1024]),
        "utri": ext("utri", [128, 128]), "ones": ext("ones", [128, 128]), "iota": ext("iota", [128, 128]),
        "blockones": ext("blockones", [128, 128]), "thrc": ext("thrc", [128, 2]),
    }
    out = nc.dram_tensor("out", [8192, 1024], F32, kind="ExternalOutput").ap()
    xmid = scr("xmid", [8192, 1024])
    xcmid = scr("xcmid", [256, 1024])
    Z_all = scr("Z_all", [8192, 512], BF16)
    KT_all = scr("KT_all", [4, 192, 66 * 128], BF16)
    V_all = scr("V_all", [66 * 128, 512], BF16)
    affT_all = scr("affT_all", [16, 8192])
    affTc = scr("affTc", [16, 256])
    thr_sh = scr("thr_sh", [128, 2])
    A_sh = scr("A_sh", [128, 128, 256], BF16)
    V = []
    for v in range(nv):
        V.append({
            "xin": scr(f"xin{v}", [TT * 128, 1024]), "ada": scr(f"ada{v}", [2, 6144]), "yaT": scr(f"yaT{v}", [256, TT * 128], BF16),
            "Z": scr(f"Z{v}", [TT * 128, 512], BF16), "QT": scr(f"QT{v}", [4, 192, TT * 128], BF16), "KT": scr(f"KT{v}", [4, 192, TT * 128], BF16),
            "V": scr(f"V{v}", [TT * 128, 512], BF16), "ybT": scr(f"ybT{v}", [256, TT * 128], BF16), "x1": scr(f"x1{v}", [TT * 128, 1024]),
            "h2": scr(f"h2{v}", [TT * 128, 1024], BF16), "aff": scr(f"aff{v}", [TT * 128, 16]), "affT": scr(f"affT{v}", [16, TT * 128]),
            "x2": scr(f"x2{v}", [TT * 128, 1024]),
        })
    M = NT * 128
    phase_no = [0]

    def phase(fn):
        phase_no[0] += 1
        with nc.cleanup_on_exit():
            fn(f"p{phase_no[0]}_")
            nc.all_engine_barrier()

    def glue(copies):
        def body(tag):
            P = Prog(nc, tag)
            for i, (dst, src) in enumerate(copies):
                P.dma("sp", dst, src)
            P.emit()
            P.close()
        phase(body)

    for l in range(nlayers):
        last = l == nlayers - 1
        xsrc, csrc = (E["x"], E["ctx"]) if l == 0 else (xmid, xcmid)
        xdst = out if last else xmid
        cp = []
        for v in range(nv):
            cp.append((V[v]["xin"][0:M, :], xsrc[v * M:(v + 1) * M, :]))
            cp.append((V[v]["xin"][M:TT * 128, :], csrc))
        glue(cp)
        for v in range(nv):
            TA = {"xin": V[v]["xin"], "cT": E["cT"], "w_ada": E["w_ada"][l], "b_ada": E["b_ada"][l:l + 1, :], "w_in": E["w_in"][l],
                  "w_sguT": E["w_sguT"][l], "b_sguT": E["b_sguT"][l], "sgu_norm": E["sgu_norm"][l:l + 1, :],
                  "q_lora_norm": E["q_lora_norm"][l:l + 1, :], "kv_lora_norm": E["kv_lora_norm"][l:l + 1, :],
                  "q_norm": E["q_norm"][l:l + 1, :], "k_norm": E["k_norm"][l:l + 1, :], "w_uq": E["w_uq"][l], "w_ukv": E["w_ukv"][l],
                  "dftc": E["dftc"], "rope_cos": E["rope_cos"][v], "rope_sin": E["rope_sin"][v],
                  "ada": V[0]["ada"], "yaT": V[v]["yaT"], "Z": V[v]["Z"], "QT": V[v]["QT"], "KT": V[v]["KT"], "V": V[v]["V"]}
            lv = last or v > 0
            phase(lambda tag, TA=TA, lv=lv, v=v: build_A(lv, ntiles=(TT if v == 0 else NT), nc=nc, T=TA, tag=tag,
                                                                 ada_src=(None if v == 0 else V[0]["ada"])))
        cp = [(KT_all[:, :, 0:NCT * 128], V[0]["KT"][:, :, M:TT * 128]), (V_all[0:NCT * 128, :], V[0]["V"][M:TT * 128, :])]
        for v in range(nv):
            cp.append((Z_all[v * M:(v + 1) * M, :], V[v]["Z"][0:M, :]))
            cp.append((KT_all[:, :, NCT * 128 + v * M:NCT * 128 + (v + 1) * M], V[v]["KT"][:, :, 0:M]))
            cp.append((V_all[NCT * 128 + v * M:NCT * 128 + (v + 1) * M, :], V[v]["V"][0:M, :]))
        glue(cp)
        for v in range(nv):
            TF = {"Z_all": Z_all, "Zc": V[v]["Z"][M:TT * 128, :], "WA": E["WA"], "TC": E["TC"][v], "TCc": E["TCc"], "ybT": V[v]["ybT"], "A_scr": A_sh}
            phase(lambda tag, TF=TF, lv=(last or v > 0): build_F(lv, nc=nc, T=TF, tag=tag, skip_stage1=(v > 0)))
        for v in range(nv):
            TB = {"QT": V[v]["QT"], "KT_all": KT_all, "V_all": V_all, "xin": V[v]["xin"], "ada": V[0]["ada"], "yaT": V[v]["yaT"], "ybT": V[v]["ybT"],
                  "w_out": E["w_out"][l], "w_router": E["w_router"][l], "x1": V[v]["x1"], "h2": V[v]["h2"], "aff": V[v]["aff"], "affT": V[v]["affT"]}
            phase(lambda tag, TB=TB, lv=(last or v > 0): build_B(lv, nc=nc, T=TB, tag=tag))
        cp = [(affTc, V[0]["affT"][:, M:TT * 128])]
        for v in range(nv):
            cp.append((affT_all[:, v * M:(v + 1) * M], V[v]["affT"][:, 0:M]))
        glue(cp)
        for v in range(nv):
            TCd = {"x1": V[v]["x1"], "h2": V[v]["h2"], "aff": V[v]["aff"], "affT_all": affT_all, "affTc": affTc, "ada": V[0]["ada"], "thr_scr": thr_sh,
                   "w_gate": E["w_gate"][l], "w_up": E["w_up"][l], "w_down": E["w_down"][l], "utri": E["utri"], "ones": E["ones"], "iota": E["iota"],
                   "blockones": E["blockones"], "thrc": E["thrc"], "x2": V[v]["x2"]}
            phase(lambda tag, TCd=TCd, lv=(last or v > 0): build_C(lv, nc=nc, T=TCd, tag=tag, skip_bisect=(v > 0)))
        cp = [(xdst[v * M:(v + 1) * M, :], V[v]["x2"][0:M, :]) for v in range(nv)]
        if not last:
            cp.append((xcmid, V[0]["x2"][M:TT * 128, :]))
        glue(cp)
    return nc


BF = ml_dtypes.bfloat16
NCORES = 8
TPC = 2048
NCTX = 256
SEQ = 8192


def f32(a):
    return np.ascontiguousarray(a, dtype=np.float32)


def const_dftc():
    c = np.arange(64)
    ang = 2 * np.pi * np.outer(c, c) / 64.0
    C, S = np.cos(ang), np.sin(ang)
    d = np.zeros((256, 512), np.float64)
    for g in range(4):
        d[g * 64:(g + 1) * 64, g * 64:(g + 1) * 64] = C
        d[g * 64:(g + 1) * 64, 256 + g * 64:256 + (g + 1) * 64] = -S
    return f32(d)


def const_rope(core):
    tok0 = (core % 4) * TPC
    n = np.arange(tok0, tok0 + TPC)
    pos_row = (n // 64).astype(np.float32)
    pos_col = (n % 64).astype(np.float32)
    freqs = (np.float32(10000.0) ** (-np.arange(16, dtype=np.float32) / np.float32(16))).astype(np.float32)
    cos = np.ones((TPC + NCTX, 64), np.float32)
    sin = np.zeros((TPC + NCTX, 64), np.float32)
    for b, pos in enumerate([pos_row, pos_col]):
        ang = (pos[:, None] * freqs[None, :]).astype(np.float32)
        cs, sn = np.cos(ang).astype(np.float32), np.sin(ang).astype(np.float32)
        cos[:TPC, b * 32:b * 32 + 16] = cs
        cos[:TPC, b * 32 + 16:b * 32 + 32] = cs
        sin[:TPC, b * 32:b * 32 + 16] = -sn
        sin[:TPC, b * 32 + 16:b * 32 + 32] = sn
    return cos, sin


def inputs_A(inp, l, x_cur, xc_cur):
    dftc = const_dftc()
    shared = {
        "w_ada": f32(inp["w_ada"][l]), "b_ada": f32(inp["b_ada"][l][None, :]), "w_in": f32(inp["w_in"][l]),
        "w_sguT": f32(np.transpose(inp["w_sgu"][l], (2, 0, 1))),
        "b_sguT": f32(inp["b_sgu"][l].T),
        "sgu_norm": f32(inp["sgu_norm"][l][None]), "q_lora_norm": f32(inp["q_lora_norm"][l][None]),
        "kv_lora_norm": f32(inp["kv_lora_norm"][l][None]), "q_norm": f32(inp["q_norm"][l][None]), "k_norm": f32(inp["k_norm"][l][None]),
        "w_uq": f32(inp["w_uq"][l]), "w_ukv": f32(inp["w_ukv"][l]), "dftc": dftc,
    }
    maps = []
    for core in range(NCORES):
        b, tok0 = core // 4, (core % 4) * TPC
        cos, sin = const_rope(core)
        cT = np.stack([np.asarray(inp["c"][b]).reshape(8, 128).T, np.asarray(inp["c_ctx"]).reshape(8, 128).T], axis=-1)
        m = dict(shared)
        m.update({"xin": f32(np.concatenate([x_cur[b, tok0:tok0 + TPC], xc_cur[b]], 0)), "cT": f32(cT), "rope_cos": cos, "rope_sin": sin})
        maps.append(m)
    return maps


def const_fft(core):
    n1 = np.arange(64)
    ang = 2 * np.pi * np.outer(n1, n1) / 64.0
    C, S = np.cos(ang), np.sin(ang)
    WA = np.zeros((128, 128))
    WA[0:64, 0:64] = C; WA[64:128, 0:64] = S; WA[0:64, 64:128] = -S; WA[64:128, 64:128] = C
    k2_0 = 32 * (core % 4)
    n2 = np.arange(128)[:, None, None]
    k1 = np.arange(64)[None, :, None]
    k2 = (k2_0 + np.arange(32))[None, None, :]
    k = k1 + 64 * k2
    th = 2 * np.pi * ((n2 * k) % 8192) / 8192.0
    nrm = 1.0 / np.sqrt(8192.0 * 64.0)
    TC = np.stack([np.cos(th) * nrm, np.sin(th) * nrm], axis=1)
    n = (np.arange(2)[None, :, None] * 128 + np.arange(128)[:, None, None])
    kk = np.arange(256)[None, None, :]
    thc = 2 * np.pi * ((n * kk) % 256) / 256.0
    nrc = 1.0 / np.sqrt(256.0 * 64.0)
    TCc = np.stack([np.cos(thc) * nrc, np.sin(thc) * nrc], axis=2)
    return f32(WA), f32(TC), f32(TCc)


def const_moe():
    k = np.arange(128)
    utri = (k[:, None] < k[None, :]).astype(np.float32)
    ones = np.ones((128, 128), np.float32)
    iota = np.tile(np.arange(128, dtype=np.float32)[None, :], (128, 1))
    blk = ((k[:, None] // 8) == (k[None, :] // 8)).astype(np.float32)
    thrc = np.tile(np.array([[2 * 1024 - 8192, 2 * 32 - 256]], np.float32), (128, 1))
    return {"utri": utri, "ones": ones, "iota": iota, "blockones": blk, "thrc": thrc}


def inputs_fused(inp):
    ropes = [const_rope(v) for v in range(4)]
    ffts = [const_fft(v) for v in range(4)]
    shared = {
        "w_ada": f32(inp["w_ada"]), "b_ada": f32(inp["b_ada"]), "w_in": f32(inp["w_in"]),
        "w_sguT": f32(np.transpose(inp["w_sgu"], (0, 3, 1, 2))), "b_sguT": f32(np.transpose(inp["b_sgu"], (0, 2, 1))),
        "sgu_norm": f32(inp["sgu_norm"]), "q_lora_norm": f32(inp["q_lora_norm"]), "kv_lora_norm": f32(inp["kv_lora_norm"]),
        "q_norm": f32(inp["q_norm"]), "k_norm": f32(inp["k_norm"]), "w_uq": f32(inp["w_uq"]), "w_ukv": f32(inp["w_ukv"]),
        "dftc": const_dftc(), "rope_cos": f32(np.stack([r[0] for r in ropes])), "rope_sin": f32(np.stack([r[1] for r in ropes])),
        "WA": ffts[0][0], "TC": f32(np.stack([f[1] for f in ffts])), "TCc": ffts[0][2],
        "w_out": f32(inp["w_out"]), "w_router": f32(inp["w_router"]), "w_gate": f32(inp["w_gate"]), "w_up": f32(inp["w_up"]),
        "w_down": f32(inp["w_down"]),
    }
    shared.update(const_moe())
    maps = []
    for b in range(2):
        cT = np.stack([np.asarray(inp["c"][b]).reshape(8, 128).T, np.asarray(inp["c_ctx"]).reshape(8, 128).T], axis=-1)
        m = dict(shared)
        m.update({"x": f32(inp["x"][b]), "ctx": f32(inp["ctx"][b]), "cT": f32(cT)})
        maps.append(m)
    return maps


def kernel(**inputs):
    inp = {k: np.asarray(v) for k, v in inputs.items()}
    nc = build_fused()
    maps = inputs_fused(inp)
    res = run_bass_kernel_spmd(nc, maps, core_ids=[0, 1]).results
    return np.stack([np.asarray(res[b]["out"], dtype=np.float32) for b in range(2)])
```

```python
import numpy as np
import ml_dtypes
from concourse.bass_utils import run_bass_kernel_spmd

import contextlib
import numpy as np
import concourse.bass as bass
import concourse.mybir as mybir

F32 = mybir.dt.float32
BF16 = mybir.dt.bfloat16
I32 = mybir.dt.int32
ALU = mybir.AluOpType
AF = mybir.ActivationFunctionType
AX = mybir.AxisListType

ENGS = ("pe", "act", "dve", "pool", "sp")
NDMASEM = 8


class Op:
    __slots__ = ("eng", "fn", "dma", "deps", "needs_sig", "sig_idx", "sem_i", "sem_val", "idx", "prev_same_sem")

    def __init__(self, eng, fn, dma):
        self.eng = eng
        self.fn = fn
        self.dma = dma
        self.deps = []
        self.needs_sig = False
        self.sig_idx = 0
        self.sem_i = -1
        self.sem_val = 0
        self.prev_same_sem = None


class BufState:
    __slots__ = ("last_w", "readers")

    def __init__(self):
        self.last_w = None
        self.readers = []


class Prog:
    def __init__(self, nc, tag=""):
        self.nc = nc
        self.tag = tag
        self.ops = {e: [] for e in ENGS}
        self.st = {}
        self.es = contextlib.ExitStack()
        self.ndma = {e: 0 for e in ENGS}
        self.dma_last = {}
        self.dma_tot = {}
        self.all_dma = []
        self.nsb = 0
        self.fence = None
        self.psum_keys = set()

    def sb(self, shape, dtype, name=None):
        self.nsb += 1
        name = name or f"sb{self.nsb}"
        return self.es.enter_context(self.nc.sbuf_tensor("s_" + self.tag + name, list(shape), dtype))

    def ps(self, shape, dtype, name=None):
        self.nsb += 1
        name = name or f"ps{self.nsb}"
        self.psum_keys.add(name)
        return self.es.enter_context(self.nc.psum_tensor("p_" + self.tag + name, list(shape), dtype))

    def _state(self, k):
        s = self.st.get(k)
        if s is None:
            s = self.st[k] = BufState()
        return s

    def capture(self, fn, *args):
        self.cap = []
        fn(*args)
        c, self.cap = self.cap, None
        return c

    def replay_interleaved(self, lists):
        idx = [0] * len(lists)
        while True:
            best, bf = -1, 2.0
            for i, l in enumerate(lists):
                if idx[i] < len(l):
                    f = idx[i] / len(l)
                    if f < bf:
                        best, bf = i, f
            if best < 0:
                break
            self.op(*lists[best][idx[best]])
            idx[best] += 1

    def op(self, eng, fn, reads=(), writes=(), dma=False):
        if getattr(self, "cap", None) is not None:
            self.cap.append((eng, fn, tuple(reads), tuple(writes), dma))
            return None
        o = Op(eng, fn, dma)
        deps = []
        pr = [k for k in reads if k in self.psum_keys]
        if pr:
            reads = [k for k in reads if k not in self.psum_keys]
            writes = list(writes) + [k for k in pr if k not in writes]
        for k in reads:
            s = self._state(k)
            if s.last_w is not None:
                deps.append((s.last_w, "raw"))
        for k in writes:
            s = self._state(k)
            if s.last_w is not None:
                deps.append((s.last_w, "waw"))
            for r in s.readers:
                deps.append((r, "war"))
        if self.fence is not None:
            deps.append((self.fence, "raw"))
        seen = set()
        for d, kind in deps:
            if d is o or id(d) in seen:
                continue
            if (not o.dma) and (not d.dma) and d.eng == o.eng:
                if o.eng == "pe":
                    continue
            seen.add(id(d))
            o.deps.append(d)
        for k in reads:
            self._state(k).readers.append(o)
        for k in writes:
            s = self._state(k)
            s.last_w = o
            s.readers = []
        if dma:
            i = self.ndma[eng] % NDMASEM
            self.ndma[eng] += 1
            o.sem_i = i
            key = (eng, i)
            o.prev_same_sem = self.dma_last.get(key)
            o.sem_val = self.dma_tot.get(key, 0) + 16
            self.dma_tot[key] = o.sem_val
            self.dma_last[key] = o
            self.all_dma.append(o)
        self.ops[eng].append(o)
        return o

    def barrier(self, bar_tile):
        nc = self.nc
        lasts = []
        for e in ENGS:
            comp = [o for o in self.ops[e] if not o.dma]
            if comp:
                lasts.append(comp[-1])
        lasts.extend(self.dma_last.values())
        old = self.fence
        self.fence = None
        b = self.op("dve", lambda: nc.vector.memset(bar_tile, 0.0), (), ())
        for d in lasts:
            if d is not b and d not in b.deps and not (d.eng == "pe" and False):
                b.deps.append(d)
        if old is not None and old not in b.deps:
            b.deps.append(old)
        self.fence = b
        return b

    def I(self, eng, method, reads=(), writes=(), **kw):
        return self.op(eng, lambda: method(**kw), reads, writes)

    def dma(self, eng, out, in_, reads=(), writes=(), **kw):
        e = {"sp": self.nc.sync, "pool": self.nc.gpsimd, "act": self.nc.scalar}[eng]
        return self.op(eng, lambda: e.dma_start(out=out, in_=in_, **kw), reads, writes, dma=True)

    def emit(self):
        nc = self.nc
        for e in ENGS:
            for o in self.ops[e]:
                for d in o.deps:
                    if not d.dma:
                        d.needs_sig = True
        for e in ENGS:
            c = 0
            for o in self.ops[e]:
                if (not o.dma) and o.needs_sig:
                    c += 1
                    o.sig_idx = c
        es = self.es
        csem = {e: nc.alloc_semaphore(name=f"c_{e}_{self.tag}") for e in ENGS}
        dsem = {}
        for e in ENGS:
            if self.ndma[e]:
                for i in range(min(NDMASEM, self.ndma[e])):
                    dsem[(e, i)] = nc.alloc_semaphore(name=f"d_{e}{i}_{self.tag}")
        block = es.enter_context(nc.Block())
        prog = self

        def stream(ename, eng):
            waited = {}

            def wait(key, sem, val):
                if waited.get(key, 0) < val:
                    eng.wait_ge(sem, val)
                    waited[key] = val

            for o in prog.ops[ename]:
                for d in o.deps:
                    if d.dma:
                        wait(("d", d.eng, d.sem_i), dsem[(d.eng, d.sem_i)], d.sem_val)
                    else:
                        wait(("c", d.eng), csem[d.eng], d.sig_idx)
                if o.dma:
                    p = o.prev_same_sem
                    if p is not None:
                        wait(("d", ename, o.sem_i), dsem[(ename, o.sem_i)], p.sem_val)
                    inst = o.fn()
                    inst.then_inc(dsem[(ename, o.sem_i)], 16)
                else:
                    inst = o.fn()
                    if o.needs_sig:
                        inst.then_inc(csem[ename], 1)
            if ename == "sp":
                for key, tot in prog.dma_tot.items():
                    wait(("d",) + key, dsem[key], tot)
                for e2 in ENGS:
                    if e2 != "sp":
                        n = max([o.sig_idx for o in prog.ops[e2] if not o.dma] + [0])
                        if n:
                            wait(("c", e2), csem[e2], n)

        @block.tensor
        def _(eng):
            stream("pe", eng)

        @block.scalar
        def _(eng):
            stream("act", eng)

        @block.vector
        def _(eng):
            stream("dve", eng)

        @block.gpsimd
        def _(eng):
            stream("pool", eng)

        @block.sync
        def _(eng):
            stream("sp", eng)

    def close(self):
        self.es.close()


DBGZ = DBGQ = DBGT = 9
SEQREPLAY = 0

NT, NCT = 16, 2
TT = NT + NCT
EPS = 1e-6


class Rot:
    def __init__(self, P, n, shape, dtype, name):
        self.bufs = [(P.sb(shape, dtype, f"{name}{i}"), f"{name}{i}") for i in range(n)]
        self.i = 0

    def next(self):
        b = self.bufs[self.i % len(self.bufs)]
        self.i += 1
        return b


def make_ident(P, nc):
    identf = P.sb([128, 128], F32, "identf")
    ident = P.sb([128, 128], BF16, "ident")
    P.I("pool", nc.gpsimd.memset, [], ["identf"], ap=identf[:], constant=0.0)
    P.I("pool", nc.gpsimd.affine_select, ["identf"], ["identf"], out=identf[:], in_=identf[:], pattern=[[-1, 128]],
        compare_op=ALU.not_equal, fill=1.0, base=0, channel_multiplier=1)
    P.I("dve", nc.vector.tensor_copy, ["identf"], ["ident"], out=ident[:], in_=identf[:])
    P.identf = identf
    return ident


def build_A(last, ntiles=TT, dbg=99, nc=None, T=None, tag="", ada_src=None):
    if nc is None:
        nc = bass.Bass("TRN2", target_bir_lowering=False)

    def din(name, shape, dt=F32):
        if T is not None:
            assert tuple(T[name].shape) == tuple(shape), (name, T[name].shape, shape)
            return T[name]
        return nc.dram_tensor(name, list(shape), dt, kind="ExternalInput").ap()

    def dout(name, shape, dt=F32):
        if T is not None:
            assert tuple(T[name].shape) == tuple(shape), (name, T[name].shape, shape)
            return T[name]
        return nc.dram_tensor(name, list(shape), dt, kind="ExternalOutput").ap()

    xin = din("xin", [TT * 128, 1024])
    cT_d = din("cT", [128, 8, 2])
    w_ada_d = din("w_ada", [1024, 6144])
    b_ada_d = din("b_ada", [1, 6144])
    w_in_d = din("w_in", [1024, 1216])
    w_sguT_d = din("w_sguT", [128, 4, 128])
    b_sguT_d = din("b_sguT", [128, 4])
    sgun_d = din("sgu_norm", [1, 256])
    qln_d = din("q_lora_norm", [1, 256])
    kvln_d = din("kv_lora_norm", [1, 128])
    qn_d = din("q_norm", [1, 192])
    kn_d = din("k_norm", [1, 192])
    w_uq_d = din("w_uq", [256, 768])
    w_ukv_d = din("w_ukv", [128, 1024])
    dftc_d = din("dftc", [256, 512])
    rcos_d = din("rope_cos", [TT * 128, 64])
    rsin_d = din("rope_sin", [TT * 128, 64])

    ada_o = dout("ada", [2, 6144])
    yaT_o = dout("yaT", [256, TT * 128], BF16)
    Z_o = dout("Z", [TT * 128, 512], BF16)
    QT_o = dout("QT", [4, 192, TT * 128], BF16)
    KT_o = dout("KT", [4, 192, TT * 128], BF16)
    V_o = dout("V", [TT * 128, 512], BF16)

    P = Prog(nc, tag)
    I = P.I
    ident = make_ident(P, nc)

    def bc_load(name, src, n):
        t = P.sb([128, n], F32, name)
        P.dma("sp", t[:], src.partition_broadcast(128), writes=[name])
        return t

    sgun = bc_load("sgun", sgun_d[0:1, :], 256)
    qln = bc_load("qln", qln_d[0:1, :], 256)
    kvln = bc_load("kvln", kvln_d[0:1, :], 128)
    qnb = bc_load("qnb", qn_d[0:1, :], 192)
    knb = bc_load("knb", kn_d[0:1, :], 192)
    b_sguT = P.sb([128, 4], F32, "b_sguT")
    P.dma("sp", b_sguT[:], b_sguT_d, writes=["b_sguT"])
    rcos = P.sb([128, TT, 64], F32, "rcos")
    rsin = P.sb([128, TT, 64], F32, "rsin")
    P.dma("sp", rcos[:], rcos_d.rearrange("(t p) d -> p t d", p=128), writes=["rcos"])
    P.dma("sp", rsin[:], rsin_d.rearrange("(t p) d -> p t d", p=128), writes=["rsin"])
    cT = P.sb([128, 8, 2], F32, "cT")
    P.dma("sp", cT[:], cT_d, writes=["cT"])
    ps_ada = P.ps([128, 512], F32, "psA")
    if ada_src is None:
        scT = P.sb([128, 8, 2], BF16, "scT")
        I("act", nc.scalar.activation, ["cT"], ["scT"], out=scT[:], in_=cT[:], func=AF.Silu)
        mbr = Rot(P, 2, [2, 512], F32, "mblk")
        bar = Rot(P, 2, [2, 512], F32, "bablk")
        warot = Rot(P, 2, [128, 8, 512], BF16, "wa")
        w_ada_v = w_ada_d.rearrange("(k p) n -> p k n", p=128)
        for nb in range(12):
            wa, kwa = warot.next()
            for hk in range(2):
                P.dma("pool", wa[:, hk * 4:(hk + 1) * 4, :], w_ada_v[:, hk * 4:(hk + 1) * 4, nb * 512:(nb + 1) * 512], writes=[kwa + f"_{hk}"])
            for k in range(8):
                I("pe", nc.tensor.matmul, ["scT", kwa + f"_{k // 4}"], ["psA"], out=ps_ada[0:2, :], lhsT=scT[:, k, :], rhs=wa[:, k, :],
                  start=(k == 0), stop=(k == 7))
            mb, kmb = mbr.next()
            ba, kba = bar.next()
            P.dma("sp", ba[:], b_ada_d[0:1, nb * 512:(nb + 1) * 512].partition_broadcast(2), writes=[kba])
            I("dve", nc.vector.tensor_tensor, ["psA", kba], [kmb], out=mb[:], in0=ps_ada[0:2, :], in1=ba[:], op=ALU.add)
            P.dma("sp", ada_o[:, nb * 512:(nb + 1) * 512], mb[:], reads=[kmb], writes=["ada_d"])

    else:
        ada_o = ada_src
    mods = []
    for r in range(2):
        md = P.sb([128, 2048], F32, f"mod{r}")
        P.dma("sp", md[:], ada_o[r:r + 1, 0:2048].partition_broadcast(128), reads=["ada_d"], writes=[f"mod{r}"])
        I("dve", nc.vector.tensor_scalar_add, [f"mod{r}"], [f"mod{r}"], out=md[:, 1024:2048], in0=md[:, 1024:2048], scalar1=1.0)
        mods.append(md)

    w_in = P.sb([128, 8, 1216], BF16, "w_in")
    w_in_v = w_in_d.rearrange("(k p) n -> p k n", p=128)
    for k in range(0, 8, 2):
        P.dma("pool", w_in[:, k:k + 2, :], w_in_v[:, k:k + 2, :], writes=[f"w_in{k}"])
    w_in_keys = [f"w_in{k}" for k in range(0, 8, 2)]
    w_uq = P.sb([128, 2, 768], BF16, "w_uq")
    P.dma("pool", w_uq[:], w_uq_d.rearrange("(k p) n -> p k n", p=128), writes=["w_uq"])
    w_ukv = P.sb([128, 1024], BF16, "w_ukv")
    P.dma("pool", w_ukv[:], w_ukv_d, writes=["w_ukv"])
    dftc = P.sb([128, 2, 512], BF16, "dftc")
    P.dma("pool", dftc[:], dftc_d.rearrange("(k p) n -> p k n", p=128), writes=["dftc"])
    w_sguT = P.sb([128, 4, 128], BF16, "w_sguT")
    P.dma("pool", w_sguT[:], w_sguT_d, writes=["w_sguT"])

    psT = P.ps([128, 1024], BF16, "psT")
    psT2 = P.ps([128, 1024], BF16, "psT2")
    psT3 = P.ps([128, 1024], BF16, "psT3")
    px0 = P.ps([128, 512], F32, "px0")
    px1 = P.ps([128, 512], F32, "px1")
    px2 = P.ps([128, 512], F32, "px2")
    psB = P.ps([128, 512], F32, "psB")

    xr_ = Rot(P, 2, [128, 1024], F32, "x")
    tmpr = Rot(P, 2, [128, 1024], F32, "tmp")
    hr = Rot(P, 2, [128, 1024], BF16, "h")
    hTr = Rot(P, 2, [128, 1024], BF16, "hT")
    junkr = Rot(P, 4, [128, 1024], BF16, "junkA")
    str_ = Rot(P, 3, [128, 40], F32, "st")
    uvr = Rot(P, 2, [128, 512], F32, "uv")
    vbr = Rot(P, 2, [128, 256], BF16, "vb")
    yar = Rot(P, 2, [128, 256], BF16, "ya")
    yaTr = Rot(P, 2, [128, 2, 128], BF16, "yaT")
    pfr = Rot(P, 2, [128, 256], BF16, "pf")
    pfTr = Rot(P, 2, [128, 2, 128], BF16, "pfT")
    Zr = Rot(P, 2, [128, 512], BF16, "Z")
    cqr = Rot(P, 2, [128, 256], BF16, "cq")
    cqTr = Rot(P, 2, [128, 2, 128], BF16, "cqT")
    qfr = Rot(P, 2, [128, 4, 192], F32, "qf")
    qnr = Rot(P, 2, [128, 4, 192], F32, "qn")
    r1r = Rot(P, 2, [128, 4, 64], F32, "r1")
    r2r = Rot(P, 2, [128, 4, 64], F32, "r2")
    qbr = Rot(P, 2, [128, 4, 256], BF16, "qb")
    for (qb_, kqb_) in qbr.bufs:
        I("pool", nc.gpsimd.memset, [], [kqb_], ap=qb_[:], constant=0.0)
    QTnr = Rot(P, 3, [128, 4, 128], BF16, "QTn")
    QTrr = Rot(P, 3, [128, 4, 128], BF16, "QTr")
    ckvr = Rot(P, 2, [128, 128], BF16, "ckv")
    ckvTr = Rot(P, 2, [128, 128], BF16, "ckvT")
    Vbr = Rot(P, 2, [128, 4, 128], BF16, "Vb")

    def rms_rstd(src, ncols, n, rkeys, st, kst, c0, nm):
        k0, k1, k2 = f"{kst}_{nm}0", f"{kst}_{nm}1", f"{kst}_{nm}2"
        junkA, kj = junkr.next()
        I("act", nc.scalar.activation, rkeys, [kj, k0], out=junkA[:, 0:ncols], in_=src, func=AF.Square, accum_out=st[:, c0:c0 + 1])
        I("act", nc.scalar.activation, [k0], [k1], out=st[:, c0 + 1:c0 + 2], in_=st[:, c0:c0 + 1], func=AF.Sqrt, scale=1.0 / n, bias=EPS)
        I("dve", nc.vector.reciprocal, [k1], [k2], out=st[:, c0 + 2:c0 + 3], in_=st[:, c0 + 1:c0 + 2])
        return k2

    def head_norm_rope_store(t, qf, kqf, normb, knormb, st, kst, c0, nm, out_d, cbase):
        if DBGQ < 2:
            return
        ks = [f"{kst}_{nm}s{h}" for h in range(4)]
        for h in range(4):
            junkA, kj = junkr.next()
            I("act", nc.scalar.activation, [kqf], [kj, ks[h]], out=junkA[:, 0:192], in_=qf[:, h, :], func=AF.Square,
              accum_out=st[:, c0 + h:c0 + h + 1])
        kq1, kq2 = f"{kst}_{nm}q1", f"{kst}_{nm}q2"
        I("act", nc.scalar.activation, ks, [kq1], out=st[:, c0 + 4:c0 + 8], in_=st[:, c0:c0 + 4], func=AF.Sqrt, scale=1.0 / 192, bias=EPS)
        I("dve", nc.vector.reciprocal, [kq1], [kq2], out=st[:, c0 + 8:c0 + 12], in_=st[:, c0 + 4:c0 + 8])
        qn, kqn = qnr.next()
        for h in range(4):
            I("dve", nc.vector.scalar_tensor_tensor, [kqf, kq2, knormb], [kqn], out=qn[:, h, :], in0=qf[:, h, :],
              scalar=st[:, c0 + 8 + h:c0 + 9 + h], in1=normb[:], op0=ALU.mult, op1=ALU.mult)
        if DBGQ < 3:
            return
        r1, kr1 = r1r.next()
        r2, kr2 = r2r.next()
        qb, kqb = qbr.next()
        xrp = qn[:, :, 128:192]
        I("dve", nc.vector.tensor_tensor, [kqn, "rcos"], [kr1], out=r1[:], in0=xrp, in1=rcos[:, t, :].unsqueeze(1).to_broadcast([128, 4, 64]),
          op=ALU.mult)
        x5 = xrp.rearrange("p h (b s d) -> p h b s d", b=2, s=2)
        o5 = r2[:].rearrange("p h (b s d) -> p h b s d", b=2, s=2)
        s5 = rsin[:, t, :].rearrange("p (b s d) -> p b s d", b=2, s=2)
        for s_ in range(2):
            I("dve", nc.vector.tensor_tensor, [kqn, "rsin"], [kr2], out=o5[:, :, :, s_, :], in0=x5[:, :, :, 1 - s_, :],
              in1=s5[:, :, s_, :].unsqueeze(1).to_broadcast([128, 4, 2, 16]), op=ALU.mult)
        I("dve", nc.vector.tensor_tensor, [kr1, kr2], [kqb], out=qb[:, :, 128:192], in0=r1[:], in1=r2[:], op=ALU.add)
        I("act", nc.scalar.copy, [kqn], [kqb], out=qb[:, :, 0:128], in_=qn[:, :, 0:128])
        if DBGQ < 4:
            return
        QTn, kQTn = QTnr.next()
        QTr, kQTr = QTrr.next()
        for rnd in range(2):
            for hh in range(2):
                h = rnd * 2 + hh
                I("pe", nc.tensor.transpose, [kqb, "ident"], ["psT3"], out=psT3[:, cbase + hh * 128:cbase + (hh + 1) * 128], in_=qb[:, h, 0:128],
                  identity=ident[:])
                I("pe", nc.tensor.transpose, [kqb, "ident"], ["psT3"], out=psT3[:, cbase + 256 + hh * 128:cbase + 256 + (hh + 1) * 128],
                  in_=qb[:, h, 128:256], identity=ident[:])
            I("act", nc.scalar.copy, ["psT3"], [kQTn], out=QTn[:, rnd * 2:rnd * 2 + 2, :],
              in_=psT3[:, cbase:cbase + 256].rearrange("p (h t) -> p h t", h=2))
            I("dve", nc.vector.tensor_copy, ["psT3"], [kQTr], out=QTr[0:64, rnd * 2:rnd * 2 + 2, :],
              in_=psT3[0:64, cbase + 256:cbase + 512].rearrange("p (h t) -> p h t", h=2))
        P.dma("sp", out_d[:, 0:128, t * 128:(t + 1) * 128].rearrange("h d t -> d h t"), QTn[:], reads=[kQTn])
        P.dma("sp", out_d[:, 128:192, t * 128:(t + 1) * 128].rearrange("h d t -> d h t"), QTr[0:64, :, :], reads=[kQTr])

    if last:
        zb = P.sb([128, 4, 128], BF16, "zb")
        I("pool", nc.gpsimd.memset, [], ["zb"], ap=zb[:], constant=0.0)
        zbf = zb[:].rearrange("p h t -> p (h t)")
        for t in range(NT, ntiles):
            P.dma("sp", yaT_o[:, t * 128:(t + 1) * 128].rearrange("(c p) t -> p c t", p=128), zb[:, 0:2, :], reads=["zb"])
            P.dma("sp", Z_o[t * 128:(t + 1) * 128, :], zbf, reads=["zb"])
            P.dma("sp", QT_o[:, 0:128, t * 128:(t + 1) * 128].rearrange("h d t -> d h t"), zb[:], reads=["zb"])
            P.dma("sp", QT_o[:, 128:192, t * 128:(t + 1) * 128].rearrange("h d t -> d h t"), zb[0:64, :, :], reads=["zb"])
    def front(t, S):
        is_ctx = t >= NT
        md = mods[1 if is_ctx else 0]
        kmd = f"mod{1 if is_ctx else 0}"
        x_t, kx = xr_.next()
        P.dma("sp", x_t[:], xin[t * 128:(t + 1) * 128, :], writes=[kx])
        st, kst = str_.next()
        S["st"], S["kst"] = st, kst
        krs = rms_rstd(x_t[:], 1024, 1024, [kx], st, kst, 0, "n1")
        tmp, ktmp = tmpr.next()
        h_t, kh = hr.next()
        I("dve", nc.vector.scalar_tensor_tensor, [kx, krs, kmd], [ktmp], out=tmp[:], in0=x_t[:], scalar=st[:, 2:3], in1=md[:, 1024:2048],
          op0=ALU.mult, op1=ALU.mult)
        I("dve", nc.vector.tensor_tensor, [ktmp, kmd], [kh], out=h_t[:], in0=tmp[:], in1=md[:, 0:1024], op=ALU.add)
        for k in range(8):
            I("pe", nc.tensor.transpose, [kh, "ident"], ["psT"], out=psT[:, k * 128:(k + 1) * 128], in_=h_t[:, k * 128:(k + 1) * 128],
              identity=ident[:])
        hT, khT = hTr.next()
        I("act", nc.scalar.copy, ["psT"], [khT], out=hT[:], in_=psT[:])
        S["hT"], S["khT"] = hT, khT

    def pxmm(t, S):
        kv_only = (t >= NT) and last
        hT, khT = S["hT"], S["khT"]
        blocks = [(px0, "px0", 0, 512), (px1, "px1", 512, 1024), (px2, "px2", 1024, 1216)]
        for (pb, kpb, c0, c1) in blocks:
            if kv_only and kpb != "px2":
                continue
            for k in range(8):
                I("pe", nc.tensor.matmul, [khT, w_in_keys[k // 2]], [kpb], out=pb[:, 0:c1 - c0], lhsT=hT[:, k * 128:(k + 1) * 128],
                  rhs=w_in[:, k, c0:c1], start=(k == 0), stop=(k == 7))
        if not kv_only:
            uv, kuv = uvr.next()
            I("act", nc.scalar.activation, ["px0"], [kuv], out=uv[:], in_=px0[:], func=AF.Gelu_apprx_tanh)
            S["uv"], S["kuv"] = uv, kuv
            p1, kp1 = p1r.next()
            I("act", nc.scalar.copy, ["px1"], [kp1], out=p1[:], in_=px1[:])
            S["p1"], S["kp1"] = p1, kp1
        p2, kp2 = p2r.next()
        I("dve", nc.vector.tensor_copy, ["px2"], [kp2], out=p2[:], in_=px2[:, 0:192])
        S["p2"], S["kp2"] = p2, kp2

    def sgu(t, S):
        st, kst = S["st"], S["kst"]
        uv, kuv = S["uv"], S["kuv"]
        krv = rms_rstd(uv[:, 256:512], 256, 256, [kuv], st, kst, 3, "v")
        vb, kvb = vbr.next()
        I("dve", nc.vector.scalar_tensor_tensor, [kuv, krv, "sgun"], [kvb], out=vb[:], in0=uv[:, 256:512], scalar=st[:, 5:6], in1=sgun[:],
          op0=ALU.mult, op1=ALU.mult)
        for h in range(4):
            I("pe", nc.tensor.matmul, [kvb, "w_sguT"], ["psA"], out=ps_ada[:, h * 64:(h + 1) * 64], lhsT=w_sguT[:, h, :],
              rhs=vb[:, h * 64:(h + 1) * 64], start=True, stop=True)
        ya, kya = yar.next()
        for h in range(4):
            I("dve", nc.vector.scalar_tensor_tensor, ["psA", "b_sguT", kuv], [kya], out=ya[:, h * 64:(h + 1) * 64],
              in0=ps_ada[:, h * 64:(h + 1) * 64], scalar=b_sguT[:, h:h + 1], in1=uv[:, h * 64:(h + 1) * 64], op0=ALU.add, op1=ALU.mult)
        for c in range(2):
            I("pe", nc.tensor.transpose, [kya, "ident"], ["psT2"], out=psT2[:, c * 128:(c + 1) * 128], in_=ya[:, c * 128:(c + 1) * 128],
              identity=ident[:])
        yaT, kyaT = yaTr.next()
        I("act", nc.scalar.copy, ["psT2"], [kyaT], out=yaT[:], in_=psT2[:, 0:256].rearrange("p (c t) -> p c t", c=2))
        P.dma("sp", yaT_o[:, t * 128:(t + 1) * 128].rearrange("(c p) t -> p c t", p=128), yaT[:], reads=[kyaT])

    def zpart(t, S):
        p1, kp1 = S["p1"], S["kp1"]
        pf, kpf = pfr.next()
        I("dve", nc.vector.tensor_copy, [kp1], [kpf], out=pf[:], in_=p1[:, 0:256])
        for c in range(2):
            I("pe", nc.tensor.transpose, [kpf, "ident"], ["psT2"], out=psT2[:, 256 + c * 128:256 + (c + 1) * 128],
              in_=pf[:, c * 128:(c + 1) * 128], identity=ident[:])
        pfT, kpfT = pfTr.next()
        I("act", nc.scalar.copy, ["psT2"], [kpfT], out=pfT[:], in_=psT2[:, 256:512].rearrange("p (c t) -> p c t", c=2))
        for c in range(2):
            I("pe", nc.tensor.matmul, [kpfT, "dftc"], ["psB"], out=psB[:], lhsT=pfT[:, c, :], rhs=dftc[:, c, :], start=(c == 0), stop=(c == 1))
        Zt, kZ = Zr.next()
        I("act", nc.scalar.copy, ["psB"], [kZ], out=Zt[:], in_=psB[:])
        P.dma("sp", Z_o[t * 128:(t + 1) * 128, :], Zt[:], reads=[kZ])

    def qpart(t, S):
        st, kst = S["st"], S["kst"]
        p1, kp1 = S["p1"], S["kp1"]
        krq = rms_rstd(p1[:, 256:512], 256, 256, [kp1], st, kst, 6, "q")
        cq, kcq = cqr.next()
        I("dve", nc.vector.scalar_tensor_tensor, [kp1, krq, "qln"], [kcq], out=cq[:], in0=p1[:, 256:512], scalar=st[:, 8:9], in1=qln[:],
          op0=ALU.mult, op1=ALU.mult)
        for c in range(2):
            I("pe", nc.tensor.transpose, [kcq, "ident"], ["psT2"], out=psT2[:, 512 + c * 128:512 + (c + 1) * 128],
              in_=cq[:, c * 128:(c + 1) * 128], identity=ident[:])
        cqT, kcqT = cqTr.next()
        I("act", nc.scalar.copy, ["psT2"], [kcqT], out=cqT[:], in_=psT2[:, 512:768].rearrange("p (c t) -> p c t", c=2))
        for c in range(2):
            I("pe", nc.tensor.matmul, [kcqT, "w_uq"], ["px1"], out=px1[:], lhsT=cqT[:, c, :], rhs=w_uq[:, c, 0:512], start=(c == 0), stop=(c == 1))
        for c in range(2):
            I("pe", nc.tensor.matmul, [kcqT, "w_uq"], ["px2"], out=px2[:, 0:256], lhsT=cqT[:, c, :], rhs=w_uq[:, c, 512:768],
              start=(c == 0), stop=(c == 1))
        qf, kqf = qfr.next()
        qf2 = qf[:].rearrange("p h d -> p (h d)")
        I("act", nc.scalar.copy, ["px1"], [kqf], out=qf2[:, 0:512], in_=px1[:])
        I("act", nc.scalar.copy, ["px2"], [kqf], out=qf2[:, 512:768], in_=px2[:, 0:256])
        head_norm_rope_store(t, qf, kqf, qnb, "qnb", st, kst, 9, "qh", QT_o, 0)

    def kvpart(t, S):
        st, kst = S["st"], S["kst"]
        p2, kp2 = S["p2"], S["kp2"]
        krk = rms_rstd(p2[:, 0:128], 128, 128, [kp2], st, kst, 21, "kv")
        ckv, kckv = ckvr.next()
        I("dve", nc.vector.scalar_tensor_tensor, [kp2, krk, "kvln"], [kckv], out=ckv[:], in0=p2[:, 0:128], scalar=st[:, 23:24], in1=kvln[:],
          op0=ALU.mult, op1=ALU.mult)
        I("pe", nc.tensor.transpose, [kckv, "ident"], ["psT2"], out=psT2[:, 768:896], in_=ckv[:], identity=ident[:])
        ckvT, kckvT = ckvTr.next()
        I("act", nc.scalar.copy, ["psT2"], [kckvT], out=ckvT[:], in_=psT2[:, 768:896])
        kf, kkf = kfr.next()
        Vb, kVb = Vbr.next()
        I("act", nc.scalar.copy, [kp2], [kkf], out=kf[:, :, 128:192], in_=p2[:, 128:192].unsqueeze(1).to_broadcast([128, 4, 64]))
        for j, (pb, kpb) in enumerate([(px0, "px0"), (px0, "px0")]):
            I("pe", nc.tensor.matmul, [kckvT, "w_ukv"], [kpb], out=pb[:], lhsT=ckvT[:], rhs=w_ukv[:, j * 512:(j + 1) * 512], start=True, stop=True)
            pv = pb[:].rearrange("p (h s d) -> p h s d", h=2, s=2)
            I("act", nc.scalar.copy, [kpb], [kVb], out=Vb[:, 2 * j:2 * j + 2, :], in_=pv[:, :, 1, :])
            I("dve", nc.vector.tensor_copy, [kpb], [kkf], out=kf[:, 2 * j:2 * j + 2, 0:128], in_=pv[:, :, 0, :])
        P.dma("sp", V_o[t * 128:(t + 1) * 128, :], Vb[:].rearrange("p h d -> p (h d)"), reads=[kVb])
        head_norm_rope_store(t, kf, kkf, knb, "knb", st, kst, 24, "kh", KT_o, 512)

    p1r = Rot(P, 2, [128, 512], F32, "p1s")
    p2r = Rot(P, 2, [128, 192], F32, "p2s")
    kfr = Rot(P, 2, [128, 4, 192], F32, "kf")
    states = [dict() for _ in range(ntiles + 1)]
    if ntiles:
        front(0, states[0])
    for t in range(ntiles):
        S = states[t]
        kv_only = (t >= NT) and last
        pxmm(t, S)
        lists = []
        if not kv_only:
            lists += [P.capture(sgu, t, S), P.capture(zpart, t, S), P.capture(qpart, t, S)]
        lists.append(P.capture(kvpart, t, S))
        if t + 1 < ntiles:
            lists.append(P.capture(front, t + 1, states[t + 1]))
        P.replay_interleaved(lists) if not SEQREPLAY else [P.op(*o) for l in lists for o in l]
    P.emit()
    P.close()
    return nc


def build_F(last, nc=None, T=None, tag="", skip_stage1=False):
    if nc is None:
        nc = bass.Bass("TRN2", target_bir_lowering=False)

    def din(name, shape, dt=F32):
        if T is not None:
            assert tuple(T[name].shape) == tuple(shape), (name, T[name].shape, shape)
            return T[name]
        return nc.dram_tensor(name, list(shape), dt, kind="ExternalInput").ap()

    Z_d = din("Z_all", [8192, 512], BF16)
    Zc_d = din("Zc", [256, 512], BF16)
    WA_d = din("WA", [128, 128])
    TC_d = din("TC", [128, 2, 64, 32])
    TCc_d = din("TCc", [128, 2, 2, 256])
    ybT_o = T["ybT"] if T is not None else nc.dram_tensor("ybT", [256, TT * 128], BF16, kind="ExternalOutput").ap()
    A_d = T["A_scr"] if (T is not None and "A_scr" in T) else nc.dram_tensor(tag + "A_scr", [128, 128, 256], BF16, kind="Internal").ap()

    P = Prog(nc, tag)
    I = P.I
    WA = P.sb([128, 128], BF16, "WA")
    P.dma("pool", WA[:], WA_d, writes=["WA"])
    TC = P.sb([128, 2, 64, 32], BF16, "TC")
    P.dma("pool", TC[:], TC_d, writes=["TC"])
    fb = [P.ps([128, 512], F32, f"f{i}") for i in range(2)]
    yb = [P.ps([128, 512], F32, f"yb{i}") for i in range(2)]
    pr = P.ps([128, 512], F32, "pr")
    zar = Rot(P, 2, [128, 16, 256], BF16, "za")
    aor = Rot(P, 2, [128, 16, 256], BF16, "ao")
    Zv = Z_d.rearrange("(n1 n2) (ri c) -> ri n1 n2 c", n2=128, ri=2)
    cnt = 0
    for ch in range(0 if skip_stage1 else 8):
        za, kza = zar.next()
        for ri in range(2):
            P.dma("sp", za[ri * 64:(ri + 1) * 64, :, :], Zv[ri, :, ch * 16:(ch + 1) * 16, :], writes=[f"{kza}_{ri}"])
        ao, kao = aor.next()
        for j in range(8):
            bk = fb[j % 2]
            I("pe", nc.tensor.matmul, [f"{kza}_0", f"{kza}_1", "WA"], [f"f{j % 2}"], out=bk[:], lhsT=WA[:],
              rhs=za[:, 2 * j:2 * j + 2, :].rearrange("p a c -> p (a c)"), start=True, stop=True)
            dst = ao[:, 2 * j:2 * j + 2, :].rearrange("p a c -> p (a c)")
            if cnt % 2 == 0:
                I("act", nc.scalar.copy, [f"f{j % 2}"], [kao], out=dst, in_=bk[:])
            else:
                I("dve", nc.vector.tensor_copy, [f"f{j % 2}"], [kao], out=dst, in_=bk[:])
            cnt += 1
        P.dma("sp", A_d[:, ch * 16:(ch + 1) * 16, :], ao[:], reads=[kao], writes=["A_d"])
    ac = P.sb([128, 128, 128], BF16, "ac")
    ybs = P.sb([128, 2, TT * 128], BF16, "ybs")
    A_v = A_d.rearrange("q n c -> n q c")
    for half in range(2):
        for qq in range(4):
            P.dma("sp", ac[:, qq * 32:(qq + 1) * 32, :], A_v[:, qq * 32:(qq + 1) * 32, half * 128:(half + 1) * 128], reads=["A_d"],
                  writes=[f"ac{qq}"])
        ackeys = [f"ac{qq}" for qq in range(4)]
        for bk in range(4):
            ybk = yb[bk % 2]
            yv = ybk[:].rearrange("p (k2 k1) -> p k2 k1", k1=64)
            for k1 in range(64):
                for ri in range(2):
                    I("pe", nc.tensor.matmul, ackeys + ["TC"], [f"yb{bk % 2}"], out=yv[:, :, k1], lhsT=ac[:, ri * 64 + k1, :],
                      rhs=TC[:, ri, k1, bk * 8:(bk + 1) * 8], start=(ri == 0), stop=(ri == 1))
            dst = ybs[:, half, bk * 512:(bk + 1) * 512]
            if bk % 2 == 0:
                I("act", nc.scalar.copy, [f"yb{bk % 2}"], ["ybs"], out=dst, in_=ybk[:])
            else:
                I("dve", nc.vector.tensor_copy, [f"yb{bk % 2}"], ["ybs"], out=dst, in_=ybk[:])
    if not last:
        Zc = P.sb([128, 2, 512], BF16, "Zc")
        P.dma("sp", Zc[:], Zc_d.rearrange("(t p) c -> p t c", p=128), writes=["Zc"])
        TCc = P.sb([128, 2, 2, 256], BF16, "TCc")
        P.dma("pool", TCc[:], TCc_d, writes=["TCc"])
        for half in range(2):
            i = 0
            for nt in range(2):
                for ri in range(2):
                    I("pe", nc.tensor.matmul, ["Zc", "TCc"], ["pr"], out=pr[:, 0:256], lhsT=Zc[:, nt, ri * 256 + half * 128:ri * 256 + (half + 1) * 128],
                      rhs=TCc[:, nt, ri, :], start=(i == 0), stop=(i == 3))
                    i += 1
            I("act", nc.scalar.copy, ["pr"], ["ybs"], out=ybs[:, half, NT * 128:TT * 128], in_=pr[:, 0:256])
    if last:
        I("pool", nc.gpsimd.memset, [], ["ybs"], ap=ybs[:, :, NT * 128:TT * 128], constant=0.0)
    ncols = TT * 128
    P.dma("sp", ybT_o[:, 0:ncols].rearrange("(c p) t -> p c t", p=128), ybs[:, :, 0:ncols], reads=["ybs"])
    P.emit()
    P.close()
    return nc


NKT = 66
SCALE = 192 ** -0.5


def build_B(last, nheads=4, nblocks=None, full=True, nc=None, T=None, tag=""):
    if nc is None:
        nc = bass.Bass("TRN2", target_bir_lowering=False)

    def din(name, shape, dt=F32):
        if T is not None:
            assert tuple(T[name].shape) == tuple(shape), (name, T[name].shape, shape)
            return T[name]
        return nc.dram_tensor(name, list(shape), dt, kind="ExternalInput").ap()

    def dout(name, shape, dt=F32):
        if T is not None:
            assert tuple(T[name].shape) == tuple(shape), (name, T[name].shape, shape)
            return T[name]
        return nc.dram_tensor(name, list(shape), dt, kind="ExternalOutput").ap()

    QT_d = din("QT", [4, 192, TT * 128], BF16)
    KT_d = din("KT_all", [4, 192, NKT * 128], BF16)
    V_d = din("V_all", [NKT * 128, 512], BF16)
    EPS = 1e-6
    if full:
        xin = din("xin", [TT * 128, 1024])
        ada_d = din("ada", [2, 6144])
        yaT_d = din("yaT", [256, TT * 128], BF16)
        ybT_d = din("ybT", [256, TT * 128], BF16)
        w_out_d = din("w_out", [1024, 1024])
        w_r_d = din("w_router", [1024, 16])
        x1_o = dout("x1", [TT * 128, 1024])
        h2_o = dout("h2", [TT * 128, 1024], BF16)
        aff_o = dout("aff", [TT * 128, 16])
        affT_o = dout("affT", [16, TT * 128])
    else:
        yc_o = dout("yc", [TT * 128, 512], BF16)

    P = Prog(nc, tag)
    I = P.I
    NCH = 6
    CT = NKT // NCH
    ktn = P.sb([128, NKT * 128], BF16, "ktn")
    ktr = P.sb([128, NKT * 128], BF16, "ktr")
    vh = P.sb([128, NKT, 129], BF16, "vh")
    I("pool", nc.gpsimd.memset, [], [f"vh{c}" for c in range(NCH)], ap=vh[:, :, 128:129], constant=1.0)
    I("pool", nc.gpsimd.memset, [], [f"ktr{c}" for c in range(NCH)], ap=ktr[64:128, :], constant=0.0)
    qnr = Rot(P, 2, [128, TT * 128], BF16, "qtn")
    qrr = Rot(P, 2, [128, TT * 128], BF16, "qtr")
    for (qb_, kqb_) in qrr.bufs:
        I("pool", nc.gpsimd.memset, [], [kqb_], ap=qb_[64:128, :], constant=0.0)
    ptr = Rot(P, 3, [128, 512], BF16, "pt")
    ycr = Rot(P, 3, [128, 128], BF16, "yct")
    rcr = Rot(P, 4, [128, 1], F32, "rc")
    sbank = [P.ps([128, 512], F32, f"sb{i}") for i in range(2)]
    obank = [P.ps([128, 512], F32, f"ob{i}") for i in range(4)]
    V_v = V_d.rearrange("(t p) (h d) -> p t h d", p=128, h=4)
    ntile = NT if last else TT
    if full:
        ident = make_ident(P, nc)
        psTb = P.ps([128, 1024], BF16, "psTb")
        tb = P.ps([128, 512], F32, "tb")
        mixT = P.sb([128, 8, TT * 128], BF16, "mixT")
        mkeys = [f"mix{t}" for t in range(TT)]
        def load_mix():
            P.dma("sp", mixT[:, 0:2, :], yaT_d.rearrange("(c p) t -> p c t", p=128), writes=mkeys)
            P.dma("sp", mixT[:, 2:4, :], ybT_d.rearrange("(c p) t -> p c t", p=128), writes=mkeys)
        w_out = P.sb([128, 8, 1024], BF16, "w_out")
        w_out_v = w_out_d.rearrange("(k p) n -> p k n", p=128)
        for k in range(0, 8, 2):
            P.dma("pool", w_out[:, k:k + 2, :], w_out_v[:, k:k + 2, :], writes=[f"w_out{k}"])
        wokeys = [f"w_out{k}" for k in range(0, 8, 2)]
        w_r = P.sb([128, 8, 16], BF16, "w_r")
        P.dma("pool", w_r[:], w_r_d.rearrange("(k p) n -> p k n", p=128), writes=["w_r"])
        mods = []
        for r in range(2):
            md = P.sb([128, 3072], F32, f"modB{r}")
            P.dma("sp", md[:], ada_d[r:r + 1, 2048:5120].partition_broadcast(128), writes=[f"modB{r}"])
            I("dve", nc.vector.tensor_scalar_add, [f"modB{r}"], [f"modB{r}"], out=md[:, 2048:3072], in0=md[:, 2048:3072], scalar1=1.0)
            mods.append(md)

    blocks = [(qb * 512, 512, NKT) for qb in range(4)]
    if not last:
        blocks.insert(0, (NT * 128, NCT * 128, NCT))
    if nblocks is not None:
        blocks = blocks[:nblocks]

    if full:
        xr_ = Rot(P, 2, [128, 1024], F32, "x")
        tmpr = Rot(P, 2, [128, 1024], F32, "tmp")
        x1r = Rot(P, 2, [128, 1024], F32, "x1")
        h2r = Rot(P, 2, [128, 1024], BF16, "h2")
        h2Tr = Rot(P, 2, [128, 1024], BF16, "h2T")
        junkr = Rot(P, 2, [128, 1024], BF16, "junk")
        str_ = Rot(P, 3, [128, 8], F32, "st")
        exr = Rot(P, 2, [128, 16], F32, "ex")
        afr = Rot(P, 2, [128, 16], F32, "af")
        aTr = Rot(P, 2, [16, 128], F32, "aT")

        def tail_tile(t):
            r = 1 if t >= NT else 0
            md, kmd = mods[r], f"modB{r}"
            x_t, kx = xr_.next()
            P.dma("sp", x_t[:], xin[t * 128:(t + 1) * 128, :], writes=[kx])
            tmp, ktmp = tmpr.next()
            x1, kx1 = x1r.next()
            for nb in range(2):
                cs = slice(nb * 512, (nb + 1) * 512)
                for k in range(8):
                    I("pe", nc.tensor.matmul, [f"mix{t}", wokeys[k // 2]], ["tb"], out=tb[:], lhsT=mixT[:, k, t * 128:(t + 1) * 128],
                      rhs=w_out[:, k, nb * 512:(nb + 1) * 512], start=(k == 0), stop=(k == 7))
                I("dve", nc.vector.tensor_tensor, ["tb", kmd], [ktmp], out=tmp[:, cs], in0=tb[:], in1=md[:, cs], op=ALU.mult)
                I("dve", nc.vector.tensor_tensor, [ktmp, kx], [kx1], out=x1[:, cs], in0=tmp[:, cs], in1=x_t[:, cs], op=ALU.add)
            P.dma("sp", x1_o[t * 128:(t + 1) * 128, :], x1[:], reads=[kx1])
            st, kst = str_.next()
            junk, kj = junkr.next()
            I("act", nc.scalar.activation, [kx1], [kj, kst + "a"], out=junk[:], in_=x1[:], func=AF.Square, accum_out=st[:, 0:1])
            I("act", nc.scalar.activation, [kst + "a"], [kst + "b"], out=st[:, 1:2], in_=st[:, 0:1], func=AF.Sqrt, scale=1.0 / 1024, bias=EPS)
            I("dve", nc.vector.reciprocal, [kst + "b"], [kst + "c"], out=st[:, 2:3], in_=st[:, 1:2])
            tmp2, ktmp2 = tmpr.next()
            h2, kh2 = h2r.next()
            I("dve", nc.vector.scalar_tensor_tensor, [kx1, kst + "c", kmd], [ktmp2], out=tmp2[:], in0=x1[:], scalar=st[:, 2:3], in1=md[:, 2048:3072],
              op0=ALU.mult, op1=ALU.mult)
            I("dve", nc.vector.tensor_tensor, [ktmp2, kmd], [kh2], out=h2[:], in0=tmp2[:], in1=md[:, 1024:2048], op=ALU.add)
            P.dma("sp", h2_o[t * 128:(t + 1) * 128, :], h2[:], reads=[kh2])
            h2T, kh2T = h2Tr.next()
            for hf in range(2):
                for k in range(4):
                    kk = hf * 4 + k
                    I("pe", nc.tensor.transpose, [kh2, "ident"], ["psTb"], out=psTb[:, 512 + k * 128:512 + (k + 1) * 128], in_=h2[:, kk * 128:(kk + 1) * 128],
                      identity=ident[:])
                I("act", nc.scalar.copy, ["psTb"], [kh2T], out=h2T[:, hf * 512:(hf + 1) * 512], in_=psTb[:, 512:1024])
            for k in range(8):
                I("pe", nc.tensor.matmul, [kh2T, "w_r"], ["tb"], out=tb[:, 0:16], lhsT=h2T[:, k * 128:(k + 1) * 128], rhs=w_r[:, k, :],
                  start=(k == 0), stop=(k == 7))
            ex, kex = exr.next()
            af, kaf = afr.next()
            I("dve", nc.vector.reduce_max, ["tb"], [kst + "d"], out=st[:, 3:4], in_=tb[:, 0:16], axis=AX.X)
            I("dve", nc.vector.tensor_scalar, [kst + "d"], [kst + "e"], out=st[:, 4:5], in0=st[:, 3:4], scalar1=-1.0, scalar2=None, op0=ALU.mult)
            I("act", nc.scalar.activation, ["tb", kst + "e"], [kex, kst + "f"], out=ex[:], in_=tb[:, 0:16], func=AF.Exp, bias=st[:, 4:5],
              scale=1.0, accum_out=st[:, 5:6])
            I("dve", nc.vector.reciprocal, [kst + "f"], [kst + "g"], out=st[:, 6:7], in_=st[:, 5:6])
            I("dve", nc.vector.tensor_scalar, [kex, kst + "g"], [kaf], out=af[:], in0=ex[:], scalar1=st[:, 6:7], scalar2=None, op0=ALU.mult)
            P.dma("sp", aff_o[t * 128:(t + 1) * 128, :], af[:], reads=[kaf])
            I("pe", nc.tensor.transpose, [kaf, "identf"], ["tb"], out=tb[0:16, 128:256], in_=af[:], identity=P.identf[:])
            aT, kaT = aTr.next()
            I("act", nc.scalar.copy, ["tb"], [kaT], out=aT[:], in_=tb[0:16, 128:256])
            P.dma("sp", affT_o[:, t * 128:(t + 1) * 128], aT[:], reads=[kaT])


    for h in range(nheads):
        qtn, kqn = qnr.next()
        qtr, kqr = qrr.next()
        P.dma("sp", qtn[:], QT_d[h, 0:128, :], writes=[kqn])
        P.dma("sp", qtr[0:64, :], QT_d[h, 128:192, :], writes=[kqr])
        for c in range(NCH):
            cs = slice(c * CT * 128, (c + 1) * CT * 128)
            P.dma("sp", ktn[:, cs], KT_d[h, 0:128, cs], writes=[f"ktn{c}"])
            P.dma("sp", ktr[0:64, cs], KT_d[h, 128:192, cs], writes=[f"ktr{c}"])
            P.dma("sp", vh[:, c * CT:(c + 1) * CT, 0:128], V_v[:, c * CT:(c + 1) * CT, h, :], writes=[f"vh{c}"])
        if full and h == 0:
            load_mix()
        def attn_block(q0, qn, nkt):
            nsub = qn // 128
            pts = {}

            def S(kt):
                c = kt // CT
                sbk = sbank[kt % 2]
                ks = slice(kt * 128, (kt + 1) * 128)
                I("pe", nc.tensor.matmul, [f"ktn{c}", kqn], [f"sb{kt % 2}"], out=sbk[:, 0:qn], lhsT=ktn[:, ks], rhs=qtn[:, q0:q0 + qn],
                  start=True, stop=False)
                I("pe", nc.tensor.matmul, [f"ktr{c}", kqr], [f"sb{kt % 2}"], out=sbk[:, 0:qn], lhsT=ktr[:, ks], rhs=qtr[:, q0:q0 + qn],
                  start=False, stop=True)

            def E(kt):
                pt, kpt = ptr.next()
                pts[kt] = (pt, kpt)
                I("act", nc.scalar.activation, [f"sb{kt % 2}"], [kpt], out=pt[:, 0:qn], in_=sbank[kt % 2][:, 0:qn], func=AF.Exp, scale=SCALE)

            def PV(kt):
                c = kt // CT
                pt, kpt = pts.pop(kt)
                for qs in range(nsub):
                    ob = obank[qs]
                    off = 0
                    I("pe", nc.tensor.matmul, [kpt, f"vh{c}"], [f"ob{qs}"], out=ob[:, off:off + 129], lhsT=pt[:, qs * 128:(qs + 1) * 128],
                      rhs=vh[:, kt, :], start=(kt == 0), stop=(kt == nkt - 1))

            S(0)
            for kt in range(nkt):
                E(kt)
                if kt + 1 < nkt:
                    S(kt + 1)
                PV(kt)
            for qs in range(nsub):
                ob = obank[qs]
                off = 0
                rc, krc = rcr.next()
                yct, kyct = ycr.next()
                I("dve", nc.vector.reciprocal, [f"ob{qs}"], [krc], out=rc[:], in_=ob[:, off + 128:off + 129])
                I("dve", nc.vector.tensor_scalar, [f"ob{qs}", krc], [kyct], out=yct[:], in0=ob[:, off:off + 128], scalar1=rc[:, 0:1],
                  scalar2=None, op0=ALU.mult)
                r0 = q0 + qs * 128
                if full:
                    I("pe", nc.tensor.transpose, [kyct, "ident"], ["psTb"], out=psTb[:, 0:128], in_=yct[:], identity=ident[:])
                    I("act", nc.scalar.copy, ["psTb"], [f"mix{r0 // 128}"], out=mixT[:, 4 + h, r0:r0 + 128], in_=psTb[:, 0:128])
                else:
                    P.dma("sp", yc_o[r0:r0 + 128, h * 128:(h + 1) * 128], yct[:], reads=[kyct])

        for bi, (q0, qn, nkt) in enumerate(blocks):
            if full and h == nheads - 1 and bi >= 1:
                pq0, pqn, _ = blocks[bi - 1]

                def tails(pq0=pq0, pqn=pqn):
                    for tt_ in range(pq0 // 128, (pq0 + pqn) // 128):
                        tail_tile(tt_)
                P.replay_interleaved([P.capture(attn_block, q0, qn, nkt), P.capture(tails)])
            else:
                attn_block(q0, qn, nkt)
    if full:
        pq0, pqn, _ = blocks[-1]
        for tt_ in range(pq0 // 128, (pq0 + pqn) // 128):
            tail_tile(tt_)
        if last:
            zt, kzt = tmpr.next()
            zh, kzh = h2r.next()
            I("pool", nc.gpsimd.memset, [], [kzt], ap=zt[:], constant=0.0)
            I("pool", nc.gpsimd.memset, [], [kzh], ap=zh[:], constant=0.0)
            for t in range(NT, TT):
                P.dma("sp", x1_o[t * 128:(t + 1) * 128, :], zt[:], reads=[kzt])
                P.dma("sp", h2_o[t * 128:(t + 1) * 128, :], zh[:], reads=[kzh])
                P.dma("sp", aff_o[t * 128:(t + 1) * 128, :], zt[:, 0:16], reads=[kzt])
                P.dma("sp", affT_o[:, t * 128:(t + 1) * 128], zt[0:16, 0:128], reads=[kzt])
    P.emit()
    P.close()
    return nc


NE = 16
CAPM, CAPC = 120, 32
NIT = 26


def build_C(last, nexp=NE, nc=None, T=None, tag="", skip_bisect=False):
    if nc is None:
        nc = bass.Bass("TRN2", target_bir_lowering=False)

    def din(name, shape, dt=F32):
        if T is not None:
            assert tuple(T[name].shape) == tuple(shape), (name, T[name].shape, shape)
            return T[name]
        return nc.dram_tensor(name, list(shape), dt, kind="ExternalInput").ap()

    x1_d = din("x1", [TT * 128, 1024])
    h2_d = din("h2", [TT * 128, 1024], BF16)
    aff_d = din("aff", [TT * 128, 16])
    affT_d = din("affT_all", [16, 8192])
    affTc_d = din("affTc", [16, 256])
    ada_d = din("ada", [2, 6144])
    wg_d = din("w_gate", [NE, 1024, 512])
    wu_d = din("w_up", [NE, 1024, 512])
    wd_d = din("w_down", [NE, 512, 1024])
    utri_d = din("utri", [128, 128])
    ones_d = din("ones", [128, 128])
    iota_d = din("iota", [128, 128])
    blk_d = din("blockones", [128, 128])
    thrc_d = din("thrc", [128, 2])
    x2_o = T["x2"] if T is not None else nc.dram_tensor("x2", [TT * 128, 1024], F32, kind="ExternalOutput").ap()
    thr_scr = T["thr_scr"] if (T is not None and "thr_scr" in T) else nc.dram_tensor(tag + "thr_scr", [128, 2], F32, kind="Internal").ap()

    P = Prog(nc, tag)
    I = P.I
    ident = make_ident(P, nc)
    ntile = NT if last else TT
    groups = [(g, [4 * g + j for j in range(4)], CAPM, g * CAPM) for g in range(4)]
    if not last:
        groups.append((4, [NT, NT + 1], CAPC, 4 * CAPM))
    nslots = 4 * CAPM + (0 if last else CAPC)
    tile_group = {}
    for (g, tiles, cap, s0) in groups:
        for t in tiles:
            tile_group[t] = (g, cap, s0)

    utri = P.sb([128, 128], BF16, "utri")
    P.dma("pool", utri[:], utri_d, writes=["utri"])
    onesb = P.sb([128, 128], BF16, "onesb")
    P.dma("pool", onesb[:], ones_d, writes=["onesb"])
    iota = P.sb([128, 128], F32, "iota")
    P.dma("sp", iota[:], iota_d, writes=["iota"])
    blk = P.sb([128, 128], F32, "blk")
    P.dma("sp", blk[:], blk_d, writes=["blk"])
    thrc = P.sb([128, 2], F32, "thrc")
    P.dma("sp", thrc[:], thrc_d, writes=["thrc"])
    acc = P.sb([128, TT, 1024], F32, "acc")
    Am = acc[:, 0, :]
    if not skip_bisect:
        P.dma("sp", Am, affT_d.rearrange("e (s t) -> (e s) t", s=8), writes=["acc0"])
    Ac = P.sb([128, 32], F32, "Ac")
    P.dma("sp", Ac[:], affTc_d.rearrange("e (s t) -> (e s) t", s=8), writes=["Ac"])
    affs = P.sb([128, TT, 16], F32, "affs")
    P.dma("sp", affs[:], aff_d.rearrange("(t p) e -> p t e", p=128), writes=["affs"])
    h2tok = P.sb([128, TT, 1024], BF16, "h2tok")
    for t0 in range(0, TT, 6):
        P.dma("sp", h2tok[:, t0:t0 + 6, :], h2_d.rearrange("(t p) d -> p t d", p=128)[:, t0:t0 + 6, :], writes=[f"h2tok{t0}"])
    h2keys = [f"h2tok{t0}" for t0 in range(0, TT, 6)]
    g2 = []
    for r in range(2):
        md = P.sb([128, 1024], F32, f"g2_{r}")
        P.dma("sp", md[:], ada_d[r:r + 1, 5120:6144].partition_broadcast(128), writes=[f"g2_{r}"])
        g2.append(md)

    pp = P.ps([128, 512], F32, "pp")
    psTb = P.ps([128, 1024], BF16, "psTb")
    gA = P.ps([128, 512], F32, "gA")
    gB = P.ps([128, 512], F32, "gB")
    Gb = P.ps([128, 512], F32, "Gb")
    Ub = P.ps([128, 512], F32, "Ub")
    yA = P.ps([128, 512], F32, "yA")
    yB = P.ps([128, 512], F32, "yB")

    hidT = P.sb([128, 4, 512], BF16, "hidT")
    junk = hidT[:, 0:2, :].rearrange("p a b -> p (a b)")
    lo = P.sb([128, 2], F32, "lo")
    negmid = P.sb([128, 2], F32, "negmid")
    ssum = P.sb([128, 2], F32, "ssum")
    cond = P.sb([128, 2], F32, "cond")
    I("dve", nc.vector.memset, [], ["lo"], ap=lo[:], constant=0.0)
    I("dve", nc.vector.memset, [], ["negmid"], ap=negmid[:], constant=-0.5)
    I("dve", nc.vector.memset, [], ["ssum0", "ssum1"], ap=ssum[:], constant=0.0)
    for it in range(0 if skip_bisect else NIT):
        w = 2.0 ** -(it + 1)
        I("act", nc.scalar.activation, ["acc0", "negmid"], ["hidT", "ssum0"], out=junk[:, 0:1024], in_=Am, func=AF.Sign, bias=negmid[:, 0:1], scale=1.0,
          accum_out=ssum[:, 0:1])
        I("act", nc.scalar.activation, ["Ac", "negmid"], ["hidT", "ssum1"], out=junk[:, 0:32], in_=Ac[:], func=AF.Sign, bias=negmid[:, 1:2], scale=1.0,
          accum_out=ssum[:, 1:2])
        I("pe", nc.tensor.matmul, ["blk", "ssum0", "ssum1"], ["pp"], out=pp[:, 0:2], lhsT=blk[:], rhs=ssum[:], start=True, stop=True)
        I("dve", nc.vector.tensor_tensor, ["pp", "thrc"], ["cond"], out=cond[:], in0=pp[:, 0:2], in1=thrc[:], op=ALU.is_ge)
        I("dve", nc.vector.scalar_tensor_tensor, ["cond", "lo"], ["lo"], out=lo[:], in0=cond[:], scalar=w, in1=lo[:], op0=ALU.mult, op1=ALU.add)
        I("dve", nc.vector.tensor_scalar, ["lo"], ["negmid"], out=negmid[:], in0=lo[:], scalar1=w * 0.5, scalar2=-1.0, op0=ALU.add, op1=ALU.mult)
    if not skip_bisect:
        P.dma("sp", thr_scr, lo[:], reads=["lo"], writes=["thr_scr"])
    thr_all = P.sb([128, 256], F32, "thr_all")
    P.dma("sp", thr_all[:], thr_scr.rearrange("p c -> (p c)").unsqueeze(0).partition_broadcast(128), reads=["thr_scr"], writes=["thr_all"])
    thr_v = thr_all[:].rearrange("p (e s c) -> p e s c", s=8, c=2)

    maskf = P.sb([128, TT, 16], F32, "maskf")
    maskb = P.sb([128, TT, 16], BF16, "maskb")
    gm = P.sb([128, TT, 16], F32, "gm")
    pos = P.sb([128, TT, 16], F32, "pos")
    for t in range(ntile):
        col = 1 if t >= NT else 0
        I("dve", nc.vector.tensor_tensor, ["affs", "thr_all"], ["maskf"], out=maskf[:, t, :], in0=affs[:, t, :], in1=thr_v[:, :, 0, col], op=ALU.is_ge)
    I("dve", nc.vector.tensor_copy, ["maskf"], ["maskb"], out=maskb[:, 0:ntile, :], in_=maskf[:, 0:ntile, :])
    I("dve", nc.vector.tensor_tensor, ["maskf", "affs"], ["gm"], out=gm[:, 0:ntile, :], in0=maskf[:, 0:ntile, :], in1=affs[:, 0:ntile, :], op=ALU.mult)
    for (g, tiles, cap, s0) in groups:
        for jj, t in enumerate(tiles):
            for i in range(jj + 1):
                I("pe", nc.tensor.matmul, ["maskb", "utri", "onesb"], ["pp"], out=pp[:, 16 * (t % 16):16 * (t % 16) + 16],
                  lhsT=(utri[:] if i == jj else onesb[:]), rhs=maskb[:, tiles[i], :], start=(i == 0), stop=(i == jj))
            I("dve", nc.vector.tensor_copy, ["pp"], ["pos"], out=pos[:, t, :], in_=pp[:, 16 * (t % 16):16 * (t % 16) + 16])

    wgr = Rot(P, 2, [128, 8, 512], BF16, "wg")
    wur = Rot(P, 2, [128, 8, 512], BF16, "wu")
    wdr = Rot(P, 1, [128, 4, 1024], BF16, "wd")
    Sall = P.sb([128, TT, 128], BF16, "Sall")
    Sgall = P.sb([128, TT, 128], BF16, "Sgall")
    SgT = P.sb([128, TT, 128], BF16, "SgT")
    I("pool", nc.gpsimd.memset, [], [f"S{t}" for t in range(TT)], ap=Sall[:], constant=0.0)
    I("pool", nc.gpsimd.memset, [], [f"Sg{t}" for t in range(TT)], ap=Sgall[:], constant=0.0)
    xsT = P.sb([128, 8, 512], BF16, "xsT")
    sgt = P.sb([128, 512], F32, "sgt")
    yr = Rot(P, 1, [128, 5, 1024], BF16, "ysb")
    cnt = 0
    W, Y = {}, {}

    def load_w(e, which):
        if which == "gu":
            wg, kwg = wgr.next()
            wu, kwu = wur.next()
            W[e] = [wg, kwg, wu, kwu, None, None]
            for hk in range(2):
                P.dma("pool", wg[:, hk * 4:(hk + 1) * 4, :], wg_d[e].rearrange("(k p) f -> p k f", p=128)[:, hk * 4:(hk + 1) * 4, :], writes=[f"{kwg}_{hk}"])
                P.dma("pool", wu[:, hk * 4:(hk + 1) * 4, :], wu_d[e].rearrange("(k p) f -> p k f", p=128)[:, hk * 4:(hk + 1) * 4, :], writes=[f"{kwu}_{hk}"])
        else:
            wd, kwd = wdr.next()
            W[e][4], W[e][5] = wd, kwd
            for hk in range(2):
                P.dma("pool", wd[:, hk * 2:(hk + 1) * 2, :], wd_d[e].rearrange("(k p) d -> p k d", p=128)[:, hk * 2:(hk + 1) * 2, :], writes=[f"{kwd}_{hk}"])

    def build_S(e):
        for t in range(ntile):
            g, cap, s0 = tile_group[t]
            I("dve", nc.vector.tensor_scalar, ["iota", "pos", "maskf"], [f"S{t}"], out=Sall[:, t, 0:cap], in0=iota[:, 0:cap], scalar1=pos[:, t, e:e + 1],
              scalar2=maskf[:, t, e:e + 1], op0=ALU.is_equal, op1=ALU.mult)

    def build_Sg(e):
        for t in range(ntile):
            g, cap, s0 = tile_group[t]
            I("dve", nc.vector.tensor_scalar, ["iota", "pos", "gm"], [f"Sg{t}"], out=Sgall[:, t, 0:cap], in0=iota[:, 0:cap], scalar1=pos[:, t, e:e + 1],
              scalar2=gm[:, t, e:e + 1], op0=ALU.is_equal, op1=ALU.mult)

    def gather(e):
        for (g, tiles, cap, s0) in groups:
            for dk in range(8):
                bank, kb = (gA, "gA") if dk < 4 else (gB, "gB")
                o0 = (dk % 4) * cap
                for jj, t in enumerate(tiles):
                    I("pe", nc.tensor.matmul, [h2keys[t // 6], f"S{t}"], [kb], out=bank[:, o0:o0 + cap], lhsT=h2tok[:, t, dk * 128:(dk + 1) * 128],
                      rhs=Sall[:, t, 0:cap], start=(jj == 0), stop=(jj == len(tiles) - 1))
            I("act", nc.scalar.copy, ["gA"], ["xsT"], out=xsT[:, 0:4, s0:s0 + cap], in_=gA[:, 0:4 * cap].rearrange("p (k s) -> p k s", k=4))
            I("act", nc.scalar.copy, ["gB"], ["xsT"], out=xsT[:, 4:8, s0:s0 + cap], in_=gB[:, 0:4 * cap].rearrange("p (k s) -> p k s", k=4))

    def ffn(e):
        wg, kwg, wu, kwu, _, _ = W[e]
        for fk in range(4):
            for k in range(8):
                I("pe", nc.tensor.matmul, ["xsT", f"{kwg}_{k // 4}"], ["Gb"], out=Gb[:, 0:nslots], lhsT=wg[:, k, fk * 128:(fk + 1) * 128], rhs=xsT[:, k, 0:nslots],
                  start=(k == 0), stop=(k == 7))
            for k in range(8):
                I("pe", nc.tensor.matmul, ["xsT", f"{kwu}_{k // 4}"], ["Ub"], out=Ub[:, 0:nslots], lhsT=wu[:, k, fk * 128:(fk + 1) * 128], rhs=xsT[:, k, 0:nslots],
                  start=(k == 0), stop=(k == 7))
            I("act", nc.scalar.activation, ["Gb"], ["sgt"], out=sgt[:, 0:nslots], in_=Gb[:, 0:nslots], func=AF.Silu)
            I("dve", nc.vector.tensor_tensor, ["sgt", "Ub"], ["hidT"], out=hidT[:, fk, 0:nslots], in0=sgt[:, 0:nslots], in1=Ub[:, 0:nslots], op=ALU.mult)

    def down(e):
        wd, kwd = W[e][4], W[e][5]
        ysb, kys = yr.next()
        Y[e] = (ysb, kys)
        for (g, tiles, cap, s0) in groups:
            for nb, (bank, kb) in enumerate([(yA, "yA"), (yB, "yB")]):
                for fk in range(4):
                    I("pe", nc.tensor.matmul, ["hidT", f"{kwd}_{fk // 2}"], [kb], out=bank[0:cap, :], lhsT=hidT[:, fk, s0:s0 + cap],
                      rhs=wd[:, fk, nb * 512:(nb + 1) * 512], start=(fk == 0), stop=(fk == 3))
            I("act", nc.scalar.copy, ["yA"], [f"{kys}_{g}"], out=ysb[0:cap, g, 0:512], in_=yA[0:cap, :])
            I("act", nc.scalar.copy, ["yB"], [f"{kys}_{g}"], out=ysb[0:cap, g, 512:1024], in_=yB[0:cap, :])

    def sgtr(e):
        for t0 in range(0, ntile, 8):
            tl = list(range(t0, min(t0 + 8, ntile)))
            for t in tl:
                I("pe", nc.tensor.transpose, [f"Sg{t}", "ident"], ["psTb"], out=psTb[:, (t - t0) * 128:(t - t0 + 1) * 128], in_=Sgall[:, t, :], identity=ident[:])
            I("act", nc.scalar.copy, ["psTb"], [f"SgT{t}" for t in tl], out=SgT[:, t0:t0 + len(tl), :],
              in_=psTb[:, 0:len(tl) * 128].rearrange("p (t s) -> p t s", s=128))

    sbanks = [(gA, "gA"), (gB, "gB"), (yA, "yA"), (yB, "yB")]

    def scatter(e):
        ysb, kys = Y[e]
        i = 0
        for t in range(ntile):
            g, cap, s0 = tile_group[t]
            for nb in range(2):
                bank, kb = sbanks[i % 4]
                i += 1
                I("pe", nc.tensor.matmul, [f"SgT{t}", f"{kys}_{g}"], [kb], out=bank[:], lhsT=SgT[0:cap, t, :], rhs=ysb[0:cap, g, nb * 512:(nb + 1) * 512],
                  start=True, stop=True)
                cs = slice(nb * 512, (nb + 1) * 512)
                if e == 0:
                    I("dve", nc.vector.tensor_copy, [kb], [f"acc{t}"], out=acc[:, t, cs], in_=bank[:])
                else:
                    I("dve", nc.vector.tensor_tensor, [kb, f"acc{t}"], [f"acc{t}"], out=acc[:, t, cs], in0=acc[:, t, cs], in1=bank[:], op=ALU.add)

    load_w(0, "gu")
    load_w(0, "d")
    build_S(0)
    build_Sg(0)
    for e in range(nexp):
        if e + 1 < nexp:
            load_w(e + 1, "gu")
        gather(e)
        if e + 1 < nexp:
            build_S(e + 1)
        ffn(e)
        down(e)
        if e + 1 < nexp:
            load_w(e + 1, "d")
        sgtr(e)
        if e + 1 < nexp:
            build_Sg(e + 1)
        scatter(e)
    x1r = Rot(P, 1, [128, 1024], F32, "x1t")
    for t in range(ntile):
        x1t, kx1 = x1r.next()
        P.dma("sp", x1t[:], x1_d[t * 128:(t + 1) * 128, :], writes=[kx1])
        r = 1 if t >= NT else 0
        I("dve", nc.vector.tensor_tensor, [f"acc{t}", f"g2_{r}"], [f"acc{t}"], out=acc[:, t, :], in0=acc[:, t, :], in1=g2[r][:], op=ALU.mult)
        I("dve", nc.vector.tensor_tensor, [f"acc{t}", kx1], [f"acc{t}"], out=acc[:, t, :], in0=acc[:, t, :], in1=x1t[:], op=ALU.add)
        P.dma("sp", x2_o[t * 128:(t + 1) * 128, :], acc[:, t, :], reads=[f"acc{t}"])
    if last:
        for t in range(NT, TT):
            I("pool", nc.gpsimd.memset, [], [f"acc{t}"], ap=acc[:, t, :], constant=0.0)
            P.dma("sp", x2_o[t * 128:(t + 1) * 128, :], acc[:, t, :], reads=[f"acc{t}"])
    P.emit()
    P.close()
    return nc


BF = ml_dtypes.bfloat16
NCORES = 8
TPC = 2048
NCTX = 256
SEQ = 8192


def f32(a):
    return np.ascontiguousarray(a, dtype=np.float32)


def const_dftc():
    c = np.arange(64)
    ang = 2 * np.pi * np.outer(c, c) / 64.0
    C, S = np.cos(ang), np.sin(ang)
    d = np.zeros((256, 512), np.float64)
    for g in range(4):
        d[g * 64:(g + 1) * 64, g * 64:(g + 1) * 64] = C
        d[g * 64:(g + 1) * 64, 256 + g * 64:256 + (g + 1) * 64] = -S
    return f32(d)


def const_rope(core):
    tok0 = (core % 4) * TPC
    n = np.arange(tok0, tok0 + TPC)
    pos_row = (n // 64).astype(np.float32)
    pos_col = (n % 64).astype(np.float32)
    freqs = (np.float32(10000.0) ** (-np.arange(16, dtype=np.float32) / np.float32(16))).astype(np.float32)
    cos = np.ones((TPC + NCTX, 64), np.float32)
    sin = np.zeros((TPC + NCTX, 64), np.float32)
    for b, pos in enumerate([pos_row, pos_col]):
        ang = (pos[:, None] * freqs[None, :]).astype(np.float32)
        cs, sn = np.cos(ang).astype(np.float32), np.sin(ang).astype(np.float32)
        cos[:TPC, b * 32:b * 32 + 16] = cs
        cos[:TPC, b * 32 + 16:b * 32 + 32] = cs
        sin[:TPC, b * 32:b * 32 + 16] = -sn
        sin[:TPC, b * 32 + 16:b * 32 + 32] = sn
    return cos, sin


def inputs_A(inp, l, x_cur, xc_cur):
    dftc = const_dftc()
    shared = {
        "w_ada": f32(inp["w_ada"][l]), "b_ada": f32(inp["b_ada"][l][None, :]), "w_in": f32(inp["w_in"][l]),
        "w_sguT": f32(np.transpose(inp["w_sgu"][l], (2, 0, 1))),
        "b_sguT": f32(inp["b_sgu"][l].T),
        "sgu_norm": f32(inp["sgu_norm"][l][None]), "q_lora_norm": f32(inp["q_lora_norm"][l][None]),
        "kv_lora_norm": f32(inp["kv_lora_norm"][l][None]), "q_norm": f32(inp["q_norm"][l][None]), "k_norm": f32(inp["k_norm"][l][None]),
        "w_uq": f32(inp["w_uq"][l]), "w_ukv": f32(inp["w_ukv"][l]), "dftc": dftc,
    }
    maps = []
    for core in range(NCORES):
        b, tok0 = core // 4, (core % 4) * TPC
        cos, sin = const_rope(core)
        cT = np.stack([np.asarray(inp["c"][b]).reshape(8, 128).T, np.asarray(inp["c_ctx"]).reshape(8, 128).T], axis=-1)
        m = dict(shared)
        m.update({"xin": f32(np.concatenate([x_cur[b, tok0:tok0 + TPC], xc_cur[b]], 0)), "cT": f32(cT), "rope_cos": cos, "rope_sin": sin})
        maps.append(m)
    return maps


def const_fft(core):
    n1 = np.arange(64)
    ang = 2 * np.pi * np.outer(n1, n1) / 64.0
    C, S = np.cos(ang), np.sin(ang)
    WA = np.zeros((128, 128))
    WA[0:64, 0:64] = C; WA[64:128, 0:64] = S; WA[0:64, 64:128] = -S; WA[64:128, 64:128] = C
    k2_0 = 32 * (core % 4)
    n2 = np.arange(128)[:, None, None]
    k1 = np.arange(64)[None, :, None]
    k2 = (k2_0 + np.arange(32))[None, None, :]
    k = k1 + 64 * k2
    th = 2 * np.pi * ((n2 * k) % 8192) / 8192.0
    nrm = 1.0 / np.sqrt(8192.0 * 64.0)
    TC = np.stack([np.cos(th) * nrm, np.sin(th) * nrm], axis=1)
    n = (np.arange(2)[None, :, None] * 128 + np.arange(128)[:, None, None])
    kk = np.arange(256)[None, None, :]
    thc = 2 * np.pi * ((n * kk) % 256) / 256.0
    nrc = 1.0 / np.sqrt(256.0 * 64.0)
    TCc = np.stack([np.cos(thc) * nrc, np.sin(thc) * nrc], axis=2)
    return f32(WA), f32(TC), f32(TCc)


def const_moe():
    k = np.arange(128)
    utri = (k[:, None] < k[None, :]).astype(np.float32)
    ones = np.ones((128, 128), np.float32)
    iota = np.tile(np.arange(128, dtype=np.float32)[None, :], (128, 1))
    blk = ((k[:, None] // 8) == (k[None, :] // 8)).astype(np.float32)
    thrc = np.tile(np.array([[2 * 1024 - 8192, 2 * 32 - 256]], np.float32), (128, 1))
    return {"utri": utri, "ones": ones, "iota": iota, "blockones": blk, "thrc": thrc}


def inputs_fused(inp):
    ropes = [const_rope(v) for v in range(4)]
    ffts = [const_fft(v) for v in range(4)]
    shared = {
        "w_ada": f32(inp["w_ada"]), "b_ada": f32(inp["b_ada"]), "w_in": f32(inp["w_in"]),
        "w_sguT": f32(np.transpose(inp["w_sgu"], (0, 3, 1, 2))), "b_sguT": f32(np.transpose(inp["b_sgu"], (0, 2, 1))),
        "sgu_norm": f32(inp["sgu_norm"]), "q_lora_norm": f32(inp["q_lora_norm"]), "kv_lora_norm": f32(inp["kv_lora_norm"]),
        "q_norm": f32(inp["q_norm"]), "k_norm": f32(inp["k_norm"]), "w_uq": f32(inp["w_uq"]), "w_ukv": f32(inp["w_ukv"]),
        "dftc": const_dftc(), "rope_cos": f32(np.stack([r[0] for r in ropes])), "rope_sin": f32(np.stack([r[1] for r in ropes])),
        "WA": ffts[0][0], "TC": f32(np.stack([f[1] for f in ffts])), "TCc": ffts[0][2],
        "w_out": f32(inp["w_out"]), "w_router": f32(inp["w_router"]), "w_gate": f32(inp["w_gate"]), "w_up": f32(inp["w_up"]),
        "w_down": f32(inp["w_down"]),
    }
    shared.update(const_moe())
    maps = []
    for b in range(2):
        cT = np.stack([np.asarray(inp["c"][b]).reshape(8, 128).T, np.asarray(inp["c_ctx"]).reshape(8, 128).T], axis=-1)
        m = dict(shared)
        m.update({"x": f32(inp["x"][b]), "ctx": f32(inp["ctx"][b]), "cT": f32(cT)})
        maps.append(m)
    return maps


def gather_kv(resA):
    KTs, Vs = [], []
    for b in range(2):
        cores = [resA[b * 4 + i] for i in range(4)]
        KTs.append(np.ascontiguousarray(np.concatenate([np.asarray(cores[0]["KT"])[:, :, TPC:]] + [np.asarray(c["KT"])[:, :, :TPC] for c in cores], axis=2)))
        Vs.append(np.ascontiguousarray(np.concatenate([np.asarray(cores[0]["V"])[TPC:]] + [np.asarray(c["V"])[:TPC] for c in cores], axis=0)))
    return KTs, Vs


def run(nc, maps):
    return run_bass_kernel_spmd(nc, maps, core_ids=list(range(NCORES))).results


def kernel(**inputs):
    inp = {k: np.asarray(v) for k, v in inputs.items()}
    x_cur = f32(inp["x"]).copy()
    xc_cur = f32(inp["ctx"]).copy()
    cm = const_moe()
    cf = [const_fft(c) for c in range(NCORES)]
    for l in range(2):
        last = l == 1
        mapsA = inputs_A(inp, l, x_cur, xc_cur)
        rA = run(build_A(last), mapsA)
        Zall = [np.ascontiguousarray(np.concatenate([np.asarray(rA[b * 4 + i]["Z"])[:TPC] for i in range(4)], 0)) for b in range(2)]
        mapsF = [{"Z_all": Zall[c // 4], "Zc": np.ascontiguousarray(np.asarray(rA[c]["Z"])[TPC:]), "WA": cf[c][0], "TC": cf[c][1], "TCc": cf[c][2]}
                 for c in range(NCORES)]
        rF = run(build_F(last), mapsF)
        KTs, Vs = gather_kv(rA)
        mapsB = [{"xin": mapsA[c]["xin"], "ada": np.asarray(rA[c]["ada"]), "yaT": np.asarray(rA[c]["yaT"]), "ybT": np.asarray(rF[c]["ybT"]),
                  "QT": np.asarray(rA[c]["QT"]), "KT_all": KTs[c // 4], "V_all": Vs[c // 4], "w_out": f32(inp["w_out"][l]),
                  "w_router": f32(inp["w_router"][l])} for c in range(NCORES)]
        rB = run(build_B(last), mapsB)
        affT = [np.ascontiguousarray(np.concatenate([np.asarray(rB[b * 4 + i]["aff"])[:TPC] for i in range(4)], 0).T) for b in range(2)]
        wg, wu, wd = f32(inp["w_gate"][l]), f32(inp["w_up"][l]), f32(inp["w_down"][l])
        mapsC = []
        for c in range(NCORES):
            m = {"x1": np.asarray(rB[c]["x1"]), "h2": np.asarray(rB[c]["h2"]), "aff": np.asarray(rB[c]["aff"]), "affT_all": affT[c // 4],
                 "affTc": np.ascontiguousarray(np.asarray(rB[c]["aff"])[TPC:].T), "ada": np.asarray(rA[c]["ada"]),
                 "w_gate": wg, "w_up": wu, "w_down": wd}
            m.update(cm)
            mapsC.append(m)
        rC = run(build_C(last), mapsC)
        for c in range(NCORES):
            b, tok0 = c // 4, (c % 4) * TPC
            x2 = np.asarray(rC[c]["x2"])
            x_cur[b, tok0:tok0 + TPC] = x2[:TPC]
            if c % 4 == 0 and not last:
                xc_cur[b] = x2[TPC:]
    return x_cur
```
